# Optimizing a Trainium2 kernel written in Bass

```python
import jax, jax.numpy as jnp
from jax import lax
import numpy as np

D_MODEL = 1024
BATCH = 8
SEQ = 2048
DEPTH = 2
DEC_BATCH = 32
DEC_SEQ = 64
PAST_LEN = 4096

CHUNK = 64
Q_BLOCK = 128
LEFT_CHUNKS = 8
BAND_CHUNKS = LEFT_CHUNKS + 1
LEFT_CTX = LEFT_CHUNKS * CHUNK
REL_CLIP = 128
N_REL = 2 * REL_CLIP + 1
EPS = 1e-6
NEG = -1e30
ROPE_THETA = 10000.0
FORGET_BIAS_INIT = 3.0

A_HEADS = 8
A_Q_LORA = 256
A_KV_LORA = 128
A_NOPE = 64
A_ROPE = 32
A_V = 64
A_WIDTH = A_HEADS * A_V
A_SCALE = (A_NOPE + A_ROPE) ** -0.5
B_HEADS = 8
B_DIM = 64
B_WIDTH = B_HEADS * B_DIM
C_HEADS = 8
C_DIM = 64
C_WIDTH = C_HEADS * C_DIM
D_HEADS = 8
D_DIM = 64
D_WIDTH = D_HEADS * D_DIM

N_EVEN = (DEPTH + 1) // 2
N_ODD = DEPTH // 2
EVEN_SIZES = (A_Q_LORA, A_KV_LORA, A_ROPE, A_WIDTH, B_WIDTH, B_WIDTH, B_WIDTH, B_WIDTH)
EVEN_IN = A_Q_LORA + A_KV_LORA + A_ROPE + A_WIDTH + 4 * B_WIDTH
ODD_SIZES = (C_WIDTH, C_WIDTH, C_WIDTH, C_WIDTH, D_WIDTH, D_WIDTH, D_WIDTH, D_HEADS, D_WIDTH)
ODD_IN = 4 * C_WIDTH + 4 * D_WIDTH + D_HEADS

kernel_name = 'hybrid_streaming_encoder_step'


def rmsnorm(x, g):
    xf = x.astype(jnp.float32)
    y = xf * lax.rsqrt(jnp.mean(xf * xf, axis=-1, keepdims=True) + EPS)
    return (y * g.astype(jnp.float32)).astype(x.dtype)


def split_cols(t, sizes):
    out, off = [], 0
    for n in sizes:
        out.append(t[..., off:off + n])
        off += n
    return out


def cat(a, b):
    return jnp.concatenate([a, b.astype(a.dtype)], axis=1)


def apply_rope(x, pos):
    half = x.shape[-1] // 2
    inv_freq = ROPE_THETA ** (-jnp.arange(half, dtype=jnp.float32) / half)
    ang = pos.astype(jnp.float32)[:, None] * inv_freq[None, :]
    shape = (1, pos.shape[0]) + (1,) * (x.ndim - 3) + (half,)
    cos, sin = jnp.cos(ang).reshape(shape), jnp.sin(ang).reshape(shape)
    xf = x.astype(jnp.float32)
    x1, x2 = xf[..., :half], xf[..., half:]
    return jnp.concatenate([x1 * cos - x2 * sin, x2 * cos + x1 * sin], axis=-1).astype(x.dtype)


def sweep_query_blocks(fn, q_args, qpos):
    n_blk = qpos.shape[0] // Q_BLOCK
    def to_blocks(a):
        return jnp.moveaxis(a.reshape((a.shape[0], n_blk, Q_BLOCK) + a.shape[2:]), 1, 0)
    blocks = tuple(to_blocks(a) for a in q_args) + (qpos.reshape(n_blk, Q_BLOCK),)
    out = jnp.moveaxis(lax.map(lambda xs: fn(*xs), blocks), 0, 1)
    return out.reshape((out.shape[0], n_blk * Q_BLOCK) + out.shape[3:])


def mla_attend(q_lat, q_rope, qpos, c_kv, k_rope, kpos):
    s = (jnp.einsum('bqhl,bkl->bhqk', q_lat, c_kv, preferred_element_type=jnp.float32)
         + jnp.einsum('bqhr,bkr->bhqk', q_rope, k_rope, preferred_element_type=jnp.float32)) * A_SCALE
    mask = (kpos[None, :] // CHUNK) <= (qpos[:, None] // CHUNK)
    p = jax.nn.softmax(jnp.where(mask, s, NEG), axis=-1).astype(c_kv.dtype)
    return jnp.einsum('bhqk,bkl->bqhl', p, c_kv)


def sb_attend(q, k, v, qpos, kpos):
    z = jnp.einsum('bqhd,bkhd->bhqk', q, k, preferred_element_type=jnp.float32) * (B_DIM ** -0.5)
    causal = kpos[None, :] < qpos[:, None]
    log_beta = jax.nn.log_sigmoid(z)
    log_1mb = jnp.where(causal, log_beta - z, 0.0)
    suffix = lax.cumsum(log_1mb, axis=3, reverse=True) - log_1mb
    w = jnp.where(causal, jnp.exp(log_beta + suffix), 0.0)
    return jnp.einsum('bhqk,bkhd->bqhd', w.astype(v.dtype), v)


def band_attend(q, k, v, qpos, kpos, rel_bias):
    s = jnp.einsum('bqhd,bkhd->bhqk', q, k, preferred_element_type=jnp.float32) * (C_DIM ** -0.5)
    rel = jnp.clip(qpos[:, None] - kpos[None, :], -REL_CLIP, REL_CLIP) + REL_CLIP
    s = s + rel_bias.astype(jnp.float32)[:, rel][None]
    qc, kc = qpos[:, None] // CHUNK, kpos[None, :] // CHUNK
    mask = (kpos[None, :] >= 0) & (kc <= qc) & (kc >= qc - LEFT_CHUNKS)
    p = jax.nn.softmax(jnp.where(mask, s, NEG), axis=-1).astype(v.dtype)
    return jnp.einsum('bhqk,bkhd->bqhd', p, v)


def fox_attend(q, k, v, cum_q, cum_k, qpos, kpos):
    s = jnp.einsum('bqhd,bkhd->bhqk', q, k, preferred_element_type=jnp.float32) * (D_DIM ** -0.5)
    s = s + (jnp.swapaxes(cum_q, 1, 2)[:, :, :, None] - jnp.swapaxes(cum_k, 1, 2)[:, :, None, :])
    mask = kpos[None, :] <= qpos[:, None]
    p = jax.nn.softmax(jnp.where(mask, s, NEG), axis=-1).astype(v.dtype)
    return jnp.einsum('bhqk,bkhd->bqhd', p, v)


def even_mixer(h, pos, past, w_in, q_norm, w_uq, kv_norm, w_uk, w_uv, w_out):
    n_b, n_s, _ = h.shape
    q_a, kv_a, k_r, g_a, q_b, k_b, v_b, g_b = split_cols(h @ w_in, EVEN_SIZES)
    c_q = rmsnorm(q_a, q_norm)
    q = jnp.einsum('bsc,chr->bshr', c_q, w_uq)
    q_lat = jnp.einsum('bshn,chn->bshc', q[..., :A_NOPE], w_uk)
    q_rope = apply_rope(q[..., A_NOPE:], pos)
    c_kv = rmsnorm(kv_a, kv_norm)
    k_rope = apply_rope(k_r, pos)
    heads = lambda t: t.reshape(n_b, n_s, B_HEADS, B_DIM)
    q_b, k_b, v_b = heads(q_b), heads(k_b), heads(v_b)
    if past is None:
        lat = sweep_query_blocks(lambda ql, qr, qp: mla_attend(ql, qr, qp, c_kv, k_rope, pos), (q_lat, q_rope), pos)
        sb = sweep_query_blocks(lambda qq, qp: sb_attend(qq, k_b, v_b, qp, pos), (q_b,), pos)
    else:
        p_ckv, p_kr, p_k, p_v = past
        kpos = jnp.arange(p_ckv.shape[1] + n_s)
        lat = mla_attend(q_lat, q_rope, pos, cat(p_ckv, c_kv), cat(p_kr, k_rope), kpos)
        sb = sb_attend(q_b, cat(p_k, k_b), cat(p_v, v_b), pos, kpos)
    out_a = jnp.einsum('bshc,chv->bshv', lat, w_uv).reshape(n_b, n_s, A_WIDTH)
    mixed = jnp.concatenate([jax.nn.silu(g_a) * out_a,
                             jax.nn.silu(g_b) * sb.reshape(n_b, n_s, B_WIDTH)], axis=-1)
    return mixed @ w_out, (c_kv, k_rope, k_b, v_b)


def odd_mixer(h, pos, past, w_in, rel_bias, forget_bias, w_out):
    n_b, n_s, _ = h.shape
    q_c, k_c, v_c, g_c, q_d, k_d, v_d, f_d, g_d = split_cols(h @ w_in, ODD_SIZES)
    hc = lambda t: t.reshape(n_b, n_s, C_HEADS, C_DIM)
    hd = lambda t: t.reshape(n_b, n_s, D_HEADS, D_DIM)
    q_c, k_c, v_c = hc(q_c), hc(k_c), hc(v_c)
    q_d, k_d, v_d = hd(q_d), hd(k_d), hd(v_d)
    log_f = jax.nn.log_sigmoid(f_d.astype(jnp.float32) + forget_bias.astype(jnp.float32))
    if past is None:
        n_c = n_s // CHUNK
        band_idx = (jnp.arange(n_c) * CHUNK)[:, None] + jnp.arange(BAND_CHUNKS * CHUNK)[None, :]
        pad = ((0, 0), (LEFT_CTX, 0), (0, 0), (0, 0))
        k_band = jnp.pad(k_c, pad)[:, band_idx]
        v_band = jnp.pad(v_c, pad)[:, band_idx]
        per_chunk = jax.vmap(band_attend, in_axes=(1, 1, 1, 0, 0, None), out_axes=1)
        band = per_chunk(q_c.reshape(n_b, n_c, CHUNK, C_HEADS, C_DIM), k_band, v_band,
                         pos.reshape(n_c, CHUNK), band_idx - LEFT_CTX, rel_bias)
        band = band.reshape(n_b, n_s, C_HEADS, C_DIM)
        cum = jnp.cumsum(log_f, axis=1)
        fox = sweep_query_blocks(lambda qq, cq, qp: fox_attend(qq, k_d, v_d, cq, cum, qp, pos), (q_d, cum), pos)
        keep = min(LEFT_CTX, n_s)
        band_state = (k_c[:, n_s - keep:], v_c[:, n_s - keep:])
    else:
        p_bk, p_bv, p_fk, p_fv, p_lf = past
        n_keep, n_past = p_bk.shape[1], p_fk.shape[1]
        bk, bv = cat(p_bk, k_c), cat(p_bv, v_c)
        kpos_band = jnp.concatenate([n_past - n_keep + jnp.arange(n_keep), pos])
        band = band_attend(q_c, bk, bv, pos, kpos_band, rel_bias)
        cum = jnp.cumsum(jnp.concatenate([p_lf.astype(jnp.float32), log_f], axis=1), axis=1)
        fox = fox_attend(q_d, cat(p_fk, k_d), cat(p_fv, v_d), cum[:, n_past:], cum, pos,
                         jnp.arange(n_past + n_s))
        band_state = (bk[:, n_s:], bv[:, n_s:])
    mixed = jnp.concatenate([jax.nn.silu(g_c) * band.reshape(n_b, n_s, C_WIDTH),
                             jax.nn.silu(g_d) * fox.reshape(n_b, n_s, D_WIDTH)], axis=-1)
    return mixed @ w_out, band_state + (k_d, v_d, log_f)


def setup_inputs(seed: int = 0) -> dict:
    key = jax.random.key(seed)
    ks = jax.random.split(key, 26)
    nrm = lambda k, shape, scale=1.0: scale * jax.random.normal(k, shape, jnp.float32)
    cl = min(LEFT_CTX, PAST_LEN)
    return {
        'x_prompt': nrm(ks[0], (BATCH, SEQ, D_MODEL)),
        'x_sample': nrm(ks[1], (DEC_BATCH, DEC_SEQ, D_MODEL)),
        'cache_mla_ckv': nrm(ks[2], (N_EVEN, DEC_BATCH, PAST_LEN, A_KV_LORA)),
        'cache_mla_krope': nrm(ks[3], (N_EVEN, DEC_BATCH, PAST_LEN, A_ROPE)),
        'cache_sb_k': nrm(ks[4], (N_EVEN, DEC_BATCH, PAST_LEN, B_HEADS, B_DIM)),
        'cache_sb_v': nrm(ks[5], (N_EVEN, DEC_BATCH, PAST_LEN, B_HEADS, B_DIM)),
        'cache_band_k': nrm(ks[6], (N_ODD, DEC_BATCH, cl, C_HEADS, C_DIM)),
        'cache_band_v': nrm(ks[7], (N_ODD, DEC_BATCH, cl, C_HEADS, C_DIM)),
        'cache_fox_k': nrm(ks[8], (N_ODD, DEC_BATCH, PAST_LEN, D_HEADS, D_DIM)),
        'cache_fox_v': nrm(ks[9], (N_ODD, DEC_BATCH, PAST_LEN, D_HEADS, D_DIM)),
        'cache_fox_logf': jax.nn.log_sigmoid(nrm(ks[10], (N_ODD, DEC_BATCH, PAST_LEN, D_HEADS)) + FORGET_BIAS_INIT),
        'norm_pre': 1.0 + nrm(ks[11], (DEPTH, D_MODEL), 0.05),
        'norm_post': 1.0 + nrm(ks[12], (DEPTH, D_MODEL), 0.05),
        'w_in_even': nrm(ks[13], (N_EVEN, D_MODEL, EVEN_IN), D_MODEL ** -0.5),
        'a_q_norm': 1.0 + nrm(ks[14], (N_EVEN, A_Q_LORA), 0.05),
        'a_w_uq': nrm(ks[15], (N_EVEN, A_Q_LORA, A_HEADS, A_NOPE + A_ROPE), A_Q_LORA ** -0.5),
        'a_kv_norm': 1.0 + nrm(ks[16], (N_EVEN, A_KV_LORA), 0.05),
        'a_w_uk': nrm(ks[17], (N_EVEN, A_KV_LORA, A_HEADS, A_NOPE), A_KV_LORA ** -0.5),
        'a_w_uv': nrm(ks[18], (N_EVEN, A_KV_LORA, A_HEADS, A_V), A_KV_LORA ** -0.5),
        'w_out_even': nrm(ks[19], (N_EVEN, A_WIDTH + B_WIDTH, D_MODEL), (A_WIDTH + B_WIDTH) ** -0.5),
        'w_in_odd': nrm(ks[20], (N_ODD, D_MODEL, ODD_IN), D_MODEL ** -0.5),
        'c_rel_bias': nrm(ks[21], (N_ODD, C_HEADS, N_REL), 0.5),
        'd_forget_bias': FORGET_BIAS_INIT + nrm(ks[22], (N_ODD, D_HEADS), 0.1),
        'w_out_odd': nrm(ks[23], (N_ODD, C_WIDTH + D_WIDTH, D_MODEL), (C_WIDTH + D_WIDTH) ** -0.5),
    }


def reference(x_prompt, x_sample, cache_mla_ckv, cache_mla_krope, cache_sb_k, cache_sb_v,
              cache_band_k, cache_band_v, cache_fox_k, cache_fox_v, cache_fox_logf,
              norm_pre, norm_post, w_in_even, a_q_norm, a_w_uq, a_kv_norm, a_w_uk, a_w_uv,
              w_out_even, w_in_odd, c_rel_bias, d_forget_bias, w_out_odd):
    pos_p = jnp.arange(x_prompt.shape[1])
    pos_s = PAST_LEN + jnp.arange(x_sample.shape[1])
    xp, xs = x_prompt, x_sample
    even_p, even_s, odd_p, odd_s = [], [], [], []
    for l in range(DEPTH):
        i = l // 2
        hp, hs = rmsnorm(xp, norm_pre[l]), rmsnorm(xs, norm_pre[l])
        if l % 2 == 0:
            w = (w_in_even[i], a_q_norm[i], a_w_uq[i], a_kv_norm[i], a_w_uk[i], a_w_uv[i], w_out_even[i])
            mp, stp = even_mixer(hp, pos_p, None, *w)
            ms, sts = even_mixer(hs, pos_s, (cache_mla_ckv[i], cache_mla_krope[i], cache_sb_k[i], cache_sb_v[i]), *w)
            even_p.append(stp)
            even_s.append(sts)
        else:
            w = (w_in_odd[i], c_rel_bias[i], d_forget_bias[i], w_out_odd[i])
            mp, stp = odd_mixer(hp, pos_p, None, *w)
            ms, sts = odd_mixer(hs, pos_s, (cache_band_k[i], cache_band_v[i], cache_fox_k[i], cache_fox_v[i], cache_fox_logf[i]), *w)
            odd_p.append(stp)
            odd_s.append(sts)
        xp = xp + rmsnorm(mp, norm_post[l])
        xs = xs + rmsnorm(ms, norm_post[l])
    st = lambda lst, j: jnp.stack([s[j] for s in lst])
    return (xp, xs,
            st(even_p, 0), st(even_p, 1), st(even_p, 2), st(even_p, 3),
            st(odd_p, 0), st(odd_p, 1), st(odd_p, 2), st(odd_p, 3), st(odd_p, 4),
            st(even_s, 0), st(even_s, 1), st(even_s, 2), st(even_s, 3),
            st(odd_s, 0), st(odd_s, 1), st(odd_s, 2), st(odd_s, 3), st(odd_s, 4))
```

```python
import numpy as np
from contextlib import ExitStack
import concourse.bass as bass
import concourse.mybir as mybir
from concourse.bass_utils import run_bass_kernel_spmd

F32 = mybir.dt.float32
BF16 = mybir.dt.bfloat16
AF = mybir.ActivationFunctionType
ALU = mybir.AluOpType

NCORES = 8
D = 1024
SEQ = 2048
NSTR = 4
DSEQ = 64
PAST = 4096
EPS = 1e-6
NEGM = -30000.0
A_SCALE = float((64 + 32) ** -0.5)
EVEN_IN = 2976
ODD_IN = 4104
BLK = 256
import os
STAGES = int(os.environ.get('KSTAGES', '6'))
NBLK_DBG = int(os.environ.get('KNBLK', '8'))
OPLIMIT = int(os.environ.get('KOPLIMIT', '100000000'))
KCORES = int(os.environ.get('KCORES', '8'))
KSAME = int(os.environ.get('KSAME', '0'))


class Res:
    __slots__ = ("w", "rs", "x", "rg")

    def __init__(self, x=False):
        self.w = None
        self.rs = []
        self.rg = None
        self.x = x


class Eng:
    def __init__(self, name, sem):
        self.name = name
        self.sem = sem
        self.cnt = 0
        self.waited = {}
        self.prog = []
        self.dq = []
        self.dcnt = []
        self.di = 0


class FW:
    def __init__(self, nc, es, ndq=8):
        self.nc = nc
        self.E = {}
        for name in ("pe", "act", "dve", "pool", "sp"):
            self.E[name] = Eng(name, es.enter_context(nc.semaphore("s_" + name)))
        for qn in ("sp", "act", "pool"):
            e = self.E[qn]
            for i in range(ndq):
                e.dq.append(es.enter_context(nc.semaphore(f"d_{qn}{i}")))
                e.dcnt.append(0)

    def _wait(self, eng, tok, force=False):
        if tok is None:
            return
        sem, val, src = tok
        if src == eng.name and src in ("pe", "sp") and not force:
            return
        key = id(sem)
        if eng.waited.get(key, 0) >= val:
            return
        eng.waited[key] = val
        eng.prog.append(("w", sem, val))

    def _deps(self, eng, reads, writes):
        for r in reads:
            self._wait(eng, r.w)
            if r.x:
                for t in r.rs:
                    if t[2] != eng.name:
                        self._wait(eng, t)
        for w in writes:
            if w.w is not None and (w.w[2] != eng.name or KSAME):
                self._wait(eng, w.w)
            for t in w.rs:
                if t[2] != eng.name or KSAME:
                    self._wait(eng, t)

    def _record(self, tok, reads, writes):
        for r in reads:
            r.rs.append(tok)
            if len(r.rs) > 16:
                best = {}
                for t in r.rs:
                    k = id(t[0])
                    if k not in best or best[k][1] < t[1]:
                        best[k] = t
                r.rs = list(best.values())
        for w in writes:
            w.w = tok
            w.rs = []

    def op(self, engname, fn, reads=(), writes=(), rg=None):
        self.nops = getattr(self, "nops", 0) + 1
        if self.nops > OPLIMIT:
            return None
        eng = self.E[engname]
        self._deps(eng, reads, writes)
        if engname == "pe":
            for w in writes:
                if rg is not None and w.rg is not None and w.rg != rg and w.w is not None and w.w[2] == "pe":
                    self._wait(eng, w.w, force=True)
                w.rg = rg
        eng.cnt += 1
        eng.prog.append(("i", fn, eng.sem, 1))
        tok = (eng.sem, eng.cnt, eng.name)
        self._record(tok, reads, writes)
        return tok

    def dma(self, q, out, in_, reads=(), writes=()):
        self.nops = getattr(self, "nops", 0) + 1
        if self.nops > OPLIMIT:
            return None
        eng = self.E[q]
        self._deps(eng, reads, writes)
        i = eng.di % len(eng.dq)
        eng.di += 1
        sem = eng.dq[i]
        if eng.dcnt[i] > 0:
            self._wait(eng, (sem, eng.dcnt[i], "dma"))
        eng.prog.append(("i", (lambda o, out=out, in_=in_: o.dma_start(out=out, in_=in_)), sem, 16))
        eng.dcnt[i] += 16
        tok = (sem, eng.dcnt[i], "dma")
        self._record(tok, reads, writes)
        return tok

    def barrier(self):
        toks = []
        for q in ("sp", "act", "pool"):
            e = self.E[q]
            for i, sem in enumerate(e.dq):
                if e.dcnt[i] > 0:
                    toks.append((sem, e.dcnt[i], "dma"))
        for n in ("pe", "act", "dve", "pool"):
            e = self.E[n]
            if e.cnt > 0:
                toks.append((e.sem, e.cnt, "x"))
        for n in ("pe", "act", "dve", "pool", "sp"):
            for t in toks:
                self._wait(self.E[n], t)

    def finish(self):
        sp = self.E["sp"]
        for q in ("sp", "act", "pool"):
            e = self.E[q]
            for i, sem in enumerate(e.dq):
                if e.dcnt[i] > 0:
                    self._wait(sp, (sem, e.dcnt[i], "dma"))
        for n in ("pe", "act", "dve", "pool"):
            e = self.E[n]
            if e.cnt > 0:
                self._wait(sp, (e.sem, e.cnt, "x"))

    def emit(self):
        nc = self.nc
        objs = {"pe": None}

        def run(eng):
            def body(obj):
                for a in eng.prog:
                    if a[0] == "w":
                        obj.wait_ge(a[1], a[2])
                    else:
                        a[1](obj).then_inc(a[2], a[3])
            return body
        with nc.Block() as block:
            block.tensor(run(self.E["pe"]))
            block.scalar(run(self.E["act"]))
            block.vector(run(self.E["dve"]))
            block.gpsimd(run(self.E["pool"]))
            block.sync(run(self.E["sp"]))


class RR:
    def __init__(self, items):
        self.items = items
        self.i = 0

    def next(self):
        it = self.items[self.i % len(self.items)]
        self.i += 1
        return it


def pipeline(units, stages, offsets=None):
    n = len(units)
    ns = len(stages)
    if offsets is None:
        offsets = list(range(ns))
    for i in range(n + max(offsets)):
        for s, st in enumerate(stages):
            j = i - offsets[s]
            if 0 <= j < n:
                st(units[j])


def build():
    nc = bass.Bass("TRN2", target_bir_lowering=False)
    din = lambda n, s: nc.dram_tensor(n, s, F32, kind="ExternalInput").ap()
    dout = lambda n, s: nc.dram_tensor(n, s, F32, kind="ExternalOutput").ap()
    xp = din("xp", [SEQ, D])
    xs = din("xs", [NSTR * DSEQ, D])
    c_ckv = din("c_ckv", [NSTR, PAST, 128])
    c_kr = din("c_kr", [NSTR, PAST, 32])
    c_sbk = din("c_sbk", [NSTR, PAST, 512])
    c_sbv = din("c_sbv", [NSTR, PAST, 512])
    c_bk = din("c_bk", [NSTR, 512, 512])
    c_bv = din("c_bv", [NSTR, 512, 512])
    c_fk = din("c_fk", [NSTR, PAST, 512])
    c_fv = din("c_fv", [NSTR, PAST, 512])
    c_lf = din("c_lf", [NSTR, PAST, 8])
    norm_pre = din("norm_pre", [128, 16])
    norm_post = din("norm_post", [2, D])
    w_in0 = din("w_in0", [D, EVEN_IN])
    q_norm = din("q_norm", [1, 256])
    w_uq = din("w_uq", [256, 768])
    kv_norm = din("kv_norm", [1, 128])
    w_ukT = din("w_ukT", [128, 1024])
    w_uv = din("w_uv", [128, 512])
    w_out0 = din("w_out0", [D, D])
    w_in1 = din("w_in1", [D, ODD_IN])
    relb = din("relb", [8, 513])
    fbias = din("fbias", [1, 8])
    relc = din("relc", [1, 8])
    w_out1 = din("w_out1", [D, D])
    rope_p = din("rope_p", [128, 16 * 64])
    rope_s = din("rope_s", [64, 64])
    y_p = dout("y_p", [SEQ, D])
    y_s = dout("y_s", [NSTR * DSEQ, D])
    o_ckv_p = dout("o_ckv_p", [SEQ, 128]); o_kr_p = dout("o_kr_p", [SEQ, 32])
    o_sbk_p = dout("o_sbk_p", [SEQ, 512]); o_sbv_p = dout("o_sbv_p", [SEQ, 512])
    o_bk_p = dout("o_bk_p", [512, 512]); o_bv_p = dout("o_bv_p", [512, 512])
    o_fk_p = dout("o_fk_p", [SEQ, 512]); o_fv_p = dout("o_fv_p", [SEQ, 512]); o_lf_p = dout("o_lf_p", [SEQ, 8])
    o_ckv_s = dout("o_ckv_s", [NSTR * DSEQ, 128]); o_kr_s = dout("o_kr_s", [NSTR * DSEQ, 32])
    o_sbk_s = dout("o_sbk_s", [NSTR * DSEQ, 512]); o_sbv_s = dout("o_sbv_s", [NSTR * DSEQ, 512])
    o_bk_s = dout("o_bk_s", [NSTR, 512, 512]); o_bv_s = dout("o_bv_s", [NSTR, 512, 512])
    o_fk_s = dout("o_fk_s", [NSTR * DSEQ, 512]); o_fv_s = dout("o_fv_s", [NSTR * DSEQ, 512])
    o_lf_s = dout("o_lf_s", [NSTR * DSEQ, 8])
    x1p = dout("x1p", [SEQ, D])
    x1s = dout("x1s", [NSTR * DSEQ, D])

    with ExitStack() as es:
        fw = FW(nc, es)
        ARN = 105500
        AR = es.enter_context(nc.sbuf_tensor("AR", [128, ARN], BF16))
        ar = {"top": 0, "peak": 0}

        class _T:
            def __init__(self, ap):
                self.ap = ap
            def __getitem__(self, k):
                return self.ap[k]

        def sbt(n, s, d=F32):
            nel = int(np.prod(s[1:]))
            nb = nel * (4 if d == F32 else 2)
            nb = (nb + 63) // 64 * 64
            off = ar["top"]
            ar["top"] += nb // 2
            ar["peak"] = max(ar["peak"], ar["top"])
            assert ar["top"] <= ARN, (n, ar["top"])
            v = AR[:, off:off + nb // 2]
            if d == F32:
                v = v.bitcast(F32)
            v = v[:, 0:nel]
            if len(s) == 3:
                v = v.rearrange("p (a b) -> p a b", a=s[1])
            elif len(s) == 4:
                v = v.rearrange("p (a b c) -> p a b c", a=s[1], b=s[2])
            if s[0] < 128:
                v = v[0:s[0]]
            return _T(v)
        ps = es.enter_context(nc.psum_tensor("ps", [128, 7, 512], F32))
        psb = es.enter_context(nc.psum_tensor("psb", [128, 1024], BF16))
        r_ps = [Res(True) for _ in range(7)]
        r_psb = Res(True)
        bank = lambda i: ps[:, i, :]

        ident = sbt("ident", [128, 128], BF16); r_ident = Res()
        fw.op("pool", lambda g: g.memset(ident[:], 1.0), writes=[r_ident])
        fw.op("pool", lambda g: g.affine_select(out=ident[:], in_=ident[:], pattern=[[-1, 128]], compare_op=ALU.is_equal,
                                                fill=0.0, base=0, channel_multiplier=1), reads=[r_ident], writes=[r_ident])
        flipJ = sbt("flipJ", [128, 128], F32); r_flip = Res()
        fw.op("pool", lambda g: g.memset(flipJ[:], 1.0), writes=[r_flip])
        fw.op("pool", lambda g: g.affine_select(out=flipJ[:], in_=flipJ[:], pattern=[[1, 128]], compare_op=ALU.is_equal,
                                                fill=0.0, base=-127, channel_multiplier=1), reads=[r_flip], writes=[r_flip])
        triF = sbt("triF", [128, 128], F32); r_tri = Res()
        fw.op("pool", lambda g: g.memset(triF[:], 1.0), writes=[r_tri])
        fw.op("pool", lambda g: g.affine_select(out=triF[:], in_=triF[:], pattern=[[1, 128]], compare_op=ALU.is_ge,
                                                fill=0.0, base=0, channel_multiplier=-1), reads=[r_tri], writes=[r_tri])
        onesF = sbt("onesF", [128, 128], F32); r_onesF = Res()
        fw.op("pool", lambda g: g.memset(onesF[:], 1.0), writes=[r_onesF])
        onesB = sbt("onesB", [128, 128], BF16); r_onesB = Res()
        fw.op("pool", lambda g: g.memset(onesB[:], 1.0), writes=[r_onesB])
        negOnes = sbt("negOnes", [128, 128], BF16); r_negOnes = Res()
        fw.op("pool", lambda g: g.memset(negOnes[:], -1.0), writes=[r_negOnes])
        negTri = sbt("negTri", [128, 128], BF16); r_negTri = Res()
        fw.op("pool", lambda g: g.memset(negTri[:], -1.0), writes=[r_negTri])
        fw.op("pool", lambda g: g.affine_select(out=negTri[:], in_=negTri[:], pattern=[[-1, 128]], compare_op=ALU.is_ge,
                                                fill=0.0, base=0, channel_multiplier=1), reads=[r_negTri], writes=[r_negTri])
        maskSB = sbt("maskSB", [128, 512], BF16); r_maskSB = Res()
        fw.op("pool", lambda g: g.memset(maskSB[:], 0.0), writes=[r_maskSB])
        for a in range(4):
            fw.op("pool", lambda g, a=a: g.affine_select(out=maskSB[:, a * 128:(a + 1) * 128], in_=maskSB[:, a * 128:(a + 1) * 128],
                                                         pattern=[[1, 128]], compare_op=ALU.is_gt, fill=NEGM, base=0,
                                                         channel_multiplier=-1), reads=[r_maskSB], writes=[r_maskSB])
        maskSB64 = sbt("maskSB64", [64, 512], BF16); r_maskSB64 = Res()
        fw.op("pool", lambda g: g.memset(maskSB64[:], 0.0), writes=[r_maskSB64])
        for a in range(8):
            fw.op("pool", lambda g, a=a: g.affine_select(out=maskSB64[:, a * 64:(a + 1) * 64], in_=maskSB64[:, a * 64:(a + 1) * 64],
                                                         pattern=[[1, 64]], compare_op=ALU.is_gt, fill=NEGM, base=0,
                                                         channel_multiplier=-1), reads=[r_maskSB64], writes=[r_maskSB64])
        maskFX = sbt("maskFX", [128, 128], F32); r_maskFX = Res()
        fw.op("pool", lambda g: g.memset(maskFX[:], 0.0), writes=[r_maskFX])
        fw.op("pool", lambda g: g.affine_select(out=maskFX[:], in_=maskFX[:], pattern=[[1, 128]], compare_op=ALU.is_ge,
                                                fill=NEGM, base=0, channel_multiplier=-1), reads=[r_maskFX], writes=[r_maskFX])
        maskCH = sbt("maskCH", [128, 128], F32); r_maskCH = Res()
        fw.op("pool", lambda g: g.memset(maskCH[:], 0.0), writes=[r_maskCH])
        fw.op("pool", lambda g: g.memset(maskCH[64:128, 0:64], NEGM), reads=[r_maskCH], writes=[r_maskCH])
        mask512 = sbt("mask512", [128, 128], F32); r_mask512 = Res()
        fw.op("pool", lambda g: g.memset(mask512[:], 0.0), writes=[r_mask512])
        fw.op("pool", lambda g: g.memset(mask512[0:64, 64:128], NEGM), reads=[r_mask512], writes=[r_mask512])

        ropePt = RR([(sbt(f"ropeP{i}", [128, 64]), Res()) for i in range(2)])
        ropeS = sbt("ropeS", [64, 64]); r_ropeS = Res()
        fw.dma("sp", ropeS[:], rope_s[:, :], writes=[r_ropeS])
        gpre = sbt("gpre", [128, 2, 8]); r_gpre = Res()
        fw.dma("sp", gpre[:].rearrange("p a b -> p (a b)"), norm_pre[:, :], writes=[r_gpre])
        gpost = sbt("gpost", [128, D]); r_gpost = Res()
        qn_bc = sbt("qn_bc", [128, 256]); r_qn = Res()
        fw.dma("sp", qn_bc[:], bass.AP(q_norm.tensor, 0, [[0, 128], [1, 256]]), writes=[r_qn])
        kvn_bc = sbt("kvn_bc", [128, 128]); r_kvn = Res()
        fw.dma("sp", kvn_bc[:], bass.AP(kv_norm.tensor, 0, [[0, 128], [1, 128]]), writes=[r_kvn])
        fb_bc = sbt("fb_bc", [128, 8]); r_fb = Res()
        fw.dma("sp", fb_bc[:], bass.AP(fbias.tensor, 0, [[0, 128], [1, 8]]), writes=[r_fb])

        wbuf = sbt("wbuf", [128, 8, ODD_IN], BF16); r_w = Res()
        wout = sbt("wout", [128, 8, D], BF16); r_wout = Res()
        wuq = _T(wbuf[:, 0:2, 2976:2976 + 768]); r_wuq = r_w
        wukT = _T(wbuf[:, 2, 2976:2976 + 1024]); r_wuk = r_w
        wuv = _T(wbuf[:, 3, 2976:2976 + 512]); r_wuv = r_w
        WST = 1026
        wst_off = ar["top"]
        wst = RR([(sbt(f"wst{i}", [128, WST]), Res()) for i in range(2)])
        wst_end = ar["top"]
        ar["top"] = wst_off
        cast_i = [0]

        def cast(out, in_, reads, writes, engs=("pool", "dve", "act")):
            e = engs[cast_i[0] % len(engs)]
            cast_i[0] += 1
            if e == "act":
                fw.op("act", lambda a: a.copy(out=out, in_=in_), reads=reads, writes=writes)
            else:
                fw.op(e, lambda v: v.tensor_copy(out=out, in_=in_), reads=reads, writes=writes)

        def load_w(dst_fn, src, nrows, ncols, r_dst, q="sp"):
            for c in range(nrows // 128):
                for c0 in range(0, ncols, WST):
                    n = min(WST, ncols - c0)
                    st, r_st = wst.next()
                    fw.dma(q, st[:, 0:n], src[c * 128:(c + 1) * 128, c0:c0 + n], writes=[r_st])
                    cast(dst_fn(c)[:, c0:c0 + n], st[:, 0:n], [r_st], [r_dst])

        pbf = RR([(sbt(f"pbf{i}", [128, 512], BF16), Res()) for i in range(3)])
        spb = RR([(sbt(f"spb{i}", [128, 512], BF16), Res()) for i in range(3)])
        ebuf = RR([(sbt(f"ebuf{i}", [128, 512]), Res()) for i in range(2)])
        ar["top"] = max(ar["top"], wst_end)
        KS = 24576
        ks_off = ar["top"]
        ks = sbt("ks", [128, KS], BF16)
        hT = sbt("hT", [128, 8, BLK], BF16); r_hT = Res()
        gT = sbt("gT", [128, 8, BLK], BF16); r_gT = [Res() for _ in range(8)]
        qA = sbt("qA", [128, 4, 2, 256], BF16); r_qA = Res()
        qBd = sbt("qB", [128, 4, 2, 256], BF16); r_qB = Res()
        qB = _T(qBd[:].rearrange("p a t q -> p (a t q)")[:, 0:4 * BLK].rearrange("p (a q) -> p a q", a=4))
        fw.op("pool", lambda g: g.memset(qA[:].rearrange("p a t q -> p (a t q)"), 0.0), writes=[r_qA])
        xin = RR([(sbt(f"xin{i}", [128, D]), Res()) for i in range(2)])
        hb = RR([(sbt(f"hb{i}", [128, D], BF16), Res()) for i in range(1)])
        junk = sbt("junk", [128, 512], BF16); r_junk = Res()
        sm = RR([(sbt(f"sm{i}", [128, 16]), Res()) for i in range(6)])
        st512 = RR([(sbt(f"st512_{i}", [128, 512]), Res()) for i in range(2)])
        tmpf = RR([(sbt(f"tmpf{i}", [128, 512]), Res()) for i in range(2)])
        fint = RR([(sbt(f"fint{i}", [128, 128]), Res()) for i in range(2)])
        Rbuf = sbt("Rbuf", [128, 512], BF16); r_R = Res()
        latb = sbt("latb", [128, 512], BF16); r_latb = Res()
        rden = sbt("rden", [128, 512]); r_rden = Res()
        ov0 = ar["top"]
        qlat = sbt("qlat", [128, 8, BLK], BF16); r_qlat = Res()
        qrT = sbt("qrT", [32, 8, BLK], BF16); r_qrT = Res()
        cqT = sbt("cqT", [128, 2, BLK], BF16); r_cqT = Res()
        cq_b = sbt("cq_b", [128, 256], BF16); r_cqb = Res()
        kvb = sbt("kvb", [128, 128 + 32], BF16); r_kvb = Res()
        qr_b = sbt("qr_b", [128, 256], BF16); r_qrb = Res()
        ropet = RR([(sbt(f"ropet{i}", [128, 256]), Res()) for i in range(2)])
        ov1 = ar["top"]
        ar["top"] = ov0
        fxb = sbt("fxb", [128, 4, 16, 8]); r_fxb = Res()
        biasq2 = sbt("biasq", [128, 2, 16, 8]); r_biasq = Res()
        accb = sbt("accb", [128, 8]); r_accb = Res()
        tblB = sbt("tblB", [128, 8, 256]); r_tblB = Res()
        cstB = sbt("cstB", [128, 8]); r_cstB = Res()
        lfb = RR([(sbt(f"lfb{i}", [128, 24]), Res()) for i in range(3)])
        lfc = sbt("lfc", [128, 32, 8]); r_lfc = Res()
        sfx = sbt("sfx", [128, 33, 8]); r_sfx = Res()
        ar["top"] = max(ar["top"], ov1)
        sv_top = ar["top"]
        ar["top"] = ks_off
        cst_f = RR([(sbt(f"cstf{i}", [128, 512]), Res()) for i in range(8)])
        vbf = RR([(sbt(f"vbf{i}", [128, 512], BF16), Res()) for i in range(5)])
        kbf = RR([(sbt(f"kbf{i}", [128, 512], BF16), Res()) for i in range(2)])
        cKT = RR([(sbt(f"cKT{i}", [128, 4, 128], BF16), Res()) for i in range(3)])
        sfxn = sbt("sfxn", [64, 4, 8]); r_sfxn = Res()
        qbd = [(sbt(f"qbd{i}", [128, 4, 128], BF16), Res()) for i in range(4)]
        skT1 = sbt("skT1", [128, 4, BLK], BF16); skT2 = sbt("skT2", [128, 4, BLK], BF16)
        sv1 = sbt("sv1", [64, 4, 512], BF16); sv2 = sbt("sv2", [64, 4, 512], BF16)
        sckvT = sbt("sckvT", [128, BLK], BF16); skrT = sbt("skrT", [32, BLK], BF16)
        sckv_tm = sbt("sckv_tm", [64, 4, 128], BF16)
        assert ar["top"] <= ks_off + KS
        ar["top"] = sv_top

        def ksv(off, shape):
            n = int(np.prod(shape))
            v = ks[:, off:off + n]
            if len(shape) == 2:
                return v.rearrange("p (a b) -> p a b", a=shape[0])
            return v
        kT1 = ksv(0, [4, SEQ]); v1 = ksv(8192, [16, 512])
        kT2 = ksv(8192, [4, SEQ]); v2 = ksv(16384, [16, 512])
        kTc = ksv(0, [4, 1024]); vc = ksv(4096, [8, 512])
        ckvT = ks[:, 16384:16384 + SEQ]
        krT = ks[:, 16384 + 2048:16384 + 4096]
        ckv_tm = ksv(16384 + 4096, [16, 128])
        r_k1 = [Res() for _ in range(20)]
        r_k2 = [Res() for _ in range(20)]
        r_k3 = [Res() for _ in range(20)]
        SOFF = 24576
        r_sk = [Res() for _ in range(8)]

        def load_layer_weights(l):
            fw.barrier()
            if l == 0:
                load_w(lambda c: wbuf[:, c, :], w_in0, D, EVEN_IN, r_w)
                load_w(lambda c: wout[:, c, :], w_out0, D, D, r_wout)
                load_w(lambda c: wuq[:, c, :], w_uq, 256, 768, r_wuq)
                load_w(lambda c: wukT[:, :], w_ukT, 128, 1024, r_wuk)
                load_w(lambda c: wuv[:, :], w_uv, 128, 512, r_wuv)
            else:
                load_w(lambda c: wbuf[:, c, :], w_in1, D, ODD_IN, r_w)
                load_w(lambda c: wout[:, c, :], w_out1, D, D, r_wout)
            fw.dma("sp", gpost[:], bass.AP(norm_post.tensor, l * D, [[0, 128], [1, D]]), writes=[r_gpost])
            fw.barrier()

        gbank = RR([6, 2, 3, 0, 1, 4, 5])

        def mm(o, l, r, st, sp_, reads, wres):
            K = l.shape[0]
            rg = None if K >= 128 else (l.base_partition(), K)
            fw.op("pe", lambda t: t.matmul(o, lhsT=l, rhs=r, start=st, stop=sp_), reads=reads, writes=[wres], rg=rg)

        def rstd_from_ss(ss_ap, n, nt, r_ss):
            s, r_s = sm.next()
            fw.op("dve", lambda v: v.tensor_scalar(out=s[0:nt, 0:1], in0=ss_ap, scalar1=1.0 / n, scalar2=EPS,
                                                   op0=ALU.mult, op1=ALU.add), reads=[r_ss], writes=[r_s])
            fw.op("act", lambda a_: a_.activation(out=s[0:nt, 3:4], in_=s[0:nt, 0:1], func=AF.Sqrt), reads=[r_s], writes=[r_s])
            fw.op("dve", lambda v: v.reciprocal(out=s[0:nt, 1:2], in_=s[0:nt, 3:4]), reads=[r_s], writes=[r_s])
            return s[0:nt, 1:2], r_s

        def sumsq(in_ap, nt, reads):
            s, r_s = sm.next()
            fw.op("act", lambda a: a.activation(out=junk[0:nt, 0:in_ap.shape[-1]], in_=in_ap, func=AF.Square,
                                                accum_out=s[0:nt, 2:3]), reads=reads, writes=[r_junk, r_s])
            return s[0:nt, 2:3], r_s

        def phase_norm(l, tiles, xsrc, r_xsrc):
            for (nt, row0, col0) in tiles:
                xt, r_xt = xin.next()
                fw.dma("sp", xt[0:nt, :], xsrc[row0:row0 + nt, :], reads=[r_xsrc[row0 // 64]] if r_xsrc else [], writes=[r_xt])
                s_, r_ss = sm.next()
                for hf in range(2):
                    fw.op("act", lambda a, hf=hf, s_=s_, xt=xt, nt=nt: a.activation(out=junk[0:nt, :], in_=xt[0:nt, hf * 512:(hf + 1) * 512], func=AF.Square,
                                                                                    accum_out=s_[0:nt, 4 + hf:5 + hf]), reads=[r_xt], writes=[r_junk, r_ss])
                fw.op("dve", lambda v, s_=s_, nt=nt: v.tensor_tensor(out=s_[0:nt, 2:3], in0=s_[0:nt, 4:5], in1=s_[0:nt, 5:6], op=ALU.add), reads=[r_ss], writes=[r_ss])
                rs, r_rs = rstd_from_ss(s_[0:nt, 2:3], D, nt, r_ss)
                h, r_h = hb.next()
                fw.op("dve", lambda v, h=h, xt=xt, rs=rs, nt=nt: v.tensor_scalar(out=h[0:nt, :], in0=xt[0:nt, :], scalar1=rs, scalar2=None,
                                                                                  op0=ALU.mult), reads=[r_xt, r_rs], writes=[r_h])
                for c in range(8):
                    fw.op("pe", lambda t, h=h, c=c, nt=nt: t.transpose(psb[:, c * 128:c * 128 + nt], h[0:nt, c * 128:(c + 1) * 128],
                                                                       ident[0:nt, 0:nt]), reads=[r_h, r_ident], writes=[r_psb])
                fw.op("dve", lambda v, nt=nt, col0=col0: v.tensor_tensor(
                    out=hT[:, :, col0:col0 + nt], in0=psb[:, :].rearrange("p (c t) -> p c t", c=8)[:, :, 0:nt],
                    in1=gpre[:, l, :].unsqueeze(2).broadcast_to([128, 8, nt]), op=ALU.mult),
                    reads=[r_psb, r_gpre], writes=[r_hT])

        def tm_proj(nt, col0, wcols, ncols, b):
            for c in range(8):
                fw.op("pe", lambda t, c=c: t.matmul(ps[0:nt, b, 0:ncols], lhsT=hT[:, c, col0:col0 + nt],
                                                    rhs=wbuf[:, c, wcols:wcols + ncols], start=(c == 0), stop=(c == 7)),
                      reads=[r_hT, r_w], writes=[r_ps[b]])

        def fm_proj(wcols, b, half):
            for c in range(8):
                fw.op("pe", lambda t, c=c: t.matmul(ps[:, b, half * BLK:(half + 1) * BLK], lhsT=wbuf[:, c, wcols:wcols + 128],
                                                    rhs=hT[:, c, :], start=(c == 0), stop=(c == 7)),
                      reads=[r_hT, r_w], writes=[r_ps[b]])

        def fm_group(wcol_list, evac):
            for i in range(0, len(wcol_list), 2):
                b = gbank.next()
                n = min(2, len(wcol_list) - i)
                for j in range(n):
                    fm_proj(wcol_list[i + j], b, j)
                for j in range(n):
                    evac(i + j, ps[:, b, j * BLK:(j + 1) * BLK], r_ps[b])

        def rope_tm(src, nheads, nt, tab, r_tab, out_ap, reads, writes):
            t1, r_t1 = ropet.next()
            t2, r_t2 = ropet.next()
            n = nheads * 32
            s3 = src.rearrange("p (h r) -> p h r", h=nheads)
            a1 = t1[0:nt, 0:n].rearrange("p (h r) -> p h r", h=nheads)
            a2 = t2[0:nt, 0:n].rearrange("p (h r) -> p h r", h=nheads)
            cosb = tab[:, 0:32].unsqueeze(1).broadcast_to([nt, nheads, 32])
            sin_lo = tab[:, 32:48].unsqueeze(1).broadcast_to([nt, nheads, 16])
            sin_hi = tab[:, 48:64].unsqueeze(1).broadcast_to([nt, nheads, 16])
            fw.op("dve", lambda v: v.tensor_tensor(out=a1, in0=s3, in1=cosb, op=ALU.mult), reads=reads + [r_tab], writes=[r_t1])
            fw.op("dve", lambda v: v.tensor_tensor(out=a2[:, :, 0:16], in0=s3[:, :, 16:32], in1=sin_lo, op=ALU.mult),
                  reads=reads + [r_tab], writes=[r_t2])
            fw.op("dve", lambda v: v.tensor_tensor(out=a2[:, :, 16:32], in0=s3[:, :, 0:16], in1=sin_hi, op=ALU.mult),
                  reads=reads + [r_tab], writes=[r_t2])
            fw.op("dve", lambda v: v.tensor_tensor(out=out_ap, in0=t1[0:nt, 0:n], in1=t2[0:nt, 0:n], op=ALU.add),
                  reads=[r_t1, r_t2], writes=writes)

        def evac_bd(dst, r_dst, i, src, rb):
            fw.op("act", lambda a: a.activation(out=dst[0:64, i, :, 0:128], in_=src[0:64, :].rearrange("p (t q) -> p t q", t=2),
                                                func=AF.Copy, scale=0.125), reads=[rb], writes=[r_dst])
            fw.op("act", lambda a: a.activation(out=dst[64:128, i, :, 128:256], in_=src[64:128, :].rearrange("p (t q) -> p t q", t=2),
                                                func=AF.Copy, scale=0.125), reads=[rb], writes=[r_dst])

        def phase_proj0(tiles, is_s, o_ckv, o_kr, o_sbk, o_sbv):
            for ti, (nt, row0, col0) in enumerate(tiles):
                gt = (row0 // 128) if not is_s else ti
                b = gbank.next()
                tm_proj(nt, col0, 0, 416, b)
                ssq, r_ssq = sumsq(ps[0:nt, b, 0:256], nt, [r_ps[b]])
                rq, r_rq = rstd_from_ss(ssq, 256, nt, r_ssq)
                fw.op("dve", lambda v, b=b, rq=rq, nt=nt: v.scalar_tensor_tensor(out=cq_b[0:nt, :], in0=ps[0:nt, b, 0:256], scalar=rq,
                                                                                  in1=qn_bc[0:nt, :], op0=ALU.mult, op1=ALU.mult),
                      reads=[r_ps[b], r_rq, r_qn], writes=[r_cqb])
                ssk, r_ssk = sumsq(ps[0:nt, b, 256:384], nt, [r_ps[b]])
                rk, r_rk = rstd_from_ss(ssk, 128, nt, r_ssk)
                stg, r_stg = st512.next()
                fw.op("dve", lambda v, b=b, rk=rk, nt=nt, stg=stg: v.scalar_tensor_tensor(out=stg[0:nt, 0:128], in0=ps[0:nt, b, 256:384], scalar=rk,
                                                                                           in1=kvn_bc[0:nt, :], op0=ALU.mult, op1=ALU.mult),
                      reads=[r_ps[b], r_rk, r_kvn], writes=[r_stg])
                if is_s:
                    tab, r_tab = ropeS[0:nt, :], r_ropeS
                else:
                    tb_, r_tab = ropePt.next()
                    fw.dma("sp", tb_[:, :], rope_p[:, gt * 64:(gt + 1) * 64], writes=[r_tab])
                    tab = tb_[0:nt, :]
                rope_tm(ps[0:nt, b, 384:416], 1, nt, tab, r_tab, stg[0:nt, 128:160], [r_ps[b]], [r_stg])
                fw.dma("sp", o_ckv[row0:row0 + nt, :], stg[0:nt, 0:128], reads=[r_stg])
                fw.dma("sp", o_kr[row0:row0 + nt, :], stg[0:nt, 128:160], reads=[r_stg])
                fw.op("act", lambda a, stg=stg, nt=nt: a.copy(out=kvb[0:nt, :], in_=stg[0:nt, 0:160]), reads=[r_stg], writes=[r_kvb])
                if is_s:
                    fw.op("pool", lambda g, ti=ti, nt=nt: g.tensor_copy(out=sckv_tm[0:nt, ti, :], in_=kvb[0:nt, 0:128]),
                          reads=[r_kvb], writes=[r_sk[4]])
                else:
                    fw.op("pool", lambda g, gt=gt: g.tensor_copy(out=ckv_tm[:, gt, :], in_=kvb[:, 0:128]), reads=[r_kvb], writes=[r_k3[gt]])
                for j in range(2):
                    fw.op("pe", lambda t, j=j, nt=nt: t.transpose(psb[:, j * 128:j * 128 + nt], cq_b[0:nt, j * 128:(j + 1) * 128],
                                                                  ident[0:nt, 0:nt]), reads=[r_cqb, r_ident], writes=[r_psb])
                fw.op("pe", lambda t, nt=nt: t.transpose(psb[:, 256:256 + nt], kvb[0:nt, 0:128], ident[0:nt, 0:nt]),
                      reads=[r_kvb, r_ident], writes=[r_psb])
                fw.op("pe", lambda t, nt=nt: t.transpose(psb[0:32, 384:384 + nt], kvb[0:nt, 128:160], ident[0:nt, 0:nt]),
                      reads=[r_kvb, r_ident], writes=[r_psb])
                fw.op("dve", lambda v, nt=nt, col0=col0: v.tensor_copy(out=cqT[:, :, col0:col0 + nt],
                                                                        in_=psb[:, 0:256].rearrange("p (c t) -> p c t", c=2)[:, :, 0:nt]),
                      reads=[r_psb], writes=[r_cqT])
                if is_s:
                    dckv, dkr, rr = sckvT[:, col0:col0 + nt], skrT[:, col0:col0 + nt], r_sk[5]
                else:
                    dckv, dkr, rr = ckvT[:, row0:row0 + nt], krT[0:32, row0:row0 + nt], r_k3[gt]
                fw.op("dve", lambda v, nt=nt, dckv=dckv: v.tensor_copy(out=dckv, in_=psb[:, 256:256 + nt]), reads=[r_psb], writes=[rr])
                fw.op("dve", lambda v, nt=nt, dkr=dkr: v.tensor_copy(out=dkr, in_=psb[0:32, 384:384 + nt]), reads=[r_psb], writes=[rr])
                b = gbank.next()
                tm_proj(nt, col0, 1440, 512, b)
                stg, r_stg = st512.next()
                fw.op("act", lambda a, b=b, stg=stg, nt=nt: a.copy(out=stg[0:nt, :], in_=ps[0:nt, b, :]), reads=[r_ps[b]], writes=[r_stg])
                fw.dma("sp", o_sbk[row0:row0 + nt, :], stg[0:nt, :], reads=[r_stg])
                b = gbank.next()
                tm_proj(nt, col0, 1952, 512, b)
                stg, r_stg = st512.next()
                fw.op("act", lambda a, b=b, stg=stg, nt=nt: a.copy(out=stg[0:nt, :], in_=ps[0:nt, b, :]), reads=[r_ps[b]], writes=[r_stg])
                fw.dma("sp", o_sbv[row0:row0 + nt, :], stg[0:nt, :], reads=[r_stg])
                if is_s:
                    fw.op("dve", lambda v, b=b, ti=ti, nt=nt: v.tensor_copy(out=sv1[0:nt, ti, :], in_=ps[0:nt, b, :]), reads=[r_ps[b]], writes=[r_sk[1]])
                else:
                    if os.environ.get("KSKIPV1") is None:
                        fw.op("dve", lambda v, b=b, gt=gt: v.tensor_copy(out=v1[:, gt, :], in_=ps[:, b, :]), reads=[r_ps[b]], writes=[r_k1[gt]])
                b = gbank.next()
                for cc in range(2):
                    fw.op("pe", lambda t, cc=cc, b=b, nt=nt, col0=col0: t.matmul(ps[0:nt, b, 0:256], lhsT=cqT[:, cc, col0:col0 + nt],
                                                                                 rhs=wuq[:, cc, 512:768], start=(cc == 0), stop=(cc == 1)),
                          reads=[r_cqT, r_wuq], writes=[r_ps[b]])
                rope_tm(ps[0:nt, b, 0:256], 8, nt, tab, r_tab, qr_b[0:nt, :], [r_ps[b]], [r_qrb])
                for h in range(8):
                    fw.op("pe", lambda t, h=h, nt=nt: t.transpose(psb[0:32, h * 128:h * 128 + nt], qr_b[0:nt, h * 32:(h + 1) * 32],
                                                                  ident[0:nt, 0:nt]), reads=[r_qrb, r_ident], writes=[r_psb])
                fw.op("dve", lambda v, nt=nt, col0=col0: v.tensor_copy(out=qrT[:, :, col0:col0 + nt],
                                                                        in_=psb[0:32, :].rearrange("p (h t) -> p h t", h=8)[:, :, 0:nt]),
                      reads=[r_psb], writes=[r_qrT])
            fm_group([928 + 128 * i for i in range(4)],
                     lambda i, src, rb: evac_bd(qA, r_qA, i, src, rb))
            if is_s:
                fm_group([1440 + 128 * i for i in range(4)],
                         lambda i, src, rb: fw.op("dve", lambda v: v.tensor_copy(out=skT1[:, i, :], in_=src), reads=[rb], writes=[r_sk[0]]))
            else:
                t0 = tiles[0][1]
                g0 = t0 // 128

                def ev(i, src, rb):
                    fw.op("dve", lambda v: v.tensor_copy(out=kT1[:, i, t0:t0 + BLK], in_=src), reads=[rb], writes=[r_k1[g0], r_k1[g0 + 1]])
                fm_group([1440 + 128 * i for i in range(4)], ev)
            fm_group([416 + 128 * i for i in range(4)] + [2464 + 128 * i for i in range(4)],
                     lambda i, src, rb: fw.op("act", lambda a: a.activation(out=gT[:, i, :], in_=src, func=AF.Silu),
                                              reads=[rb], writes=[r_gT[i]]))
            for i in range(0, 4, 2):
                b = gbank.next()
                for j in range(2):
                    for cc in range(2):
                        fw.op("pe", lambda t, cc=cc, b=b, i=i, j=j: t.matmul(ps[:, b, j * BLK:(j + 1) * BLK], lhsT=wuq[:, cc, (i + j) * 128:(i + j + 1) * 128],
                                                                             rhs=cqT[:, cc, :], start=(cc == 0), stop=(cc == 1)),
                              reads=[r_cqT, r_wuq], writes=[r_ps[b]])
                fw.op("dve", lambda v, b=b, i=i: v.tensor_copy(out=qB[:, i:i + 2, :], in_=ps[:, b, :].rearrange("p (a t) -> p a t", a=2)),
                      reads=[r_ps[b]], writes=[r_qB])
            for h0 in range(0, 8, 2):
                b = gbank.next()
                for j in range(2):
                    h = h0 + j
                    pb = 64 * (h % 2)
                    mm(ps[:, b, j * BLK:(j + 1) * BLK], wukT[pb:pb + 64, h * 128:(h + 1) * 128], qB[pb:pb + 64, h // 2, :], True, True,
                       [r_qB, r_wuk], r_ps[b])
                for j in range(2):
                    fw.op("act", lambda a, b=b, h0=h0, j=j: a.copy(out=qlat[:, h0 + j, :], in_=ps[:, b, j * BLK:(j + 1) * BLK]),
                          reads=[r_ps[b]], writes=[r_qlat])

        def fin_softmax(slots, ob, db, has_den=True):
            if has_den:
                fw.op("dve", lambda v: v.reciprocal(out=rden[:, :], in_=ps[:, db, :]), reads=[r_ps[db]], writes=[r_rden])
            for (h, c0, n, gc, g0) in slots:
                pb = 64 * (h % 2)
                if has_den:
                    t, r_t = fint.next()
                    fw.op("dve", lambda v, t=t, pb=pb, c0=c0, n=n: v.tensor_tensor(out=t[pb:pb + 64, 0:n], in0=ps[pb:pb + 64, ob, c0:c0 + n],
                                                                                   in1=rden[pb:pb + 64, c0:c0 + n], op=ALU.mult),
                          reads=[r_ps[ob], r_rden], writes=[r_t])
                    fw.op("dve", lambda v, t=t, pb=pb, n=n, gc=gc, g0=g0: v.tensor_tensor(out=gT[pb:pb + 64, gc, g0:g0 + n], in0=t[pb:pb + 64, 0:n],
                                                                                          in1=gT[pb:pb + 64, gc, g0:g0 + n], op=ALU.mult),
                          reads=[r_t, r_gT[gc]], writes=[r_gT[gc]])
                else:
                    fw.op("dve", lambda v, pb=pb, c0=c0, n=n, gc=gc, g0=g0: v.tensor_tensor(out=gT[pb:pb + 64, gc, g0:g0 + n], in0=ps[pb:pb + 64, ob, c0:c0 + n],
                                                                                            in1=gT[pb:pb + 64, gc, g0:g0 + n], op=ALU.mult),
                          reads=[r_ps[ob], r_gT[gc]], writes=[r_gT[gc]])

        odpair = RR([(4, 5), (2, 3)])
        sbank = RR([0, 1])
        abank = RR([2, 3])
        obank = RR([4, 5])

        class U:
            dma = None
            prep = None
            mask = None
            pre = None
            scale = 1.0

        def sprep(u):
            if u.prep is not None:
                u.prep(u)

        def sdma(u):
            if u.dma is not None:
                u.dma(u)

        def sb_chain(units):
            def s0(u):
                u.sb = sbank.next()
                nk = u.nk
                nz = len(u.zmm)
                for i, (l, r, c0, n) in enumerate(u.zmm):
                    mm(ps[0:u.nk, u.sb, c0:c0 + n], l, r, (i == 0), (u.mask is None and i == nz - 1), u.reads, r_ps[u.sb])
                if u.mask is not None:
                    mm(ps[0:u.nk, u.sb, :], ident[0:u.nk, 0:u.nk], u.mask, False, True, [r_ident, u.rmask], r_ps[u.sb])
                e, r_e = ebuf.next()
                fw.op("act", lambda a, u=u, e=e: a.activation(out=e[0:u.nk, :], in_=ps[0:u.nk, u.sb, :], func=AF.Exp), reads=[r_ps[u.sb]], writes=[r_e])
                u.sp, u.r_sp = spb.next()
                fw.op("act", lambda a, u=u, e=e: a.activation(out=u.sp[0:u.nk, :], in_=e[0:u.nk, :], func=AF.Ln, bias=1.0), reads=[r_e], writes=[u.r_sp])

            def s1(u):
                u.ab = abank.next()
                for i, (l, r, c0, n) in enumerate(u.zmm):
                    mm(ps[0:u.nk, u.ab, c0:c0 + n], l, r, (i == 0), False, u.reads, r_ps[u.ab])
                if u.mask is not None:
                    mm(ps[0:u.nk, u.ab, :], ident[0:u.nk, 0:u.nk], u.mask, False, False, [r_ident, u.rmask], r_ps[u.ab])
                mm(ps[0:u.nk, u.ab, :], negTri[0:u.nk, 0:u.nk], u.sp[0:u.nk, :], False, u.first, [r_negTri, u.r_sp], r_ps[u.ab])
                if not u.first:
                    mm(ps[0:u.nk, u.ab, :], negOnes[:, 0:u.nk], Rbuf[:, :], False, True, [r_negOnes, r_R], r_ps[u.ab])
                if not u.last:
                    if u.first:
                        if u.nk < 128:
                            fw.op("pool", lambda g: g.memset(Rbuf[:, :], 0.0), writes=[r_R])
                        fw.op("pool", lambda g, u=u: g.tensor_copy(out=Rbuf[0:u.nk, :], in_=u.sp[0:u.nk, :]), reads=[u.r_sp], writes=[r_R])
                    else:
                        fw.op("pool", lambda g, u=u: g.tensor_tensor(out=Rbuf[0:u.nk, :], in0=Rbuf[0:u.nk, :], in1=u.sp[0:u.nk, :], op=ALU.add),
                              reads=[u.r_sp, r_R], writes=[r_R])
                u.w, u.r_w = pbf.next()
                fw.op("act", lambda a, u=u: a.activation(out=u.w[0:u.nk, :], in_=ps[0:u.nk, u.ab, :], func=AF.Exp), reads=[r_ps[u.ab]], writes=[u.r_w])

            def s2(u):
                if u.first:
                    u.chain["ob"] = obank.next()
                ob = u.chain["ob"]
                for vi, (lv, c0, n) in enumerate(u.vmm):
                    mm(ps[:, ob, c0:c0 + n], lv, u.w[0:u.nk, c0:c0 + n], (u.first and vi == 0), (u.last and vi == len(u.vmm) - 1),
                       u.vreads + [u.r_w], r_ps[ob])
                if u.last:
                    u.fin(ob)
            pipeline(units, [sdma, sprep, s0, s1, s2], [0, 3, 4, 5, 6])

        def sm_chain(units):
            def s0(u):
                u.sb = sbank.next()
                for (l, r, c0, n, st, sp_) in u.zmm:
                    o = ps[0:u.nk, u.sb, c0:c0 + n]
                    if len(r.shape) == 3:
                        o = o.rearrange("p (h q) -> p h q", h=r.shape[1])
                    mm(o, l, r, st, sp_, u.reads, r_ps[u.sb])
                if u.pre is not None:
                    u.src, u.r_src = u.pre(u)
                else:
                    u.src, u.r_src = ps[0:u.nk, u.sb, :], r_ps[u.sb]

            def s1(u):
                u.p, u.r_p = pbf.next()
                fw.op("act", lambda a, u=u: a.activation(out=u.p[0:u.nk, :], in_=u.src, func=AF.Exp, scale=u.scale), reads=[u.r_src], writes=[u.r_p])

            def s2(u):
                if u.first:
                    u.chain["ob"], u.chain["db"] = odpair.next()
                ob, db = u.chain["ob"], u.chain["db"]
                for vi, (lv, c0, n) in enumerate(u.vmm):
                    mm(ps[:, ob, c0:c0 + n], lv, u.p[0:u.nk, c0:c0 + n], (u.first and vi == 0), (u.last and vi == len(u.vmm) - 1),
                       u.vreads + [u.r_p], r_ps[ob])
                mm(ps[:, db, :], onesB[0:u.nk, :], u.p[0:u.nk, :], u.first, u.last, [r_onesB, u.r_p], r_ps[db])
                if u.last:
                    u.fin(ob, db)
            pipeline(units, [sdma, sprep, s0, s1, s2], [0, 3, 4, 5, 6])

        def pre_add(u, in1, r_in1, scale=1.0, extra=None):
            t, r_t = tmpf.next()
            nk = u.nk
            H = in1.shape[1]
            n = 512 // H
            fw.op("dve", lambda v: v.scalar_tensor_tensor(out=t[0:nk, :].rearrange("p (h q) -> p h q", h=H),
                                                          in0=ps[0:nk, u.sb, :].rearrange("p (h q) -> p h q", h=H), scalar=scale,
                                                          in1=in1, op0=ALU.mult, op1=ALU.add), reads=[r_ps[u.sb]] + r_in1, writes=[r_t])
            if extra is not None:
                ex, r_ex = extra
                fw.op("dve", lambda v: v.tensor_tensor(out=t[0:nk, :].rearrange("p (h q) -> p h q", h=H),
                                                       in0=t[0:nk, :].rearrange("p (h q) -> p h q", h=H), in1=ex, op=ALU.add),
                      reads=[r_t] + r_ex, writes=[r_t])
            return t[0:nk, :], r_t

        def mla_fin_factory(slots_fn):
            def fin(ob, db):
                fw.op("act", lambda a: a.copy(out=latb[:, :], in_=ps[:, ob, :]), reads=[r_ps[ob]], writes=[r_latb])
                slots = slots_fn()
                gb = 6
                for (h, c0, n, gc, g0) in slots:
                    fw.op("pe", lambda t, h=h, c0=c0, n=n: t.matmul(ps[:, gb, c0:c0 + n], lhsT=wuv[:, (h // 2) * 128:(h // 2 + 1) * 128],
                                                                    rhs=latb[:, c0:c0 + n], start=True, stop=True),
                          reads=[r_latb, r_wuv], writes=[r_ps[gb]])
                fin_softmax(slots, gb, db)
            return fin

        def attn0_prompt(bi):
            all_sb = []
            all_mla = []
            for qt in (2 * bi, 2 * bi + 1):
                qc = (qt % 2) * 128
                for hg in range(2):
                    chain = {}
                    units = []
                    for kt in range(qt, -1, -1):
                        u = U()
                        u.nk = 128; u.chain = chain
                        u.first = (kt == qt); u.last = (kt == 0)
                        u.zmm = []
                        u.vmm = []
                        for pp in range(2):
                            u.zmm.append((kT1[:, 2 * hg + pp, kt * 128:(kt + 1) * 128], qA[:, 2 * hg + pp, qt % 2, :], pp * 256, 256))
                        for pp in range(2):
                            u.vmm.append((v1[:, kt, (2 * hg + pp) * 128:(2 * hg + pp + 1) * 128], pp * 256, 256))
                        u.mask = maskSB[:, :] if kt == qt else None
                        u.rmask = r_maskSB
                        u.reads = [r_k1[kt], r_qA]
                        u.vreads = [r_k1[kt]]
                        slots = [(4 * hg + s, s * 128, 128, 4 + (4 * hg + s) // 2, qc) for s in range(4)]
                        u.fin = (lambda ob, slots=slots: fin_softmax(slots, ob, None, has_den=False))
                        units.append(u)
                    all_sb += units
                    chain = {}
                    units = []
                    for kt in range(qt, -1, -1):
                        u = U()
                        u.nk = 128; u.chain = chain
                        u.first = (kt == qt); u.last = (kt == 0)
                        u.zmm = [(ckvT[:, kt * 128:(kt + 1) * 128], qlat[:, 4 * hg:4 * hg + 4, qc:qc + 128], 0, 512, True, False),
                                 (krT[0:32, kt * 128:(kt + 1) * 128], qrT[:, 4 * hg:4 * hg + 4, qc:qc + 128], 0, 512, False, True)]
                        u.reads = [r_k3[kt], r_qlat, r_qrT]
                        u.scale = A_SCALE
                        if kt == qt:
                            u.pre = lambda u: pre_add(u, maskCH[:, :].unsqueeze(1).broadcast_to([128, 4, 128]), [r_maskCH], scale=A_SCALE)
                            u.scale = 1.0
                        else:
                            u.pre = None
                        u.vmm = [(ckv_tm[:, kt, :], 0, 512)]
                        u.vreads = [r_k3[kt]]
                        slots = [(4 * hg + s, s * 128, 128, (4 * hg + s) // 2, qc) for s in range(4)]
                        u.fin = mla_fin_factory(lambda slots=slots: slots)
                        units.append(u)
                    all_mla += units
            sb_chain(all_sb)
            sm_chain(all_mla)


        HORD = (0, 2, 4, 6, 1, 3, 5, 7)

        def dma_kv_tile(u, kdram, vdram, s, t):
            u.kf, u.r_kf = cst_f.next()
            fw.dma("sp", u.kf[:, :], kdram[s, t * 128:(t + 1) * 128, :], writes=[u.r_kf])
            u.vf, u.r_vf = cst_f.next()
            fw.dma("sp", u.vf[:, :], vdram[s, t * 128:(t + 1) * 128, :], writes=[u.r_vf])

        def load_kv_tile(u):
            kf, r_kf, vf, r_vf = u.kf, u.r_kf, u.vf, u.r_vf
            kb, r_kb = kbf.next()
            cast(kb[:, :], kf[:, :], [r_kf], [r_kb])
            for c in range(4):
                fw.op("pe", lambda t_, c=c, kb=kb: t_.transpose(psb[:, c * 128:(c + 1) * 128], kb[:, c * 128:(c + 1) * 128], ident[:, :]),
                      reads=[r_kb, r_ident], writes=[r_psb])
            kt_, r_kt = cKT.next()
            fw.op("dve", lambda v, kt_=kt_: v.tensor_copy(out=kt_[:, :, :], in_=psb[:, 0:512].rearrange("p (c t) -> p c t", c=4)),
                  reads=[r_psb], writes=[r_kt])
            vb, r_vb = vbf.next()
            cast(vb[:, :], vf[:, :], [r_vf], [r_vb])
            return kt_, r_kt, vb, r_vb

        def kv_units(s, ncache, kdram, vdram, qsrc, r_q, knew, vnew, r_new, slots):
            qb, r_qb = qbd[s]
            fw.op("pool", lambda g: g.memset(qb[:, :, :], 0.0), writes=[r_qb])
            so = (s % 2) * 64
            fw.op("pool", lambda g: g.tensor_copy(out=qb[0:64, :, 0:64], in_=qsrc[0:64, :, s // 2, so:so + 64]), reads=[r_q], writes=[r_qb])
            fw.op("pool", lambda g: g.tensor_copy(out=qb[64:128, :, 64:128], in_=qsrc[64:128, :, s // 2, 128 + so:128 + so + 64]), reads=[r_q], writes=[r_qb])
            chain = {}
            units = []
            u = U(); u.nk = 64; u.chain = chain; u.first = True; u.last = False; u.tile = ncache
            u.zmm = []; u.vmm = []
            for p in range(4):
                u.zmm.append((knew[:, p, 64 * s:64 * s + 64], qb[:, p, :], p * 128, 128))
                u.vmm.append((vnew[0:64, s, p * 128:(p + 1) * 128], p * 128, 128))
            u.reads = [r_new, r_qb]; u.vreads = [r_new]
            units.append(u)
            for t in range(ncache - 1, -1, -1):
                u = U(); u.nk = 128; u.chain = chain; u.first = False; u.last = (t == 0); u.tile = t
                u.dma = (lambda u, t=t: dma_kv_tile(u, kdram, vdram, s, t))

                def prep(u, t=t):
                    kt_, r_kt, vb, r_vb = load_kv_tile(u)
                    u.zmm = []; u.vmm = []
                    for p in range(4):
                        u.zmm.append((kt_[:, p, :], qb[:, p, :], p * 128, 128))
                        u.vmm.append((vb[:, p * 128:(p + 1) * 128], p * 128, 128))
                    u.reads = [r_kt, r_qb]; u.vreads = [r_vb]
                u.prep = prep
                units.append(u)
            return units

        def zmm4(u):
            z = []
            for i, (l, r, c0, n) in enumerate(u.zmm):
                z.append((l, r, c0, n, i == 0, i == len(u.zmm) - 1))
            u.zmm = z

        def attn0_sample():
            all_sb = []
            all_mla = []
            for s in range(NSTR):
                slots = [(h, h * 64, 64, 4 + h // 2, 64 * s) for h in range(8)]
                units = kv_units(s, PAST // 128, c_sbk, c_sbv, qA, r_qA, skT1, sv1, r_sk[0], slots)
                units[0].mask = maskSB64[:, :]; units[0].rmask = r_maskSB64
                units[0].reads = [r_sk[0], qbd[s][1]]; units[0].vreads = [r_sk[1]]
                for u in units:
                    u.rmask = r_maskSB64
                    u.fin = (lambda ob, slots=slots: fin_softmax(slots, ob, None, has_den=False))
                all_sb += units
            sb_chain(all_sb)
            for s in range(NSTR):
                chain = {}
                units = []
                slots = [(h, h * 64, 64, h // 2, 64 * s) for h in range(8)]
                u = U(); u.nk = 64; u.chain = chain; u.first = True; u.last = False
                u.zmm = [(sckvT[:, 64 * s:64 * s + 64], qlat[:, :, 64 * s:64 * s + 64], 0, 512, True, False),
                         (skrT[0:32, 64 * s:64 * s + 64], qrT[:, :, 64 * s:64 * s + 64], 0, 512, False, True)]
                u.reads = [r_sk[5], r_qlat, r_qrT]; u.scale = A_SCALE
                u.vmm = [(sckv_tm[0:64, s, :], 0, 512)]; u.vreads = [r_sk[4]]
                units.append(u)
                for t in range(PAST // 128 - 1, -1, -1):
                    u = U(); u.nk = 128; u.chain = chain; u.first = False; u.last = (t == 0); u.scale = A_SCALE

                    def dma_(u, t=t, s=s):
                        u.cf, u.r_cf = cst_f.next()
                        fw.dma("sp", u.cf[:, 0:128], c_ckv[s, t * 128:(t + 1) * 128, :], writes=[u.r_cf])
                        fw.dma("sp", u.cf[:, 128:160], c_kr[s, t * 128:(t + 1) * 128, :], writes=[u.r_cf])
                    u.dma = dma_

                    def prep(u, t=t, s=s):
                        cf, r_cf = u.cf, u.r_cf
                        vb, r_vb = vbf.next()
                        cast(vb[:, 0:160], cf[:, 0:160], [r_cf], [r_vb])
                        fw.op("pe", lambda t_, vb=vb: t_.transpose(psb[:, 0:128], vb[:, 0:128], ident[:, :]), reads=[r_vb, r_ident], writes=[r_psb])
                        fw.op("pe", lambda t_, vb=vb: t_.transpose(psb[0:32, 128:256], vb[:, 128:160], ident[:, :]), reads=[r_vb, r_ident], writes=[r_psb])
                        kt_, r_kt = cKT.next()
                        fw.op("dve", lambda v, kt_=kt_: v.tensor_copy(out=kt_[:, 0, :], in_=psb[:, 0:128]), reads=[r_psb], writes=[r_kt])
                        fw.op("dve", lambda v, kt_=kt_: v.tensor_copy(out=kt_[0:32, 1, :], in_=psb[0:32, 128:256]), reads=[r_psb], writes=[r_kt])
                        u.zmm = [(kt_[:, 0, :], qlat[:, :, 64 * s:64 * s + 64], 0, 512, True, False),
                                 (kt_[0:32, 1, :], qrT[:, :, 64 * s:64 * s + 64], 0, 512, False, True)]
                        u.reads = [r_kt, r_qlat, r_qrT]
                        u.vmm = [(vb[:, 0:128], 0, 512)]; u.vreads = [r_vb]
                    u.prep = prep
                    units.append(u)
                for u in units:
                    u.fin = mla_fin_factory(lambda slots=slots: slots)
                all_mla += units
            sm_chain(all_mla)

        r_kc = [Res() for _ in range(8)]

        def setup_l1_tables():
            tb2 = tblB[:].rearrange("p a b -> p (a b)")
            fw.dma("sp", tblB[:], bass.AP(relb.tensor, 1, [[1, 128], [513, 8], [1, 256]]), writes=[r_tblB])
            for j in range(4):
                fw.op("pe", lambda t_, j=j: t_.matmul(ps[:, j, :], lhsT=flipJ[:, :], rhs=tb2[:, j * 512:(j + 1) * 512], start=True, stop=True),
                      reads=[r_flip, r_tblB], writes=[r_ps[j]])
            for j in range(4):
                fw.op("dve", lambda v, j=j: v.tensor_copy(out=tb2[:, j * 512:(j + 1) * 512], in_=ps[:, j, :]), reads=[r_ps[j]], writes=[r_tblB])
            fw.op("dve", lambda v: v.tensor_tensor(out=tblB[:, :, 0:128], in0=tblB[:, :, 0:128],
                                                   in1=maskCH[:, :].unsqueeze(1).broadcast_to([128, 8, 128]), op=ALU.add),
                  reads=[r_tblB, r_maskCH], writes=[r_tblB])
            fw.dma("sp", cstB[:], bass.AP(relc.tensor, 0, [[0, 128], [1, 8]]), writes=[r_cstB])

        def phase_proj1(tiles, is_s, o_fk, o_fv, o_lf):
            for ti, (nt, row0, col0) in enumerate(tiles):
                gt = (row0 // 128) if not is_s else ti
                b = gbank.next()
                tm_proj(nt, col0, 512, 512, b)
                stg, r_stg = st512.next()
                fw.op("act", lambda a, b=b, stg=stg, nt=nt: a.copy(out=stg[0:nt, :], in_=ps[0:nt, b, :]), reads=[r_ps[b]], writes=[r_stg])
                if is_s:
                    fw.dma("sp", o_bk_s[ti, 448:512, :], stg[0:nt, :], reads=[r_stg])
                elif gt >= 12:
                    fw.dma("sp", o_bk_p[(gt - 12) * 128:(gt - 11) * 128, :], stg[0:nt, :], reads=[r_stg])
                b = gbank.next()
                tm_proj(nt, col0, 1024, 512, b)
                stg, r_stg = st512.next()
                fw.op("act", lambda a, b=b, stg=stg, nt=nt: a.copy(out=stg[0:nt, :], in_=ps[0:nt, b, :]), reads=[r_ps[b]], writes=[r_stg])
                if is_s:
                    fw.dma("sp", o_bv_s[ti, 448:512, :], stg[0:nt, :], reads=[r_stg])
                    fw.op("dve", lambda v, stg=stg, ti=ti, nt=nt: v.tensor_copy(out=sv1[0:nt, ti, :], in_=stg[0:nt, :]), reads=[r_stg], writes=[r_sk[1]])
                else:
                    if gt >= 12:
                        fw.dma("sp", o_bv_p[(gt - 12) * 128:(gt - 11) * 128, :], stg[0:nt, :], reads=[r_stg])
                    fw.op("dve", lambda v, stg=stg, gt=gt: v.tensor_copy(out=vc[:, gt % 8, :], in_=stg[:, :]), reads=[r_stg], writes=[r_kc[gt % 8]])
                b = gbank.next()
                tm_proj(nt, col0, 2560, 512, b)
                stg, r_stg = st512.next()
                fw.op("act", lambda a, b=b, stg=stg, nt=nt: a.copy(out=stg[0:nt, :], in_=ps[0:nt, b, :]), reads=[r_ps[b]], writes=[r_stg])
                fw.dma("sp", o_fk[row0:row0 + nt, :], stg[0:nt, :], reads=[r_stg])
                b = gbank.next()
                tm_proj(nt, col0, 3072, 512, b)
                stg, r_stg = st512.next()
                fw.op("act", lambda a, b=b, stg=stg, nt=nt: a.copy(out=stg[0:nt, :], in_=ps[0:nt, b, :]), reads=[r_ps[b]], writes=[r_stg])
                fw.dma("sp", o_fv[row0:row0 + nt, :], stg[0:nt, :], reads=[r_stg])
                if is_s:
                    fw.op("dve", lambda v, stg=stg, ti=ti, nt=nt: v.tensor_copy(out=sv2[0:nt, ti, :], in_=stg[0:nt, :]), reads=[r_stg], writes=[r_sk[3]])
                else:
                    fw.op("dve", lambda v, stg=stg, gt=gt: v.tensor_copy(out=v2[:, gt, :], in_=stg[:, :]), reads=[r_stg], writes=[r_k2[gt]])
                b = gbank.next()
                tm_proj(nt, col0, 3584, 8, b)
                lf, r_lf = lfb.next()
                fw.op("dve", lambda v, b=b, lf=lf, nt=nt: v.tensor_tensor(out=lf[0:nt, 0:8], in0=ps[0:nt, b, 0:8], in1=fb_bc[0:nt, :], op=ALU.add),
                      reads=[r_ps[b], r_fb], writes=[r_lf])
                fw.op("act", lambda a, lf=lf, nt=nt: a.activation(out=lf[0:nt, 8:16], in_=lf[0:nt, 0:8], func=AF.Exp, scale=-1.0), reads=[r_lf], writes=[r_lf])
                fw.op("act", lambda a, lf=lf, nt=nt: a.activation(out=lf[0:nt, 0:8], in_=lf[0:nt, 8:16], func=AF.Ln, bias=1.0), reads=[r_lf], writes=[r_lf])
                fw.op("dve", lambda v, lf=lf, nt=nt: v.tensor_scalar(out=lf[0:nt, 16:24], in0=lf[0:nt, 0:8], scalar1=-1.0, scalar2=None, op0=ALU.mult),
                      reads=[r_lf], writes=[r_lf])
                fw.dma("sp", o_lf[row0:row0 + nt, :], lf[0:nt, 16:24], reads=[r_lf])
                b2 = gbank.next()
                if not is_s:
                    fw.op("pe", lambda t_, b2=b2, lf=lf: t_.matmul(ps[:, b2, 0:8], lhsT=triF[:, :], rhs=lf[:, 16:24], start=True, stop=False),
                          reads=[r_tri, r_lf], writes=[r_ps[b2]])
                    fw.op("pe", lambda t_, b2=b2, lf=lf: t_.matmul(ps[:, b2, 8:16], lhsT=onesF[:, :], rhs=lf[:, 16:24], start=False, stop=True),
                          reads=[r_onesF, r_lf], writes=[r_ps[b2]])
                    fw.op("dve", lambda v, b2=b2, gt=gt: v.tensor_scalar(out=fxb[:, 0, gt, :], in0=ps[:, b2, 0:8], scalar1=-1.0, scalar2=None, op0=ALU.mult),
                          reads=[r_ps[b2]], writes=[r_fxb])
                    fw.op("dve", lambda v, b2=b2, gt=gt: v.tensor_copy(out=fxb[:, 2, gt, :], in_=ps[:, b2, 8:16]), reads=[r_ps[b2]], writes=[r_fxb])
                    fw.op("dve", lambda v, gt=gt: v.tensor_tensor(out=fxb[:, 1, gt, :], in0=fxb[:, 2, gt, :], in1=fxb[:, 0, gt, :], op=ALU.add),
                          reads=[r_fxb], writes=[r_fxb])
                else:
                    fw.op("pe", lambda t_, b2=b2, lf=lf: t_.matmul(ps[0:64, b2, 0:8], lhsT=triF[0:64, 0:64], rhs=lf[0:64, 16:24], start=True, stop=True),
                          reads=[r_tri, r_lf], writes=[r_ps[b2]])
                    fw.op("dve", lambda v, b2=b2, ti=ti: v.tensor_scalar(out=sfxn[0:64, ti, :], in0=ps[0:64, b2, 0:8], scalar1=-1.0, scalar2=None, op0=ALU.mult),
                          reads=[r_ps[b2]], writes=[r_sfxn])
            fm_group([0 + 128 * i for i in range(4)],
                     lambda i, src, rb: evac_bd(qA, r_qA, i, src, rb))
            fm_group([2048 + 128 * i for i in range(4)],
                     lambda i, src, rb: evac_bd(qBd, r_qB, i, src, rb))
            if is_s:
                fm_group([512 + 128 * i for i in range(4)],
                         lambda i, src, rb: fw.op("dve", lambda v: v.tensor_copy(out=skT1[:, i, :], in_=src), reads=[rb], writes=[r_sk[0]]))
                fm_group([2560 + 128 * i for i in range(4)],
                         lambda i, src, rb: fw.op("dve", lambda v: v.tensor_copy(out=skT2[:, i, :], in_=src), reads=[rb], writes=[r_sk[2]]))
            else:
                t0 = tiles[0][1]
                g0 = t0 // 128
                rc = (t0 % 1024)

                def evc(i, src, rb):
                    fw.op("dve", lambda v: v.tensor_copy(out=kTc[:, i, rc:rc + BLK], in_=src), reads=[rb], writes=[r_kc[g0 % 8], r_kc[(g0 + 1) % 8]])

                def evd(i, src, rb):
                    fw.op("dve", lambda v: v.tensor_copy(out=kT2[:, i, t0:t0 + BLK], in_=src), reads=[rb], writes=[r_k2[g0], r_k2[g0 + 1]])
                fm_group([512 + 128 * i for i in range(4)], evc)
                fm_group([2560 + 128 * i for i in range(4)], evd)
            fm_group([1536 + 128 * i for i in range(4)] + [3592 + 128 * i for i in range(4)],
                     lambda i, src, rb: fw.op("act", lambda a: a.activation(out=gT[:, i, :], in_=src, func=AF.Silu),
                                              reads=[rb], writes=[r_gT[i]]))

        def band_pre(u, dd, hs, nq):
            nk = u.nk
            H = hs.stop - hs.start
            if dd == 0:
                return pre_add(u, tblB[0:nk, hs, 0:nq], [r_tblB])
            if dd == 1:
                return pre_add(u, tblB[0:nk, hs, 128:128 + nq], [r_tblB])
            cst = cstB[0:nk, hs].unsqueeze(2).broadcast_to([nk, H, nq])
            if dd == 4:
                return pre_add(u, cst, [r_cstB], extra=(mask512[:, :].unsqueeze(1).broadcast_to([128, H, 128]), [r_mask512]))
            return pre_add(u, cst, [r_cstB])

        def attn1_prompt(bi):
            all_band = []
            all_fox = []
            for qt in (2 * bi, 2 * bi + 1):
                qc = (qt % 2) * 128
                biasq = _T(biasq2[:, qt % 2])
                for kt in range(qt - 1, -1, -1):
                    if kt == qt - 1:
                        fw.op("dve", lambda v, kt=kt, biasq=biasq: v.tensor_copy(out=biasq[:, kt, :], in_=fxb[:, 1, kt, :]), reads=[r_fxb], writes=[r_biasq])
                        fw.op("dve", lambda v, kt=kt, biasq=biasq: v.tensor_copy(out=accb[:, :], in_=fxb[:, 2, kt, :]), reads=[r_fxb], writes=[r_accb])
                    else:
                        fw.op("dve", lambda v, kt=kt, biasq=biasq: v.tensor_tensor(out=biasq[:, kt, :], in0=fxb[:, 1, kt, :], in1=accb[:, :], op=ALU.add),
                              reads=[r_fxb, r_accb], writes=[r_biasq])
                        fw.op("dve", lambda v, kt=kt, biasq=biasq: v.tensor_tensor(out=accb[:, :], in0=accb[:, :], in1=fxb[:, 2, kt, :], op=ALU.add),
                              reads=[r_fxb, r_accb], writes=[r_accb])
                for hg in range(2):
                    hs = slice(4 * hg, 4 * hg + 4)
                    chain = {}
                    units = []
                    kts = list(range(qt, max(-1, qt - 5), -1))
                    for kt in kts:
                        u = U(); u.nk = 128; u.chain = chain
                        u.first = (kt == kts[0]); u.last = (kt == kts[-1])
                        u.zmm = []; u.vmm = []
                        sl = kt % 8
                        for pp in range(2):
                            u.zmm.append((kTc[:, 2 * hg + pp, sl * 128:(sl + 1) * 128], qA[:, 2 * hg + pp, qt % 2, :], pp * 256, 256))
                        for pp in range(2):
                            u.vmm.append((vc[:, sl, (2 * hg + pp) * 128:(2 * hg + pp + 1) * 128], pp * 256, 256))
                        zmm4(u)
                        u.reads = [r_kc[sl], r_qA]; u.vreads = [r_kc[sl]]
                        u.pre = (lambda u, dd=qt - kt, hs=hs: band_pre(u, dd, hs, 128))
                        slots = [(4 * hg + s_, s_ * 128, 128, (4 * hg + s_) // 2, qc) for s_ in range(4)]
                        u.fin = (lambda ob, db, slots=slots: fin_softmax(slots, ob, db))
                        units.append(u)
                    all_band += units
                    chain = {}
                    units = []
                    for kt in range(qt, -1, -1):
                        u = U(); u.nk = 128; u.chain = chain
                        u.first = (kt == qt); u.last = (kt == 0)
                        u.zmm = []; u.vmm = []
                        for pp in range(2):
                            u.zmm.append((kT2[:, 2 * hg + pp, kt * 128:(kt + 1) * 128], qBd[:, 2 * hg + pp, qt % 2, :], pp * 256, 256))
                        for pp in range(2):
                            u.vmm.append((v2[:, kt, (2 * hg + pp) * 128:(2 * hg + pp + 1) * 128], pp * 256, 256))
                        zmm4(u)
                        u.reads = [r_k2[kt], r_qB]; u.vreads = [r_k2[kt]]
                        if kt == qt:
                            u.pre = (lambda u, qt=qt, hs=hs: pre_add(u, fxb[:, 0, qt, hs].unsqueeze(2).broadcast_to([128, 4, 128]), [r_fxb],
                                                                      extra=(maskFX[:, :].unsqueeze(1).broadcast_to([128, 4, 128]), [r_maskFX])))
                        else:
                            u.pre = (lambda u, kt=kt, hs=hs, biasq=biasq: pre_add(u, biasq[:, kt, hs].unsqueeze(2).broadcast_to([128, 4, 128]), [r_biasq]))
                        slots = [(4 * hg + s_, s_ * 128, 128, 4 + (4 * hg + s_) // 2, qc) for s_ in range(4)]
                        u.fin = (lambda ob, db, slots=slots: fin_softmax(slots, ob, db))
                        units.append(u)
                    all_fox += units
            sm_chain(all_band)
            sm_chain(all_fox)

        def attn1_sample():
            hs8 = slice(0, 8)
            for s in range(NSTR):
                fw.dma("sp", o_bk_s[s, 0:448, :], c_bk[s, 64:512, :])
                fw.dma("sp", o_bv_s[s, 0:448, :], c_bv[s, 64:512, :])
                slots = [(h, h * 64, 64, h // 2, 64 * s) for h in range(8)]
                units = kv_units(s, 4, c_bk, c_bv, qA, r_qA, skT1, sv1, r_sk[0], slots)
                units[0].reads = [r_sk[0], qbd[s][1]]; units[0].vreads = [r_sk[1]]
                zmm4(units[0])
                units[0].pre = (lambda u: band_pre(u, 0, hs8, 64))
                for u in units[1:]:
                    dd = 4 - u.tile
                    op_ = u.prep

                    def prep2(u, op_=op_):
                        op_(u)
                        zmm4(u)
                    u.prep = prep2
                    u.pre = (lambda u, dd=dd: band_pre(u, 1 if dd == 1 else 2, hs8, 64))
                for u in units:
                    u.fin = (lambda ob, db, slots=slots: fin_softmax(slots, ob, db))
                sm_chain(units)
            for s in range(NSTR):
                fw.dma("sp", lfc[:], c_lf[s].rearrange("(t p) h -> p t h", p=128), writes=[r_lfc])
                lf2 = lfc[:].rearrange("p t h -> p (t h)")
                b2 = gbank.next()
                fw.op("pe", lambda t_, b2=b2: t_.matmul(ps[:, b2, 0:256], lhsT=triF[:, :], rhs=lf2, start=True, stop=False),
                      reads=[r_tri, r_lfc], writes=[r_ps[b2]])
                fw.op("pe", lambda t_, b2=b2: t_.matmul(ps[:, b2, 256:512], lhsT=onesF[:, :], rhs=lf2, start=False, stop=True),
                      reads=[r_onesF, r_lfc], writes=[r_ps[b2]])
                fw.op("dve", lambda v, b2=b2: v.tensor_copy(out=lf2, in_=ps[:, b2, 256:512]), reads=[r_ps[b2]], writes=[r_lfc])
                sf2 = sfx[:, 0:32, :].rearrange("p t h -> p (t h)")
                fw.op("dve", lambda v, b2=b2: v.tensor_tensor(out=sf2, in0=lf2, in1=ps[:, b2, 0:256], op=ALU.subtract),
                      reads=[r_ps[b2], r_lfc], writes=[r_sfx])
                for t in range(30, -1, -1):
                    if t == 30:
                        fw.op("dve", lambda v: v.tensor_copy(out=accb[:, :], in_=lfc[:, 31, :]), reads=[r_lfc], writes=[r_accb])
                    else:
                        fw.op("dve", lambda v, t=t: v.tensor_tensor(out=accb[:, :], in0=accb[:, :], in1=lfc[:, t + 1, :], op=ALU.add),
                              reads=[r_lfc, r_accb], writes=[r_accb])
                    fw.op("dve", lambda v, t=t: v.tensor_tensor(out=sfx[:, t, :], in0=sfx[:, t, :], in1=accb[:, :], op=ALU.add),
                          reads=[r_sfx, r_accb], writes=[r_sfx])
                slots = [(h, h * 64, 64, 4 + h // 2, 64 * s) for h in range(8)]
                units = kv_units(s, PAST // 128, c_fk, c_fv, qBd, r_qB, skT2, sv2, r_sk[2], slots)
                units[0].reads = [r_sk[2], qbd[s][1]]; units[0].vreads = [r_sk[3]]
                zmm4(units[0])
                units[0].pre = (lambda u, s=s: pre_add(u, sfxn[0:64, s, :].unsqueeze(2).broadcast_to([64, 8, 64]), [r_sfxn],
                                                       extra=(maskFX[0:64, 0:64].unsqueeze(1).broadcast_to([64, 8, 64]), [r_maskFX])))
                for u in units[1:]:
                    op_ = u.prep

                    def prep3(u, op_=op_):
                        op_(u)
                        zmm4(u)
                    u.prep = prep3
                    u.pre = (lambda u: pre_add(u, sfx[:, u.tile, :].unsqueeze(2).broadcast_to([128, 8, 64]), [r_sfx]))
                for u in units:
                    u.fin = (lambda ob, db, slots=slots: fin_softmax(slots, ob, db))
                sm_chain(units)

        def phase_out(l, tiles, xsrc, r_xsrc, ydst, r_ydst):
            for (nt, row0, col0) in tiles:
                b0 = gbank.next(); b1 = gbank.next()
                for half, b in ((0, b0), (1, b1)):
                    for c in range(8):
                        fw.op("pe", lambda t, c=c, b=b, half=half, nt=nt, col0=col0: t.matmul(ps[0:nt, b, :], lhsT=gT[:, c, col0:col0 + nt],
                                                                                               rhs=wout[:, c, half * 512:(half + 1) * 512],
                                                                                               start=(c == 0), stop=(c == 7)),
                              reads=[r_gT[c], r_wout], writes=[r_ps[b]])
                s, r_s = sm.next()
                fw.op("act", lambda a, s=s, nt=nt, b0=b0: a.activation(out=junk[0:nt, :], in_=ps[0:nt, b0, :], func=AF.Square, accum_out=s[0:nt, 4:5]),
                      reads=[r_ps[b0]], writes=[r_junk, r_s])
                fw.op("act", lambda a, s=s, nt=nt, b1=b1: a.activation(out=junk[0:nt, :], in_=ps[0:nt, b1, :], func=AF.Square, accum_out=s[0:nt, 5:6]),
                      reads=[r_ps[b1]], writes=[r_junk, r_s])
                fw.op("dve", lambda v, s=s, nt=nt: v.tensor_tensor(out=s[0:nt, 2:3], in0=s[0:nt, 4:5], in1=s[0:nt, 5:6], op=ALU.add), reads=[r_s], writes=[r_s])
                rs, r_rs = rstd_from_ss(s[0:nt, 2:3], D, nt, r_s)
                xt, r_xt = xin.next()
                fw.dma("sp", xt[0:nt, :], xsrc[row0:row0 + nt, :], reads=[r_xsrc[row0 // 64]] if r_xsrc else [], writes=[r_xt])
                for half, b in ((0, b0), (1, b1)):
                    y, r_y = tmpf.next()
                    fw.op("dve", lambda v, half=half, b=b, y=y, rs=rs, nt=nt: v.scalar_tensor_tensor(
                        out=y[0:nt, :], in0=ps[0:nt, b, :], scalar=rs, in1=gpost[0:nt, half * 512:(half + 1) * 512],
                        op0=ALU.mult, op1=ALU.mult), reads=[r_ps[b], r_rs, r_gpost], writes=[r_y])
                    fw.op("pool", lambda g, y=y, xt=xt, nt=nt, half=half: g.tensor_tensor(out=xt[0:nt, half * 512:(half + 1) * 512], in0=y[0:nt, :],
                                                                                          in1=xt[0:nt, half * 512:(half + 1) * 512], op=ALU.add),
                          reads=[r_y, r_xt], writes=[r_xt])
                fw.dma("sp", ydst[row0:row0 + nt, :], xt[0:nt, :], reads=[r_xt], writes=[r_ydst[row0 // 64]] if r_ydst else [])

        r_x1p = [Res() for _ in range(SEQ // 64)]
        r_x1s = [Res() for _ in range(NSTR * DSEQ // 64)]
        ptiles = lambda bi: [(128, bi * BLK, 0), (128, bi * BLK + 128, 128)]
        stiles = [(64, 64 * s, 64 * s) for s in range(NSTR)]

        load_layer_weights(0)
        nblk = min(SEQ // BLK, NBLK_DBG)
        for bi in range(nblk):
            phase_norm(0, ptiles(bi), xp, None)
            phase_proj0(ptiles(bi), False, o_ckv_p, o_kr_p, o_sbk_p, o_sbv_p)
            if STAGES >= 2:
                attn0_prompt(bi)
            phase_out(0, ptiles(bi), xp, None, x1p, r_x1p)
        if STAGES >= 1:
            fw.barrier()
            phase_norm(0, stiles, xs, None)
            phase_proj0(stiles, True, o_ckv_s, o_kr_s, o_sbk_s, o_sbv_s)
            if STAGES >= 3:
                attn0_sample()
            phase_out(0, stiles, xs, None, x1s, r_x1s)
        if STAGES >= 4:
            load_layer_weights(1)
            setup_l1_tables()
            fw.op("pool", lambda g: g.memset(qBd[:].rearrange("p a t q -> p (a t q)"), 0.0), writes=[r_qB])
            fw.barrier()
            for bi in range(nblk):
                phase_norm(1, ptiles(bi), x1p, r_x1p)
                phase_proj1(ptiles(bi), False, o_fk_p, o_fv_p, o_lf_p)
                if STAGES >= 5:
                    attn1_prompt(bi)
                phase_out(1, ptiles(bi), x1p, r_x1p, y_p, None)
            fw.barrier()
            phase_norm(1, stiles, x1s, r_x1s)
            phase_proj1(stiles, True, o_fk_s, o_fv_s, o_lf_s)
            if STAGES >= 6:
                attn1_sample()
            phase_out(1, stiles, x1s, r_x1s, y_s, None)

        print("fw ops recorded:", getattr(fw, "nops", 0), {k: e.cnt for k, e in fw.E.items()})
        fw.finish()
        fw.emit()
    return nc


def _rope_tables():
    half = 16
    inv = (10000.0 ** (-np.arange(half, dtype=np.float32) / half)).astype(np.float32)

    def tab(pos):
        ang = pos.astype(np.float32)[:, None] * inv[None, :]
        c = np.cos(ang).astype(np.float32)
        s = np.sin(ang).astype(np.float32)
        return np.concatenate([c, c, -s, s], axis=1).astype(np.float32)
    tp = tab(np.arange(SEQ)).reshape(16, 128, 64).transpose(1, 0, 2).reshape(128, 16 * 64)
    tsm = tab(PAST + np.arange(DSEQ))
    return np.ascontiguousarray(tp), np.ascontiguousarray(tsm)


_NC_CACHE = {}


def kernel(**inp):
    f = lambda a: np.ascontiguousarray(np.asarray(a, dtype=np.float32))
    x_prompt = f(inp["x_prompt"]); x_sample = f(inp["x_sample"])
    rope_p, rope_s = _rope_tables()
    w_uq = f(inp["a_w_uq"])[0]
    w_uq_l = np.concatenate([w_uq[:, :, :64].reshape(256, 512), w_uq[:, :, 64:].reshape(256, 256)], axis=1)
    w_uk = f(inp["a_w_uk"])[0]
    w_ukT = np.transpose(w_uk, (2, 1, 0)).reshape(64, 1024)
    w_ukT = np.concatenate([w_ukT, w_ukT], axis=0)
    relb = f(inp["c_rel_bias"])[0]
    relb_pad = np.concatenate([relb, np.repeat(relb[:, -1:], 256, axis=1)], axis=1)
    shared = {
        "norm_pre": np.ascontiguousarray(f(inp["norm_pre"]).reshape(2, 8, 128).transpose(2, 0, 1).reshape(128, 16)), "norm_post": f(inp["norm_post"]),
        "w_in0": f(inp["w_in_even"])[0], "q_norm": f(inp["a_q_norm"]).reshape(1, 256),
        "w_uq": np.ascontiguousarray(w_uq_l), "kv_norm": f(inp["a_kv_norm"]).reshape(1, 128),
        "w_ukT": np.ascontiguousarray(w_ukT), "w_uv": f(inp["a_w_uv"])[0].reshape(128, 512),
        "w_out0": f(inp["w_out_even"])[0], "w_in1": f(inp["w_in_odd"])[0],
        "relb": np.ascontiguousarray(relb_pad), "fbias": f(inp["d_forget_bias"]).reshape(1, 8),
        "relc": np.ascontiguousarray(relb[:, 256].reshape(1, 8)),
        "w_out1": f(inp["w_out_odd"])[0], "rope_p": rope_p, "rope_s": rope_s,
    }
    caches = {k: f(inp[k])[0] for k in ("cache_mla_ckv", "cache_mla_krope", "cache_sb_k", "cache_sb_v", "cache_band_k",
                                        "cache_band_v", "cache_fox_k", "cache_fox_v", "cache_fox_logf")}
    in_maps = []
    for c in range(NCORES):
        sl = slice(NSTR * c, NSTR * (c + 1))
        m = dict(shared)
        m["xp"] = x_prompt[c]
        m["xs"] = x_sample[sl].reshape(NSTR * DSEQ, D)
        m["c_ckv"] = caches["cache_mla_ckv"][sl]
        m["c_kr"] = caches["cache_mla_krope"][sl]
        m["c_sbk"] = caches["cache_sb_k"][sl].reshape(NSTR, PAST, 512)
        m["c_sbv"] = caches["cache_sb_v"][sl].reshape(NSTR, PAST, 512)
        m["c_bk"] = caches["cache_band_k"][sl].reshape(NSTR, 512, 512)
        m["c_bv"] = caches["cache_band_v"][sl].reshape(NSTR, 512, 512)
        m["c_fk"] = caches["cache_fox_k"][sl].reshape(NSTR, PAST, 512)
        m["c_fv"] = caches["cache_fox_v"][sl].reshape(NSTR, PAST, 512)
        m["c_lf"] = caches["cache_fox_logf"][sl]
        in_maps.append({k: np.ascontiguousarray(v) for k, v in m.items()})
    if "nc" not in _NC_CACHE:
        _NC_CACHE["nc"] = build()
    nc = _NC_CACHE["nc"]
    if KCORES < NCORES:
        res = run_bass_kernel_spmd(nc, in_maps[:KCORES], core_ids=list(range(KCORES)))
        R = list(res.results) + [res.results[0]] * (NCORES - KCORES)
    else:
        res = run_bass_kernel_spmd(nc, in_maps, core_ids=list(range(NCORES)))
        R = res.results
    cat = lambda k: np.stack([R[c][k] for c in range(NCORES)], axis=0)
    B = NCORES
    SB = NCORES * NSTR
    outs = (
        cat("y_p").reshape(B, SEQ, D),
        cat("y_s").reshape(SB, DSEQ, D),
        cat("o_ckv_p").reshape(1, B, SEQ, 128), cat("o_kr_p").reshape(1, B, SEQ, 32),
        cat("o_sbk_p").reshape(1, B, SEQ, 8, 64), cat("o_sbv_p").reshape(1, B, SEQ, 8, 64),
        cat("o_bk_p").reshape(1, B, 512, 8, 64), cat("o_bv_p").reshape(1, B, 512, 8, 64),
        cat("o_fk_p").reshape(1, B, SEQ, 8, 64), cat("o_fv_p").reshape(1, B, SEQ, 8, 64), cat("o_lf_p").reshape(1, B, SEQ, 8),
        cat("o_ckv_s").reshape(1, SB, DSEQ, 128), cat("o_kr_s").reshape(1, SB, DSEQ, 32),
        cat("o_sbk_s").reshape(1, SB, DSEQ, 8, 64), cat("o_sbv_s").reshape(1, SB, DSEQ, 8, 64),
        cat("o_bk_s").reshape(1, SB, 512, 8, 64), cat("o_bv_s").reshape(1, SB, 512, 8, 64),
        cat("o_fk_s").reshape(1, SB, DSEQ, 8, 64), cat("o_fv_s").reshape(1, SB, DSEQ, 8, 64), cat("o_lf_s").reshape(1, SB, DSEQ, 8),
    )
    _NC_CACHE["x1"] = (cat("x1p"), cat("x1s"))
    return tuple(np.ascontiguousarray(o.astype(np.float32)) for o in outs)
```

```python
import numpy as np
from contextlib import ExitStack
import concourse.bass as bass
import concourse.mybir as mybir
from concourse.bass_utils import run_bass_kernel_spmd

F32 = mybir.dt.float32
BF16 = mybir.dt.bfloat16
AF = mybir.ActivationFunctionType
ALU = mybir.AluOpType

NCORES = 8
D = 1024
SEQ = 2048
NSTR = 4
DSEQ = 64
PAST = 4096
EPS = 1e-6
NEGM = -30000.0
A_SCALE = float((64 + 32) ** -0.5)
EVEN_IN = 2976
ODD_IN = 4104
BLK = 256
import os
STAGES = int(os.environ.get('KSTAGES', '6'))
NBLK_DBG = int(os.environ.get('KNBLK', '8'))
OPLIMIT = int(os.environ.get('KOPLIMIT', '100000000'))
KCORES = int(os.environ.get('KCORES', '8'))
KSAME = int(os.environ.get('KSAME', '0'))


class Res:
    __slots__ = ("w", "rs", "x", "rg")

    def __init__(self, x=False):
        self.w = None
        self.rs = []
        self.rg = None
        self.x = x


class Eng:
    def __init__(self, name, sem):
        self.name = name
        self.sem = sem
        self.cnt = 0
        self.waited = {}
        self.prog = []
        self.dq = []
        self.dcnt = []
        self.di = 0


class FW:
    def __init__(self, nc, es, ndq=8):
        self.nc = nc
        self.E = {}
        for name in ("pe", "act", "dve", "pool", "sp"):
            self.E[name] = Eng(name, es.enter_context(nc.semaphore("s_" + name)))
        for qn in ("sp", "act", "pool"):
            e = self.E[qn]
            for i in range(ndq):
                e.dq.append(es.enter_context(nc.semaphore(f"d_{qn}{i}")))
                e.dcnt.append(0)

    def _wait(self, eng, tok, force=False):
        if tok is None:
            return
        sem, val, src = tok
        if src == eng.name and src in ("pe", "sp") and not force:
            return
        key = id(sem)
        if eng.waited.get(key, 0) >= val:
            return
        eng.waited[key] = val
        eng.prog.append(("w", sem, val))

    def _deps(self, eng, reads, writes):
        for r in reads:
            self._wait(eng, r.w)
            if r.x:
                for t in r.rs:
                    if t[2] != eng.name:
                        self._wait(eng, t)
        for w in writes:
            if w.w is not None and (w.w[2] != eng.name or KSAME):
                self._wait(eng, w.w)
            for t in w.rs:
                if t[2] != eng.name or KSAME:
                    self._wait(eng, t)

    def _record(self, tok, reads, writes):
        for r in reads:
            r.rs.append(tok)
            if len(r.rs) > 16:
                best = {}
                for t in r.rs:
                    k = id(t[0])
                    if k not in best or best[k][1] < t[1]:
                        best[k] = t
                r.rs = list(best.values())
        for w in writes:
            w.w = tok
            w.rs = []

    def op(self, engname, fn, reads=(), writes=(), rg=None):
        self.nops = getattr(self, "nops", 0) + 1
        if self.nops > OPLIMIT:
            return None
        eng = self.E[engname]
        self._deps(eng, reads, writes)
        if engname == "pe":
            for w in writes:
                if rg is not None and w.rg is not None and w.rg != rg and w.w is not None and w.w[2] == "pe":
                    self._wait(eng, w.w, force=True)
                w.rg = rg
        eng.cnt += 1
        eng.prog.append(("i", fn, eng.sem, 1))
        tok = (eng.sem, eng.cnt, eng.name)
        self._record(tok, reads, writes)
        return tok

    def dma(self, q, out, in_, reads=(), writes=()):
        self.nops = getattr(self, "nops", 0) + 1
        if self.nops > OPLIMIT:
            return None
        eng = self.E[q]
        self._deps(eng, reads, writes)
        i = eng.di % len(eng.dq)
        eng.di += 1
        sem = eng.dq[i]
        if eng.dcnt[i] > 0:
            self._wait(eng, (sem, eng.dcnt[i], "dma"))
        eng.prog.append(("i", (lambda o, out=out, in_=in_: o.dma_start(out=out, in_=in_)), sem, 16))
        eng.dcnt[i] += 16
        tok = (sem, eng.dcnt[i], "dma")
        self._record(tok, reads, writes)
        return tok

    def barrier(self):
        toks = []
        for q in ("sp", "act", "pool"):
            e = self.E[q]
            for i, sem in enumerate(e.dq):
                if e.dcnt[i] > 0:
                    toks.append((sem, e.dcnt[i], "dma"))
        for n in ("pe", "act", "dve", "pool"):
            e = self.E[n]
            if e.cnt > 0:
                toks.append((e.sem, e.cnt, "x"))
        for n in ("pe", "act", "dve", "pool", "sp"):
            for t in toks:
                self._wait(self.E[n], t)

    def finish(self):
        sp = self.E["sp"]
        for q in ("sp", "act", "pool"):
            e = self.E[q]
            for i, sem in enumerate(e.dq):
                if e.dcnt[i] > 0:
                    self._wait(sp, (sem, e.dcnt[i], "dma"))
        for n in ("pe", "act", "dve", "pool"):
            e = self.E[n]
            if e.cnt > 0:
                self._wait(sp, (e.sem, e.cnt, "x"))

    def emit(self):
        nc = self.nc
        objs = {"pe": None}

        def run(eng):
            def body(obj):
                for a in eng.prog:
                    if a[0] == "w":
                        obj.wait_ge(a[1], a[2])
                    else:
                        a[1](obj).then_inc(a[2], a[3])
            return body
        with nc.Block() as block:
            block.tensor(run(self.E["pe"]))
            block.scalar(run(self.E["act"]))
            block.vector(run(self.E["dve"]))
            block.gpsimd(run(self.E["pool"]))
            block.sync(run(self.E["sp"]))


class RR:
    def __init__(self, items):
        self.items = items
        self.i = 0

    def next(self):
        it = self.items[self.i % len(self.items)]
        self.i += 1
        return it


def pipeline(units, stages, offsets=None):
    n = len(units)
    ns = len(stages)
    if offsets is None:
        offsets = list(range(ns))
    for i in range(n + max(offsets)):
        for s, st in enumerate(stages):
            j = i - offsets[s]
            if 0 <= j < n:
                st(units[j])


def build():
    nc = bass.Bass("TRN2", target_bir_lowering=False)
    din = lambda n, s: nc.dram_tensor(n, s, F32, kind="ExternalInput").ap()
    dout = lambda n, s: nc.dram_tensor(n, s, F32, kind="ExternalOutput").ap()
    xp = din("xp", [SEQ, D])
    xs = din("xs", [NSTR * DSEQ, D])
    c_ckv = din("c_ckv", [NSTR, PAST, 128])
    c_kr = din("c_kr", [NSTR, PAST, 32])
    c_sbk = din("c_sbk", [NSTR, PAST, 512])
    c_sbv = din("c_sbv", [NSTR, PAST, 512])
    c_bk = din("c_bk", [NSTR, 512, 512])
    c_bv = din("c_bv", [NSTR, 512, 512])
    c_fk = din("c_fk", [NSTR, PAST, 512])
    c_fv = din("c_fv", [NSTR, PAST, 512])
    c_lf = din("c_lf", [NSTR, PAST, 8])
    norm_pre = din("norm_pre", [128, 16])
    norm_post = din("norm_post", [2, D])
    w_in0 = din("w_in0", [D, EVEN_IN])
    q_norm = din("q_norm", [1, 256])
    w_uq = din("w_uq", [256, 768])
    kv_norm = din("kv_norm", [1, 128])
    w_ukT = din("w_ukT", [128, 1024])
    w_uv = din("w_uv", [128, 512])
    w_out0 = din("w_out0", [D, D])
    w_in1 = din("w_in1", [D, ODD_IN])
    relb = din("relb", [8, 513])
    fbias = din("fbias", [1, 8])
    relc = din("relc", [1, 8])
    w_out1 = din("w_out1", [D, D])
    rope_p = din("rope_p", [128, 16 * 64])
    rope_s = din("rope_s", [64, 64])
    y_p = dout("y_p", [SEQ, D])
    y_s = dout("y_s", [NSTR * DSEQ, D])
    o_ckv_p = dout("o_ckv_p", [SEQ, 128]); o_kr_p = dout("o_kr_p", [SEQ, 32])
    o_sbk_p = dout("o_sbk_p", [SEQ, 512]); o_sbv_p = dout("o_sbv_p", [SEQ, 512])
    o_bk_p = dout("o_bk_p", [512, 512]); o_bv_p = dout("o_bv_p", [512, 512])
    o_fk_p = dout("o_fk_p", [SEQ, 512]); o_fv_p = dout("o_fv_p", [SEQ, 512]); o_lf_p = dout("o_lf_p", [SEQ, 8])
    o_ckv_s = dout("o_ckv_s", [NSTR * DSEQ, 128]); o_kr_s = dout("o_kr_s", [NSTR * DSEQ, 32])
    o_sbk_s = dout("o_sbk_s", [NSTR * DSEQ, 512]); o_sbv_s = dout("o_sbv_s", [NSTR * DSEQ, 512])
    o_bk_s = dout("o_bk_s", [NSTR, 512, 512]); o_bv_s = dout("o_bv_s", [NSTR, 512, 512])
    o_fk_s = dout("o_fk_s", [NSTR * DSEQ, 512]); o_fv_s = dout("o_fv_s", [NSTR * DSEQ, 512])
    o_lf_s = dout("o_lf_s", [NSTR * DSEQ, 8])
    x1p = dout("x1p", [SEQ, D])
    x1s = dout("x1s", [NSTR * DSEQ, D])

    with ExitStack() as es:
        fw = FW(nc, es)
        ARN = 105500
        AR = es.enter_context(nc.sbuf_tensor("AR", [128, ARN], BF16))
        ar = {"top": 0, "peak": 0}

        class _T:
            def __init__(self, ap):
                self.ap = ap
            def __getitem__(self, k):
                return self.ap[k]

        def sbt(n, s, d=F32):
            nel = int(np.prod(s[1:]))
            nb = nel * (4 if d == F32 else 2)
            nb = (nb + 63) // 64 * 64
            off = ar["top"]
            ar["top"] += nb // 2
            ar["peak"] = max(ar["peak"], ar["top"])
            assert ar["top"] <= ARN, (n, ar["top"])
            v = AR[:, off:off + nb // 2]
            if d == F32:
                v = v.bitcast(F32)
            v = v[:, 0:nel]
            if len(s) == 3:
                v = v.rearrange("p (a b) -> p a b", a=s[1])
            elif len(s) == 4:
                v = v.rearrange("p (a b c) -> p a b c", a=s[1], b=s[2])
            if s[0] < 128:
                v = v[0:s[0]]
            return _T(v)
        ps = es.enter_context(nc.psum_tensor("ps", [128, 7, 512], F32))
        psb = es.enter_context(nc.psum_tensor("psb", [128, 1024], BF16))
        r_ps = [Res(True) for _ in range(7)]
        r_psb = Res(True)
        bank = lambda i: ps[:, i, :]

        ident = sbt("ident", [128, 128], BF16); r_ident = Res()
        fw.op("pool", lambda g: g.memset(ident[:], 1.0), writes=[r_ident])
        fw.op("pool", lambda g: g.affine_select(out=ident[:], in_=ident[:], pattern=[[-1, 128]], compare_op=ALU.is_equal,
                                                fill=0.0, base=0, channel_multiplier=1), reads=[r_ident], writes=[r_ident])
        flipJ = sbt("flipJ", [128, 128], F32); r_flip = Res()
        fw.op("pool", lambda g: g.memset(flipJ[:], 1.0), writes=[r_flip])
        fw.op("pool", lambda g: g.affine_select(out=flipJ[:], in_=flipJ[:], pattern=[[1, 128]], compare_op=ALU.is_equal,
                                                fill=0.0, base=-127, channel_multiplier=1), reads=[r_flip], writes=[r_flip])
        triF = sbt("triF", [128, 128], F32); r_tri = Res()
        fw.op("pool", lambda g: g.memset(triF[:], 1.0), writes=[r_tri])
        fw.op("pool", lambda g: g.affine_select(out=triF[:], in_=triF[:], pattern=[[1, 128]], compare_op=ALU.is_ge,
                                                fill=0.0, base=0, channel_multiplier=-1), reads=[r_tri], writes=[r_tri])
        onesF = sbt("onesF", [128, 128], F32); r_onesF = Res()
        fw.op("pool", lambda g: g.memset(onesF[:], 1.0), writes=[r_onesF])
        onesB = sbt("onesB", [128, 128], BF16); r_onesB = Res()
        fw.op("pool", lambda g: g.memset(onesB[:], 1.0), writes=[r_onesB])
        negOnes = sbt("negOnes", [128, 128], BF16); r_negOnes = Res()
        fw.op("pool", lambda g: g.memset(negOnes[:], -1.0), writes=[r_negOnes])
        negTri = sbt("negTri", [128, 128], BF16); r_negTri = Res()
        fw.op("pool", lambda g: g.memset(negTri[:], -1.0), writes=[r_negTri])
        fw.op("pool", lambda g: g.affine_select(out=negTri[:], in_=negTri[:], pattern=[[-1, 128]], compare_op=ALU.is_ge,
                                                fill=0.0, base=0, channel_multiplier=1), reads=[r_negTri], writes=[r_negTri])
        maskSB = sbt("maskSB", [128, 512], BF16); r_maskSB = Res()
        fw.op("pool", lambda g: g.memset(maskSB[:], 0.0), writes=[r_maskSB])
        for a in range(4):
            fw.op("pool", lambda g, a=a: g.affine_select(out=maskSB[:, a * 128:(a + 1) * 128], in_=maskSB[:, a * 128:(a + 1) * 128],
                                                         pattern=[[1, 128]], compare_op=ALU.is_gt, fill=NEGM, base=0,
                                                         channel_multiplier=-1), reads=[r_maskSB], writes=[r_maskSB])
        maskSB64 = sbt("maskSB64", [64, 512], BF16); r_maskSB64 = Res()
        fw.op("pool", lambda g: g.memset(maskSB64[:], 0.0), writes=[r_maskSB64])
        for a in range(8):
            fw.op("pool", lambda g, a=a: g.affine_select(out=maskSB64[:, a * 64:(a + 1) * 64], in_=maskSB64[:, a * 64:(a + 1) * 64],
                                                         pattern=[[1, 64]], compare_op=ALU.is_gt, fill=NEGM, base=0,
                                                         channel_multiplier=-1), reads=[r_maskSB64], writes=[r_maskSB64])
        maskFX = sbt("maskFX", [128, 128], F32); r_maskFX = Res()
        fw.op("pool", lambda g: g.memset(maskFX[:], 0.0), writes=[r_maskFX])
        fw.op("pool", lambda g: g.affine_select(out=maskFX[:], in_=maskFX[:], pattern=[[1, 128]], compare_op=ALU.is_ge,
                                                fill=NEGM, base=0, channel_multiplier=-1), reads=[r_maskFX], writes=[r_maskFX])
        maskCH = sbt("maskCH", [128, 128], F32); r_maskCH = Res()
        fw.op("pool", lambda g: g.memset(maskCH[:], 0.0), writes=[r_maskCH])
        fw.op("pool", lambda g: g.memset(maskCH[64:128, 0:64], NEGM), reads=[r_maskCH], writes=[r_maskCH])
        mask512 = sbt("mask512", [128, 128], F32); r_mask512 = Res()
        fw.op("pool", lambda g: g.memset(mask512[:], 0.0), writes=[r_mask512])
        fw.op("pool", lambda g: g.memset(mask512[0:64, 64:128], NEGM), reads=[r_mask512], writes=[r_mask512])

        ropePt = RR([(sbt(f"ropeP{i}", [128, 64]), Res()) for i in range(2)])
        ropeS = sbt("ropeS", [64, 64]); r_ropeS = Res()
        fw.dma("sp", ropeS[:], rope_s[:, :], writes=[r_ropeS])
        gpre = sbt("gpre", [128, 2, 8]); r_gpre = Res()
        fw.dma("sp", gpre[:].rearrange("p a b -> p (a b)"), norm_pre[:, :], writes=[r_gpre])
        gpost = sbt("gpost", [128, D]); r_gpost = Res()
        qn_bc = sbt("qn_bc", [128, 256]); r_qn = Res()
        fw.dma("sp", qn_bc[:], bass.AP(q_norm.tensor, 0, [[0, 128], [1, 256]]), writes=[r_qn])
        kvn_bc = sbt("kvn_bc", [128, 128]); r_kvn = Res()
        fw.dma("sp", kvn_bc[:], bass.AP(kv_norm.tensor, 0, [[0, 128], [1, 128]]), writes=[r_kvn])
        fb_bc = sbt("fb_bc", [128, 8]); r_fb = Res()
        fw.dma("sp", fb_bc[:], bass.AP(fbias.tensor, 0, [[0, 128], [1, 8]]), writes=[r_fb])

        wbuf = sbt("wbuf", [128, 8, ODD_IN], BF16); r_w = Res()
        wout = sbt("wout", [128, 8, D], BF16); r_wout = Res()
        wuq = _T(wbuf[:, 0:2, 2976:2976 + 768]); r_wuq = r_w
        wukT = _T(wbuf[:, 2, 2976:2976 + 1024]); r_wuk = r_w
        wuv = _T(wbuf[:, 3, 2976:2976 + 512]); r_wuv = r_w
        WST = 1026
        wst_off = ar["top"]
        wst = RR([(sbt(f"wst{i}", [128, WST]), Res()) for i in range(2)])
        wst_end = ar["top"]
        ar["top"] = wst_off
        cast_i = [0]
        wq_i = [0]

        CAST_ENGS = [("pool", "dve", "act")]
        EVAC_ENGS = [("dve",)]

        def cast(out, in_, reads, writes, engs=None):
            engs = engs or CAST_ENGS[0]
            e = engs[cast_i[0] % len(engs)]
            cast_i[0] += 1
            if e == "act":
                fw.op("act", lambda a: a.copy(out=out, in_=in_), reads=reads, writes=writes)
            else:
                fw.op(e, lambda v: v.tensor_copy(out=out, in_=in_), reads=reads, writes=writes)

        def load_w(dst_fn, src, nrows, ncols, r_dst, q="sp"):
            for c in range(nrows // 128):
                for c0 in range(0, ncols, WST):
                    n = min(WST, ncols - c0)
                    st, r_st = wst.next()
                    wq_i[0] += 1
                    fw.dma(("sp", "act")[wq_i[0] % 2], st[:, 0:n], src[c * 128:(c + 1) * 128, c0:c0 + n], writes=[r_st])
                    cast(dst_fn(c)[:, c0:c0 + n], st[:, 0:n], [r_st], [r_dst], engs=("dve", "act"))

        pbf = RR([(sbt(f"pbf{i}", [128, 512], BF16), Res()) for i in range(3)])
        spb = RR([(sbt(f"spb{i}", [128, 512], BF16), Res()) for i in range(3)])
        ebuf = RR([(sbt(f"ebuf{i}", [128, 512]), Res()) for i in range(2)])
        ar["top"] = max(ar["top"], wst_end)
        KS = 24576
        ks_off = ar["top"]
        ks = sbt("ks", [128, KS], BF16)
        hT = sbt("hT", [128, 8, BLK], BF16); r_hT = Res()
        gT = sbt("gT", [128, 8, BLK], BF16); r_gT = [Res() for _ in range(8)]
        qA = sbt("qA", [128, 4, 2, 256], BF16); r_qA = Res()
        qBd = sbt("qB", [128, 4, 2, 256], BF16); r_qB = Res()
        qB = _T(qBd[:].rearrange("p a t q -> p (a t q)")[:, 0:4 * BLK].rearrange("p (a q) -> p a q", a=4))
        fw.op("pool", lambda g: g.memset(qA[:].rearrange("p a t q -> p (a t q)"), 0.0), writes=[r_qA])
        xin = RR([(sbt(f"xin{i}", [128, D]), Res()) for i in range(2)])
        hb = RR([(sbt(f"hb{i}", [128, D], BF16), Res()) for i in range(1)])
        junk = sbt("junk", [128, 512], BF16); r_junk = Res()
        sm = RR([(sbt(f"sm{i}", [128, 16]), Res()) for i in range(6)])
        st512 = RR([(sbt(f"st512_{i}", [128, 512]), Res()) for i in range(2)])
        tmpf = RR([(sbt(f"tmpf{i}", [128, 512]), Res()) for i in range(2)])
        fint = RR([(sbt(f"fint{i}", [128, 256]), Res()) for i in range(2)])
        Rbuf = sbt("Rbuf", [128, 512], BF16); r_R = Res()
        latb = sbt("latb", [128, 512], BF16); r_latb = Res()
        rden = sbt("rden", [128, 512]); r_rden = Res()
        ov0 = ar["top"]
        qlat = sbt("qlat", [128, 8, BLK], BF16); r_qlat = Res()
        qrT = sbt("qrT", [32, 8, BLK], BF16); r_qrT = Res()
        cqT = sbt("cqT", [128, 2, BLK], BF16); r_cqT = Res()
        cq_b = sbt("cq_b", [128, 256], BF16); r_cqb = Res()
        kvb = sbt("kvb", [128, 128 + 32], BF16); r_kvb = Res()
        qr_b = sbt("qr_b", [128, 256], BF16); r_qrb = Res()
        ropet = RR([(sbt(f"ropet{i}", [128, 256]), Res()) for i in range(2)])
        ov1 = ar["top"]
        ar["top"] = ov0
        fxb = sbt("fxb", [128, 4, 16, 8]); r_fxb = Res()
        biasq2 = sbt("biasq", [128, 2, 16, 8]); r_biasq = Res()
        accb = sbt("accb", [128, 8]); r_accb = Res()
        tblB = sbt("tblB", [128, 8, 256]); r_tblB = Res()
        cstB = sbt("cstB", [128, 8]); r_cstB = Res()
        lfb = RR([(sbt(f"lfb{i}", [128, 24]), Res()) for i in range(3)])
        lfc = sbt("lfc", [128, 32, 8]); r_lfc = Res()
        sfx = sbt("sfx", [128, 33, 8]); r_sfx = Res()
        ar["top"] = max(ar["top"], ov1)
        sv_top = ar["top"]
        ar["top"] = ks_off
        cst_f = RR([(sbt(f"cstf{i}", [128, 512]), Res()) for i in range(8)])
        vbf = RR([(sbt(f"vbf{i}", [128, 512], BF16), Res()) for i in range(5)])
        kbf = RR([(sbt(f"kbf{i}", [128, 512], BF16), Res()) for i in range(2)])
        cKT = RR([(sbt(f"cKT{i}", [128, 4, 128], BF16), Res()) for i in range(3)])
        sfxn = sbt("sfxn", [64, 4, 8]); r_sfxn = Res()
        qbd = [(sbt(f"qbd{i}", [128, 4, 128], BF16), Res()) for i in range(4)]
        skT1 = sbt("skT1", [128, 4, BLK], BF16); skT2 = sbt("skT2", [128, 4, BLK], BF16)
        sv1 = sbt("sv1", [64, 4, 512], BF16); sv2 = sbt("sv2", [64, 4, 512], BF16)
        sckvT = sbt("sckvT", [128, BLK], BF16); skrT = sbt("skrT", [32, BLK], BF16)
        sckv_tm = sbt("sckv_tm", [64, 4, 128], BF16)
        assert ar["top"] <= ks_off + KS
        ar["top"] = sv_top

        def ksv(off, shape):
            n = int(np.prod(shape))
            v = ks[:, off:off + n]
            if len(shape) == 2:
                return v.rearrange("p (a b) -> p a b", a=shape[0])
            return v
        kT1 = ksv(0, [4, SEQ]); v1 = ksv(8192, [16, 512])
        kT2 = ksv(8192, [4, SEQ]); v2 = ksv(16384, [16, 512])
        kTc = ksv(0, [4, 1024]); vc = ksv(4096, [8, 512])
        ckvT = ks[:, 16384:16384 + SEQ]
        krT = ks[:, 16384 + 2048:16384 + 4096]
        ckv_tm = ksv(16384 + 4096, [16, 128])
        r_k1 = [Res() for _ in range(20)]
        r_k2 = [Res() for _ in range(20)]
        r_k3 = [Res() for _ in range(20)]
        SOFF = 24576
        r_sk = [Res() for _ in range(8)]

        def load_layer_weights(l):
            fw.barrier()
            if l == 0:
                load_w(lambda c: wbuf[:, c, :], w_in0, D, EVEN_IN, r_w)
                load_w(lambda c: wout[:, c, :], w_out0, D, D, r_wout)
                load_w(lambda c: wuq[:, c, :], w_uq, 256, 768, r_wuq)
                load_w(lambda c: wukT[:, :], w_ukT, 128, 1024, r_wuk)
                load_w(lambda c: wuv[:, :], w_uv, 128, 512, r_wuv)
            else:
                load_w(lambda c: wbuf[:, c, :], w_in1, D, ODD_IN, r_w)
                load_w(lambda c: wout[:, c, :], w_out1, D, D, r_wout)
            fw.dma("sp", gpost[:], bass.AP(norm_post.tensor, l * D, [[0, 128], [1, D]]), writes=[r_gpost])
            fw.barrier()

        gbank = RR([6, 2, 3, 0, 1, 4, 5])

        def mm(o, l, r, st, sp_, reads, wres):
            K = l.shape[0]
            rg = None if K >= 128 else (l.base_partition(), K)
            fw.op("pe", lambda t: t.matmul(o, lhsT=l, rhs=r, start=st, stop=sp_), reads=reads, writes=[wres], rg=rg)

        def rstd_from_ss(ss_ap, n, nt, r_ss):
            s, r_s = sm.next()
            fw.op("dve", lambda v: v.tensor_scalar(out=s[0:nt, 0:1], in0=ss_ap, scalar1=1.0 / n, scalar2=EPS,
                                                   op0=ALU.mult, op1=ALU.add), reads=[r_ss], writes=[r_s])
            fw.op("act", lambda a_: a_.activation(out=s[0:nt, 3:4], in_=s[0:nt, 0:1], func=AF.Sqrt), reads=[r_s], writes=[r_s])
            fw.op("dve", lambda v: v.reciprocal(out=s[0:nt, 1:2], in_=s[0:nt, 3:4]), reads=[r_s], writes=[r_s])
            return s[0:nt, 1:2], r_s

        def sumsq(in_ap, nt, reads):
            s, r_s = sm.next()
            fw.op("act", lambda a: a.activation(out=junk[0:nt, 0:in_ap.shape[-1]], in_=in_ap, func=AF.Square,
                                                accum_out=s[0:nt, 2:3]), reads=reads, writes=[r_junk, r_s])
            return s[0:nt, 2:3], r_s

        def phase_norm(l, tiles, xsrc, r_xsrc):
            for (nt, row0, col0) in tiles:
                xt, r_xt = xin.next()
                fw.dma("act", xt[0:nt, :], xsrc[row0:row0 + nt, :], reads=[r_xsrc[row0 // 64]] if r_xsrc else [], writes=[r_xt])
                s_, r_ss = sm.next()
                for hf in range(2):
                    fw.op("act", lambda a, hf=hf, s_=s_, xt=xt, nt=nt: a.activation(out=junk[0:nt, :], in_=xt[0:nt, hf * 512:(hf + 1) * 512], func=AF.Square,
                                                                                    accum_out=s_[0:nt, 4 + hf:5 + hf]), reads=[r_xt], writes=[r_junk, r_ss])
                fw.op("dve", lambda v, s_=s_, nt=nt: v.tensor_tensor(out=s_[0:nt, 2:3], in0=s_[0:nt, 4:5], in1=s_[0:nt, 5:6], op=ALU.add), reads=[r_ss], writes=[r_ss])
                rs, r_rs = rstd_from_ss(s_[0:nt, 2:3], D, nt, r_ss)
                h, r_h = hb.next()
                fw.op("dve", lambda v, h=h, xt=xt, rs=rs, nt=nt: v.tensor_scalar(out=h[0:nt, :], in0=xt[0:nt, :], scalar1=rs, scalar2=None,
                                                                                  op0=ALU.mult), reads=[r_xt, r_rs], writes=[r_h])
                for c in range(8):
                    fw.op("pe", lambda t, h=h, c=c, nt=nt: t.transpose(psb[:, c * 128:c * 128 + nt], h[0:nt, c * 128:(c + 1) * 128],
                                                                       ident[0:nt, 0:nt]), reads=[r_h, r_ident], writes=[r_psb])
                fw.op("dve", lambda v, nt=nt, col0=col0: v.tensor_tensor(
                    out=hT[:, :, col0:col0 + nt], in0=psb[:, :].rearrange("p (c t) -> p c t", c=8)[:, :, 0:nt],
                    in1=gpre[:, l, :].unsqueeze(2).broadcast_to([128, 8, nt]), op=ALU.mult),
                    reads=[r_psb, r_gpre], writes=[r_hT])

        def tm_proj(nt, col0, wcols, ncols, b):
            for c in range(8):
                fw.op("pe", lambda t, c=c: t.matmul(ps[0:nt, b, 0:ncols], lhsT=hT[:, c, col0:col0 + nt],
                                                    rhs=wbuf[:, c, wcols:wcols + ncols], start=(c == 0), stop=(c == 7)),
                      reads=[r_hT, r_w], writes=[r_ps[b]])

        def fm_proj(wcols, b, half):
            for c in range(8):
                fw.op("pe", lambda t, c=c: t.matmul(ps[:, b, half * BLK:(half + 1) * BLK], lhsT=wbuf[:, c, wcols:wcols + 128],
                                                    rhs=hT[:, c, :], start=(c == 0), stop=(c == 7)),
                      reads=[r_hT, r_w], writes=[r_ps[b]])

        def fm_group(wcol_list, evac):
            for i in range(0, len(wcol_list), 2):
                b = gbank.next()
                n = min(2, len(wcol_list) - i)
                for j in range(n):
                    fm_proj(wcol_list[i + j], b, j)
                for j in range(n):
                    evac(i + j, ps[:, b, j * BLK:(j + 1) * BLK], r_ps[b])

        def rope_tm(src, nheads, nt, tab, r_tab, out_ap, reads, writes):
            t1, r_t1 = ropet.next()
            t2, r_t2 = ropet.next()
            n = nheads * 32
            s3 = src.rearrange("p (h r) -> p h r", h=nheads)
            a1 = t1[0:nt, 0:n].rearrange("p (h r) -> p h r", h=nheads)
            a2 = t2[0:nt, 0:n].rearrange("p (h r) -> p h r", h=nheads)
            cosb = tab[:, 0:32].unsqueeze(1).broadcast_to([nt, nheads, 32])
            sin_lo = tab[:, 32:48].unsqueeze(1).broadcast_to([nt, nheads, 16])
            sin_hi = tab[:, 48:64].unsqueeze(1).broadcast_to([nt, nheads, 16])
            fw.op("dve", lambda v: v.tensor_tensor(out=a1, in0=s3, in1=cosb, op=ALU.mult), reads=reads + [r_tab], writes=[r_t1])
            fw.op("dve", lambda v: v.tensor_tensor(out=a2[:, :, 0:16], in0=s3[:, :, 16:32], in1=sin_lo, op=ALU.mult),
                  reads=reads + [r_tab], writes=[r_t2])
            fw.op("dve", lambda v: v.tensor_tensor(out=a2[:, :, 16:32], in0=s3[:, :, 0:16], in1=sin_hi, op=ALU.mult),
                  reads=reads + [r_tab], writes=[r_t2])
            fw.op("dve", lambda v: v.tensor_tensor(out=out_ap, in0=t1[0:nt, 0:n], in1=t2[0:nt, 0:n], op=ALU.add),
                  reads=[r_t1, r_t2], writes=writes)

        def evac_bd(dst, r_dst, i, src, rb):
            fw.op("act", lambda a: a.activation(out=dst[0:64, i, :, 0:128], in_=src[0:64, :].rearrange("p (t q) -> p t q", t=2),
                                                func=AF.Copy, scale=0.125), reads=[rb], writes=[r_dst])
            fw.op("act", lambda a: a.activation(out=dst[64:128, i, :, 128:256], in_=src[64:128, :].rearrange("p (t q) -> p t q", t=2),
                                                func=AF.Copy, scale=0.125), reads=[rb], writes=[r_dst])

        def phase_proj0(tiles, is_s, o_ckv, o_kr, o_sbk, o_sbv):
            for ti, (nt, row0, col0) in enumerate(tiles):
                gt = (row0 // 128) if not is_s else ti
                b = gbank.next()
                tm_proj(nt, col0, 0, 416, b)
                ssq, r_ssq = sumsq(ps[0:nt, b, 0:256], nt, [r_ps[b]])
                rq, r_rq = rstd_from_ss(ssq, 256, nt, r_ssq)
                fw.op("dve", lambda v, b=b, rq=rq, nt=nt: v.scalar_tensor_tensor(out=cq_b[0:nt, :], in0=ps[0:nt, b, 0:256], scalar=rq,
                                                                                  in1=qn_bc[0:nt, :], op0=ALU.mult, op1=ALU.mult),
                      reads=[r_ps[b], r_rq, r_qn], writes=[r_cqb])
                ssk, r_ssk = sumsq(ps[0:nt, b, 256:384], nt, [r_ps[b]])
                rk, r_rk = rstd_from_ss(ssk, 128, nt, r_ssk)
                stg, r_stg = st512.next()
                fw.op("dve", lambda v, b=b, rk=rk, nt=nt, stg=stg: v.scalar_tensor_tensor(out=stg[0:nt, 0:128], in0=ps[0:nt, b, 256:384], scalar=rk,
                                                                                           in1=kvn_bc[0:nt, :], op0=ALU.mult, op1=ALU.mult),
                      reads=[r_ps[b], r_rk, r_kvn], writes=[r_stg])
                if is_s:
                    tab, r_tab = ropeS[0:nt, :], r_ropeS
                else:
                    tb_, r_tab = ropePt.next()
                    fw.dma("sp", tb_[:, :], rope_p[:, gt * 64:(gt + 1) * 64], writes=[r_tab])
                    tab = tb_[0:nt, :]
                rope_tm(ps[0:nt, b, 384:416], 1, nt, tab, r_tab, stg[0:nt, 128:160], [r_ps[b]], [r_stg])
                fw.dma("sp", o_ckv[row0:row0 + nt, :], stg[0:nt, 0:128], reads=[r_stg])
                fw.dma("sp", o_kr[row0:row0 + nt, :], stg[0:nt, 128:160], reads=[r_stg])
                fw.op("act", lambda a, stg=stg, nt=nt: a.copy(out=kvb[0:nt, :], in_=stg[0:nt, 0:160]), reads=[r_stg], writes=[r_kvb])
                if is_s:
                    fw.op("pool", lambda g, ti=ti, nt=nt: g.tensor_copy(out=sckv_tm[0:nt, ti, :], in_=kvb[0:nt, 0:128]),
                          reads=[r_kvb], writes=[r_sk[4]])
                else:
                    fw.op("pool", lambda g, gt=gt: g.tensor_copy(out=ckv_tm[:, gt, :], in_=kvb[:, 0:128]), reads=[r_kvb], writes=[r_k3[gt]])
                for j in range(2):
                    fw.op("pe", lambda t, j=j, nt=nt: t.transpose(psb[:, j * 128:j * 128 + nt], cq_b[0:nt, j * 128:(j + 1) * 128],
                                                                  ident[0:nt, 0:nt]), reads=[r_cqb, r_ident], writes=[r_psb])
                fw.op("pe", lambda t, nt=nt: t.transpose(psb[:, 256:256 + nt], kvb[0:nt, 0:128], ident[0:nt, 0:nt]),
                      reads=[r_kvb, r_ident], writes=[r_psb])
                fw.op("pe", lambda t, nt=nt: t.transpose(psb[0:32, 384:384 + nt], kvb[0:nt, 128:160], ident[0:nt, 0:nt]),
                      reads=[r_kvb, r_ident], writes=[r_psb])
                fw.op("dve", lambda v, nt=nt, col0=col0: v.tensor_copy(out=cqT[:, :, col0:col0 + nt],
                                                                        in_=psb[:, 0:256].rearrange("p (c t) -> p c t", c=2)[:, :, 0:nt]),
                      reads=[r_psb], writes=[r_cqT])
                if is_s:
                    dckv, dkr, rr = sckvT[:, col0:col0 + nt], skrT[:, col0:col0 + nt], r_sk[5]
                else:
                    dckv, dkr, rr = ckvT[:, row0:row0 + nt], krT[0:32, row0:row0 + nt], r_k3[gt]
                fw.op("dve", lambda v, nt=nt, dckv=dckv: v.tensor_copy(out=dckv, in_=psb[:, 256:256 + nt]), reads=[r_psb], writes=[rr])
                fw.op("dve", lambda v, nt=nt, dkr=dkr: v.tensor_copy(out=dkr, in_=psb[0:32, 384:384 + nt]), reads=[r_psb], writes=[rr])
                b = gbank.next()
                tm_proj(nt, col0, 1440, 512, b)
                stg, r_stg = st512.next()
                fw.op("act", lambda a, b=b, stg=stg, nt=nt: a.copy(out=stg[0:nt, :], in_=ps[0:nt, b, :]), reads=[r_ps[b]], writes=[r_stg])
                fw.dma("sp", o_sbk[row0:row0 + nt, :], stg[0:nt, :], reads=[r_stg])
                b = gbank.next()
                tm_proj(nt, col0, 1952, 512, b)
                stg, r_stg = st512.next()
                fw.op("act", lambda a, b=b, stg=stg, nt=nt: a.copy(out=stg[0:nt, :], in_=ps[0:nt, b, :]), reads=[r_ps[b]], writes=[r_stg])
                fw.dma("sp", o_sbv[row0:row0 + nt, :], stg[0:nt, :], reads=[r_stg])
                if is_s:
                    fw.op("dve", lambda v, b=b, ti=ti, nt=nt: v.tensor_copy(out=sv1[0:nt, ti, :], in_=ps[0:nt, b, :]), reads=[r_ps[b]], writes=[r_sk[1]])
                else:
                    if os.environ.get("KSKIPV1") is None:
                        fw.op("dve", lambda v, b=b, gt=gt: v.tensor_copy(out=v1[:, gt, :], in_=ps[:, b, :]), reads=[r_ps[b]], writes=[r_k1[gt]])
                b = gbank.next()
                for cc in range(2):
                    fw.op("pe", lambda t, cc=cc, b=b, nt=nt, col0=col0: t.matmul(ps[0:nt, b, 0:256], lhsT=cqT[:, cc, col0:col0 + nt],
                                                                                 rhs=wuq[:, cc, 512:768], start=(cc == 0), stop=(cc == 1)),
                          reads=[r_cqT, r_wuq], writes=[r_ps[b]])
                rope_tm(ps[0:nt, b, 0:256], 8, nt, tab, r_tab, qr_b[0:nt, :], [r_ps[b]], [r_qrb])
                for h in range(8):
                    fw.op("pe", lambda t, h=h, nt=nt: t.transpose(psb[0:32, h * 128:h * 128 + nt], qr_b[0:nt, h * 32:(h + 1) * 32],
                                                                  ident[0:nt, 0:nt]), reads=[r_qrb, r_ident], writes=[r_psb])
                fw.op("dve", lambda v, nt=nt, col0=col0: v.tensor_copy(out=qrT[:, :, col0:col0 + nt],
                                                                        in_=psb[0:32, :].rearrange("p (h t) -> p h t", h=8)[:, :, 0:nt]),
                      reads=[r_psb], writes=[r_qrT])
            fm_group([928 + 128 * i for i in range(4)],
                     lambda i, src, rb: evac_bd(qA, r_qA, i, src, rb))
            if is_s:
                fm_group([1440 + 128 * i for i in range(4)],
                         lambda i, src, rb: fw.op("dve", lambda v: v.tensor_copy(out=skT1[:, i, :], in_=src), reads=[rb], writes=[r_sk[0]]))
            else:
                t0 = tiles[0][1]
                g0 = t0 // 128

                def ev(i, src, rb):
                    fw.op("dve", lambda v: v.tensor_copy(out=kT1[:, i, t0:t0 + BLK], in_=src), reads=[rb], writes=[r_k1[g0], r_k1[g0 + 1]])
                fm_group([1440 + 128 * i for i in range(4)], ev)
            fm_group([416 + 128 * i for i in range(4)] + [2464 + 128 * i for i in range(4)],
                     lambda i, src, rb: fw.op("act", lambda a: a.activation(out=gT[:, i, :], in_=src, func=AF.Silu),
                                              reads=[rb], writes=[r_gT[i]]))
            for i in range(0, 4, 2):
                b = gbank.next()
                for j in range(2):
                    for cc in range(2):
                        fw.op("pe", lambda t, cc=cc, b=b, i=i, j=j: t.matmul(ps[:, b, j * BLK:(j + 1) * BLK], lhsT=wuq[:, cc, (i + j) * 128:(i + j + 1) * 128],
                                                                             rhs=cqT[:, cc, :], start=(cc == 0), stop=(cc == 1)),
                              reads=[r_cqT, r_wuq], writes=[r_ps[b]])
                fw.op("dve", lambda v, b=b, i=i: v.tensor_copy(out=qB[:, i:i + 2, :], in_=ps[:, b, :].rearrange("p (a t) -> p a t", a=2)),
                      reads=[r_ps[b]], writes=[r_qB])
            for h0 in range(0, 8, 2):
                b = gbank.next()
                for j in range(2):
                    h = h0 + j
                    pb = 64 * (h % 2)
                    mm(ps[:, b, j * BLK:(j + 1) * BLK], wukT[pb:pb + 64, h * 128:(h + 1) * 128], qB[pb:pb + 64, h // 2, :], True, True,
                       [r_qB, r_wuk], r_ps[b])
                for j in range(2):
                    fw.op("act", lambda a, b=b, h0=h0, j=j: a.copy(out=qlat[:, h0 + j, :], in_=ps[:, b, j * BLK:(j + 1) * BLK]),
                          reads=[r_ps[b]], writes=[r_qlat])

        def fin_softmax(slots, ob, db, has_den=True):
            if has_den:
                fw.op("dve", lambda v: v.reciprocal(out=rden[:, :], in_=ps[:, db, :]), reads=[r_ps[db]], writes=[r_rden])
            if len(slots) == 4 and slots[0][2] == 128:
                for par in (0, 1):
                    h, c0, n, gc, g0 = slots[par]
                    pb = 64 * par
                    ov = ps[pb:pb + 64, ob, :].rearrange("p (s q) -> p s q", s=4)[:, par:4:2, :]
                    gv = gT[pb:pb + 64, gc:gc + 2, g0:g0 + 128]
                    rgs = [r_gT[gc], r_gT[gc + 1]]
                    if has_den:
                        t, r_t = fint.next()
                        tv = t[pb:pb + 64, 0:256].rearrange("p (s q) -> p s q", s=2)
                        rv = rden[pb:pb + 64, :].rearrange("p (s q) -> p s q", s=4)[:, par:4:2, :]
                        fw.op("dve", lambda v, tv=tv, ov=ov, rv=rv: v.tensor_tensor(out=tv, in0=ov, in1=rv, op=ALU.mult),
                              reads=[r_ps[ob], r_rden], writes=[r_t])
                        fw.op("dve", lambda v, tv=tv, gv=gv: v.tensor_tensor(out=gv, in0=tv, in1=gv, op=ALU.mult), reads=[r_t] + rgs, writes=rgs)
                    else:
                        fw.op("dve", lambda v, ov=ov, gv=gv: v.tensor_tensor(out=gv, in0=ov, in1=gv, op=ALU.mult), reads=[r_ps[ob]] + rgs, writes=rgs)
                return
            for (h, c0, n, gc, g0) in slots:
                pb = 64 * (h % 2)
                if has_den:
                    t, r_t = fint.next()
                    fw.op("dve", lambda v, t=t, pb=pb, c0=c0, n=n: v.tensor_tensor(out=t[pb:pb + 64, 0:n], in0=ps[pb:pb + 64, ob, c0:c0 + n],
                                                                                   in1=rden[pb:pb + 64, c0:c0 + n], op=ALU.mult),
                          reads=[r_ps[ob], r_rden], writes=[r_t])
                    fw.op("dve", lambda v, t=t, pb=pb, n=n, gc=gc, g0=g0: v.tensor_tensor(out=gT[pb:pb + 64, gc, g0:g0 + n], in0=t[pb:pb + 64, 0:n],
                                                                                          in1=gT[pb:pb + 64, gc, g0:g0 + n], op=ALU.mult),
                          reads=[r_t, r_gT[gc]], writes=[r_gT[gc]])
                else:
                    fw.op("dve", lambda v, pb=pb, c0=c0, n=n, gc=gc, g0=g0: v.tensor_tensor(out=gT[pb:pb + 64, gc, g0:g0 + n], in0=ps[pb:pb + 64, ob, c0:c0 + n],
                                                                                            in1=gT[pb:pb + 64, gc, g0:g0 + n], op=ALU.mult),
                          reads=[r_ps[ob], r_gT[gc]], writes=[r_gT[gc]])

        odpair = RR([(4, 5), (2, 3)])
        sbank = RR([0, 1])
        abank = RR([2, 3])
        obank = RR([4, 5])

        class U:
            hbias = None
            dma = None
            prep = None
            mask = None
            pre = None
            scale = 1.0

        def sprep(u):
            if u.prep is not None:
                u.prep(u)

        def sdma(u):
            if u.dma is not None:
                u.dma(u)

        def sb_chain(units):
            def s0(u):
                u.sb = sbank.next()
                nk = u.nk
                nz = len(u.zmm)
                for i, (l, r, c0, n) in enumerate(u.zmm):
                    mm(ps[0:u.nk, u.sb, c0:c0 + n], l, r, (i == 0), (u.mask is None and i == nz - 1), u.reads, r_ps[u.sb])
                if u.mask is not None:
                    mm(ps[0:u.nk, u.sb, :], ident[0:u.nk, 0:u.nk], u.mask, False, True, [r_ident, u.rmask], r_ps[u.sb])
                e, r_e = ebuf.next()
                fw.op("act", lambda a, u=u, e=e: a.activation(out=e[0:u.nk, :], in_=ps[0:u.nk, u.sb, :], func=AF.Exp), reads=[r_ps[u.sb]], writes=[r_e])
                u.sp, u.r_sp = spb.next()
                fw.op("act", lambda a, u=u, e=e: a.activation(out=u.sp[0:u.nk, :], in_=e[0:u.nk, :], func=AF.Ln, bias=1.0), reads=[r_e], writes=[u.r_sp])

            def s1(u):
                u.ab = abank.next()
                for i, (l, r, c0, n) in enumerate(u.zmm):
                    mm(ps[0:u.nk, u.ab, c0:c0 + n], l, r, (i == 0), False, u.reads, r_ps[u.ab])
                if u.mask is not None:
                    mm(ps[0:u.nk, u.ab, :], ident[0:u.nk, 0:u.nk], u.mask, False, False, [r_ident, u.rmask], r_ps[u.ab])
                mm(ps[0:u.nk, u.ab, :], negTri[0:u.nk, 0:u.nk], u.sp[0:u.nk, :], False, u.first, [r_negTri, u.r_sp], r_ps[u.ab])
                if not u.first:
                    mm(ps[0:u.nk, u.ab, :], negOnes[:, 0:u.nk], Rbuf[:, :], False, True, [r_negOnes, r_R], r_ps[u.ab])
                if not u.last:
                    if u.first:
                        if u.nk < 128:
                            fw.op("pool", lambda g: g.memset(Rbuf[:, :], 0.0), writes=[r_R])
                        fw.op("pool", lambda g, u=u: g.tensor_copy(out=Rbuf[0:u.nk, :], in_=u.sp[0:u.nk, :]), reads=[u.r_sp], writes=[r_R])
                    else:
                        fw.op("pool", lambda g, u=u: g.tensor_tensor(out=Rbuf[0:u.nk, :], in0=Rbuf[0:u.nk, :], in1=u.sp[0:u.nk, :], op=ALU.add),
                              reads=[u.r_sp, r_R], writes=[r_R])
                u.w, u.r_w = pbf.next()
                fw.op("act", lambda a, u=u: a.activation(out=u.w[0:u.nk, :], in_=ps[0:u.nk, u.ab, :], func=AF.Exp), reads=[r_ps[u.ab]], writes=[u.r_w])

            def s2(u):
                if u.first:
                    u.chain["ob"] = obank.next()
                ob = u.chain["ob"]
                for vi, (lv, c0, n) in enumerate(u.vmm):
                    mm(ps[:, ob, c0:c0 + n], lv, u.w[0:u.nk, c0:c0 + n], (u.first and vi == 0), (u.last and vi == len(u.vmm) - 1),
                       u.vreads + [u.r_w], r_ps[ob])
                if u.last:
                    u.fin(ob)
            pipeline(units, [sdma, sprep, s0, s1, s2], [0, 3, 4, 5, 6])

        def sm_chain(units):
            def s0(u):
                u.sb = sbank.next()
                for (l, r, c0, n, st, sp_) in u.zmm:
                    o = ps[0:u.nk, u.sb, c0:c0 + n]
                    if len(r.shape) == 3:
                        o = o.rearrange("p (h q) -> p h q", h=r.shape[1])
                    mm(o, l, r, st, sp_, u.reads, r_ps[u.sb])
                if u.hbias is not None:
                    u.src, u.r_src = None, None
                elif u.pre is not None:
                    u.src, u.r_src = u.pre(u)
                else:
                    u.src, u.r_src = ps[0:u.nk, u.sb, :], r_ps[u.sb]

            def s1(u):
                u.p, u.r_p = pbf.next()
                if u.hbias is not None:
                    w_ = 512 // len(u.hbias)
                    for j, (bap, rb) in enumerate(u.hbias):
                        fw.op("act", lambda a, u=u, j=j, bap=bap, w_=w_: a.activation(out=u.p[0:u.nk, j * w_:(j + 1) * w_], in_=ps[0:u.nk, u.sb, j * w_:(j + 1) * w_],
                                                                                  func=AF.Exp, bias=bap, scale=u.scale),
                              reads=[r_ps[u.sb], rb], writes=[u.r_p])
                else:
                    fw.op("act", lambda a, u=u: a.activation(out=u.p[0:u.nk, :], in_=u.src, func=AF.Exp, scale=u.scale), reads=[u.r_src], writes=[u.r_p])

            def s2(u):
                if u.first:
                    u.chain["ob"], u.chain["db"] = odpair.next()
                ob, db = u.chain["ob"], u.chain["db"]
                for vi, (lv, c0, n) in enumerate(u.vmm):
                    mm(ps[:, ob, c0:c0 + n], lv, u.p[0:u.nk, c0:c0 + n], (u.first and vi == 0), (u.last and vi == len(u.vmm) - 1),
                       u.vreads + [u.r_p], r_ps[ob])
                mm(ps[:, db, :], onesB[0:u.nk, :], u.p[0:u.nk, :], u.first, u.last, [r_onesB, u.r_p], r_ps[db])
                if u.last:
                    u.fin(ob, db)
            pipeline(units, [sdma, sprep, s0, s1, s2], [0, 3, 4, 5, 6])

        def pre_add(u, in1, r_in1, scale=1.0, extra=None):
            t, r_t = tmpf.next()
            nk = u.nk
            H = in1.shape[1]
            n = 512 // H
            fw.op("dve", lambda v: v.scalar_tensor_tensor(out=t[0:nk, :].rearrange("p (h q) -> p h q", h=H),
                                                          in0=ps[0:nk, u.sb, :].rearrange("p (h q) -> p h q", h=H), scalar=scale,
                                                          in1=in1, op0=ALU.mult, op1=ALU.add), reads=[r_ps[u.sb]] + r_in1, writes=[r_t])
            if extra is not None:
                ex, r_ex = extra
                fw.op("dve", lambda v: v.tensor_tensor(out=t[0:nk, :].rearrange("p (h q) -> p h q", h=H),
                                                       in0=t[0:nk, :].rearrange("p (h q) -> p h q", h=H), in1=ex, op=ALU.add),
                      reads=[r_t] + r_ex, writes=[r_t])
            return t[0:nk, :], r_t

        def mla_fin_factory(slots_fn):
            def fin(ob, db):
                fw.op("act", lambda a: a.copy(out=latb[:, :], in_=ps[:, ob, :]), reads=[r_ps[ob]], writes=[r_latb])
                slots = slots_fn()
                gb = 6
                for (h, c0, n, gc, g0) in slots:
                    fw.op("pe", lambda t, h=h, c0=c0, n=n: t.matmul(ps[:, gb, c0:c0 + n], lhsT=wuv[:, (h // 2) * 128:(h // 2 + 1) * 128],
                                                                    rhs=latb[:, c0:c0 + n], start=True, stop=True),
                          reads=[r_latb, r_wuv], writes=[r_ps[gb]])
                fin_softmax(slots, gb, db)
            return fin

        def attn0_prompt(bi):
            all_sb = []
            all_mla = []
            for qt in (2 * bi, 2 * bi + 1):
                qc = (qt % 2) * 128
                for hg in range(2):
                    chain = {}
                    units = []
                    for kt in range(qt, -1, -1):
                        u = U()
                        u.nk = 128; u.chain = chain
                        u.first = (kt == qt); u.last = (kt == 0)
                        u.zmm = []
                        u.vmm = []
                        for pp in range(2):
                            u.zmm.append((kT1[:, 2 * hg + pp, kt * 128:(kt + 1) * 128], qA[:, 2 * hg + pp, qt % 2, :], pp * 256, 256))
                        for pp in range(2):
                            u.vmm.append((v1[:, kt, (2 * hg + pp) * 128:(2 * hg + pp + 1) * 128], pp * 256, 256))
                        u.mask = maskSB[:, :] if kt == qt else None
                        u.rmask = r_maskSB
                        u.reads = [r_k1[kt], r_qA]
                        u.vreads = [r_k1[kt]]
                        slots = [(4 * hg + s, s * 128, 128, 4 + (4 * hg + s) // 2, qc) for s in range(4)]
                        u.fin = (lambda ob, slots=slots: fin_softmax(slots, ob, None, has_den=False))
                        units.append(u)
                    all_sb += units
                    chain = {}
                    units = []
                    for kt in range(qt, -1, -1):
                        u = U()
                        u.nk = 128; u.chain = chain
                        u.first = (kt == qt); u.last = (kt == 0)
                        u.zmm = [(ckvT[:, kt * 128:(kt + 1) * 128], qlat[:, 4 * hg:4 * hg + 4, qc:qc + 128], 0, 512, True, False),
                                 (krT[0:32, kt * 128:(kt + 1) * 128], qrT[:, 4 * hg:4 * hg + 4, qc:qc + 128], 0, 512, False, True)]
                        u.reads = [r_k3[kt], r_qlat, r_qrT]
                        u.scale = A_SCALE
                        if kt == qt:
                            u.pre = lambda u: pre_add(u, maskCH[:, :].unsqueeze(1).broadcast_to([128, 4, 128]), [r_maskCH], scale=A_SCALE)
                            u.scale = 1.0
                        else:
                            u.pre = None
                        u.vmm = [(ckv_tm[:, kt, :], 0, 512)]
                        u.vreads = [r_k3[kt]]
                        slots = [(4 * hg + s, s * 128, 128, (4 * hg + s) // 2, qc) for s in range(4)]
                        u.fin = mla_fin_factory(lambda slots=slots: slots)
                        units.append(u)
                    all_mla += units
            sb_chain(all_sb)
            sm_chain(all_mla)


        HORD = (0, 2, 4, 6, 1, 3, 5, 7)

        def dma_kv_tile(u, kdram, vdram, s, t):
            u.kf, u.r_kf = cst_f.next()
            fw.dma("sp", u.kf[:, :], kdram[s, t * 128:(t + 1) * 128, :], writes=[u.r_kf])
            u.vf, u.r_vf = cst_f.next()
            fw.dma("sp", u.vf[:, :], vdram[s, t * 128:(t + 1) * 128, :], writes=[u.r_vf])

        def load_kv_tile(u):
            kf, r_kf, vf, r_vf = u.kf, u.r_kf, u.vf, u.r_vf
            kb, r_kb = kbf.next()
            cast(kb[:, :], kf[:, :], [r_kf], [r_kb])
            for c in range(4):
                fw.op("pe", lambda t_, c=c, kb=kb: t_.transpose(psb[:, c * 128:(c + 1) * 128], kb[:, c * 128:(c + 1) * 128], ident[:, :]),
                      reads=[r_kb, r_ident], writes=[r_psb])
            kt_, r_kt = cKT.next()
            ee = EVAC_ENGS[0][cast_i[0] % len(EVAC_ENGS[0])]
            if ee == "act":
                fw.op("act", lambda a, kt_=kt_: a.copy(out=kt_[:, :, :], in_=psb[:, 0:512].rearrange("p (c t) -> p c t", c=4)),
                      reads=[r_psb], writes=[r_kt])
            else:
                fw.op("dve", lambda v, kt_=kt_: v.tensor_copy(out=kt_[:, :, :], in_=psb[:, 0:512].rearrange("p (c t) -> p c t", c=4)),
                      reads=[r_psb], writes=[r_kt])
            vb, r_vb = vbf.next()
            cast(vb[:, :], vf[:, :], [r_vf], [r_vb])
            return kt_, r_kt, vb, r_vb

        def kv_units(s, ncache, kdram, vdram, qsrc, r_q, knew, vnew, r_new, slots):
            qb, r_qb = qbd[s]
            fw.op("pool", lambda g: g.memset(qb[:, :, :], 0.0), writes=[r_qb])
            so = (s % 2) * 64
            fw.op("pool", lambda g: g.tensor_copy(out=qb[0:64, :, 0:64], in_=qsrc[0:64, :, s // 2, so:so + 64]), reads=[r_q], writes=[r_qb])
            fw.op("pool", lambda g: g.tensor_copy(out=qb[64:128, :, 64:128], in_=qsrc[64:128, :, s // 2, 128 + so:128 + so + 64]), reads=[r_q], writes=[r_qb])
            chain = {}
            units = []
            u = U(); u.nk = 64; u.chain = chain; u.first = True; u.last = False; u.tile = ncache
            u.zmm = []; u.vmm = []
            for p in range(4):
                u.zmm.append((knew[:, p, 64 * s:64 * s + 64], qb[:, p, :], p * 128, 128))
                u.vmm.append((vnew[0:64, s, p * 128:(p + 1) * 128], p * 128, 128))
            u.reads = [r_new, r_qb]; u.vreads = [r_new]
            units.append(u)
            for t in range(ncache - 1, -1, -1):
                u = U(); u.nk = 128; u.chain = chain; u.first = False; u.last = (t == 0); u.tile = t
                u.dma = (lambda u, t=t: dma_kv_tile(u, kdram, vdram, s, t))

                def prep(u, t=t):
                    kt_, r_kt, vb, r_vb = load_kv_tile(u)
                    u.zmm = []; u.vmm = []
                    for p in range(4):
                        u.zmm.append((kt_[:, p, :], qb[:, p, :], p * 128, 128))
                        u.vmm.append((vb[:, p * 128:(p + 1) * 128], p * 128, 128))
                    u.reads = [r_kt, r_qb]; u.vreads = [r_vb]
                u.prep = prep
                units.append(u)
            return units

        def zmm4(u):
            z = []
            for i, (l, r, c0, n) in enumerate(u.zmm):
                z.append((l, r, c0, n, i == 0, i == len(u.zmm) - 1))
            u.zmm = z

        def attn0_sample():
            all_sb = []
            all_mla = []
            for s in range(NSTR):
                slots = [(h, h * 64, 64, 4 + h // 2, 64 * s) for h in range(8)]
                units = kv_units(s, PAST // 128, c_sbk, c_sbv, qA, r_qA, skT1, sv1, r_sk[0], slots)
                units[0].mask = maskSB64[:, :]; units[0].rmask = r_maskSB64
                units[0].reads = [r_sk[0], qbd[s][1]]; units[0].vreads = [r_sk[1]]
                for u in units:
                    u.rmask = r_maskSB64
                    u.fin = (lambda ob, slots=slots: fin_softmax(slots, ob, None, has_den=False))
                all_sb += units
            sb_chain(all_sb)
            for s in range(NSTR):
                chain = {}
                units = []
                slots = [(h, h * 64, 64, h // 2, 64 * s) for h in range(8)]
                u = U(); u.nk = 64; u.chain = chain; u.first = True; u.last = False
                u.zmm = [(sckvT[:, 64 * s:64 * s + 64], qlat[:, :, 64 * s:64 * s + 64], 0, 512, True, False),
                         (skrT[0:32, 64 * s:64 * s + 64], qrT[:, :, 64 * s:64 * s + 64], 0, 512, False, True)]
                u.reads = [r_sk[5], r_qlat, r_qrT]; u.scale = A_SCALE
                u.vmm = [(sckv_tm[0:64, s, :], 0, 512)]; u.vreads = [r_sk[4]]
                units.append(u)
                for t in range(PAST // 128 - 1, -1, -1):
                    u = U(); u.nk = 128; u.chain = chain; u.first = False; u.last = (t == 0); u.scale = A_SCALE

                    def dma_(u, t=t, s=s):
                        u.cf, u.r_cf = cst_f.next()
                        fw.dma("sp", u.cf[:, 0:128], c_ckv[s, t * 128:(t + 1) * 128, :], writes=[u.r_cf])
                        fw.dma("sp", u.cf[:, 128:160], c_kr[s, t * 128:(t + 1) * 128, :], writes=[u.r_cf])
                    u.dma = dma_

                    def prep(u, t=t, s=s):
                        cf, r_cf = u.cf, u.r_cf
                        vb, r_vb = vbf.next()
                        cast(vb[:, 0:160], cf[:, 0:160], [r_cf], [r_vb])
                        fw.op("pe", lambda t_, vb=vb: t_.transpose(psb[:, 0:128], vb[:, 0:128], ident[:, :]), reads=[r_vb, r_ident], writes=[r_psb])
                        fw.op("pe", lambda t_, vb=vb: t_.transpose(psb[0:32, 128:256], vb[:, 128:160], ident[:, :]), reads=[r_vb, r_ident], writes=[r_psb])
                        kt_, r_kt = cKT.next()
                        fw.op("dve", lambda v, kt_=kt_: v.tensor_copy(out=kt_[:, 0, :], in_=psb[:, 0:128]), reads=[r_psb], writes=[r_kt])
                        fw.op("dve", lambda v, kt_=kt_: v.tensor_copy(out=kt_[0:32, 1, :], in_=psb[0:32, 128:256]), reads=[r_psb], writes=[r_kt])
                        u.zmm = [(kt_[:, 0, :], qlat[:, :, 64 * s:64 * s + 64], 0, 512, True, False),
                                 (kt_[0:32, 1, :], qrT[:, :, 64 * s:64 * s + 64], 0, 512, False, True)]
                        u.reads = [r_kt, r_qlat, r_qrT]
                        u.vmm = [(vb[:, 0:128], 0, 512)]; u.vreads = [r_vb]
                    u.prep = prep
                    units.append(u)
                for u in units:
                    u.fin = mla_fin_factory(lambda slots=slots: slots)
                all_mla += units
            sm_chain(all_mla)

        r_kc = [Res() for _ in range(8)]

        def setup_l1_tables():
            tb2 = tblB[:].rearrange("p a b -> p (a b)")
            fw.dma("sp", tblB[:], bass.AP(relb.tensor, 1, [[1, 128], [513, 8], [1, 256]]), writes=[r_tblB])
            for j in range(4):
                fw.op("pe", lambda t_, j=j: t_.matmul(ps[:, j, :], lhsT=flipJ[:, :], rhs=tb2[:, j * 512:(j + 1) * 512], start=True, stop=True),
                      reads=[r_flip, r_tblB], writes=[r_ps[j]])
            for j in range(4):
                fw.op("dve", lambda v, j=j: v.tensor_copy(out=tb2[:, j * 512:(j + 1) * 512], in_=ps[:, j, :]), reads=[r_ps[j]], writes=[r_tblB])
            fw.op("dve", lambda v: v.tensor_tensor(out=tblB[:, :, 0:128], in0=tblB[:, :, 0:128],
                                                   in1=maskCH[:, :].unsqueeze(1).broadcast_to([128, 8, 128]), op=ALU.add),
                  reads=[r_tblB, r_maskCH], writes=[r_tblB])
            fw.dma("sp", cstB[:], bass.AP(relc.tensor, 0, [[0, 128], [1, 8]]), writes=[r_cstB])

        def phase_proj1(tiles, is_s, o_fk, o_fv, o_lf):
            for ti, (nt, row0, col0) in enumerate(tiles):
                gt = (row0 // 128) if not is_s else ti
                b = gbank.next()
                tm_proj(nt, col0, 512, 512, b)
                stg, r_stg = st512.next()
                fw.op("act", lambda a, b=b, stg=stg, nt=nt: a.copy(out=stg[0:nt, :], in_=ps[0:nt, b, :]), reads=[r_ps[b]], writes=[r_stg])
                if is_s:
                    fw.dma("sp", o_bk_s[ti, 448:512, :], stg[0:nt, :], reads=[r_stg])
                elif gt >= 12:
                    fw.dma("sp", o_bk_p[(gt - 12) * 128:(gt - 11) * 128, :], stg[0:nt, :], reads=[r_stg])
                b = gbank.next()
                tm_proj(nt, col0, 1024, 512, b)
                stg, r_stg = st512.next()
                fw.op("act", lambda a, b=b, stg=stg, nt=nt: a.copy(out=stg[0:nt, :], in_=ps[0:nt, b, :]), reads=[r_ps[b]], writes=[r_stg])
                if is_s:
                    fw.dma("sp", o_bv_s[ti, 448:512, :], stg[0:nt, :], reads=[r_stg])
                    fw.op("dve", lambda v, stg=stg, ti=ti, nt=nt: v.tensor_copy(out=sv1[0:nt, ti, :], in_=stg[0:nt, :]), reads=[r_stg], writes=[r_sk[1]])
                else:
                    if gt >= 12:
                        fw.dma("sp", o_bv_p[(gt - 12) * 128:(gt - 11) * 128, :], stg[0:nt, :], reads=[r_stg])
                    fw.op("dve", lambda v, stg=stg, gt=gt: v.tensor_copy(out=vc[:, gt % 8, :], in_=stg[:, :]), reads=[r_stg], writes=[r_kc[gt % 8]])
                b = gbank.next()
                tm_proj(nt, col0, 2560, 512, b)
                stg, r_stg = st512.next()
                fw.op("act", lambda a, b=b, stg=stg, nt=nt: a.copy(out=stg[0:nt, :], in_=ps[0:nt, b, :]), reads=[r_ps[b]], writes=[r_stg])
                fw.dma("sp", o_fk[row0:row0 + nt, :], stg[0:nt, :], reads=[r_stg])
                b = gbank.next()
                tm_proj(nt, col0, 3072, 512, b)
                stg, r_stg = st512.next()
                fw.op("act", lambda a, b=b, stg=stg, nt=nt: a.copy(out=stg[0:nt, :], in_=ps[0:nt, b, :]), reads=[r_ps[b]], writes=[r_stg])
                fw.dma("sp", o_fv[row0:row0 + nt, :], stg[0:nt, :], reads=[r_stg])
                if is_s:
                    fw.op("dve", lambda v, stg=stg, ti=ti, nt=nt: v.tensor_copy(out=sv2[0:nt, ti, :], in_=stg[0:nt, :]), reads=[r_stg], writes=[r_sk[3]])
                else:
                    fw.op("dve", lambda v, stg=stg, gt=gt: v.tensor_copy(out=v2[:, gt, :], in_=stg[:, :]), reads=[r_stg], writes=[r_k2[gt]])
                b = gbank.next()
                tm_proj(nt, col0, 3584, 8, b)
                lf, r_lf = lfb.next()
                fw.op("dve", lambda v, b=b, lf=lf, nt=nt: v.tensor_tensor(out=lf[0:nt, 0:8], in0=ps[0:nt, b, 0:8], in1=fb_bc[0:nt, :], op=ALU.add),
                      reads=[r_ps[b], r_fb], writes=[r_lf])
                fw.op("act", lambda a, lf=lf, nt=nt: a.activation(out=lf[0:nt, 8:16], in_=lf[0:nt, 0:8], func=AF.Exp, scale=-1.0), reads=[r_lf], writes=[r_lf])
                fw.op("act", lambda a, lf=lf, nt=nt: a.activation(out=lf[0:nt, 0:8], in_=lf[0:nt, 8:16], func=AF.Ln, bias=1.0), reads=[r_lf], writes=[r_lf])
                fw.op("dve", lambda v, lf=lf, nt=nt: v.tensor_scalar(out=lf[0:nt, 16:24], in0=lf[0:nt, 0:8], scalar1=-1.0, scalar2=None, op0=ALU.mult),
                      reads=[r_lf], writes=[r_lf])
                fw.dma("sp", o_lf[row0:row0 + nt, :], lf[0:nt, 16:24], reads=[r_lf])
                b2 = gbank.next()
                if not is_s:
                    fw.op("pe", lambda t_, b2=b2, lf=lf: t_.matmul(ps[:, b2, 0:8], lhsT=triF[:, :], rhs=lf[:, 16:24], start=True, stop=False),
                          reads=[r_tri, r_lf], writes=[r_ps[b2]])
                    fw.op("pe", lambda t_, b2=b2, lf=lf: t_.matmul(ps[:, b2, 8:16], lhsT=onesF[:, :], rhs=lf[:, 16:24], start=False, stop=True),
                          reads=[r_onesF, r_lf], writes=[r_ps[b2]])
                    fw.op("dve", lambda v, b2=b2, gt=gt: v.tensor_scalar(out=fxb[:, 0, gt, :], in0=ps[:, b2, 0:8], scalar1=-1.0, scalar2=None, op0=ALU.mult),
                          reads=[r_ps[b2]], writes=[r_fxb])
                    fw.op("dve", lambda v, b2=b2, gt=gt: v.tensor_copy(out=fxb[:, 2, gt, :], in_=ps[:, b2, 8:16]), reads=[r_ps[b2]], writes=[r_fxb])
                    fw.op("dve", lambda v, gt=gt: v.tensor_tensor(out=fxb[:, 1, gt, :], in0=fxb[:, 2, gt, :], in1=fxb[:, 0, gt, :], op=ALU.add),
                          reads=[r_fxb], writes=[r_fxb])
                else:
                    fw.op("pe", lambda t_, b2=b2, lf=lf: t_.matmul(ps[0:64, b2, 0:8], lhsT=triF[0:64, 0:64], rhs=lf[0:64, 16:24], start=True, stop=True),
                          reads=[r_tri, r_lf], writes=[r_ps[b2]])
                    fw.op("dve", lambda v, b2=b2, ti=ti: v.tensor_scalar(out=sfxn[0:64, ti, :], in0=ps[0:64, b2, 0:8], scalar1=-1.0, scalar2=None, op0=ALU.mult),
                          reads=[r_ps[b2]], writes=[r_sfxn])
            fm_group([0 + 128 * i for i in range(4)],
                     lambda i, src, rb: evac_bd(qA, r_qA, i, src, rb))
            fm_group([2048 + 128 * i for i in range(4)],
                     lambda i, src, rb: evac_bd(qBd, r_qB, i, src, rb))
            if is_s:
                fm_group([512 + 128 * i for i in range(4)],
                         lambda i, src, rb: fw.op("dve", lambda v: v.tensor_copy(out=skT1[:, i, :], in_=src), reads=[rb], writes=[r_sk[0]]))
                fm_group([2560 + 128 * i for i in range(4)],
                         lambda i, src, rb: fw.op("dve", lambda v: v.tensor_copy(out=skT2[:, i, :], in_=src), reads=[rb], writes=[r_sk[2]]))
            else:
                t0 = tiles[0][1]
                g0 = t0 // 128
                rc = (t0 % 1024)

                def evc(i, src, rb):
                    fw.op("dve", lambda v: v.tensor_copy(out=kTc[:, i, rc:rc + BLK], in_=src), reads=[rb], writes=[r_kc[g0 % 8], r_kc[(g0 + 1) % 8]])

                def evd(i, src, rb):
                    fw.op("dve", lambda v: v.tensor_copy(out=kT2[:, i, t0:t0 + BLK], in_=src), reads=[rb], writes=[r_k2[g0], r_k2[g0 + 1]])
                fm_group([512 + 128 * i for i in range(4)], evc)
                fm_group([2560 + 128 * i for i in range(4)], evd)
            fm_group([1536 + 128 * i for i in range(4)] + [3592 + 128 * i for i in range(4)],
                     lambda i, src, rb: fw.op("act", lambda a: a.activation(out=gT[:, i, :], in_=src, func=AF.Silu),
                                              reads=[rb], writes=[r_gT[i]]))

        def band_pre(u, dd, hs, nq):
            nk = u.nk
            H = hs.stop - hs.start
            if dd == 0:
                return pre_add(u, tblB[0:nk, hs, 0:nq], [r_tblB])
            if dd == 1:
                return pre_add(u, tblB[0:nk, hs, 128:128 + nq], [r_tblB])
            cst = cstB[0:nk, hs].unsqueeze(2).broadcast_to([nk, H, nq])
            if dd == 4:
                return pre_add(u, cst, [r_cstB], extra=(mask512[:, :].unsqueeze(1).broadcast_to([128, H, 128]), [r_mask512]))
            return pre_add(u, cst, [r_cstB])

        def attn1_prompt(bi):
            all_band = []
            all_fox = []
            for qt in (2 * bi, 2 * bi + 1):
                qc = (qt % 2) * 128
                biasq = _T(biasq2[:, qt % 2])
                for kt in range(qt - 1, -1, -1):
                    if kt == qt - 1:
                        fw.op("dve", lambda v, kt=kt, biasq=biasq: v.tensor_copy(out=biasq[:, kt, :], in_=fxb[:, 1, kt, :]), reads=[r_fxb], writes=[r_biasq])
                        fw.op("dve", lambda v, kt=kt, biasq=biasq: v.tensor_copy(out=accb[:, :], in_=fxb[:, 2, kt, :]), reads=[r_fxb], writes=[r_accb])
                    else:
                        fw.op("dve", lambda v, kt=kt, biasq=biasq: v.tensor_tensor(out=biasq[:, kt, :], in0=fxb[:, 1, kt, :], in1=accb[:, :], op=ALU.add),
                              reads=[r_fxb, r_accb], writes=[r_biasq])
                        fw.op("dve", lambda v, kt=kt, biasq=biasq: v.tensor_tensor(out=accb[:, :], in0=accb[:, :], in1=fxb[:, 2, kt, :], op=ALU.add),
                              reads=[r_fxb, r_accb], writes=[r_accb])
                for hg in range(2):
                    hs = slice(4 * hg, 4 * hg + 4)
                    chain = {}
                    units = []
                    kts = list(range(qt, max(-1, qt - 5), -1))
                    for kt in kts:
                        u = U(); u.nk = 128; u.chain = chain
                        u.first = (kt == kts[0]); u.last = (kt == kts[-1])
                        u.zmm = []; u.vmm = []
                        sl = kt % 8
                        for pp in range(2):
                            u.zmm.append((kTc[:, 2 * hg + pp, sl * 128:(sl + 1) * 128], qA[:, 2 * hg + pp, qt % 2, :], pp * 256, 256))
                        for pp in range(2):
                            u.vmm.append((vc[:, sl, (2 * hg + pp) * 128:(2 * hg + pp + 1) * 128], pp * 256, 256))
                        zmm4(u)
                        u.reads = [r_kc[sl], r_qA]; u.vreads = [r_kc[sl]]
                        if (qt - kt) in (2, 3):
                            u.hbias = [(cstB[:, 4 * hg + j:4 * hg + j + 1], r_cstB) for j in range(4)]
                        u.pre = (lambda u, dd=qt - kt, hs=hs: band_pre(u, dd, hs, 128))
                        slots = [(4 * hg + s_, s_ * 128, 128, (4 * hg + s_) // 2, qc) for s_ in range(4)]
                        u.fin = (lambda ob, db, slots=slots: fin_softmax(slots, ob, db))
                        units.append(u)
                    all_band += units
                    chain = {}
                    units = []
                    for kt in range(qt, -1, -1):
                        u = U(); u.nk = 128; u.chain = chain
                        u.first = (kt == qt); u.last = (kt == 0)
                        u.zmm = []; u.vmm = []
                        for pp in range(2):
                            u.zmm.append((kT2[:, 2 * hg + pp, kt * 128:(kt + 1) * 128], qBd[:, 2 * hg + pp, qt % 2, :], pp * 256, 256))
                        for pp in range(2):
                            u.vmm.append((v2[:, kt, (2 * hg + pp) * 128:(2 * hg + pp + 1) * 128], pp * 256, 256))
                        zmm4(u)
                        u.reads = [r_k2[kt], r_qB]; u.vreads = [r_k2[kt]]
                        if kt == qt:
                            u.pre = (lambda u, qt=qt, hs=hs: pre_add(u, fxb[:, 0, qt, hs].unsqueeze(2).broadcast_to([128, 4, 128]), [r_fxb],
                                                                      extra=(maskFX[:, :].unsqueeze(1).broadcast_to([128, 4, 128]), [r_maskFX])))
                        else:
                            if kt % 2 == 1:
                                u.hbias = [(biasq[:, kt, 4 * hg + j:4 * hg + j + 1], r_biasq) for j in range(4)]
                            u.pre = (lambda u, kt=kt, hs=hs, biasq=biasq: pre_add(u, biasq[:, kt, hs].unsqueeze(2).broadcast_to([128, 4, 128]), [r_biasq]))
                        slots = [(4 * hg + s_, s_ * 128, 128, 4 + (4 * hg + s_) // 2, qc) for s_ in range(4)]
                        u.fin = (lambda ob, db, slots=slots: fin_softmax(slots, ob, db))
                        units.append(u)
                    all_fox += units
            sm_chain(all_band)
            sm_chain(all_fox)

        def attn1_sample():
            hs8 = slice(0, 8)
            for s in range(NSTR):
                fw.dma("sp", o_bk_s[s, 0:448, :], c_bk[s, 64:512, :])
                fw.dma("sp", o_bv_s[s, 0:448, :], c_bv[s, 64:512, :])
                slots = [(h, h * 64, 64, h // 2, 64 * s) for h in range(8)]
                units = kv_units(s, 4, c_bk, c_bv, qA, r_qA, skT1, sv1, r_sk[0], slots)
                units[0].reads = [r_sk[0], qbd[s][1]]; units[0].vreads = [r_sk[1]]
                zmm4(units[0])
                units[0].pre = (lambda u: band_pre(u, 0, hs8, 64))
                for u in units[1:]:
                    dd = 4 - u.tile
                    op_ = u.prep

                    def prep2(u, op_=op_):
                        op_(u)
                        zmm4(u)
                    u.prep = prep2
                    u.pre = (lambda u, dd=dd: band_pre(u, 1 if dd == 1 else 2, hs8, 64))
                for u in units:
                    u.fin = (lambda ob, db, slots=slots: fin_softmax(slots, ob, db))
                sm_chain(units)
            for s in range(NSTR):
                fw.dma("sp", lfc[:], c_lf[s].rearrange("(t p) h -> p t h", p=128), writes=[r_lfc])
                lf2 = lfc[:].rearrange("p t h -> p (t h)")
                b2 = gbank.next()
                fw.op("pe", lambda t_, b2=b2: t_.matmul(ps[:, b2, 0:256], lhsT=triF[:, :], rhs=lf2, start=True, stop=False),
                      reads=[r_tri, r_lfc], writes=[r_ps[b2]])
                fw.op("pe", lambda t_, b2=b2: t_.matmul(ps[:, b2, 256:512], lhsT=onesF[:, :], rhs=lf2, start=False, stop=True),
                      reads=[r_onesF, r_lfc], writes=[r_ps[b2]])
                fw.op("dve", lambda v, b2=b2: v.tensor_copy(out=lf2, in_=ps[:, b2, 256:512]), reads=[r_ps[b2]], writes=[r_lfc])
                sf2 = sfx[:, 0:32, :].rearrange("p t h -> p (t h)")
                fw.op("dve", lambda v, b2=b2: v.tensor_tensor(out=sf2, in0=lf2, in1=ps[:, b2, 0:256], op=ALU.subtract),
                      reads=[r_ps[b2], r_lfc], writes=[r_sfx])
                for t in range(30, -1, -1):
                    if t == 30:
                        fw.op("dve", lambda v: v.tensor_copy(out=accb[:, :], in_=lfc[:, 31, :]), reads=[r_lfc], writes=[r_accb])
                    else:
                        fw.op("dve", lambda v, t=t: v.tensor_tensor(out=accb[:, :], in0=accb[:, :], in1=lfc[:, t + 1, :], op=ALU.add),
                              reads=[r_lfc, r_accb], writes=[r_accb])
                    fw.op("dve", lambda v, t=t: v.tensor_tensor(out=sfx[:, t, :], in0=sfx[:, t, :], in1=accb[:, :], op=ALU.add),
                          reads=[r_sfx, r_accb], writes=[r_sfx])
                slots = [(h, h * 64, 64, 4 + h // 2, 64 * s) for h in range(8)]
                units = kv_units(s, PAST // 128, c_fk, c_fv, qBd, r_qB, skT2, sv2, r_sk[2], slots)
                units[0].reads = [r_sk[2], qbd[s][1]]; units[0].vreads = [r_sk[3]]
                zmm4(units[0])
                units[0].pre = (lambda u, s=s: pre_add(u, sfxn[0:64, s, :].unsqueeze(2).broadcast_to([64, 8, 64]), [r_sfxn],
                                                       extra=(maskFX[0:64, 0:64].unsqueeze(1).broadcast_to([64, 8, 64]), [r_maskFX])))
                for u in units[1:]:
                    op_ = u.prep

                    def prep3(u, op_=op_):
                        op_(u)
                        zmm4(u)
                    u.prep = prep3
                    u.pre = (lambda u: pre_add(u, sfx[:, u.tile, :].unsqueeze(2).broadcast_to([128, 8, 64]), [r_sfx]))
                for u in units:
                    u.fin = (lambda ob, db, slots=slots: fin_softmax(slots, ob, db))
                sm_chain(units)

        def phase_out(l, tiles, xsrc, r_xsrc, ydst, r_ydst):
            for (nt, row0, col0) in tiles:
                b0 = gbank.next(); b1 = gbank.next()
                for half, b in ((0, b0), (1, b1)):
                    for c in range(8):
                        fw.op("pe", lambda t, c=c, b=b, half=half, nt=nt, col0=col0: t.matmul(ps[0:nt, b, :], lhsT=gT[:, c, col0:col0 + nt],
                                                                                               rhs=wout[:, c, half * 512:(half + 1) * 512],
                                                                                               start=(c == 0), stop=(c == 7)),
                              reads=[r_gT[c], r_wout], writes=[r_ps[b]])
                s, r_s = sm.next()
                fw.op("act", lambda a, s=s, nt=nt, b0=b0: a.activation(out=junk[0:nt, :], in_=ps[0:nt, b0, :], func=AF.Square, accum_out=s[0:nt, 4:5]),
                      reads=[r_ps[b0]], writes=[r_junk, r_s])
                fw.op("act", lambda a, s=s, nt=nt, b1=b1: a.activation(out=junk[0:nt, :], in_=ps[0:nt, b1, :], func=AF.Square, accum_out=s[0:nt, 5:6]),
                      reads=[r_ps[b1]], writes=[r_junk, r_s])
                fw.op("dve", lambda v, s=s, nt=nt: v.tensor_tensor(out=s[0:nt, 2:3], in0=s[0:nt, 4:5], in1=s[0:nt, 5:6], op=ALU.add), reads=[r_s], writes=[r_s])
                rs, r_rs = rstd_from_ss(s[0:nt, 2:3], D, nt, r_s)
                xt, r_xt = xin.next()
                fw.dma("act", xt[0:nt, :], xsrc[row0:row0 + nt, :], reads=[r_xsrc[row0 // 64]] if r_xsrc else [], writes=[r_xt])
                for half, b in ((0, b0), (1, b1)):
                    y, r_y = tmpf.next()
                    fw.op("dve", lambda v, half=half, b=b, y=y, rs=rs, nt=nt: v.scalar_tensor_tensor(
                        out=y[0:nt, :], in0=ps[0:nt, b, :], scalar=rs, in1=gpost[0:nt, half * 512:(half + 1) * 512],
                        op0=ALU.mult, op1=ALU.mult), reads=[r_ps[b], r_rs, r_gpost], writes=[r_y])
                    fw.op("pool", lambda g, y=y, xt=xt, nt=nt, half=half: g.tensor_tensor(out=xt[0:nt, half * 512:(half + 1) * 512], in0=y[0:nt, :],
                                                                                          in1=xt[0:nt, half * 512:(half + 1) * 512], op=ALU.add),
                          reads=[r_y, r_xt], writes=[r_xt])
                fw.dma("sp", ydst[row0:row0 + nt, :], xt[0:nt, :], reads=[r_xt], writes=[r_ydst[row0 // 64]] if r_ydst else [])

        r_x1p = [Res() for _ in range(SEQ // 64)]
        r_x1s = [Res() for _ in range(NSTR * DSEQ // 64)]
        ptiles = lambda bi: [(128, bi * BLK, 0), (128, bi * BLK + 128, 128)]
        stiles = [(64, 64 * s, 64 * s) for s in range(NSTR)]

        load_layer_weights(0)
        nblk = min(SEQ // BLK, NBLK_DBG)
        for bi in range(nblk):
            phase_norm(0, ptiles(bi), xp, None)
            phase_proj0(ptiles(bi), False, o_ckv_p, o_kr_p, o_sbk_p, o_sbv_p)
            if STAGES >= 2:
                attn0_prompt(bi)
            phase_out(0, ptiles(bi), xp, None, x1p, r_x1p)
        if STAGES >= 1:
            fw.barrier()
            phase_norm(0, stiles, xs, None)
            phase_proj0(stiles, True, o_ckv_s, o_kr_s, o_sbk_s, o_sbv_s)
            if STAGES >= 3:
                CAST_ENGS[0] = ("dve", "act", "dve", "pool")
                attn0_sample()
                CAST_ENGS[0] = ("pool", "dve", "act")
            phase_out(0, stiles, xs, None, x1s, r_x1s)
        if STAGES >= 4:
            load_layer_weights(1)
            setup_l1_tables()
            fw.op("pool", lambda g: g.memset(qBd[:].rearrange("p a t q -> p (a t q)"), 0.0), writes=[r_qB])
            fw.barrier()
            for bi in range(nblk):
                phase_norm(1, ptiles(bi), x1p, r_x1p)
                phase_proj1(ptiles(bi), False, o_fk_p, o_fv_p, o_lf_p)
                if STAGES >= 5:
                    attn1_prompt(bi)
                phase_out(1, ptiles(bi), x1p, r_x1p, y_p, None)
            fw.barrier()
            phase_norm(1, stiles, x1s, r_x1s)
            phase_proj1(stiles, True, o_fk_s, o_fv_s, o_lf_s)
            if STAGES >= 6:
                CAST_ENGS[0] = ("act", "pool", "act")
                EVAC_ENGS[0] = ("dve", "act")
                attn1_sample()
            phase_out(1, stiles, x1s, r_x1s, y_s, None)

        print("fw ops recorded:", getattr(fw, "nops", 0), {k: e.cnt for k, e in fw.E.items()})
        fw.finish()
        fw.emit()
    return nc


def _rope_tables():
    half = 16
    inv = (10000.0 ** (-np.arange(half, dtype=np.float32) / half)).astype(np.float32)

    def tab(pos):
        ang = pos.astype(np.float32)[:, None] * inv[None, :]
        c = np.cos(ang).astype(np.float32)
        s = np.sin(ang).astype(np.float32)
        return np.concatenate([c, c, -s, s], axis=1).astype(np.float32)
    tp = tab(np.arange(SEQ)).reshape(16, 128, 64).transpose(1, 0, 2).reshape(128, 16 * 64)
    tsm = tab(PAST + np.arange(DSEQ))
    return np.ascontiguousarray(tp), np.ascontiguousarray(tsm)


_NC_CACHE = {}


def kernel(**inp):
    f = lambda a: np.ascontiguousarray(np.asarray(a, dtype=np.float32))
    x_prompt = f(inp["x_prompt"]); x_sample = f(inp["x_sample"])
    rope_p, rope_s = _rope_tables()
    w_uq = f(inp["a_w_uq"])[0]
    w_uq_l = np.concatenate([w_uq[:, :, :64].reshape(256, 512), w_uq[:, :, 64:].reshape(256, 256)], axis=1)
    w_uk = f(inp["a_w_uk"])[0]
    w_ukT = np.transpose(w_uk, (2, 1, 0)).reshape(64, 1024)
    w_ukT = np.concatenate([w_ukT, w_ukT], axis=0)
    relb = f(inp["c_rel_bias"])[0]
    relb_pad = np.concatenate([relb, np.repeat(relb[:, -1:], 256, axis=1)], axis=1)
    shared = {
        "norm_pre": np.ascontiguousarray(f(inp["norm_pre"]).reshape(2, 8, 128).transpose(2, 0, 1).reshape(128, 16)), "norm_post": f(inp["norm_post"]),
        "w_in0": f(inp["w_in_even"])[0], "q_norm": f(inp["a_q_norm"]).reshape(1, 256),
        "w_uq": np.ascontiguousarray(w_uq_l), "kv_norm": f(inp["a_kv_norm"]).reshape(1, 128),
        "w_ukT": np.ascontiguousarray(w_ukT), "w_uv": f(inp["a_w_uv"])[0].reshape(128, 512),
        "w_out0": f(inp["w_out_even"])[0], "w_in1": f(inp["w_in_odd"])[0],
        "relb": np.ascontiguousarray(relb_pad), "fbias": f(inp["d_forget_bias"]).reshape(1, 8),
        "relc": np.ascontiguousarray(relb[:, 256].reshape(1, 8)),
        "w_out1": f(inp["w_out_odd"])[0], "rope_p": rope_p, "rope_s": rope_s,
    }
    caches = {k: f(inp[k])[0] for k in ("cache_mla_ckv", "cache_mla_krope", "cache_sb_k", "cache_sb_v", "cache_band_k",
                                        "cache_band_v", "cache_fox_k", "cache_fox_v", "cache_fox_logf")}
    in_maps = []
    for c in range(NCORES):
        sl = slice(NSTR * c, NSTR * (c + 1))
        m = dict(shared)
        m["xp"] = x_prompt[c]
        m["xs"] = x_sample[sl].reshape(NSTR * DSEQ, D)
        m["c_ckv"] = caches["cache_mla_ckv"][sl]
        m["c_kr"] = caches["cache_mla_krope"][sl]
        m["c_sbk"] = caches["cache_sb_k"][sl].reshape(NSTR, PAST, 512)
        m["c_sbv"] = caches["cache_sb_v"][sl].reshape(NSTR, PAST, 512)
        m["c_bk"] = caches["cache_band_k"][sl].reshape(NSTR, 512, 512)
        m["c_bv"] = caches["cache_band_v"][sl].reshape(NSTR, 512, 512)
        m["c_fk"] = caches["cache_fox_k"][sl].reshape(NSTR, PAST, 512)
        m["c_fv"] = caches["cache_fox_v"][sl].reshape(NSTR, PAST, 512)
        m["c_lf"] = caches["cache_fox_logf"][sl]
        in_maps.append({k: np.ascontiguousarray(v) for k, v in m.items()})
    if "nc" not in _NC_CACHE:
        _NC_CACHE["nc"] = build()
    nc = _NC_CACHE["nc"]
    if KCORES < NCORES:
        res = run_bass_kernel_spmd(nc, in_maps[:KCORES], core_ids=list(range(KCORES)))
        R = list(res.results) + [res.results[0]] * (NCORES - KCORES)
    else:
        res = run_bass_kernel_spmd(nc, in_maps, core_ids=list(range(NCORES)))
        R = res.results
    cat = lambda k: np.stack([R[c][k] for c in range(NCORES)], axis=0)
    B = NCORES
    SB = NCORES * NSTR
    outs = (
        cat("y_p").reshape(B, SEQ, D),
        cat("y_s").reshape(SB, DSEQ, D),
        cat("o_ckv_p").reshape(1, B, SEQ, 128), cat("o_kr_p").reshape(1, B, SEQ, 32),
        cat("o_sbk_p").reshape(1, B, SEQ, 8, 64), cat("o_sbv_p").reshape(1, B, SEQ, 8, 64),
        cat("o_bk_p").reshape(1, B, 512, 8, 64), cat("o_bv_p").reshape(1, B, 512, 8, 64),
        cat("o_fk_p").reshape(1, B, SEQ, 8, 64), cat("o_fv_p").reshape(1, B, SEQ, 8, 64), cat("o_lf_p").reshape(1, B, SEQ, 8),
        cat("o_ckv_s").reshape(1, SB, DSEQ, 128), cat("o_kr_s").reshape(1, SB, DSEQ, 32),
        cat("o_sbk_s").reshape(1, SB, DSEQ, 8, 64), cat("o_sbv_s").reshape(1, SB, DSEQ, 8, 64),
        cat("o_bk_s").reshape(1, SB, 512, 8, 64), cat("o_bv_s").reshape(1, SB, 512, 8, 64),
        cat("o_fk_s").reshape(1, SB, DSEQ, 8, 64), cat("o_fv_s").reshape(1, SB, DSEQ, 8, 64), cat("o_lf_s").reshape(1, SB, DSEQ, 8),
    )
    _NC_CACHE["x1"] = (cat("x1p"), cat("x1s"))
    return tuple(np.ascontiguousarray(o.astype(np.float32)) for o in outs)
```

```python
import numpy as np
from contextlib import ExitStack
import concourse.bass as bass
import concourse.mybir as mybir
from concourse.bass_utils import run_bass_kernel_spmd

F32 = mybir.dt.float32
BF16 = mybir.dt.bfloat16
AF = mybir.ActivationFunctionType
ALU = mybir.AluOpType

NCORES = 8
D = 1024
SEQ = 2048
NSTR = 4
DSEQ = 64
PAST = 4096
EPS = 1e-6
NEGM = -30000.0
A_SCALE = float((64 + 32) ** -0.5)
EVEN_IN = 2976
ODD_IN = 4104
BLK = 256
import os
STAGES = int(os.environ.get('KSTAGES', '6'))
NBLK_DBG = int(os.environ.get('KNBLK', '8'))
OPLIMIT = int(os.environ.get('KOPLIMIT', '100000000'))
KCORES = int(os.environ.get('KCORES', '8'))
KSAME = int(os.environ.get('KSAME', '0'))


class Res:
    __slots__ = ("w", "rs", "x", "rg")

    def __init__(self, x=False):
        self.w = None
        self.rs = []
        self.rg = None
        self.x = x


class Eng:
    def __init__(self, name, sem):
        self.name = name
        self.sem = sem
        self.cnt = 0
        self.waited = {}
        self.prog = []
        self.dq = []
        self.dcnt = []
        self.di = 0


class FW:
    def __init__(self, nc, es, ndq=8):
        self.nc = nc
        self.E = {}
        for name in ("pe", "act", "dve", "pool", "sp"):
            self.E[name] = Eng(name, es.enter_context(nc.semaphore("s_" + name)))
        for qn in ("sp", "act", "pool"):
            e = self.E[qn]
            for i in range(ndq):
                e.dq.append(es.enter_context(nc.semaphore(f"d_{qn}{i}")))
                e.dcnt.append(0)

    def _wait(self, eng, tok, force=False):
        if tok is None:
            return
        sem, val, src = tok
        if src == eng.name and src in ("pe", "sp") and not force:
            return
        key = id(sem)
        if eng.waited.get(key, 0) >= val:
            return
        eng.waited[key] = val
        eng.prog.append(("w", sem, val))

    def _deps(self, eng, reads, writes):
        for r in reads:
            self._wait(eng, r.w)
            if r.x:
                for t in r.rs:
                    if t[2] != eng.name:
                        self._wait(eng, t)
        for w in writes:
            if w.w is not None and (w.w[2] != eng.name or KSAME):
                self._wait(eng, w.w)
            for t in w.rs:
                if t[2] != eng.name or KSAME:
                    self._wait(eng, t)

    def _record(self, tok, reads, writes):
        for r in reads:
            r.rs.append(tok)
            if len(r.rs) > 16:
                best = {}
                for t in r.rs:
                    k = id(t[0])
                    if k not in best or best[k][1] < t[1]:
                        best[k] = t
                r.rs = list(best.values())
        for w in writes:
            w.w = tok
            w.rs = []

    def op(self, engname, fn, reads=(), writes=(), rg=None):
        self.nops = getattr(self, "nops", 0) + 1
        if self.nops > OPLIMIT:
            return None
        eng = self.E[engname]
        self._deps(eng, reads, writes)
        if engname == "pe":
            for w in writes:
                if rg is not None and w.rg is not None and w.rg != rg and w.w is not None and w.w[2] == "pe":
                    self._wait(eng, w.w, force=True)
                w.rg = rg
        eng.cnt += 1
        eng.prog.append(("i", fn, eng.sem, 1))
        tok = (eng.sem, eng.cnt, eng.name)
        self._record(tok, reads, writes)
        return tok

    def dma(self, q, out, in_, reads=(), writes=()):
        self.nops = getattr(self, "nops", 0) + 1
        if self.nops > OPLIMIT:
            return None
        eng = self.E[q]
        self._deps(eng, reads, writes)
        i = eng.di % len(eng.dq)
        eng.di += 1
        sem = eng.dq[i]
        if eng.dcnt[i] > 0:
            self._wait(eng, (sem, eng.dcnt[i], "dma"))
        eng.prog.append(("i", (lambda o, out=out, in_=in_: o.dma_start(out=out, in_=in_)), sem, 16))
        eng.dcnt[i] += 16
        tok = (sem, eng.dcnt[i], "dma")
        self._record(tok, reads, writes)
        return tok

    def barrier(self):
        toks = []
        for q in ("sp", "act", "pool"):
            e = self.E[q]
            for i, sem in enumerate(e.dq):
                if e.dcnt[i] > 0:
                    toks.append((sem, e.dcnt[i], "dma"))
        for n in ("pe", "act", "dve", "pool"):
            e = self.E[n]
            if e.cnt > 0:
                toks.append((e.sem, e.cnt, "x"))
        for n in ("pe", "act", "dve", "pool", "sp"):
            for t in toks:
                self._wait(self.E[n], t)

    def finish(self):
        sp = self.E["sp"]
        for q in ("sp", "act", "pool"):
            e = self.E[q]
            for i, sem in enumerate(e.dq):
                if e.dcnt[i] > 0:
                    self._wait(sp, (sem, e.dcnt[i], "dma"))
        for n in ("pe", "act", "dve", "pool"):
            e = self.E[n]
            if e.cnt > 0:
                self._wait(sp, (e.sem, e.cnt, "x"))

    def emit(self):
        nc = self.nc
        objs = {"pe": None}

        def run(eng):
            def body(obj):
                for a in eng.prog:
                    if a[0] == "w":
                        obj.wait_ge(a[1], a[2])
                    else:
                        a[1](obj).then_inc(a[2], a[3])
            return body
        with nc.Block() as block:
            block.tensor(run(self.E["pe"]))
            block.scalar(run(self.E["act"]))
            block.vector(run(self.E["dve"]))
            block.gpsimd(run(self.E["pool"]))
            block.sync(run(self.E["sp"]))


class RR:
    def __init__(self, items):
        self.items = items
        self.i = 0

    def next(self):
        it = self.items[self.i % len(self.items)]
        self.i += 1
        return it


def pipeline(units, stages, offsets=None):
    n = len(units)
    ns = len(stages)
    if offsets is None:
        offsets = list(range(ns))
    for i in range(n + max(offsets)):
        for s, st in enumerate(stages):
            j = i - offsets[s]
            if 0 <= j < n:
                st(units[j])


def build():
    nc = bass.Bass("TRN2", target_bir_lowering=False)
    din = lambda n, s: nc.dram_tensor(n, s, F32, kind="ExternalInput").ap()
    dout = lambda n, s: nc.dram_tensor(n, s, F32, kind="ExternalOutput").ap()
    xp = din("xp", [SEQ, D])
    xs = din("xs", [NSTR * DSEQ, D])
    c_ckv = din("c_ckv", [NSTR, PAST, 128])
    c_kr = din("c_kr", [NSTR, PAST, 32])
    c_sbk = din("c_sbk", [NSTR, PAST, 512])
    c_sbv = din("c_sbv", [NSTR, PAST, 512])
    c_bk = din("c_bk", [NSTR, 512, 512])
    c_bv = din("c_bv", [NSTR, 512, 512])
    c_fk = din("c_fk", [NSTR, PAST, 512])
    c_fv = din("c_fv", [NSTR, PAST, 512])
    c_lf = din("c_lf", [NSTR, PAST, 8])
    norm_pre = din("norm_pre", [128, 16])
    norm_post = din("norm_post", [2, D])
    w_in0 = din("w_in0", [D, EVEN_IN])
    q_norm = din("q_norm", [1, 256])
    w_uq = din("w_uq", [256, 768])
    kv_norm = din("kv_norm", [1, 128])
    w_ukT = din("w_ukT", [128, 1024])
    w_uv = din("w_uv", [128, 512])
    w_out0 = din("w_out0", [D, D])
    w_in1 = din("w_in1", [D, ODD_IN])
    relb = din("relb", [8, 513])
    fbias = din("fbias", [1, 8])
    relc = din("relc", [1, 8])
    w_out1 = din("w_out1", [D, D])
    rope_p = din("rope_p", [128, 16 * 64])
    rope_s = din("rope_s", [64, 64])
    y_p = dout("y_p", [SEQ, D])
    y_s = dout("y_s", [NSTR * DSEQ, D])
    o_ckv_p = dout("o_ckv_p", [SEQ, 128]); o_kr_p = dout("o_kr_p", [SEQ, 32])
    o_sbk_p = dout("o_sbk_p", [SEQ, 512]); o_sbv_p = dout("o_sbv_p", [SEQ, 512])
    o_bk_p = dout("o_bk_p", [512, 512]); o_bv_p = dout("o_bv_p", [512, 512])
    o_fk_p = dout("o_fk_p", [SEQ, 512]); o_fv_p = dout("o_fv_p", [SEQ, 512]); o_lf_p = dout("o_lf_p", [SEQ, 8])
    o_ckv_s = dout("o_ckv_s", [NSTR * DSEQ, 128]); o_kr_s = dout("o_kr_s", [NSTR * DSEQ, 32])
    o_sbk_s = dout("o_sbk_s", [NSTR * DSEQ, 512]); o_sbv_s = dout("o_sbv_s", [NSTR * DSEQ, 512])
    o_bk_s = dout("o_bk_s", [NSTR, 512, 512]); o_bv_s = dout("o_bv_s", [NSTR, 512, 512])
    o_fk_s = dout("o_fk_s", [NSTR * DSEQ, 512]); o_fv_s = dout("o_fv_s", [NSTR * DSEQ, 512])
    o_lf_s = dout("o_lf_s", [NSTR * DSEQ, 8])
    x1p = dout("x1p", [SEQ, D])
    x1s = dout("x1s", [NSTR * DSEQ, D])

    with ExitStack() as es:
        fw = FW(nc, es)
        ARN = 105500
        AR = es.enter_context(nc.sbuf_tensor("AR", [128, ARN], BF16))
        ar = {"top": 0, "peak": 0}

        class _T:
            def __init__(self, ap):
                self.ap = ap
            def __getitem__(self, k):
                return self.ap[k]

        def sbt(n, s, d=F32):
            nel = int(np.prod(s[1:]))
            nb = nel * (4 if d == F32 else 2)
            nb = (nb + 63) // 64 * 64
            off = ar["top"]
            ar["top"] += nb // 2
            ar["peak"] = max(ar["peak"], ar["top"])
            assert ar["top"] <= ARN, (n, ar["top"])
            v = AR[:, off:off + nb // 2]
            if d == F32:
                v = v.bitcast(F32)
            v = v[:, 0:nel]
            if len(s) == 3:
                v = v.rearrange("p (a b) -> p a b", a=s[1])
            elif len(s) == 4:
                v = v.rearrange("p (a b c) -> p a b c", a=s[1], b=s[2])
            if s[0] < 128:
                v = v[0:s[0]]
            return _T(v)
        ps = es.enter_context(nc.psum_tensor("ps", [128, 7, 512], F32))
        psb = es.enter_context(nc.psum_tensor("psb", [128, 1024], BF16))
        r_ps = [Res(True) for _ in range(7)]
        r_psb = Res(True)
        bank = lambda i: ps[:, i, :]

        ident = sbt("ident", [128, 128], BF16); r_ident = Res()
        fw.op("pool", lambda g: g.memset(ident[:], 1.0), writes=[r_ident])
        fw.op("pool", lambda g: g.affine_select(out=ident[:], in_=ident[:], pattern=[[-1, 128]], compare_op=ALU.is_equal,
                                                fill=0.0, base=0, channel_multiplier=1), reads=[r_ident], writes=[r_ident])
        flipJ = sbt("flipJ", [128, 128], F32); r_flip = Res()
        fw.op("pool", lambda g: g.memset(flipJ[:], 1.0), writes=[r_flip])
        fw.op("pool", lambda g: g.affine_select(out=flipJ[:], in_=flipJ[:], pattern=[[1, 128]], compare_op=ALU.is_equal,
                                                fill=0.0, base=-127, channel_multiplier=1), reads=[r_flip], writes=[r_flip])
        triF = sbt("triF", [128, 128], F32); r_tri = Res()
        fw.op("pool", lambda g: g.memset(triF[:], 1.0), writes=[r_tri])
        fw.op("pool", lambda g: g.affine_select(out=triF[:], in_=triF[:], pattern=[[1, 128]], compare_op=ALU.is_ge,
                                                fill=0.0, base=0, channel_multiplier=-1), reads=[r_tri], writes=[r_tri])
        onesF = sbt("onesF", [128, 128], F32); r_onesF = Res()
        fw.op("pool", lambda g: g.memset(onesF[:], 1.0), writes=[r_onesF])
        onesB = sbt("onesB", [128, 128], BF16); r_onesB = Res()
        fw.op("pool", lambda g: g.memset(onesB[:], 1.0), writes=[r_onesB])
        negOnes = sbt("negOnes", [128, 128], BF16); r_negOnes = Res()
        fw.op("pool", lambda g: g.memset(negOnes[:], -1.0), writes=[r_negOnes])
        negTri = sbt("negTri", [128, 128], BF16); r_negTri = Res()
        fw.op("pool", lambda g: g.memset(negTri[:], -1.0), writes=[r_negTri])
        fw.op("pool", lambda g: g.affine_select(out=negTri[:], in_=negTri[:], pattern=[[-1, 128]], compare_op=ALU.is_ge,
                                                fill=0.0, base=0, channel_multiplier=1), reads=[r_negTri], writes=[r_negTri])
        maskSB = sbt("maskSB", [128, 512], BF16); r_maskSB = Res()
        fw.op("pool", lambda g: g.memset(maskSB[:], 0.0), writes=[r_maskSB])
        for a in range(4):
            fw.op("pool", lambda g, a=a: g.affine_select(out=maskSB[:, a * 128:(a + 1) * 128], in_=maskSB[:, a * 128:(a + 1) * 128],
                                                         pattern=[[1, 128]], compare_op=ALU.is_gt, fill=NEGM, base=0,
                                                         channel_multiplier=-1), reads=[r_maskSB], writes=[r_maskSB])
        maskSB64 = sbt("maskSB64", [64, 512], BF16); r_maskSB64 = Res()
        fw.op("pool", lambda g: g.memset(maskSB64[:], 0.0), writes=[r_maskSB64])
        for a in range(8):
            fw.op("pool", lambda g, a=a: g.affine_select(out=maskSB64[:, a * 64:(a + 1) * 64], in_=maskSB64[:, a * 64:(a + 1) * 64],
                                                         pattern=[[1, 64]], compare_op=ALU.is_gt, fill=NEGM, base=0,
                                                         channel_multiplier=-1), reads=[r_maskSB64], writes=[r_maskSB64])
        maskFX = sbt("maskFX", [128, 128], F32); r_maskFX = Res()
        fw.op("pool", lambda g: g.memset(maskFX[:], 0.0), writes=[r_maskFX])
        fw.op("pool", lambda g: g.affine_select(out=maskFX[:], in_=maskFX[:], pattern=[[1, 128]], compare_op=ALU.is_ge,
                                                fill=NEGM, base=0, channel_multiplier=-1), reads=[r_maskFX], writes=[r_maskFX])
        maskCH = sbt("maskCH", [128, 128], F32); r_maskCH = Res()
        fw.op("pool", lambda g: g.memset(maskCH[:], 0.0), writes=[r_maskCH])
        fw.op("pool", lambda g: g.memset(maskCH[64:128, 0:64], NEGM), reads=[r_maskCH], writes=[r_maskCH])
        mask512 = sbt("mask512", [128, 128], F32); r_mask512 = Res()
        fw.op("pool", lambda g: g.memset(mask512[:], 0.0), writes=[r_mask512])
        fw.op("pool", lambda g: g.memset(mask512[0:64, 64:128], NEGM), reads=[r_mask512], writes=[r_mask512])

        ropePt = RR([(sbt(f"ropeP{i}", [128, 64]), Res()) for i in range(2)])
        ropeS = sbt("ropeS", [64, 64]); r_ropeS = Res()
        fw.dma("sp", ropeS[:], rope_s[:, :], writes=[r_ropeS])
        gpre = sbt("gpre", [128, 2, 8]); r_gpre = Res()
        fw.dma("sp", gpre[:].rearrange("p a b -> p (a b)"), norm_pre[:, :], writes=[r_gpre])
        gpost = sbt("gpost", [128, D]); r_gpost = Res()
        qn_bc = sbt("qn_bc", [128, 256]); r_qn = Res()
        fw.dma("sp", qn_bc[:], bass.AP(q_norm.tensor, 0, [[0, 128], [1, 256]]), writes=[r_qn])
        kvn_bc = sbt("kvn_bc", [128, 128]); r_kvn = Res()
        fw.dma("sp", kvn_bc[:], bass.AP(kv_norm.tensor, 0, [[0, 128], [1, 128]]), writes=[r_kvn])
        fb_bc = sbt("fb_bc", [128, 8]); r_fb = Res()
        fw.dma("sp", fb_bc[:], bass.AP(fbias.tensor, 0, [[0, 128], [1, 8]]), writes=[r_fb])

        wbuf = sbt("wbuf", [128, 8, ODD_IN], BF16); r_w = Res()
        wout = sbt("wout", [128, 8, D], BF16); r_wout = Res()
        wuq = _T(wbuf[:, 0:2, 2976:2976 + 768]); r_wuq = r_w
        wukT = _T(wbuf[:, 2, 2976:2976 + 1024]); r_wuk = r_w
        wuv = _T(wbuf[:, 3, 2976:2976 + 512]); r_wuv = r_w
        WST = 1026
        wst_off = ar["top"]
        wst = RR([(sbt(f"wst{i}", [128, WST]), Res()) for i in range(2)])
        wst_end = ar["top"]
        ar["top"] = wst_off
        cast_i = [0]
        wq_i = [0]

        CAST_ENGS = [("pool", "dve", "act")]
        EVAC_ENGS = [("dve",)]

        def cast(out, in_, reads, writes, engs=None):
            engs = engs or CAST_ENGS[0]
            e = engs[cast_i[0] % len(engs)]
            cast_i[0] += 1
            if e == "act":
                fw.op("act", lambda a: a.copy(out=out, in_=in_), reads=reads, writes=writes)
            else:
                fw.op(e, lambda v: v.tensor_copy(out=out, in_=in_), reads=reads, writes=writes)

        def load_w(dst_fn, src, nrows, ncols, r_dst, q="sp"):
            for c in range(nrows // 128):
                for c0 in range(0, ncols, WST):
                    n = min(WST, ncols - c0)
                    st, r_st = wst.next()
                    wq_i[0] += 1
                    fw.dma(("sp", "act")[wq_i[0] % 2], st[:, 0:n], src[c * 128:(c + 1) * 128, c0:c0 + n], writes=[r_st])
                    cast(dst_fn(c)[:, c0:c0 + n], st[:, 0:n], [r_st], [r_dst], engs=("dve", "act"))

        pbf = RR([(sbt(f"pbf{i}", [128, 512], BF16), Res()) for i in range(3)])
        spb = RR([(sbt(f"spb{i}", [128, 512], BF16), Res()) for i in range(3)])
        ebuf = RR([(sbt(f"ebuf{i}", [128, 512]), Res()) for i in range(2)])
        ar["top"] = max(ar["top"], wst_end)
        KS = 24576
        ks_off = ar["top"]
        ks = sbt("ks", [128, KS], BF16)
        hT = sbt("hT", [128, 8, BLK], BF16); r_hT = Res()
        gT = sbt("gT", [128, 8, BLK], BF16); r_gT = [Res() for _ in range(8)]
        qA = sbt("qA", [128, 4, 2, 256], BF16); r_qA = Res()
        qBd = sbt("qB", [128, 4, 2, 256], BF16); r_qB = Res()
        qB = _T(qBd[:].rearrange("p a t q -> p (a t q)")[:, 0:4 * BLK].rearrange("p (a q) -> p a q", a=4))
        fw.op("pool", lambda g: g.memset(qA[:].rearrange("p a t q -> p (a t q)"), 0.0), writes=[r_qA])
        xin = RR([(sbt(f"xin{i}", [128, D]), Res()) for i in range(2)])
        hb = RR([(sbt(f"hb{i}", [128, D], BF16), Res()) for i in range(1)])
        junk = sbt("junk", [128, 512], BF16); r_junk = Res()
        sm = RR([(sbt(f"sm{i}", [128, 16]), Res()) for i in range(6)])
        st512 = RR([(sbt(f"st512_{i}", [128, 512]), Res()) for i in range(2)])
        tmpf = RR([(sbt(f"tmpf{i}", [128, 512]), Res()) for i in range(2)])
        fint = RR([(sbt(f"fint{i}", [128, 256]), Res()) for i in range(2)])
        Rbuf = sbt("Rbuf", [128, 512], BF16); r_R = Res()
        latb = sbt("latb", [128, 512], BF16); r_latb = Res()
        rden = sbt("rden", [128, 512]); r_rden = Res()
        ov0 = ar["top"]
        qlat = sbt("qlat", [128, 8, BLK], BF16); r_qlat = Res()
        qrT = sbt("qrT", [32, 8, BLK], BF16); r_qrT = Res()
        cqT = sbt("cqT", [128, 2, BLK], BF16); r_cqT = Res()
        cq_b = sbt("cq_b", [128, 256], BF16); r_cqb = Res()
        kvb = sbt("kvb", [128, 128 + 32], BF16); r_kvb = Res()
        qr_b = sbt("qr_b", [128, 256], BF16); r_qrb = Res()
        ropet = RR([(sbt(f"ropet{i}", [128, 256]), Res()) for i in range(2)])
        ov1 = ar["top"]
        ar["top"] = ov0
        fxb = sbt("fxb", [128, 4, 16, 8]); r_fxb = Res()
        biasq2 = sbt("biasq", [128, 2, 16, 8]); r_biasq = Res()
        accb = sbt("accb", [128, 8]); r_accb = Res()
        tblB = sbt("tblB", [128, 8, 256]); r_tblB = Res()
        cstB = sbt("cstB", [128, 8]); r_cstB = Res()
        lfb = RR([(sbt(f"lfb{i}", [128, 24]), Res()) for i in range(3)])
        lfc = sbt("lfc", [128, 32, 8]); r_lfc = Res()
        sfx = sbt("sfx", [128, 33, 8]); r_sfx = Res()
        ar["top"] = max(ar["top"], ov1)
        sv_top = ar["top"]
        ar["top"] = ks_off
        cst_f = RR([(sbt(f"cstf{i}", [128, 512]), Res()) for i in range(8)])
        vbf = RR([(sbt(f"vbf{i}", [128, 512], BF16), Res()) for i in range(5)])
        kbf = RR([(sbt(f"kbf{i}", [128, 512], BF16), Res()) for i in range(2)])
        cKT = RR([(sbt(f"cKT{i}", [128, 4, 128], BF16), Res()) for i in range(3)])
        sfxn = sbt("sfxn", [64, 4, 8]); r_sfxn = Res()
        qbd = [(sbt(f"qbd{i}", [128, 4, 128], BF16), Res()) for i in range(4)]
        skT1 = sbt("skT1", [128, 4, BLK], BF16); skT2 = sbt("skT2", [128, 4, BLK], BF16)
        sv1 = sbt("sv1", [64, 4, 512], BF16); sv2 = sbt("sv2", [64, 4, 512], BF16)
        sckvT = sbt("sckvT", [128, BLK], BF16); skrT = sbt("skrT", [32, BLK], BF16)
        sckv_tm = sbt("sckv_tm", [64, 4, 128], BF16)
        assert ar["top"] <= ks_off + KS
        ar["top"] = sv_top

        def ksv(off, shape):
            n = int(np.prod(shape))
            v = ks[:, off:off + n]
            if len(shape) == 2:
                return v.rearrange("p (a b) -> p a b", a=shape[0])
            return v
        kT1 = ksv(0, [4, SEQ]); v1 = ksv(8192, [16, 512])
        kT2 = ksv(8192, [4, SEQ]); v2 = ksv(16384, [16, 512])
        kTc = ksv(0, [4, 1024]); vc = ksv(4096, [8, 512])
        ckvT = ks[:, 16384:16384 + SEQ]
        krT = ks[:, 16384 + 2048:16384 + 4096]
        ckv_tm = ksv(16384 + 4096, [16, 128])
        r_k1 = [Res() for _ in range(20)]
        r_k2 = [Res() for _ in range(20)]
        r_k3 = [Res() for _ in range(20)]
        SOFF = 24576
        r_sk = [Res() for _ in range(8)]

        def load_layer_weights(l):
            fw.barrier()
            if l == 0:
                load_w(lambda c: wbuf[:, c, :], w_in0, D, EVEN_IN, r_w)
                load_w(lambda c: wout[:, c, :], w_out0, D, D, r_wout)
                load_w(lambda c: wuq[:, c, :], w_uq, 256, 768, r_wuq)
                load_w(lambda c: wukT[:, :], w_ukT, 128, 1024, r_wuk)
                load_w(lambda c: wuv[:, :], w_uv, 128, 512, r_wuv)
            else:
                load_w(lambda c: wbuf[:, c, :], w_in1, D, ODD_IN, r_w)
                load_w(lambda c: wout[:, c, :], w_out1, D, D, r_wout)
            fw.dma("sp", gpost[:], bass.AP(norm_post.tensor, l * D, [[0, 128], [1, D]]), writes=[r_gpost])
            fw.barrier()

        gbank = RR([6, 2, 3, 0, 1, 4, 5])

        def mm(o, l, r, st, sp_, reads, wres):
            K = l.shape[0]
            rg = None if K >= 128 else (l.base_partition(), K)
            fw.op("pe", lambda t: t.matmul(o, lhsT=l, rhs=r, start=st, stop=sp_), reads=reads, writes=[wres], rg=rg)

        def rstd_from_ss(ss_ap, n, nt, r_ss):
            s, r_s = sm.next()
            fw.op("dve", lambda v: v.tensor_scalar(out=s[0:nt, 0:1], in0=ss_ap, scalar1=1.0 / n, scalar2=EPS,
                                                   op0=ALU.mult, op1=ALU.add), reads=[r_ss], writes=[r_s])
            fw.op("act", lambda a_: a_.activation(out=s[0:nt, 3:4], in_=s[0:nt, 0:1], func=AF.Sqrt), reads=[r_s], writes=[r_s])
            fw.op("dve", lambda v: v.reciprocal(out=s[0:nt, 1:2], in_=s[0:nt, 3:4]), reads=[r_s], writes=[r_s])
            return s[0:nt, 1:2], r_s

        def sumsq(in_ap, nt, reads):
            s, r_s = sm.next()
            fw.op("act", lambda a: a.activation(out=junk[0:nt, 0:in_ap.shape[-1]], in_=in_ap, func=AF.Square,
                                                accum_out=s[0:nt, 2:3]), reads=reads, writes=[r_junk, r_s])
            return s[0:nt, 2:3], r_s

        def phase_norm(l, tiles, xsrc, r_xsrc):
            kept = []
            for (nt, row0, col0) in tiles:
                xt, r_xt = xin.next()
                kept.append((xt, r_xt))
                fw.dma("act", xt[0:nt, :], xsrc[row0:row0 + nt, :], reads=[r_xsrc[row0 // 64]] if r_xsrc else [], writes=[r_xt])
                s_, r_ss = sm.next()
                for hf in range(2):
                    fw.op("act", lambda a, hf=hf, s_=s_, xt=xt, nt=nt: a.activation(out=junk[0:nt, :], in_=xt[0:nt, hf * 512:(hf + 1) * 512], func=AF.Square,
                                                                                    accum_out=s_[0:nt, 4 + hf:5 + hf]), reads=[r_xt], writes=[r_junk, r_ss])
                fw.op("dve", lambda v, s_=s_, nt=nt: v.tensor_tensor(out=s_[0:nt, 2:3], in0=s_[0:nt, 4:5], in1=s_[0:nt, 5:6], op=ALU.add), reads=[r_ss], writes=[r_ss])
                rs, r_rs = rstd_from_ss(s_[0:nt, 2:3], D, nt, r_ss)
                h, r_h = hb.next()
                fw.op("dve", lambda v, h=h, xt=xt, rs=rs, nt=nt: v.tensor_scalar(out=h[0:nt, :], in0=xt[0:nt, :], scalar1=rs, scalar2=None,
                                                                                  op0=ALU.mult), reads=[r_xt, r_rs], writes=[r_h])
                for c in range(8):
                    fw.op("pe", lambda t, h=h, c=c, nt=nt: t.transpose(psb[:, c * 128:c * 128 + nt], h[0:nt, c * 128:(c + 1) * 128],
                                                                       ident[0:nt, 0:nt]), reads=[r_h, r_ident], writes=[r_psb])
                fw.op("dve", lambda v, nt=nt, col0=col0: v.tensor_tensor(
                    out=hT[:, :, col0:col0 + nt], in0=psb[:, :].rearrange("p (c t) -> p c t", c=8)[:, :, 0:nt],
                    in1=gpre[:, l, :].unsqueeze(2).broadcast_to([128, 8, nt]), op=ALU.mult),
                    reads=[r_psb, r_gpre], writes=[r_hT])
            return kept

        def tm_proj(nt, col0, wcols, ncols, b):
            for c in range(8):
                fw.op("pe", lambda t, c=c: t.matmul(ps[0:nt, b, 0:ncols], lhsT=hT[:, c, col0:col0 + nt],
                                                    rhs=wbuf[:, c, wcols:wcols + ncols], start=(c == 0), stop=(c == 7)),
                      reads=[r_hT, r_w], writes=[r_ps[b]])

        def fm_proj(wcols, b, half):
            for c in range(8):
                fw.op("pe", lambda t, c=c: t.matmul(ps[:, b, half * BLK:(half + 1) * BLK], lhsT=wbuf[:, c, wcols:wcols + 128],
                                                    rhs=hT[:, c, :], start=(c == 0), stop=(c == 7)),
                      reads=[r_hT, r_w], writes=[r_ps[b]])

        def fm_group(wcol_list, evac):
            for i in range(0, len(wcol_list), 2):
                b = gbank.next()
                n = min(2, len(wcol_list) - i)
                for j in range(n):
                    fm_proj(wcol_list[i + j], b, j)
                for j in range(n):
                    evac(i + j, ps[:, b, j * BLK:(j + 1) * BLK], r_ps[b])

        def rope_tm(src, nheads, nt, tab, r_tab, out_ap, reads, writes):
            t1, r_t1 = ropet.next()
            t2, r_t2 = ropet.next()
            n = nheads * 32
            s3 = src.rearrange("p (h r) -> p h r", h=nheads)
            a1 = t1[0:nt, 0:n].rearrange("p (h r) -> p h r", h=nheads)
            a2 = t2[0:nt, 0:n].rearrange("p (h r) -> p h r", h=nheads)
            cosb = tab[:, 0:32].unsqueeze(1).broadcast_to([nt, nheads, 32])
            sin_lo = tab[:, 32:48].unsqueeze(1).broadcast_to([nt, nheads, 16])
            sin_hi = tab[:, 48:64].unsqueeze(1).broadcast_to([nt, nheads, 16])
            fw.op("dve", lambda v: v.tensor_tensor(out=a1, in0=s3, in1=cosb, op=ALU.mult), reads=reads + [r_tab], writes=[r_t1])
            fw.op("dve", lambda v: v.tensor_tensor(out=a2[:, :, 0:16], in0=s3[:, :, 16:32], in1=sin_lo, op=ALU.mult),
                  reads=reads + [r_tab], writes=[r_t2])
            fw.op("dve", lambda v: v.tensor_tensor(out=a2[:, :, 16:32], in0=s3[:, :, 0:16], in1=sin_hi, op=ALU.mult),
                  reads=reads + [r_tab], writes=[r_t2])
            fw.op("dve", lambda v: v.tensor_tensor(out=out_ap, in0=t1[0:nt, 0:n], in1=t2[0:nt, 0:n], op=ALU.add),
                  reads=[r_t1, r_t2], writes=writes)

        def evac_bd(dst, r_dst, i, src, rb):
            fw.op("act", lambda a: a.activation(out=dst[0:64, i, :, 0:128], in_=src[0:64, :].rearrange("p (t q) -> p t q", t=2),
                                                func=AF.Copy, scale=0.125), reads=[rb], writes=[r_dst])
            fw.op("act", lambda a: a.activation(out=dst[64:128, i, :, 128:256], in_=src[64:128, :].rearrange("p (t q) -> p t q", t=2),
                                                func=AF.Copy, scale=0.125), reads=[rb], writes=[r_dst])

        def phase_proj0(tiles, is_s, o_ckv, o_kr, o_sbk, o_sbv):
            for ti, (nt, row0, col0) in enumerate(tiles):
                gt = (row0 // 128) if not is_s else ti
                b = gbank.next()
                tm_proj(nt, col0, 0, 416, b)
                ssq, r_ssq = sumsq(ps[0:nt, b, 0:256], nt, [r_ps[b]])
                rq, r_rq = rstd_from_ss(ssq, 256, nt, r_ssq)
                fw.op("dve", lambda v, b=b, rq=rq, nt=nt: v.scalar_tensor_tensor(out=cq_b[0:nt, :], in0=ps[0:nt, b, 0:256], scalar=rq,
                                                                                  in1=qn_bc[0:nt, :], op0=ALU.mult, op1=ALU.mult),
                      reads=[r_ps[b], r_rq, r_qn], writes=[r_cqb])
                ssk, r_ssk = sumsq(ps[0:nt, b, 256:384], nt, [r_ps[b]])
                rk, r_rk = rstd_from_ss(ssk, 128, nt, r_ssk)
                stg, r_stg = st512.next()
                fw.op("dve", lambda v, b=b, rk=rk, nt=nt, stg=stg: v.scalar_tensor_tensor(out=stg[0:nt, 0:128], in0=ps[0:nt, b, 256:384], scalar=rk,
                                                                                           in1=kvn_bc[0:nt, :], op0=ALU.mult, op1=ALU.mult),
                      reads=[r_ps[b], r_rk, r_kvn], writes=[r_stg])
                if is_s:
                    tab, r_tab = ropeS[0:nt, :], r_ropeS
                else:
                    tb_, r_tab = ropePt.next()
                    fw.dma("sp", tb_[:, :], rope_p[:, gt * 64:(gt + 1) * 64], writes=[r_tab])
                    tab = tb_[0:nt, :]
                rope_tm(ps[0:nt, b, 384:416], 1, nt, tab, r_tab, stg[0:nt, 128:160], [r_ps[b]], [r_stg])
                fw.dma("sp", o_ckv[row0:row0 + nt, :], stg[0:nt, 0:128], reads=[r_stg])
                fw.dma("sp", o_kr[row0:row0 + nt, :], stg[0:nt, 128:160], reads=[r_stg])
                fw.op("act", lambda a, stg=stg, nt=nt: a.copy(out=kvb[0:nt, :], in_=stg[0:nt, 0:160]), reads=[r_stg], writes=[r_kvb])
                if is_s:
                    fw.op("pool", lambda g, ti=ti, nt=nt: g.tensor_copy(out=sckv_tm[0:nt, ti, :], in_=kvb[0:nt, 0:128]),
                          reads=[r_kvb], writes=[r_sk[4]])
                else:
                    fw.op("pool", lambda g, gt=gt: g.tensor_copy(out=ckv_tm[:, gt, :], in_=kvb[:, 0:128]), reads=[r_kvb], writes=[r_k3[gt]])
                for j in range(2):
                    fw.op("pe", lambda t, j=j, nt=nt: t.transpose(psb[:, j * 128:j * 128 + nt], cq_b[0:nt, j * 128:(j + 1) * 128],
                                                                  ident[0:nt, 0:nt]), reads=[r_cqb, r_ident], writes=[r_psb])
                fw.op("pe", lambda t, nt=nt: t.transpose(psb[:, 256:256 + nt], kvb[0:nt, 0:128], ident[0:nt, 0:nt]),
                      reads=[r_kvb, r_ident], writes=[r_psb])
                fw.op("pe", lambda t, nt=nt: t.transpose(psb[0:32, 384:384 + nt], kvb[0:nt, 128:160], ident[0:nt, 0:nt]),
                      reads=[r_kvb, r_ident], writes=[r_psb])
                fw.op("dve", lambda v, nt=nt, col0=col0: v.tensor_copy(out=cqT[:, :, col0:col0 + nt],
                                                                        in_=psb[:, 0:256].rearrange("p (c t) -> p c t", c=2)[:, :, 0:nt]),
                      reads=[r_psb], writes=[r_cqT])
                if is_s:
                    dckv, dkr, rr = sckvT[:, col0:col0 + nt], skrT[:, col0:col0 + nt], r_sk[5]
                else:
                    dckv, dkr, rr = ckvT[:, row0:row0 + nt], krT[0:32, row0:row0 + nt], r_k3[gt]
                fw.op("dve", lambda v, nt=nt, dckv=dckv: v.tensor_copy(out=dckv, in_=psb[:, 256:256 + nt]), reads=[r_psb], writes=[rr])
                fw.op("dve", lambda v, nt=nt, dkr=dkr: v.tensor_copy(out=dkr, in_=psb[0:32, 384:384 + nt]), reads=[r_psb], writes=[rr])
                b = gbank.next()
                tm_proj(nt, col0, 1440, 512, b)
                stg, r_stg = st512.next()
                fw.op("act", lambda a, b=b, stg=stg, nt=nt: a.copy(out=stg[0:nt, :], in_=ps[0:nt, b, :]), reads=[r_ps[b]], writes=[r_stg])
                fw.dma("sp", o_sbk[row0:row0 + nt, :], stg[0:nt, :], reads=[r_stg])
                b = gbank.next()
                tm_proj(nt, col0, 1952, 512, b)
                stg, r_stg = st512.next()
                fw.op("act", lambda a, b=b, stg=stg, nt=nt: a.copy(out=stg[0:nt, :], in_=ps[0:nt, b, :]), reads=[r_ps[b]], writes=[r_stg])
                fw.dma("sp", o_sbv[row0:row0 + nt, :], stg[0:nt, :], reads=[r_stg])
                if is_s:
                    fw.op("dve", lambda v, b=b, ti=ti, nt=nt: v.tensor_copy(out=sv1[0:nt, ti, :], in_=ps[0:nt, b, :]), reads=[r_ps[b]], writes=[r_sk[1]])
                else:
                    if os.environ.get("KSKIPV1") is None:
                        fw.op("dve", lambda v, b=b, gt=gt: v.tensor_copy(out=v1[:, gt, :], in_=ps[:, b, :]), reads=[r_ps[b]], writes=[r_k1[gt]])
                b = gbank.next()
                for cc in range(2):
                    fw.op("pe", lambda t, cc=cc, b=b, nt=nt, col0=col0: t.matmul(ps[0:nt, b, 0:256], lhsT=cqT[:, cc, col0:col0 + nt],
                                                                                 rhs=wuq[:, cc, 512:768], start=(cc == 0), stop=(cc == 1)),
                          reads=[r_cqT, r_wuq], writes=[r_ps[b]])
                rope_tm(ps[0:nt, b, 0:256], 8, nt, tab, r_tab, qr_b[0:nt, :], [r_ps[b]], [r_qrb])
                for h in range(8):
                    fw.op("pe", lambda t, h=h, nt=nt: t.transpose(psb[0:32, h * 128:h * 128 + nt], qr_b[0:nt, h * 32:(h + 1) * 32],
                                                                  ident[0:nt, 0:nt]), reads=[r_qrb, r_ident], writes=[r_psb])
                fw.op("dve", lambda v, nt=nt, col0=col0: v.tensor_copy(out=qrT[:, :, col0:col0 + nt],
                                                                        in_=psb[0:32, :].rearrange("p (h t) -> p h t", h=8)[:, :, 0:nt]),
                      reads=[r_psb], writes=[r_qrT])
            fm_group([928 + 128 * i for i in range(4)],
                     lambda i, src, rb: evac_bd(qA, r_qA, i, src, rb))
            if is_s:
                fm_group([1440 + 128 * i for i in range(4)],
                         lambda i, src, rb: fw.op("dve", lambda v: v.tensor_copy(out=skT1[:, i, :], in_=src), reads=[rb], writes=[r_sk[0]]))
            else:
                t0 = tiles[0][1]
                g0 = t0 // 128

                def ev(i, src, rb):
                    fw.op("dve", lambda v: v.tensor_copy(out=kT1[:, i, t0:t0 + BLK], in_=src), reads=[rb], writes=[r_k1[g0], r_k1[g0 + 1]])
                fm_group([1440 + 128 * i for i in range(4)], ev)
            fm_group([416 + 128 * i for i in range(4)] + [2464 + 128 * i for i in range(4)],
                     lambda i, src, rb: fw.op("act", lambda a: a.activation(out=gT[:, i, :], in_=src, func=AF.Silu),
                                              reads=[rb], writes=[r_gT[i]]))
            for i in range(0, 4, 2):
                b = gbank.next()
                for j in range(2):
                    for cc in range(2):
                        fw.op("pe", lambda t, cc=cc, b=b, i=i, j=j: t.matmul(ps[:, b, j * BLK:(j + 1) * BLK], lhsT=wuq[:, cc, (i + j) * 128:(i + j + 1) * 128],
                                                                             rhs=cqT[:, cc, :], start=(cc == 0), stop=(cc == 1)),
                              reads=[r_cqT, r_wuq], writes=[r_ps[b]])
                fw.op("dve", lambda v, b=b, i=i: v.tensor_copy(out=qB[:, i:i + 2, :], in_=ps[:, b, :].rearrange("p (a t) -> p a t", a=2)),
                      reads=[r_ps[b]], writes=[r_qB])
            for h0 in range(0, 8, 2):
                b = gbank.next()
                for j in range(2):
                    h = h0 + j
                    pb = 64 * (h % 2)
                    mm(ps[:, b, j * BLK:(j + 1) * BLK], wukT[pb:pb + 64, h * 128:(h + 1) * 128], qB[pb:pb + 64, h // 2, :], True, True,
                       [r_qB, r_wuk], r_ps[b])
                for j in range(2):
                    fw.op("act", lambda a, b=b, h0=h0, j=j: a.copy(out=qlat[:, h0 + j, :], in_=ps[:, b, j * BLK:(j + 1) * BLK]),
                          reads=[r_ps[b]], writes=[r_qlat])

        def fin_softmax(slots, ob, db, has_den=True):
            if has_den:
                fw.op("dve", lambda v: v.reciprocal(out=rden[:, :], in_=ps[:, db, :]), reads=[r_ps[db]], writes=[r_rden])
            if len(slots) == 4 and slots[0][2] == 128:
                for par in (0, 1):
                    h, c0, n, gc, g0 = slots[par]
                    pb = 64 * par
                    ov = ps[pb:pb + 64, ob, :].rearrange("p (s q) -> p s q", s=4)[:, par:4:2, :]
                    gv = gT[pb:pb + 64, gc:gc + 2, g0:g0 + 128]
                    rgs = [r_gT[gc], r_gT[gc + 1]]
                    if has_den:
                        t, r_t = fint.next()
                        tv = t[pb:pb + 64, 0:256].rearrange("p (s q) -> p s q", s=2)
                        rv = rden[pb:pb + 64, :].rearrange("p (s q) -> p s q", s=4)[:, par:4:2, :]
                        fw.op("dve", lambda v, tv=tv, ov=ov, rv=rv: v.tensor_tensor(out=tv, in0=ov, in1=rv, op=ALU.mult),
                              reads=[r_ps[ob], r_rden], writes=[r_t])
                        fw.op("dve", lambda v, tv=tv, gv=gv: v.tensor_tensor(out=gv, in0=tv, in1=gv, op=ALU.mult), reads=[r_t] + rgs, writes=rgs)
                    else:
                        fw.op("dve", lambda v, ov=ov, gv=gv: v.tensor_tensor(out=gv, in0=ov, in1=gv, op=ALU.mult), reads=[r_ps[ob]] + rgs, writes=rgs)
                return
            for (h, c0, n, gc, g0) in slots:
                pb = 64 * (h % 2)
                if has_den:
                    t, r_t = fint.next()
                    fw.op("dve", lambda v, t=t, pb=pb, c0=c0, n=n: v.tensor_tensor(out=t[pb:pb + 64, 0:n], in0=ps[pb:pb + 64, ob, c0:c0 + n],
                                                                                   in1=rden[pb:pb + 64, c0:c0 + n], op=ALU.mult),
                          reads=[r_ps[ob], r_rden], writes=[r_t])
                    fw.op("dve", lambda v, t=t, pb=pb, n=n, gc=gc, g0=g0: v.tensor_tensor(out=gT[pb:pb + 64, gc, g0:g0 + n], in0=t[pb:pb + 64, 0:n],
                                                                                          in1=gT[pb:pb + 64, gc, g0:g0 + n], op=ALU.mult),
                          reads=[r_t, r_gT[gc]], writes=[r_gT[gc]])
                else:
                    fw.op("dve", lambda v, pb=pb, c0=c0, n=n, gc=gc, g0=g0: v.tensor_tensor(out=gT[pb:pb + 64, gc, g0:g0 + n], in0=ps[pb:pb + 64, ob, c0:c0 + n],
                                                                                            in1=gT[pb:pb + 64, gc, g0:g0 + n], op=ALU.mult),
                          reads=[r_ps[ob], r_gT[gc]], writes=[r_gT[gc]])

        odpair = RR([(4, 5), (2, 3)])
        sbank = RR([0, 1])
        abank = RR([2, 3])
        obank = RR([4, 5])

        class U:
            hbias = None
            dma = None
            prep = None
            mask = None
            pre = None
            scale = 1.0

        def sprep(u):
            if u.prep is not None:
                u.prep(u)

        def sdma(u):
            if u.dma is not None:
                u.dma(u)

        def sb_chain(units):
            def s0(u):
                u.sb = sbank.next()
                nk = u.nk
                nz = len(u.zmm)
                for i, (l, r, c0, n) in enumerate(u.zmm):
                    mm(ps[0:u.nk, u.sb, c0:c0 + n], l, r, (i == 0), (u.mask is None and i == nz - 1), u.reads, r_ps[u.sb])
                if u.mask is not None:
                    mm(ps[0:u.nk, u.sb, :], ident[0:u.nk, 0:u.nk], u.mask, False, True, [r_ident, u.rmask], r_ps[u.sb])
                e, r_e = ebuf.next()
                fw.op("act", lambda a, u=u, e=e: a.activation(out=e[0:u.nk, :], in_=ps[0:u.nk, u.sb, :], func=AF.Exp), reads=[r_ps[u.sb]], writes=[r_e])
                u.sp, u.r_sp = spb.next()
                fw.op("act", lambda a, u=u, e=e: a.activation(out=u.sp[0:u.nk, :], in_=e[0:u.nk, :], func=AF.Ln, bias=1.0), reads=[r_e], writes=[u.r_sp])

            def s1(u):
                u.ab = abank.next()
                for i, (l, r, c0, n) in enumerate(u.zmm):
                    mm(ps[0:u.nk, u.ab, c0:c0 + n], l, r, (i == 0), False, u.reads, r_ps[u.ab])
                if u.mask is not None:
                    mm(ps[0:u.nk, u.ab, :], ident[0:u.nk, 0:u.nk], u.mask, False, False, [r_ident, u.rmask], r_ps[u.ab])
                mm(ps[0:u.nk, u.ab, :], negTri[0:u.nk, 0:u.nk], u.sp[0:u.nk, :], False, u.first, [r_negTri, u.r_sp], r_ps[u.ab])
                if not u.first:
                    mm(ps[0:u.nk, u.ab, :], negOnes[:, 0:u.nk], Rbuf[:, :], False, True, [r_negOnes, r_R], r_ps[u.ab])
                if not u.last:
                    if u.first:
                        if u.nk < 128:
                            fw.op("pool", lambda g: g.memset(Rbuf[:, :], 0.0), writes=[r_R])
                        fw.op("pool", lambda g, u=u: g.tensor_copy(out=Rbuf[0:u.nk, :], in_=u.sp[0:u.nk, :]), reads=[u.r_sp], writes=[r_R])
                    else:
                        fw.op("pool", lambda g, u=u: g.tensor_tensor(out=Rbuf[0:u.nk, :], in0=Rbuf[0:u.nk, :], in1=u.sp[0:u.nk, :], op=ALU.add),
                              reads=[u.r_sp, r_R], writes=[r_R])
                u.w, u.r_w = pbf.next()
                fw.op("act", lambda a, u=u: a.activation(out=u.w[0:u.nk, :], in_=ps[0:u.nk, u.ab, :], func=AF.Exp), reads=[r_ps[u.ab]], writes=[u.r_w])

            def s2(u):
                if u.first:
                    u.chain["ob"] = obank.next()
                ob = u.chain["ob"]
                for vi, (lv, c0, n) in enumerate(u.vmm):
                    mm(ps[:, ob, c0:c0 + n], lv, u.w[0:u.nk, c0:c0 + n], (u.first and vi == 0), (u.last and vi == len(u.vmm) - 1),
                       u.vreads + [u.r_w], r_ps[ob])
                if u.last:
                    u.fin(ob)
            pipeline(units, [sdma, sprep, s0, s1, s2], [0, 3, 4, 5, 6])

        def sm_chain(units):
            def s0(u):
                u.sb = sbank.next()
                for (l, r, c0, n, st, sp_) in u.zmm:
                    o = ps[0:u.nk, u.sb, c0:c0 + n]
                    if len(r.shape) == 3:
                        o = o.rearrange("p (h q) -> p h q", h=r.shape[1])
                    mm(o, l, r, st, sp_, u.reads, r_ps[u.sb])
                if u.hbias is not None:
                    u.src, u.r_src = None, None
                elif u.pre is not None:
                    u.src, u.r_src = u.pre(u)
                else:
                    u.src, u.r_src = ps[0:u.nk, u.sb, :], r_ps[u.sb]

            def s1(u):
                u.p, u.r_p = pbf.next()
                if u.hbias is not None:
                    w_ = 512 // len(u.hbias)
                    for j, (bap, rb) in enumerate(u.hbias):
                        fw.op("act", lambda a, u=u, j=j, bap=bap, w_=w_: a.activation(out=u.p[0:u.nk, j * w_:(j + 1) * w_], in_=ps[0:u.nk, u.sb, j * w_:(j + 1) * w_],
                                                                                  func=AF.Exp, bias=bap, scale=u.scale),
                              reads=[r_ps[u.sb], rb], writes=[u.r_p])
                else:
                    fw.op("act", lambda a, u=u: a.activation(out=u.p[0:u.nk, :], in_=u.src, func=AF.Exp, scale=u.scale), reads=[u.r_src], writes=[u.r_p])

            def s2(u):
                if u.first:
                    u.chain["ob"], u.chain["db"] = odpair.next()
                ob, db = u.chain["ob"], u.chain["db"]
                for vi, (lv, c0, n) in enumerate(u.vmm):
                    mm(ps[:, ob, c0:c0 + n], lv, u.p[0:u.nk, c0:c0 + n], (u.first and vi == 0), (u.last and vi == len(u.vmm) - 1),
                       u.vreads + [u.r_p], r_ps[ob])
                mm(ps[:, db, :], onesB[0:u.nk, :], u.p[0:u.nk, :], u.first, u.last, [r_onesB, u.r_p], r_ps[db])
                if u.last:
                    u.fin(ob, db)
            pipeline(units, [sdma, sprep, s0, s1, s2], [0, 3, 4, 5, 6])

        def pre_add(u, in1, r_in1, scale=1.0, extra=None):
            t, r_t = tmpf.next()
            nk = u.nk
            H = in1.shape[1]
            n = 512 // H
            fw.op("dve", lambda v: v.scalar_tensor_tensor(out=t[0:nk, :].rearrange("p (h q) -> p h q", h=H),
                                                          in0=ps[0:nk, u.sb, :].rearrange("p (h q) -> p h q", h=H), scalar=scale,
                                                          in1=in1, op0=ALU.mult, op1=ALU.add), reads=[r_ps[u.sb]] + r_in1, writes=[r_t])
            if extra is not None:
                ex, r_ex = extra
                fw.op("dve", lambda v: v.tensor_tensor(out=t[0:nk, :].rearrange("p (h q) -> p h q", h=H),
                                                       in0=t[0:nk, :].rearrange("p (h q) -> p h q", h=H), in1=ex, op=ALU.add),
                      reads=[r_t] + r_ex, writes=[r_t])
            return t[0:nk, :], r_t

        def mla_fin_factory(slots_fn):
            def fin(ob, db):
                fw.op("act", lambda a: a.copy(out=latb[:, :], in_=ps[:, ob, :]), reads=[r_ps[ob]], writes=[r_latb])
                slots = slots_fn()
                gb = 6
                for (h, c0, n, gc, g0) in slots:
                    fw.op("pe", lambda t, h=h, c0=c0, n=n: t.matmul(ps[:, gb, c0:c0 + n], lhsT=wuv[:, (h // 2) * 128:(h // 2 + 1) * 128],
                                                                    rhs=latb[:, c0:c0 + n], start=True, stop=True),
                          reads=[r_latb, r_wuv], writes=[r_ps[gb]])
                fin_softmax(slots, gb, db)
            return fin

        def attn0_prompt(bi):
            all_sb = []
            all_mla = []
            for qt in (2 * bi, 2 * bi + 1):
                qc = (qt % 2) * 128
                for hg in range(2):
                    chain = {}
                    units = []
                    for kt in range(qt, -1, -1):
                        u = U()
                        u.nk = 128; u.chain = chain
                        u.first = (kt == qt); u.last = (kt == 0)
                        u.zmm = []
                        u.vmm = []
                        for pp in range(2):
                            u.zmm.append((kT1[:, 2 * hg + pp, kt * 128:(kt + 1) * 128], qA[:, 2 * hg + pp, qt % 2, :], pp * 256, 256))
                        for pp in range(2):
                            u.vmm.append((v1[:, kt, (2 * hg + pp) * 128:(2 * hg + pp + 1) * 128], pp * 256, 256))
                        u.mask = maskSB[:, :] if kt == qt else None
                        u.rmask = r_maskSB
                        u.reads = [r_k1[kt], r_qA]
                        u.vreads = [r_k1[kt]]
                        slots = [(4 * hg + s, s * 128, 128, 4 + (4 * hg + s) // 2, qc) for s in range(4)]
                        u.fin = (lambda ob, slots=slots: fin_softmax(slots, ob, None, has_den=False))
                        units.append(u)
                    all_sb += units
                    chain = {}
                    units = []
                    for kt in range(qt, -1, -1):
                        u = U()
                        u.nk = 128; u.chain = chain
                        u.first = (kt == qt); u.last = (kt == 0)
                        u.zmm = [(ckvT[:, kt * 128:(kt + 1) * 128], qlat[:, 4 * hg:4 * hg + 4, qc:qc + 128], 0, 512, True, False),
                                 (krT[0:32, kt * 128:(kt + 1) * 128], qrT[:, 4 * hg:4 * hg + 4, qc:qc + 128], 0, 512, False, True)]
                        u.reads = [r_k3[kt], r_qlat, r_qrT]
                        u.scale = A_SCALE
                        if kt == qt:
                            u.pre = lambda u: pre_add(u, maskCH[:, :].unsqueeze(1).broadcast_to([128, 4, 128]), [r_maskCH], scale=A_SCALE)
                            u.scale = 1.0
                        else:
                            u.pre = None
                        u.vmm = [(ckv_tm[:, kt, :], 0, 512)]
                        u.vreads = [r_k3[kt]]
                        slots = [(4 * hg + s, s * 128, 128, (4 * hg + s) // 2, qc) for s in range(4)]
                        u.fin = mla_fin_factory(lambda slots=slots: slots)
                        units.append(u)
                    all_mla += units
            sb_chain(all_sb)
            sm_chain(all_mla)


        HORD = (0, 2, 4, 6, 1, 3, 5, 7)

        def dma_kv_tile(u, kdram, vdram, s, t):
            u.kf, u.r_kf = cst_f.next()
            fw.dma("sp", u.kf[:, :], kdram[s, t * 128:(t + 1) * 128, :], writes=[u.r_kf])
            u.vf, u.r_vf = cst_f.next()
            fw.dma("sp", u.vf[:, :], vdram[s, t * 128:(t + 1) * 128, :], writes=[u.r_vf])

        def load_kv_tile(u):
            kf, r_kf, vf, r_vf = u.kf, u.r_kf, u.vf, u.r_vf
            kb, r_kb = kbf.next()
            cast(kb[:, :], kf[:, :], [r_kf], [r_kb])
            for c in range(4):
                fw.op("pe", lambda t_, c=c, kb=kb: t_.transpose(psb[:, c * 128:(c + 1) * 128], kb[:, c * 128:(c + 1) * 128], ident[:, :]),
                      reads=[r_kb, r_ident], writes=[r_psb])
            kt_, r_kt = cKT.next()
            ee = EVAC_ENGS[0][cast_i[0] % len(EVAC_ENGS[0])]
            if ee == "act":
                fw.op("act", lambda a, kt_=kt_: a.copy(out=kt_[:, :, :], in_=psb[:, 0:512].rearrange("p (c t) -> p c t", c=4)),
                      reads=[r_psb], writes=[r_kt])
            else:
                fw.op("dve", lambda v, kt_=kt_: v.tensor_copy(out=kt_[:, :, :], in_=psb[:, 0:512].rearrange("p (c t) -> p c t", c=4)),
                      reads=[r_psb], writes=[r_kt])
            vb, r_vb = vbf.next()
            cast(vb[:, :], vf[:, :], [r_vf], [r_vb])
            return kt_, r_kt, vb, r_vb

        def kv_units(s, ncache, kdram, vdram, qsrc, r_q, knew, vnew, r_new, slots):
            qb, r_qb = qbd[s]
            fw.op("pool", lambda g: g.memset(qb[:, :, :], 0.0), writes=[r_qb])
            so = (s % 2) * 64
            fw.op("pool", lambda g: g.tensor_copy(out=qb[0:64, :, 0:64], in_=qsrc[0:64, :, s // 2, so:so + 64]), reads=[r_q], writes=[r_qb])
            fw.op("pool", lambda g: g.tensor_copy(out=qb[64:128, :, 64:128], in_=qsrc[64:128, :, s // 2, 128 + so:128 + so + 64]), reads=[r_q], writes=[r_qb])
            chain = {}
            units = []
            u = U(); u.nk = 64; u.chain = chain; u.first = True; u.last = False; u.tile = ncache
            u.zmm = []; u.vmm = []
            for p in range(4):
                u.zmm.append((knew[:, p, 64 * s:64 * s + 64], qb[:, p, :], p * 128, 128))
                u.vmm.append((vnew[0:64, s, p * 128:(p + 1) * 128], p * 128, 128))
            u.reads = [r_new, r_qb]; u.vreads = [r_new]
            units.append(u)
            for t in range(ncache - 1, -1, -1):
                u = U(); u.nk = 128; u.chain = chain; u.first = False; u.last = (t == 0); u.tile = t
                u.dma = (lambda u, t=t: dma_kv_tile(u, kdram, vdram, s, t))

                def prep(u, t=t):
                    kt_, r_kt, vb, r_vb = load_kv_tile(u)
                    u.zmm = []; u.vmm = []
                    for p in range(4):
                        u.zmm.append((kt_[:, p, :], qb[:, p, :], p * 128, 128))
                        u.vmm.append((vb[:, p * 128:(p + 1) * 128], p * 128, 128))
                    u.reads = [r_kt, r_qb]; u.vreads = [r_vb]
                u.prep = prep
                units.append(u)
            return units

        def zmm4(u):
            z = []
            for i, (l, r, c0, n) in enumerate(u.zmm):
                z.append((l, r, c0, n, i == 0, i == len(u.zmm) - 1))
            u.zmm = z

        def attn0_sample():
            all_sb = []
            all_mla = []
            for s in range(NSTR):
                slots = [(h, h * 64, 64, 4 + h // 2, 64 * s) for h in range(8)]
                units = kv_units(s, PAST // 128, c_sbk, c_sbv, qA, r_qA, skT1, sv1, r_sk[0], slots)
                units[0].mask = maskSB64[:, :]; units[0].rmask = r_maskSB64
                units[0].reads = [r_sk[0], qbd[s][1]]; units[0].vreads = [r_sk[1]]
                for u in units:
                    u.rmask = r_maskSB64
                    u.fin = (lambda ob, slots=slots: fin_softmax(slots, ob, None, has_den=False))
                all_sb += units
            sb_chain(all_sb)
            for s in range(NSTR):
                chain = {}
                units = []
                slots = [(h, h * 64, 64, h // 2, 64 * s) for h in range(8)]
                u = U(); u.nk = 64; u.chain = chain; u.first = True; u.last = False
                u.zmm = [(sckvT[:, 64 * s:64 * s + 64], qlat[:, :, 64 * s:64 * s + 64], 0, 512, True, False),
                         (skrT[0:32, 64 * s:64 * s + 64], qrT[:, :, 64 * s:64 * s + 64], 0, 512, False, True)]
                u.reads = [r_sk[5], r_qlat, r_qrT]; u.scale = A_SCALE
                u.vmm = [(sckv_tm[0:64, s, :], 0, 512)]; u.vreads = [r_sk[4]]
                units.append(u)
                for t in range(PAST // 128 - 1, -1, -1):
                    u = U(); u.nk = 128; u.chain = chain; u.first = False; u.last = (t == 0); u.scale = A_SCALE

                    def dma_(u, t=t, s=s):
                        u.cf, u.r_cf = cst_f.next()
                        fw.dma("sp", u.cf[:, 0:128], c_ckv[s, t * 128:(t + 1) * 128, :], writes=[u.r_cf])
                        fw.dma("sp", u.cf[:, 128:160], c_kr[s, t * 128:(t + 1) * 128, :], writes=[u.r_cf])
                    u.dma = dma_

                    def prep(u, t=t, s=s):
                        cf, r_cf = u.cf, u.r_cf
                        vb, r_vb = vbf.next()
                        cast(vb[:, 0:160], cf[:, 0:160], [r_cf], [r_vb])
                        fw.op("pe", lambda t_, vb=vb: t_.transpose(psb[:, 0:128], vb[:, 0:128], ident[:, :]), reads=[r_vb, r_ident], writes=[r_psb])
                        fw.op("pe", lambda t_, vb=vb: t_.transpose(psb[0:32, 128:256], vb[:, 128:160], ident[:, :]), reads=[r_vb, r_ident], writes=[r_psb])
                        kt_, r_kt = cKT.next()
                        fw.op("dve", lambda v, kt_=kt_: v.tensor_copy(out=kt_[:, 0, :], in_=psb[:, 0:128]), reads=[r_psb], writes=[r_kt])
                        fw.op("dve", lambda v, kt_=kt_: v.tensor_copy(out=kt_[0:32, 1, :], in_=psb[0:32, 128:256]), reads=[r_psb], writes=[r_kt])
                        u.zmm = [(kt_[:, 0, :], qlat[:, :, 64 * s:64 * s + 64], 0, 512, True, False),
                                 (kt_[0:32, 1, :], qrT[:, :, 64 * s:64 * s + 64], 0, 512, False, True)]
                        u.reads = [r_kt, r_qlat, r_qrT]
                        u.vmm = [(vb[:, 0:128], 0, 512)]; u.vreads = [r_vb]
                    u.prep = prep
                    units.append(u)
                for u in units:
                    u.fin = mla_fin_factory(lambda slots=slots: slots)
                all_mla += units
            sm_chain(all_mla)

        r_kc = [Res() for _ in range(8)]

        def setup_l1_tables():
            tb2 = tblB[:].rearrange("p a b -> p (a b)")
            fw.dma("sp", tblB[:], bass.AP(relb.tensor, 1, [[1, 128], [513, 8], [1, 256]]), writes=[r_tblB])
            for j in range(4):
                fw.op("pe", lambda t_, j=j: t_.matmul(ps[:, j, :], lhsT=flipJ[:, :], rhs=tb2[:, j * 512:(j + 1) * 512], start=True, stop=True),
                      reads=[r_flip, r_tblB], writes=[r_ps[j]])
            for j in range(4):
                fw.op("dve", lambda v, j=j: v.tensor_copy(out=tb2[:, j * 512:(j + 1) * 512], in_=ps[:, j, :]), reads=[r_ps[j]], writes=[r_tblB])
            fw.op("dve", lambda v: v.tensor_tensor(out=tblB[:, :, 0:128], in0=tblB[:, :, 0:128],
                                                   in1=maskCH[:, :].unsqueeze(1).broadcast_to([128, 8, 128]), op=ALU.add),
                  reads=[r_tblB, r_maskCH], writes=[r_tblB])
            fw.dma("sp", cstB[:], bass.AP(relc.tensor, 0, [[0, 128], [1, 8]]), writes=[r_cstB])

        def phase_proj1(tiles, is_s, o_fk, o_fv, o_lf):
            for ti, (nt, row0, col0) in enumerate(tiles):
                gt = (row0 // 128) if not is_s else ti
                b = gbank.next()
                tm_proj(nt, col0, 512, 512, b)
                stg, r_stg = st512.next()
                fw.op("act", lambda a, b=b, stg=stg, nt=nt: a.copy(out=stg[0:nt, :], in_=ps[0:nt, b, :]), reads=[r_ps[b]], writes=[r_stg])
                if is_s:
                    fw.dma("sp", o_bk_s[ti, 448:512, :], stg[0:nt, :], reads=[r_stg])
                elif gt >= 12:
                    fw.dma("sp", o_bk_p[(gt - 12) * 128:(gt - 11) * 128, :], stg[0:nt, :], reads=[r_stg])
                b = gbank.next()
                tm_proj(nt, col0, 1024, 512, b)
                stg, r_stg = st512.next()
                fw.op("act", lambda a, b=b, stg=stg, nt=nt: a.copy(out=stg[0:nt, :], in_=ps[0:nt, b, :]), reads=[r_ps[b]], writes=[r_stg])
                if is_s:
                    fw.dma("sp", o_bv_s[ti, 448:512, :], stg[0:nt, :], reads=[r_stg])
                    fw.op("dve", lambda v, stg=stg, ti=ti, nt=nt: v.tensor_copy(out=sv1[0:nt, ti, :], in_=stg[0:nt, :]), reads=[r_stg], writes=[r_sk[1]])
                else:
                    if gt >= 12:
                        fw.dma("sp", o_bv_p[(gt - 12) * 128:(gt - 11) * 128, :], stg[0:nt, :], reads=[r_stg])
                    fw.op("dve", lambda v, stg=stg, gt=gt: v.tensor_copy(out=vc[:, gt % 8, :], in_=stg[:, :]), reads=[r_stg], writes=[r_kc[gt % 8]])
                b = gbank.next()
                tm_proj(nt, col0, 2560, 512, b)
                stg, r_stg = st512.next()
                fw.op("act", lambda a, b=b, stg=stg, nt=nt: a.copy(out=stg[0:nt, :], in_=ps[0:nt, b, :]), reads=[r_ps[b]], writes=[r_stg])
                fw.dma("sp", o_fk[row0:row0 + nt, :], stg[0:nt, :], reads=[r_stg])
                b = gbank.next()
                tm_proj(nt, col0, 3072, 512, b)
                stg, r_stg = st512.next()
                fw.op("act", lambda a, b=b, stg=stg, nt=nt: a.copy(out=stg[0:nt, :], in_=ps[0:nt, b, :]), reads=[r_ps[b]], writes=[r_stg])
                fw.dma("sp", o_fv[row0:row0 + nt, :], stg[0:nt, :], reads=[r_stg])
                if is_s:
                    fw.op("dve", lambda v, stg=stg, ti=ti, nt=nt: v.tensor_copy(out=sv2[0:nt, ti, :], in_=stg[0:nt, :]), reads=[r_stg], writes=[r_sk[3]])
                else:
                    fw.op("dve", lambda v, stg=stg, gt=gt: v.tensor_copy(out=v2[:, gt, :], in_=stg[:, :]), reads=[r_stg], writes=[r_k2[gt]])
                b = gbank.next()
                tm_proj(nt, col0, 3584, 8, b)
                lf, r_lf = lfb.next()
                fw.op("dve", lambda v, b=b, lf=lf, nt=nt: v.tensor_tensor(out=lf[0:nt, 0:8], in0=ps[0:nt, b, 0:8], in1=fb_bc[0:nt, :], op=ALU.add),
                      reads=[r_ps[b], r_fb], writes=[r_lf])
                fw.op("act", lambda a, lf=lf, nt=nt: a.activation(out=lf[0:nt, 8:16], in_=lf[0:nt, 0:8], func=AF.Exp, scale=-1.0), reads=[r_lf], writes=[r_lf])
                fw.op("act", lambda a, lf=lf, nt=nt: a.activation(out=lf[0:nt, 0:8], in_=lf[0:nt, 8:16], func=AF.Ln, bias=1.0), reads=[r_lf], writes=[r_lf])
                fw.op("dve", lambda v, lf=lf, nt=nt: v.tensor_scalar(out=lf[0:nt, 16:24], in0=lf[0:nt, 0:8], scalar1=-1.0, scalar2=None, op0=ALU.mult),
                      reads=[r_lf], writes=[r_lf])
                fw.dma("sp", o_lf[row0:row0 + nt, :], lf[0:nt, 16:24], reads=[r_lf])
                b2 = gbank.next()
                if not is_s:
                    fw.op("pe", lambda t_, b2=b2, lf=lf: t_.matmul(ps[:, b2, 0:8], lhsT=triF[:, :], rhs=lf[:, 16:24], start=True, stop=False),
                          reads=[r_tri, r_lf], writes=[r_ps[b2]])
                    fw.op("pe", lambda t_, b2=b2, lf=lf: t_.matmul(ps[:, b2, 8:16], lhsT=onesF[:, :], rhs=lf[:, 16:24], start=False, stop=True),
                          reads=[r_onesF, r_lf], writes=[r_ps[b2]])
                    fw.op("dve", lambda v, b2=b2, gt=gt: v.tensor_scalar(out=fxb[:, 0, gt, :], in0=ps[:, b2, 0:8], scalar1=-1.0, scalar2=None, op0=ALU.mult),
                          reads=[r_ps[b2]], writes=[r_fxb])
                    fw.op("dve", lambda v, b2=b2, gt=gt: v.tensor_copy(out=fxb[:, 2, gt, :], in_=ps[:, b2, 8:16]), reads=[r_ps[b2]], writes=[r_fxb])
                    fw.op("dve", lambda v, gt=gt: v.tensor_tensor(out=fxb[:, 1, gt, :], in0=fxb[:, 2, gt, :], in1=fxb[:, 0, gt, :], op=ALU.add),
                          reads=[r_fxb], writes=[r_fxb])
                else:
                    fw.op("pe", lambda t_, b2=b2, lf=lf: t_.matmul(ps[0:64, b2, 0:8], lhsT=triF[0:64, 0:64], rhs=lf[0:64, 16:24], start=True, stop=True),
                          reads=[r_tri, r_lf], writes=[r_ps[b2]])
                    fw.op("dve", lambda v, b2=b2, ti=ti: v.tensor_scalar(out=sfxn[0:64, ti, :], in0=ps[0:64, b2, 0:8], scalar1=-1.0, scalar2=None, op0=ALU.mult),
                          reads=[r_ps[b2]], writes=[r_sfxn])
            fm_group([0 + 128 * i for i in range(4)],
                     lambda i, src, rb: evac_bd(qA, r_qA, i, src, rb))
            fm_group([2048 + 128 * i for i in range(4)],
                     lambda i, src, rb: evac_bd(qBd, r_qB, i, src, rb))
            if is_s:
                fm_group([512 + 128 * i for i in range(4)],
                         lambda i, src, rb: fw.op("dve", lambda v: v.tensor_copy(out=skT1[:, i, :], in_=src), reads=[rb], writes=[r_sk[0]]))
                fm_group([2560 + 128 * i for i in range(4)],
                         lambda i, src, rb: fw.op("dve", lambda v: v.tensor_copy(out=skT2[:, i, :], in_=src), reads=[rb], writes=[r_sk[2]]))
            else:
                t0 = tiles[0][1]
                g0 = t0 // 128
                rc = (t0 % 1024)

                def evc(i, src, rb):
                    fw.op("dve", lambda v: v.tensor_copy(out=kTc[:, i, rc:rc + BLK], in_=src), reads=[rb], writes=[r_kc[g0 % 8], r_kc[(g0 + 1) % 8]])

                def evd(i, src, rb):
                    fw.op("dve", lambda v: v.tensor_copy(out=kT2[:, i, t0:t0 + BLK], in_=src), reads=[rb], writes=[r_k2[g0], r_k2[g0 + 1]])
                fm_group([512 + 128 * i for i in range(4)], evc)
                fm_group([2560 + 128 * i for i in range(4)], evd)
            fm_group([1536 + 128 * i for i in range(4)] + [3592 + 128 * i for i in range(4)],
                     lambda i, src, rb: fw.op("act", lambda a: a.activation(out=gT[:, i, :], in_=src, func=AF.Silu),
                                              reads=[rb], writes=[r_gT[i]]))

        def band_pre(u, dd, hs, nq):
            nk = u.nk
            H = hs.stop - hs.start
            if dd == 0:
                return pre_add(u, tblB[0:nk, hs, 0:nq], [r_tblB])
            if dd == 1:
                return pre_add(u, tblB[0:nk, hs, 128:128 + nq], [r_tblB])
            cst = cstB[0:nk, hs].unsqueeze(2).broadcast_to([nk, H, nq])
            if dd == 4:
                return pre_add(u, cst, [r_cstB], extra=(mask512[:, :].unsqueeze(1).broadcast_to([128, H, 128]), [r_mask512]))
            return pre_add(u, cst, [r_cstB])

        def attn1_prompt(bi):
            all_band = []
            all_fox = []
            for qt in (2 * bi, 2 * bi + 1):
                qc = (qt % 2) * 128
                biasq = _T(biasq2[:, qt % 2])
                for kt in range(qt - 1, -1, -1):
                    if kt == qt - 1:
                        fw.op("dve", lambda v, kt=kt, biasq=biasq: v.tensor_copy(out=biasq[:, kt, :], in_=fxb[:, 1, kt, :]), reads=[r_fxb], writes=[r_biasq])
                        fw.op("dve", lambda v, kt=kt, biasq=biasq: v.tensor_copy(out=accb[:, :], in_=fxb[:, 2, kt, :]), reads=[r_fxb], writes=[r_accb])
                    else:
                        fw.op("dve", lambda v, kt=kt, biasq=biasq: v.tensor_tensor(out=biasq[:, kt, :], in0=fxb[:, 1, kt, :], in1=accb[:, :], op=ALU.add),
                              reads=[r_fxb, r_accb], writes=[r_biasq])
                        fw.op("dve", lambda v, kt=kt, biasq=biasq: v.tensor_tensor(out=accb[:, :], in0=accb[:, :], in1=fxb[:, 2, kt, :], op=ALU.add),
                              reads=[r_fxb, r_accb], writes=[r_accb])
                for hg in range(2):
                    hs = slice(4 * hg, 4 * hg + 4)
                    chain = {}
                    units = []
                    kts = list(range(qt, max(-1, qt - 5), -1))
                    for kt in kts:
                        u = U(); u.nk = 128; u.chain = chain
                        u.first = (kt == kts[0]); u.last = (kt == kts[-1])
                        u.zmm = []; u.vmm = []
                        sl = kt % 8
                        for pp in range(2):
                            u.zmm.append((kTc[:, 2 * hg + pp, sl * 128:(sl + 1) * 128], qA[:, 2 * hg + pp, qt % 2, :], pp * 256, 256))
                        for pp in range(2):
                            u.vmm.append((vc[:, sl, (2 * hg + pp) * 128:(2 * hg + pp + 1) * 128], pp * 256, 256))
                        zmm4(u)
                        u.reads = [r_kc[sl], r_qA]; u.vreads = [r_kc[sl]]
                        if (qt - kt) in (2, 3):
                            u.hbias = [(cstB[:, 4 * hg + j:4 * hg + j + 1], r_cstB) for j in range(4)]
                        u.pre = (lambda u, dd=qt - kt, hs=hs: band_pre(u, dd, hs, 128))
                        slots = [(4 * hg + s_, s_ * 128, 128, (4 * hg + s_) // 2, qc) for s_ in range(4)]
                        u.fin = (lambda ob, db, slots=slots: fin_softmax(slots, ob, db))
                        units.append(u)
                    all_band += units
                    chain = {}
                    units = []
                    for kt in range(qt, -1, -1):
                        u = U(); u.nk = 128; u.chain = chain
                        u.first = (kt == qt); u.last = (kt == 0)
                        u.zmm = []; u.vmm = []
                        for pp in range(2):
                            u.zmm.append((kT2[:, 2 * hg + pp, kt * 128:(kt + 1) * 128], qBd[:, 2 * hg + pp, qt % 2, :], pp * 256, 256))
                        for pp in range(2):
                            u.vmm.append((v2[:, kt, (2 * hg + pp) * 128:(2 * hg + pp + 1) * 128], pp * 256, 256))
                        zmm4(u)
                        u.reads = [r_k2[kt], r_qB]; u.vreads = [r_k2[kt]]
                        if kt == qt:
                            u.pre = (lambda u, qt=qt, hs=hs: pre_add(u, fxb[:, 0, qt, hs].unsqueeze(2).broadcast_to([128, 4, 128]), [r_fxb],
                                                                      extra=(maskFX[:, :].unsqueeze(1).broadcast_to([128, 4, 128]), [r_maskFX])))
                        else:
                            if kt % 2 == 1:
                                u.hbias = [(biasq[:, kt, 4 * hg + j:4 * hg + j + 1], r_biasq) for j in range(4)]
                            u.pre = (lambda u, kt=kt, hs=hs, biasq=biasq: pre_add(u, biasq[:, kt, hs].unsqueeze(2).broadcast_to([128, 4, 128]), [r_biasq]))
                        slots = [(4 * hg + s_, s_ * 128, 128, 4 + (4 * hg + s_) // 2, qc) for s_ in range(4)]
                        u.fin = (lambda ob, db, slots=slots: fin_softmax(slots, ob, db))
                        units.append(u)
                    all_fox += units
            sm_chain(all_band)
            sm_chain(all_fox)

        def attn1_sample():
            hs8 = slice(0, 8)
            for s in range(NSTR):
                fw.dma("sp", o_bk_s[s, 0:448, :], c_bk[s, 64:512, :])
                fw.dma("sp", o_bv_s[s, 0:448, :], c_bv[s, 64:512, :])
                slots = [(h, h * 64, 64, h // 2, 64 * s) for h in range(8)]
                units = kv_units(s, 4, c_bk, c_bv, qA, r_qA, skT1, sv1, r_sk[0], slots)
                units[0].reads = [r_sk[0], qbd[s][1]]; units[0].vreads = [r_sk[1]]
                zmm4(units[0])
                units[0].pre = (lambda u: band_pre(u, 0, hs8, 64))
                for u in units[1:]:
                    dd = 4 - u.tile
                    op_ = u.prep

                    def prep2(u, op_=op_):
                        op_(u)
                        zmm4(u)
                    u.prep = prep2
                    u.pre = (lambda u, dd=dd: band_pre(u, 1 if dd == 1 else 2, hs8, 64))
                for u in units:
                    u.fin = (lambda ob, db, slots=slots: fin_softmax(slots, ob, db))
                sm_chain(units)
            for s in range(NSTR):
                fw.dma("sp", lfc[:], c_lf[s].rearrange("(t p) h -> p t h", p=128), writes=[r_lfc])
                lf2 = lfc[:].rearrange("p t h -> p (t h)")
                b2 = gbank.next()
                fw.op("pe", lambda t_, b2=b2: t_.matmul(ps[:, b2, 0:256], lhsT=triF[:, :], rhs=lf2, start=True, stop=False),
                      reads=[r_tri, r_lfc], writes=[r_ps[b2]])
                fw.op("pe", lambda t_, b2=b2: t_.matmul(ps[:, b2, 256:512], lhsT=onesF[:, :], rhs=lf2, start=False, stop=True),
                      reads=[r_onesF, r_lfc], writes=[r_ps[b2]])
                fw.op("dve", lambda v, b2=b2: v.tensor_copy(out=lf2, in_=ps[:, b2, 256:512]), reads=[r_ps[b2]], writes=[r_lfc])
                sf2 = sfx[:, 0:32, :].rearrange("p t h -> p (t h)")
                fw.op("dve", lambda v, b2=b2: v.tensor_tensor(out=sf2, in0=lf2, in1=ps[:, b2, 0:256], op=ALU.subtract),
                      reads=[r_ps[b2], r_lfc], writes=[r_sfx])
                for t in range(30, -1, -1):
                    if t == 30:
                        fw.op("dve", lambda v: v.tensor_copy(out=accb[:, :], in_=lfc[:, 31, :]), reads=[r_lfc], writes=[r_accb])
                    else:
                        fw.op("dve", lambda v, t=t: v.tensor_tensor(out=accb[:, :], in0=accb[:, :], in1=lfc[:, t + 1, :], op=ALU.add),
                              reads=[r_lfc, r_accb], writes=[r_accb])
                    fw.op("dve", lambda v, t=t: v.tensor_tensor(out=sfx[:, t, :], in0=sfx[:, t, :], in1=accb[:, :], op=ALU.add),
                          reads=[r_sfx, r_accb], writes=[r_sfx])
                slots = [(h, h * 64, 64, 4 + h // 2, 64 * s) for h in range(8)]
                units = kv_units(s, PAST // 128, c_fk, c_fv, qBd, r_qB, skT2, sv2, r_sk[2], slots)
                units[0].reads = [r_sk[2], qbd[s][1]]; units[0].vreads = [r_sk[3]]
                zmm4(units[0])
                units[0].pre = (lambda u, s=s: pre_add(u, sfxn[0:64, s, :].unsqueeze(2).broadcast_to([64, 8, 64]), [r_sfxn],
                                                       extra=(maskFX[0:64, 0:64].unsqueeze(1).broadcast_to([64, 8, 64]), [r_maskFX])))
                for u in units[1:]:
                    op_ = u.prep

                    def prep3(u, op_=op_):
                        op_(u)
                        zmm4(u)
                    u.prep = prep3
                    u.pre = (lambda u: pre_add(u, sfx[:, u.tile, :].unsqueeze(2).broadcast_to([128, 8, 64]), [r_sfx]))
                for u in units:
                    u.fin = (lambda ob, db, slots=slots: fin_softmax(slots, ob, db))
                sm_chain(units)

        def phase_out(l, tiles, xsrc, r_xsrc, ydst, r_ydst, kept=None):
            for ti_, (nt, row0, col0) in enumerate(tiles):
                b0 = gbank.next(); b1 = gbank.next()
                for half, b in ((0, b0), (1, b1)):
                    for c in range(8):
                        fw.op("pe", lambda t, c=c, b=b, half=half, nt=nt, col0=col0: t.matmul(ps[0:nt, b, :], lhsT=gT[:, c, col0:col0 + nt],
                                                                                               rhs=wout[:, c, half * 512:(half + 1) * 512],
                                                                                               start=(c == 0), stop=(c == 7)),
                              reads=[r_gT[c], r_wout], writes=[r_ps[b]])
                s, r_s = sm.next()
                fw.op("act", lambda a, s=s, nt=nt, b0=b0: a.activation(out=junk[0:nt, :], in_=ps[0:nt, b0, :], func=AF.Square, accum_out=s[0:nt, 4:5]),
                      reads=[r_ps[b0]], writes=[r_junk, r_s])
                fw.op("act", lambda a, s=s, nt=nt, b1=b1: a.activation(out=junk[0:nt, :], in_=ps[0:nt, b1, :], func=AF.Square, accum_out=s[0:nt, 5:6]),
                      reads=[r_ps[b1]], writes=[r_junk, r_s])
                fw.op("dve", lambda v, s=s, nt=nt: v.tensor_tensor(out=s[0:nt, 2:3], in0=s[0:nt, 4:5], in1=s[0:nt, 5:6], op=ALU.add), reads=[r_s], writes=[r_s])
                rs, r_rs = rstd_from_ss(s[0:nt, 2:3], D, nt, r_s)
                if kept is not None:
                    xt, r_xt = kept[ti_]
                else:
                    xt, r_xt = xin.next()
                    fw.dma("act", xt[0:nt, :], xsrc[row0:row0 + nt, :], reads=[r_xsrc[row0 // 64]] if r_xsrc else [], writes=[r_xt])
                for half, b in ((0, b0), (1, b1)):
                    y, r_y = tmpf.next()
                    fw.op("dve", lambda v, half=half, b=b, y=y, rs=rs, nt=nt: v.scalar_tensor_tensor(
                        out=y[0:nt, :], in0=ps[0:nt, b, :], scalar=rs, in1=gpost[0:nt, half * 512:(half + 1) * 512],
                        op0=ALU.mult, op1=ALU.mult), reads=[r_ps[b], r_rs, r_gpost], writes=[r_y])
                    fw.op("pool", lambda g, y=y, xt=xt, nt=nt, half=half: g.tensor_tensor(out=xt[0:nt, half * 512:(half + 1) * 512], in0=y[0:nt, :],
                                                                                          in1=xt[0:nt, half * 512:(half + 1) * 512], op=ALU.add),
                          reads=[r_y, r_xt], writes=[r_xt])
                fw.dma("sp", ydst[row0:row0 + nt, :], xt[0:nt, :], reads=[r_xt], writes=[r_ydst[row0 // 64]] if r_ydst else [])

        r_x1p = [Res() for _ in range(SEQ // 64)]
        r_x1s = [Res() for _ in range(NSTR * DSEQ // 64)]
        ptiles = lambda bi: [(128, bi * BLK, 0), (128, bi * BLK + 128, 128)]
        stiles = [(64, 64 * s, 64 * s) for s in range(NSTR)]

        load_layer_weights(0)
        nblk = min(SEQ // BLK, NBLK_DBG)
        for bi in range(nblk):
            kx = phase_norm(0, ptiles(bi), xp, None)
            phase_proj0(ptiles(bi), False, o_ckv_p, o_kr_p, o_sbk_p, o_sbv_p)
            if STAGES >= 2:
                attn0_prompt(bi)
            phase_out(0, ptiles(bi), xp, None, x1p, r_x1p, kept=kx)
        if STAGES >= 1:
            fw.barrier()
            phase_norm(0, stiles, xs, None)
            phase_proj0(stiles, True, o_ckv_s, o_kr_s, o_sbk_s, o_sbv_s)
            if STAGES >= 3:
                CAST_ENGS[0] = ("dve", "act", "dve", "pool")
                attn0_sample()
                CAST_ENGS[0] = ("pool", "dve", "act")
            phase_out(0, stiles, xs, None, x1s, r_x1s)
        if STAGES >= 4:
            load_layer_weights(1)
            setup_l1_tables()
            fw.op("pool", lambda g: g.memset(qBd[:].rearrange("p a t q -> p (a t q)"), 0.0), writes=[r_qB])
            fw.barrier()
            for bi in range(nblk):
                kx = phase_norm(1, ptiles(bi), x1p, r_x1p)
                phase_proj1(ptiles(bi), False, o_fk_p, o_fv_p, o_lf_p)
                if STAGES >= 5:
                    attn1_prompt(bi)
                phase_out(1, ptiles(bi), x1p, r_x1p, y_p, None, kept=kx)
            fw.barrier()
            phase_norm(1, stiles, x1s, r_x1s)
            phase_proj1(stiles, True, o_fk_s, o_fv_s, o_lf_s)
            if STAGES >= 6:
                CAST_ENGS[0] = ("act", "pool", "act")
                EVAC_ENGS[0] = ("dve", "act")
                attn1_sample()
            phase_out(1, stiles, x1s, r_x1s, y_s, None)

        print("fw ops recorded:", getattr(fw, "nops", 0), {k: e.cnt for k, e in fw.E.items()})
        fw.finish()
        fw.emit()
    return nc


def _rope_tables():
    half = 16
    inv = (10000.0 ** (-np.arange(half, dtype=np.float32) / half)).astype(np.float32)

    def tab(pos):
        ang = pos.astype(np.float32)[:, None] * inv[None, :]
        c = np.cos(ang).astype(np.float32)
        s = np.sin(ang).astype(np.float32)
        return np.concatenate([c, c, -s, s], axis=1).astype(np.float32)
    tp = tab(np.arange(SEQ)).reshape(16, 128, 64).transpose(1, 0, 2).reshape(128, 16 * 64)
    tsm = tab(PAST + np.arange(DSEQ))
    return np.ascontiguousarray(tp), np.ascontiguousarray(tsm)


_NC_CACHE = {}


def kernel(**inp):
    f = lambda a: np.ascontiguousarray(np.asarray(a, dtype=np.float32))
    x_prompt = f(inp["x_prompt"]); x_sample = f(inp["x_sample"])
    rope_p, rope_s = _rope_tables()
    w_uq = f(inp["a_w_uq"])[0]
    w_uq_l = np.concatenate([w_uq[:, :, :64].reshape(256, 512), w_uq[:, :, 64:].reshape(256, 256)], axis=1)
    w_uk = f(inp["a_w_uk"])[0]
    w_ukT = np.transpose(w_uk, (2, 1, 0)).reshape(64, 1024)
    w_ukT = np.concatenate([w_ukT, w_ukT], axis=0)
    relb = f(inp["c_rel_bias"])[0]
    relb_pad = np.concatenate([relb, np.repeat(relb[:, -1:], 256, axis=1)], axis=1)
    shared = {
        "norm_pre": np.ascontiguousarray(f(inp["norm_pre"]).reshape(2, 8, 128).transpose(2, 0, 1).reshape(128, 16)), "norm_post": f(inp["norm_post"]),
        "w_in0": f(inp["w_in_even"])[0], "q_norm": f(inp["a_q_norm"]).reshape(1, 256),
        "w_uq": np.ascontiguousarray(w_uq_l), "kv_norm": f(inp["a_kv_norm"]).reshape(1, 128),
        "w_ukT": np.ascontiguousarray(w_ukT), "w_uv": f(inp["a_w_uv"])[0].reshape(128, 512),
        "w_out0": f(inp["w_out_even"])[0], "w_in1": f(inp["w_in_odd"])[0],
        "relb": np.ascontiguousarray(relb_pad), "fbias": f(inp["d_forget_bias"]).reshape(1, 8),
        "relc": np.ascontiguousarray(relb[:, 256].reshape(1, 8)),
        "w_out1": f(inp["w_out_odd"])[0], "rope_p": rope_p, "rope_s": rope_s,
    }
    caches = {k: f(inp[k])[0] for k in ("cache_mla_ckv", "cache_mla_krope", "cache_sb_k", "cache_sb_v", "cache_band_k",
                                        "cache_band_v", "cache_fox_k", "cache_fox_v", "cache_fox_logf")}
    in_maps = []
    for c in range(NCORES):
        sl = slice(NSTR * c, NSTR * (c + 1))
        m = dict(shared)
        m["xp"] = x_prompt[c]
        m["xs"] = x_sample[sl].reshape(NSTR * DSEQ, D)
        m["c_ckv"] = caches["cache_mla_ckv"][sl]
        m["c_kr"] = caches["cache_mla_krope"][sl]
        m["c_sbk"] = caches["cache_sb_k"][sl].reshape(NSTR, PAST, 512)
        m["c_sbv"] = caches["cache_sb_v"][sl].reshape(NSTR, PAST, 512)
        m["c_bk"] = caches["cache_band_k"][sl].reshape(NSTR, 512, 512)
        m["c_bv"] = caches["cache_band_v"][sl].reshape(NSTR, 512, 512)
        m["c_fk"] = caches["cache_fox_k"][sl].reshape(NSTR, PAST, 512)
        m["c_fv"] = caches["cache_fox_v"][sl].reshape(NSTR, PAST, 512)
        m["c_lf"] = caches["cache_fox_logf"][sl]
        in_maps.append({k: np.ascontiguousarray(v) for k, v in m.items()})
    if "nc" not in _NC_CACHE:
        _NC_CACHE["nc"] = build()
    nc = _NC_CACHE["nc"]
    if KCORES < NCORES:
        res = run_bass_kernel_spmd(nc, in_maps[:KCORES], core_ids=list(range(KCORES)))
        R = list(res.results) + [res.results[0]] * (NCORES - KCORES)
    else:
        res = run_bass_kernel_spmd(nc, in_maps, core_ids=list(range(NCORES)))
        R = res.results
    cat = lambda k: np.stack([R[c][k] for c in range(NCORES)], axis=0)
    B = NCORES
    SB = NCORES * NSTR
    outs = (
        cat("y_p").reshape(B, SEQ, D),
        cat("y_s").reshape(SB, DSEQ, D),
        cat("o_ckv_p").reshape(1, B, SEQ, 128), cat("o_kr_p").reshape(1, B, SEQ, 32),
        cat("o_sbk_p").reshape(1, B, SEQ, 8, 64), cat("o_sbv_p").reshape(1, B, SEQ, 8, 64),
        cat("o_bk_p").reshape(1, B, 512, 8, 64), cat("o_bv_p").reshape(1, B, 512, 8, 64),
        cat("o_fk_p").reshape(1, B, SEQ, 8, 64), cat("o_fv_p").reshape(1, B, SEQ, 8, 64), cat("o_lf_p").reshape(1, B, SEQ, 8),
        cat("o_ckv_s").reshape(1, SB, DSEQ, 128), cat("o_kr_s").reshape(1, SB, DSEQ, 32),
        cat("o_sbk_s").reshape(1, SB, DSEQ, 8, 64), cat("o_sbv_s").reshape(1, SB, DSEQ, 8, 64),
        cat("o_bk_s").reshape(1, SB, 512, 8, 64), cat("o_bv_s").reshape(1, SB, 512, 8, 64),
        cat("o_fk_s").reshape(1, SB, DSEQ, 8, 64), cat("o_fv_s").reshape(1, SB, DSEQ, 8, 64), cat("o_lf_s").reshape(1, SB, DSEQ, 8),
    )
    _NC_CACHE["x1"] = (cat("x1p"), cat("x1s"))
    return tuple(np.ascontiguousarray(o.astype(np.float32)) for o in outs)
```

```python
import numpy as np
from contextlib import ExitStack
import concourse.bass as bass
import concourse.mybir as mybir
from concourse.bass_utils import run_bass_kernel_spmd

F32 = mybir.dt.float32
BF16 = mybir.dt.bfloat16
AF = mybir.ActivationFunctionType
ALU = mybir.AluOpType

NCORES = 8
D = 1024
SEQ = 2048
NSTR = 4
DSEQ = 64
PAST = 4096
EPS = 1e-6
NEGM = -30000.0
A_SCALE = float((64 + 32) ** -0.5)
EVEN_IN = 2976
ODD_IN = 4104
BLK = 256
import os
STAGES = int(os.environ.get('KSTAGES', '6'))
NBLK_DBG = int(os.environ.get('KNBLK', '8'))
OPLIMIT = int(os.environ.get('KOPLIMIT', '100000000'))
KCORES = int(os.environ.get('KCORES', '8'))
KSAME = int(os.environ.get('KSAME', '0'))


class Res:
    __slots__ = ("w", "rs", "x", "rg")

    def __init__(self, x=False):
        self.w = None
        self.rs = []
        self.rg = None
        self.x = x


class Eng:
    def __init__(self, name, sem):
        self.name = name
        self.sem = sem
        self.cnt = 0
        self.waited = {}
        self.prog = []
        self.dq = []
        self.dcnt = []
        self.di = 0


class FW:
    def __init__(self, nc, es, ndq=8):
        self.nc = nc
        self.E = {}
        for name in ("pe", "act", "dve", "pool", "sp"):
            self.E[name] = Eng(name, es.enter_context(nc.semaphore("s_" + name)))
        for qn in ("sp", "act", "pool"):
            e = self.E[qn]
            for i in range(ndq):
                e.dq.append(es.enter_context(nc.semaphore(f"d_{qn}{i}")))
                e.dcnt.append(0)

    def _wait(self, eng, tok, force=False):
        if tok is None:
            return
        sem, val, src = tok
        if src == eng.name and src in ("pe", "sp") and not force:
            return
        key = id(sem)
        if eng.waited.get(key, 0) >= val:
            return
        eng.waited[key] = val
        eng.prog.append(("w", sem, val))

    def _deps(self, eng, reads, writes):
        for r in reads:
            self._wait(eng, r.w)
            if r.x:
                for t in r.rs:
                    if t[2] != eng.name:
                        self._wait(eng, t)
        for w in writes:
            if w.w is not None and (w.w[2] != eng.name or KSAME):
                self._wait(eng, w.w)
            for t in w.rs:
                if t[2] != eng.name or KSAME:
                    self._wait(eng, t)

    def _record(self, tok, reads, writes):
        for r in reads:
            r.rs.append(tok)
            if len(r.rs) > 16:
                best = {}
                for t in r.rs:
                    k = id(t[0])
                    if k not in best or best[k][1] < t[1]:
                        best[k] = t
                r.rs = list(best.values())
        for w in writes:
            w.w = tok
            w.rs = []

    def op(self, engname, fn, reads=(), writes=(), rg=None):
        self.nops = getattr(self, "nops", 0) + 1
        if self.nops > OPLIMIT:
            return None
        eng = self.E[engname]
        self._deps(eng, reads, writes)
        if engname == "pe":
            for w in writes:
                if rg is not None and w.rg is not None and w.rg != rg and w.w is not None and w.w[2] == "pe":
                    self._wait(eng, w.w, force=True)
                w.rg = rg
        eng.cnt += 1
        eng.prog.append(("i", fn, eng.sem, 1))
        tok = (eng.sem, eng.cnt, eng.name)
        self._record(tok, reads, writes)
        return tok

    def dma(self, q, out, in_, reads=(), writes=()):
        self.nops = getattr(self, "nops", 0) + 1
        if self.nops > OPLIMIT:
            return None
        eng = self.E[q]
        self._deps(eng, reads, writes)
        i = eng.di % len(eng.dq)
        eng.di += 1
        sem = eng.dq[i]
        if eng.dcnt[i] > 0:
            self._wait(eng, (sem, eng.dcnt[i], "dma"))
        eng.prog.append(("i", (lambda o, out=out, in_=in_: o.dma_start(out=out, in_=in_)), sem, 16))
        eng.dcnt[i] += 16
        tok = (sem, eng.dcnt[i], "dma")
        self._record(tok, reads, writes)
        return tok

    def barrier(self):
        toks = []
        for q in ("sp", "act", "pool"):
            e = self.E[q]
            for i, sem in enumerate(e.dq):
                if e.dcnt[i] > 0:
                    toks.append((sem, e.dcnt[i], "dma"))
        for n in ("pe", "act", "dve", "pool"):
            e = self.E[n]
            if e.cnt > 0:
                toks.append((e.sem, e.cnt, "x"))
        for n in ("pe", "act", "dve", "pool", "sp"):
            for t in toks:
                self._wait(self.E[n], t)

    def finish(self):
        sp = self.E["sp"]
        for q in ("sp", "act", "pool"):
            e = self.E[q]
            for i, sem in enumerate(e.dq):
                if e.dcnt[i] > 0:
                    self._wait(sp, (sem, e.dcnt[i], "dma"))
        for n in ("pe", "act", "dve", "pool"):
            e = self.E[n]
            if e.cnt > 0:
                self._wait(sp, (e.sem, e.cnt, "x"))

    def emit(self):
        nc = self.nc
        objs = {"pe": None}

        def run(eng):
            def body(obj):
                for a in eng.prog:
                    if a[0] == "w":
                        obj.wait_ge(a[1], a[2])
                    else:
                        a[1](obj).then_inc(a[2], a[3])
            return body
        with nc.Block() as block:
            block.tensor(run(self.E["pe"]))
            block.scalar(run(self.E["act"]))
            block.vector(run(self.E["dve"]))
            block.gpsimd(run(self.E["pool"]))
            block.sync(run(self.E["sp"]))


class RR:
    def __init__(self, items):
        self.items = items
        self.i = 0

    def next(self):
        it = self.items[self.i % len(self.items)]
        self.i += 1
        return it


def pipeline(units, stages, offsets=None):
    n = len(units)
    ns = len(stages)
    if offsets is None:
        offsets = list(range(ns))
    for i in range(n + max(offsets)):
        for s, st in enumerate(stages):
            j = i - offsets[s]
            if 0 <= j < n:
                st(units[j])


def build():
    nc = bass.Bass("TRN2", target_bir_lowering=False)
    din = lambda n, s: nc.dram_tensor(n, s, F32, kind="ExternalInput").ap()
    dout = lambda n, s: nc.dram_tensor(n, s, F32, kind="ExternalOutput").ap()
    xp = din("xp", [SEQ, D])
    xs = din("xs", [NSTR * DSEQ, D])
    c_ckv = din("c_ckv", [NSTR, PAST, 128])
    c_kr = din("c_kr", [NSTR, PAST, 32])
    c_sbk = din("c_sbk", [NSTR, PAST, 512])
    c_sbv = din("c_sbv", [NSTR, PAST, 512])
    c_bk = din("c_bk", [NSTR, 512, 512])
    c_bv = din("c_bv", [NSTR, 512, 512])
    c_fk = din("c_fk", [NSTR, PAST, 512])
    c_fv = din("c_fv", [NSTR, PAST, 512])
    c_lf = din("c_lf", [NSTR, PAST, 8])
    norm_pre = din("norm_pre", [128, 16])
    norm_post = din("norm_post", [2, D])
    w_in0 = din("w_in0", [D, EVEN_IN])
    q_norm = din("q_norm", [1, 256])
    w_uq = din("w_uq", [256, 768])
    kv_norm = din("kv_norm", [1, 128])
    w_ukT = din("w_ukT", [128, 1024])
    w_uv = din("w_uv", [128, 512])
    w_out0 = din("w_out0", [D, D])
    w_in1 = din("w_in1", [D, ODD_IN])
    relb = din("relb", [8, 513])
    fbias = din("fbias", [1, 8])
    relc = din("relc", [1, 8])
    w_out1 = din("w_out1", [D, D])
    rope_p = din("rope_p", [128, 16 * 64])
    rope_s = din("rope_s", [64, 64])
    y_p = dout("y_p", [SEQ, D])
    y_s = dout("y_s", [NSTR * DSEQ, D])
    o_ckv_p = dout("o_ckv_p", [SEQ, 128]); o_kr_p = dout("o_kr_p", [SEQ, 32])
    o_sbk_p = dout("o_sbk_p", [SEQ, 512]); o_sbv_p = dout("o_sbv_p", [SEQ, 512])
    o_bk_p = dout("o_bk_p", [512, 512]); o_bv_p = dout("o_bv_p", [512, 512])
    o_fk_p = dout("o_fk_p", [SEQ, 512]); o_fv_p = dout("o_fv_p", [SEQ, 512]); o_lf_p = dout("o_lf_p", [SEQ, 8])
    o_ckv_s = dout("o_ckv_s", [NSTR * DSEQ, 128]); o_kr_s = dout("o_kr_s", [NSTR * DSEQ, 32])
    o_sbk_s = dout("o_sbk_s", [NSTR * DSEQ, 512]); o_sbv_s = dout("o_sbv_s", [NSTR * DSEQ, 512])
    o_bk_s = dout("o_bk_s", [NSTR, 512, 512]); o_bv_s = dout("o_bv_s", [NSTR, 512, 512])
    o_fk_s = dout("o_fk_s", [NSTR * DSEQ, 512]); o_fv_s = dout("o_fv_s", [NSTR * DSEQ, 512])
    o_lf_s = dout("o_lf_s", [NSTR * DSEQ, 8])
    x1p = dout("x1p", [SEQ, D])
    x1s = dout("x1s", [NSTR * DSEQ, D])

    with ExitStack() as es:
        fw = FW(nc, es)
        ARN = 105500
        AR = es.enter_context(nc.sbuf_tensor("AR", [128, ARN], BF16))
        ar = {"top": 0, "peak": 0}

        class _T:
            def __init__(self, ap):
                self.ap = ap
            def __getitem__(self, k):
                return self.ap[k]

        def sbt(n, s, d=F32):
            nel = int(np.prod(s[1:]))
            nb = nel * (4 if d == F32 else 2)
            nb = (nb + 63) // 64 * 64
            off = ar["top"]
            ar["top"] += nb // 2
            ar["peak"] = max(ar["peak"], ar["top"])
            assert ar["top"] <= ARN, (n, ar["top"])
            v = AR[:, off:off + nb // 2]
            if d == F32:
                v = v.bitcast(F32)
            v = v[:, 0:nel]
            if len(s) == 3:
                v = v.rearrange("p (a b) -> p a b", a=s[1])
            elif len(s) == 4:
                v = v.rearrange("p (a b c) -> p a b c", a=s[1], b=s[2])
            if s[0] < 128:
                v = v[0:s[0]]
            return _T(v)
        ps = es.enter_context(nc.psum_tensor("ps", [128, 7, 512], F32))
        psb = es.enter_context(nc.psum_tensor("psb", [128, 1024], BF16))
        r_ps = [Res(True) for _ in range(7)]
        r_psb = Res(True)
        bank = lambda i: ps[:, i, :]

        ident = sbt("ident", [128, 128], BF16); r_ident = Res()
        fw.op("pool", lambda g: g.memset(ident[:], 1.0), writes=[r_ident])
        fw.op("pool", lambda g: g.affine_select(out=ident[:], in_=ident[:], pattern=[[-1, 128]], compare_op=ALU.is_equal,
                                                fill=0.0, base=0, channel_multiplier=1), reads=[r_ident], writes=[r_ident])
        flipJ = sbt("flipJ", [128, 128], F32); r_flip = Res()
        fw.op("pool", lambda g: g.memset(flipJ[:], 1.0), writes=[r_flip])
        fw.op("pool", lambda g: g.affine_select(out=flipJ[:], in_=flipJ[:], pattern=[[1, 128]], compare_op=ALU.is_equal,
                                                fill=0.0, base=-127, channel_multiplier=1), reads=[r_flip], writes=[r_flip])
        triF = sbt("triF", [128, 128], F32); r_tri = Res()
        fw.op("pool", lambda g: g.memset(triF[:], 1.0), writes=[r_tri])
        fw.op("pool", lambda g: g.affine_select(out=triF[:], in_=triF[:], pattern=[[1, 128]], compare_op=ALU.is_ge,
                                                fill=0.0, base=0, channel_multiplier=-1), reads=[r_tri], writes=[r_tri])
        onesF = sbt("onesF", [128, 128], F32); r_onesF = Res()
        fw.op("pool", lambda g: g.memset(onesF[:], 1.0), writes=[r_onesF])
        onesB = sbt("onesB", [128, 128], BF16); r_onesB = Res()
        fw.op("pool", lambda g: g.memset(onesB[:], 1.0), writes=[r_onesB])
        negOnes = sbt("negOnes", [128, 128], BF16); r_negOnes = Res()
        fw.op("pool", lambda g: g.memset(negOnes[:], -1.0), writes=[r_negOnes])
        negTri = sbt("negTri", [128, 128], BF16); r_negTri = Res()
        fw.op("pool", lambda g: g.memset(negTri[:], -1.0), writes=[r_negTri])
        fw.op("pool", lambda g: g.affine_select(out=negTri[:], in_=negTri[:], pattern=[[-1, 128]], compare_op=ALU.is_ge,
                                                fill=0.0, base=0, channel_multiplier=1), reads=[r_negTri], writes=[r_negTri])
        maskSB = sbt("maskSB", [128, 512], BF16); r_maskSB = Res()
        fw.op("pool", lambda g: g.memset(maskSB[:], 0.0), writes=[r_maskSB])
        for a in range(4):
            fw.op("pool", lambda g, a=a: g.affine_select(out=maskSB[:, a * 128:(a + 1) * 128], in_=maskSB[:, a * 128:(a + 1) * 128],
                                                         pattern=[[1, 128]], compare_op=ALU.is_gt, fill=NEGM, base=0,
                                                         channel_multiplier=-1), reads=[r_maskSB], writes=[r_maskSB])
        maskSB64 = sbt("maskSB64", [64, 512], BF16); r_maskSB64 = Res()
        fw.op("pool", lambda g: g.memset(maskSB64[:], 0.0), writes=[r_maskSB64])
        for a in range(8):
            fw.op("pool", lambda g, a=a: g.affine_select(out=maskSB64[:, a * 64:(a + 1) * 64], in_=maskSB64[:, a * 64:(a + 1) * 64],
                                                         pattern=[[1, 64]], compare_op=ALU.is_gt, fill=NEGM, base=0,
                                                         channel_multiplier=-1), reads=[r_maskSB64], writes=[r_maskSB64])
        maskFX = sbt("maskFX", [128, 128], F32); r_maskFX = Res()
        fw.op("pool", lambda g: g.memset(maskFX[:], 0.0), writes=[r_maskFX])
        fw.op("pool", lambda g: g.affine_select(out=maskFX[:], in_=maskFX[:], pattern=[[1, 128]], compare_op=ALU.is_ge,
                                                fill=NEGM, base=0, channel_multiplier=-1), reads=[r_maskFX], writes=[r_maskFX])
        maskCH = sbt("maskCH", [128, 128], F32); r_maskCH = Res()
        fw.op("pool", lambda g: g.memset(maskCH[:], 0.0), writes=[r_maskCH])
        fw.op("pool", lambda g: g.memset(maskCH[64:128, 0:64], NEGM), reads=[r_maskCH], writes=[r_maskCH])
        mask512 = sbt("mask512", [128, 128], F32); r_mask512 = Res()
        fw.op("pool", lambda g: g.memset(mask512[:], 0.0), writes=[r_mask512])
        fw.op("pool", lambda g: g.memset(mask512[0:64, 64:128], NEGM), reads=[r_mask512], writes=[r_mask512])

        ropePt = RR([(sbt(f"ropeP{i}", [128, 64]), Res()) for i in range(2)])
        ropeS = sbt("ropeS", [64, 64]); r_ropeS = Res()
        fw.dma("sp", ropeS[:], rope_s[:, :], writes=[r_ropeS])
        gpre = sbt("gpre", [128, 2, 8]); r_gpre = Res()
        fw.dma("sp", gpre[:].rearrange("p a b -> p (a b)"), norm_pre[:, :], writes=[r_gpre])
        gpost = sbt("gpost", [128, D]); r_gpost = Res()
        qn_bc = sbt("qn_bc", [128, 256]); r_qn = Res()
        fw.dma("sp", qn_bc[:], bass.AP(q_norm.tensor, 0, [[0, 128], [1, 256]]), writes=[r_qn])
        kvn_bc = sbt("kvn_bc", [128, 128]); r_kvn = Res()
        fw.dma("sp", kvn_bc[:], bass.AP(kv_norm.tensor, 0, [[0, 128], [1, 128]]), writes=[r_kvn])
        fb_bc = sbt("fb_bc", [128, 8]); r_fb = Res()
        fw.dma("sp", fb_bc[:], bass.AP(fbias.tensor, 0, [[0, 128], [1, 8]]), writes=[r_fb])

        wbuf = sbt("wbuf", [128, 8, ODD_IN], BF16); r_w = Res()
        wout = sbt("wout", [128, 8, D], BF16); r_wout = Res()
        wuq = _T(wbuf[:, 0:2, 2976:2976 + 768]); r_wuq = r_w
        wukT = _T(wbuf[:, 2, 2976:2976 + 1024]); r_wuk = r_w
        wuv = _T(wbuf[:, 3, 2976:2976 + 512]); r_wuv = r_w
        WST = 1026
        wst_off = ar["top"]
        wst = RR([(sbt(f"wst{i}", [128, WST]), Res()) for i in range(2)])
        wst_end = ar["top"]
        ar["top"] = wst_off
        cast_i = [0]
        wq_i = [0]

        CAST_ENGS = [("pool", "dve", "act")]
        EVAC_ENGS = [("dve",)]

        def cast(out, in_, reads, writes, engs=None):
            engs = engs or CAST_ENGS[0]
            e = engs[cast_i[0] % len(engs)]
            cast_i[0] += 1
            if e == "act":
                fw.op("act", lambda a: a.copy(out=out, in_=in_), reads=reads, writes=writes)
            else:
                fw.op(e, lambda v: v.tensor_copy(out=out, in_=in_), reads=reads, writes=writes)

        def load_w(dst_fn, src, nrows, ncols, r_dst, q="sp"):
            for c in range(nrows // 128):
                for c0 in range(0, ncols, WST):
                    n = min(WST, ncols - c0)
                    st, r_st = wst.next()
                    wq_i[0] += 1
                    fw.dma(("sp", "act")[wq_i[0] % 2], st[:, 0:n], src[c * 128:(c + 1) * 128, c0:c0 + n], writes=[r_st])
                    cast(dst_fn(c)[:, c0:c0 + n], st[:, 0:n], [r_st], [r_dst], engs=("dve", "act"))

        pbf = RR([(sbt(f"pbf{i}", [128, 512], BF16), Res()) for i in range(3)])
        spb = RR([(sbt(f"spb{i}", [128, 512], BF16), Res()) for i in range(3)])
        ebuf = RR([(sbt(f"ebuf{i}", [128, 512]), Res()) for i in range(2)])
        ar["top"] = max(ar["top"], wst_end)
        KS = 24576
        ks_off = ar["top"]
        ks = sbt("ks", [128, KS], BF16)
        hT = sbt("hT", [128, 8, BLK], BF16); r_hT = Res()
        gT = sbt("gT", [128, 8, BLK], BF16); r_gT = [Res() for _ in range(8)]
        qA = sbt("qA", [128, 4, 2, 256], BF16); r_qA = Res()
        qBd = sbt("qB", [128, 4, 2, 256], BF16); r_qB = Res()
        qB = _T(qBd[:].rearrange("p a t q -> p (a t q)")[:, 0:4 * BLK].rearrange("p (a q) -> p a q", a=4))
        fw.op("pool", lambda g: g.memset(qA[:].rearrange("p a t q -> p (a t q)"), 0.0), writes=[r_qA])
        xin = RR([(sbt(f"xin{i}", [128, D]), Res()) for i in range(2)])
        hb = RR([(sbt(f"hb{i}", [128, D], BF16), Res()) for i in range(1)])
        junk = sbt("junk", [128, 512], BF16); r_junk = Res()
        sm = RR([(sbt(f"sm{i}", [128, 16]), Res()) for i in range(6)])
        st512 = RR([(sbt(f"st512_{i}", [128, 512]), Res()) for i in range(2)])
        tmpf = RR([(sbt(f"tmpf{i}", [128, 512]), Res()) for i in range(2)])
        fint = RR([(sbt(f"fint{i}", [128, 256]), Res()) for i in range(2)])
        Rbuf = sbt("Rbuf", [128, 512], BF16); r_R = Res()
        latb = sbt("latb", [128, 512], BF16); r_latb = Res()
        rden = sbt("rden", [128, 512]); r_rden = Res()
        ov0 = ar["top"]
        qlat = sbt("qlat", [128, 8, BLK], BF16); r_qlat = Res()
        qrT = sbt("qrT", [32, 8, BLK], BF16); r_qrT = Res()
        cqT = sbt("cqT", [128, 2, BLK], BF16); r_cqT = Res()
        cq_b = sbt("cq_b", [128, 256], BF16); r_cqb = Res()
        kvb = sbt("kvb", [128, 128 + 32], BF16); r_kvb = Res()
        qr_b = sbt("qr_b", [128, 256], BF16); r_qrb = Res()
        ropet = RR([(sbt(f"ropet{i}", [128, 256]), Res()) for i in range(2)])
        ov1 = ar["top"]
        ar["top"] = ov0
        fxb = sbt("fxb", [128, 4, 16, 8]); r_fxb = Res()
        biasq2 = sbt("biasq", [128, 2, 16, 8]); r_biasq = Res()
        accb = sbt("accb", [128, 8]); r_accb = Res()
        tblB = sbt("tblB", [128, 8, 256]); r_tblB = Res()
        cstB = sbt("cstB", [128, 8]); r_cstB = Res()
        lfb = RR([(sbt(f"lfb{i}", [128, 24]), Res()) for i in range(3)])
        lfc = sbt("lfc", [128, 32, 8]); r_lfc = Res()
        sfx = sbt("sfx", [128, 33, 8]); r_sfx = Res()
        ar["top"] = max(ar["top"], ov1)
        sv_top = ar["top"]
        ar["top"] = ks_off
        cst_f = RR([(sbt(f"cstf{i}", [128, 512]), Res()) for i in range(8)])
        vbf = RR([(sbt(f"vbf{i}", [128, 512], BF16), Res()) for i in range(6)])
        kbf = RR([(sbt(f"kbf{i}", [128, 512], BF16), Res()) for i in range(2)])
        cKT = RR([(sbt(f"cKT{i}", [128, 4, 128], BF16), Res()) for i in range(4)])
        sfxn = sbt("sfxn", [64, 4, 8]); r_sfxn = Res()
        qbd = [(sbt(f"qbd{i}", [128, 4, 128], BF16), Res()) for i in range(4)]
        skT1 = sbt("skT1", [128, 4, BLK], BF16); skT2 = sbt("skT2", [128, 4, BLK], BF16)
        sv1 = sbt("sv1", [64, 4, 512], BF16); sv2 = sbt("sv2", [64, 4, 512], BF16)
        sckvT = sbt("sckvT", [128, BLK], BF16); skrT = sbt("skrT", [32, BLK], BF16)
        sckv_tm = sbt("sckv_tm", [64, 4, 128], BF16)
        assert ar["top"] <= ks_off + KS
        ar["top"] = sv_top

        def ksv(off, shape):
            n = int(np.prod(shape))
            v = ks[:, off:off + n]
            if len(shape) == 2:
                return v.rearrange("p (a b) -> p a b", a=shape[0])
            return v
        kT1 = ksv(0, [4, SEQ]); v1 = ksv(8192, [16, 512])
        kT2 = ksv(8192, [4, SEQ]); v2 = ksv(16384, [16, 512])
        kTc = ksv(0, [4, 1024]); vc = ksv(4096, [8, 512])
        ckvT = ks[:, 16384:16384 + SEQ]
        krT = ks[:, 16384 + 2048:16384 + 4096]
        ckv_tm = ksv(16384 + 4096, [16, 128])
        r_k1 = [Res() for _ in range(20)]
        r_k2 = [Res() for _ in range(20)]
        r_k3 = [Res() for _ in range(20)]
        SOFF = 24576
        r_sk = [Res() for _ in range(8)]

        def load_layer_weights(l):
            fw.barrier()
            if l == 0:
                load_w(lambda c: wbuf[:, c, :], w_in0, D, EVEN_IN, r_w)
                load_w(lambda c: wout[:, c, :], w_out0, D, D, r_wout)
                load_w(lambda c: wuq[:, c, :], w_uq, 256, 768, r_wuq)
                load_w(lambda c: wukT[:, :], w_ukT, 128, 1024, r_wuk)
                load_w(lambda c: wuv[:, :], w_uv, 128, 512, r_wuv)
            else:
                load_w(lambda c: wbuf[:, c, :], w_in1, D, ODD_IN, r_w)
                load_w(lambda c: wout[:, c, :], w_out1, D, D, r_wout)
            fw.dma("sp", gpost[:], bass.AP(norm_post.tensor, l * D, [[0, 128], [1, D]]), writes=[r_gpost])
            fw.barrier()

        gbank = RR([6, 2, 3, 0, 1, 4, 5])

        def mm(o, l, r, st, sp_, reads, wres):
            K = l.shape[0]
            rg = None if K >= 128 else (l.base_partition(), K)
            fw.op("pe", lambda t: t.matmul(o, lhsT=l, rhs=r, start=st, stop=sp_), reads=reads, writes=[wres], rg=rg)

        def rstd_from_ss(ss_ap, n, nt, r_ss):
            s, r_s = sm.next()
            fw.op("dve", lambda v: v.tensor_scalar(out=s[0:nt, 0:1], in0=ss_ap, scalar1=1.0 / n, scalar2=EPS,
                                                   op0=ALU.mult, op1=ALU.add), reads=[r_ss], writes=[r_s])
            fw.op("act", lambda a_: a_.activation(out=s[0:nt, 3:4], in_=s[0:nt, 0:1], func=AF.Sqrt), reads=[r_s], writes=[r_s])
            fw.op("dve", lambda v: v.reciprocal(out=s[0:nt, 1:2], in_=s[0:nt, 3:4]), reads=[r_s], writes=[r_s])
            return s[0:nt, 1:2], r_s

        def sumsq(in_ap, nt, reads):
            s, r_s = sm.next()
            fw.op("act", lambda a: a.activation(out=junk[0:nt, 0:in_ap.shape[-1]], in_=in_ap, func=AF.Square,
                                                accum_out=s[0:nt, 2:3]), reads=reads, writes=[r_junk, r_s])
            return s[0:nt, 2:3], r_s

        def phase_norm(l, tiles, xsrc, r_xsrc):
            for (nt, row0, col0) in tiles:
                xt, r_xt = xin.next()
                fw.dma("act", xt[0:nt, :], xsrc[row0:row0 + nt, :], reads=[r_xsrc[row0 // 64]] if r_xsrc else [], writes=[r_xt])
                s_, r_ss = sm.next()
                for hf in range(2):
                    fw.op("act", lambda a, hf=hf, s_=s_, xt=xt, nt=nt: a.activation(out=junk[0:nt, :], in_=xt[0:nt, hf * 512:(hf + 1) * 512], func=AF.Square,
                                                                                    accum_out=s_[0:nt, 4 + hf:5 + hf]), reads=[r_xt], writes=[r_junk, r_ss])
                fw.op("dve", lambda v, s_=s_, nt=nt: v.tensor_tensor(out=s_[0:nt, 2:3], in0=s_[0:nt, 4:5], in1=s_[0:nt, 5:6], op=ALU.add), reads=[r_ss], writes=[r_ss])
                rs, r_rs = rstd_from_ss(s_[0:nt, 2:3], D, nt, r_ss)
                h, r_h = hb.next()
                fw.op("dve", lambda v, h=h, xt=xt, rs=rs, nt=nt: v.tensor_scalar(out=h[0:nt, :], in0=xt[0:nt, :], scalar1=rs, scalar2=None,
                                                                                  op0=ALU.mult), reads=[r_xt, r_rs], writes=[r_h])
                for c in range(8):
                    fw.op("pe", lambda t, h=h, c=c, nt=nt: t.transpose(psb[:, c * 128:c * 128 + nt], h[0:nt, c * 128:(c + 1) * 128],
                                                                       ident[0:nt, 0:nt]), reads=[r_h, r_ident], writes=[r_psb])
                fw.op("dve", lambda v, nt=nt, col0=col0: v.tensor_tensor(
                    out=hT[:, :, col0:col0 + nt], in0=psb[:, :].rearrange("p (c t) -> p c t", c=8)[:, :, 0:nt],
                    in1=gpre[:, l, :].unsqueeze(2).broadcast_to([128, 8, nt]), op=ALU.mult),
                    reads=[r_psb, r_gpre], writes=[r_hT])

        def tm_proj(nt, col0, wcols, ncols, b):
            for c in range(8):
                fw.op("pe", lambda t, c=c: t.matmul(ps[0:nt, b, 0:ncols], lhsT=hT[:, c, col0:col0 + nt],
                                                    rhs=wbuf[:, c, wcols:wcols + ncols], start=(c == 0), stop=(c == 7)),
                      reads=[r_hT, r_w], writes=[r_ps[b]])

        def fm_proj(wcols, b, half):
            for c in range(8):
                fw.op("pe", lambda t, c=c: t.matmul(ps[:, b, half * BLK:(half + 1) * BLK], lhsT=wbuf[:, c, wcols:wcols + 128],
                                                    rhs=hT[:, c, :], start=(c == 0), stop=(c == 7)),
                      reads=[r_hT, r_w], writes=[r_ps[b]])

        def fm_group(wcol_list, evac):
            for i in range(0, len(wcol_list), 2):
                b = gbank.next()
                n = min(2, len(wcol_list) - i)
                for j in range(n):
                    fm_proj(wcol_list[i + j], b, j)
                for j in range(n):
                    evac(i + j, ps[:, b, j * BLK:(j + 1) * BLK], r_ps[b])

        def rope_tm(src, nheads, nt, tab, r_tab, out_ap, reads, writes):
            t1, r_t1 = ropet.next()
            t2, r_t2 = ropet.next()
            n = nheads * 32
            s3 = src.rearrange("p (h r) -> p h r", h=nheads)
            a1 = t1[0:nt, 0:n].rearrange("p (h r) -> p h r", h=nheads)
            a2 = t2[0:nt, 0:n].rearrange("p (h r) -> p h r", h=nheads)
            cosb = tab[:, 0:32].unsqueeze(1).broadcast_to([nt, nheads, 32])
            sin_lo = tab[:, 32:48].unsqueeze(1).broadcast_to([nt, nheads, 16])
            sin_hi = tab[:, 48:64].unsqueeze(1).broadcast_to([nt, nheads, 16])
            fw.op("dve", lambda v: v.tensor_tensor(out=a1, in0=s3, in1=cosb, op=ALU.mult), reads=reads + [r_tab], writes=[r_t1])
            fw.op("dve", lambda v: v.tensor_tensor(out=a2[:, :, 0:16], in0=s3[:, :, 16:32], in1=sin_lo, op=ALU.mult),
                  reads=reads + [r_tab], writes=[r_t2])
            fw.op("dve", lambda v: v.tensor_tensor(out=a2[:, :, 16:32], in0=s3[:, :, 0:16], in1=sin_hi, op=ALU.mult),
                  reads=reads + [r_tab], writes=[r_t2])
            fw.op("dve", lambda v: v.tensor_tensor(out=out_ap, in0=t1[0:nt, 0:n], in1=t2[0:nt, 0:n], op=ALU.add),
                  reads=[r_t1, r_t2], writes=writes)

        def evac_bd(dst, r_dst, i, src, rb):
            fw.op("act", lambda a: a.activation(out=dst[0:64, i, :, 0:128], in_=src[0:64, :].rearrange("p (t q) -> p t q", t=2),
                                                func=AF.Copy, scale=0.125), reads=[rb], writes=[r_dst])
            fw.op("act", lambda a: a.activation(out=dst[64:128, i, :, 128:256], in_=src[64:128, :].rearrange("p (t q) -> p t q", t=2),
                                                func=AF.Copy, scale=0.125), reads=[rb], writes=[r_dst])

        def phase_proj0(tiles, is_s, o_ckv, o_kr, o_sbk, o_sbv):
            for ti, (nt, row0, col0) in enumerate(tiles):
                gt = (row0 // 128) if not is_s else ti
                b = gbank.next()
                tm_proj(nt, col0, 0, 416, b)
                ssq, r_ssq = sumsq(ps[0:nt, b, 0:256], nt, [r_ps[b]])
                rq, r_rq = rstd_from_ss(ssq, 256, nt, r_ssq)
                fw.op("dve", lambda v, b=b, rq=rq, nt=nt: v.scalar_tensor_tensor(out=cq_b[0:nt, :], in0=ps[0:nt, b, 0:256], scalar=rq,
                                                                                  in1=qn_bc[0:nt, :], op0=ALU.mult, op1=ALU.mult),
                      reads=[r_ps[b], r_rq, r_qn], writes=[r_cqb])
                ssk, r_ssk = sumsq(ps[0:nt, b, 256:384], nt, [r_ps[b]])
                rk, r_rk = rstd_from_ss(ssk, 128, nt, r_ssk)
                stg, r_stg = st512.next()
                fw.op("dve", lambda v, b=b, rk=rk, nt=nt, stg=stg: v.scalar_tensor_tensor(out=stg[0:nt, 0:128], in0=ps[0:nt, b, 256:384], scalar=rk,
                                                                                           in1=kvn_bc[0:nt, :], op0=ALU.mult, op1=ALU.mult),
                      reads=[r_ps[b], r_rk, r_kvn], writes=[r_stg])
                if is_s:
                    tab, r_tab = ropeS[0:nt, :], r_ropeS
                else:
                    tb_, r_tab = ropePt.next()
                    fw.dma("sp", tb_[:, :], rope_p[:, gt * 64:(gt + 1) * 64], writes=[r_tab])
                    tab = tb_[0:nt, :]
                rope_tm(ps[0:nt, b, 384:416], 1, nt, tab, r_tab, stg[0:nt, 128:160], [r_ps[b]], [r_stg])
                fw.dma("sp", o_ckv[row0:row0 + nt, :], stg[0:nt, 0:128], reads=[r_stg])
                fw.dma("sp", o_kr[row0:row0 + nt, :], stg[0:nt, 128:160], reads=[r_stg])
                fw.op("act", lambda a, stg=stg, nt=nt: a.copy(out=kvb[0:nt, :], in_=stg[0:nt, 0:160]), reads=[r_stg], writes=[r_kvb])
                if is_s:
                    fw.op("pool", lambda g, ti=ti, nt=nt: g.tensor_copy(out=sckv_tm[0:nt, ti, :], in_=kvb[0:nt, 0:128]),
                          reads=[r_kvb], writes=[r_sk[4]])
                else:
                    fw.op("pool", lambda g, gt=gt: g.tensor_copy(out=ckv_tm[:, gt, :], in_=kvb[:, 0:128]), reads=[r_kvb], writes=[r_k3[gt]])
                for j in range(2):
                    fw.op("pe", lambda t, j=j, nt=nt: t.transpose(psb[:, j * 128:j * 128 + nt], cq_b[0:nt, j * 128:(j + 1) * 128],
                                                                  ident[0:nt, 0:nt]), reads=[r_cqb, r_ident], writes=[r_psb])
                fw.op("pe", lambda t, nt=nt: t.transpose(psb[:, 256:256 + nt], kvb[0:nt, 0:128], ident[0:nt, 0:nt]),
                      reads=[r_kvb, r_ident], writes=[r_psb])
                fw.op("pe", lambda t, nt=nt: t.transpose(psb[0:32, 384:384 + nt], kvb[0:nt, 128:160], ident[0:nt, 0:nt]),
                      reads=[r_kvb, r_ident], writes=[r_psb])
                fw.op("dve", lambda v, nt=nt, col0=col0: v.tensor_copy(out=cqT[:, :, col0:col0 + nt],
                                                                        in_=psb[:, 0:256].rearrange("p (c t) -> p c t", c=2)[:, :, 0:nt]),
                      reads=[r_psb], writes=[r_cqT])
                if is_s:
                    dckv, dkr, rr = sckvT[:, col0:col0 + nt], skrT[:, col0:col0 + nt], r_sk[5]
                else:
                    dckv, dkr, rr = ckvT[:, row0:row0 + nt], krT[0:32, row0:row0 + nt], r_k3[gt]
                fw.op("dve", lambda v, nt=nt, dckv=dckv: v.tensor_copy(out=dckv, in_=psb[:, 256:256 + nt]), reads=[r_psb], writes=[rr])
                fw.op("dve", lambda v, nt=nt, dkr=dkr: v.tensor_copy(out=dkr, in_=psb[0:32, 384:384 + nt]), reads=[r_psb], writes=[rr])
                b = gbank.next()
                tm_proj(nt, col0, 1440, 512, b)
                stg, r_stg = st512.next()
                fw.op("act", lambda a, b=b, stg=stg, nt=nt: a.copy(out=stg[0:nt, :], in_=ps[0:nt, b, :]), reads=[r_ps[b]], writes=[r_stg])
                fw.dma("sp", o_sbk[row0:row0 + nt, :], stg[0:nt, :], reads=[r_stg])
                b = gbank.next()
                tm_proj(nt, col0, 1952, 512, b)
                stg, r_stg = st512.next()
                fw.op("act", lambda a, b=b, stg=stg, nt=nt: a.copy(out=stg[0:nt, :], in_=ps[0:nt, b, :]), reads=[r_ps[b]], writes=[r_stg])
                fw.dma("sp", o_sbv[row0:row0 + nt, :], stg[0:nt, :], reads=[r_stg])
                if is_s:
                    fw.op("dve", lambda v, b=b, ti=ti, nt=nt: v.tensor_copy(out=sv1[0:nt, ti, :], in_=ps[0:nt, b, :]), reads=[r_ps[b]], writes=[r_sk[1]])
                else:
                    if os.environ.get("KSKIPV1") is None:
                        fw.op("dve", lambda v, b=b, gt=gt: v.tensor_copy(out=v1[:, gt, :], in_=ps[:, b, :]), reads=[r_ps[b]], writes=[r_k1[gt]])
                b = gbank.next()
                for cc in range(2):
                    fw.op("pe", lambda t, cc=cc, b=b, nt=nt, col0=col0: t.matmul(ps[0:nt, b, 0:256], lhsT=cqT[:, cc, col0:col0 + nt],
                                                                                 rhs=wuq[:, cc, 512:768], start=(cc == 0), stop=(cc == 1)),
                          reads=[r_cqT, r_wuq], writes=[r_ps[b]])
                rope_tm(ps[0:nt, b, 0:256], 8, nt, tab, r_tab, qr_b[0:nt, :], [r_ps[b]], [r_qrb])
                for h in range(8):
                    fw.op("pe", lambda t, h=h, nt=nt: t.transpose(psb[0:32, h * 128:h * 128 + nt], qr_b[0:nt, h * 32:(h + 1) * 32],
                                                                  ident[0:nt, 0:nt]), reads=[r_qrb, r_ident], writes=[r_psb])
                fw.op("dve", lambda v, nt=nt, col0=col0: v.tensor_copy(out=qrT[:, :, col0:col0 + nt],
                                                                        in_=psb[0:32, :].rearrange("p (h t) -> p h t", h=8)[:, :, 0:nt]),
                      reads=[r_psb], writes=[r_qrT])
            fm_group([928 + 128 * i for i in range(4)],
                     lambda i, src, rb: evac_bd(qA, r_qA, i, src, rb))
            if is_s:
                fm_group([1440 + 128 * i for i in range(4)],
                         lambda i, src, rb: fw.op("dve", lambda v: v.tensor_copy(out=skT1[:, i, :], in_=src), reads=[rb], writes=[r_sk[0]]))
            else:
                t0 = tiles[0][1]
                g0 = t0 // 128

                def ev(i, src, rb):
                    fw.op("dve", lambda v: v.tensor_copy(out=kT1[:, i, t0:t0 + BLK], in_=src), reads=[rb], writes=[r_k1[g0], r_k1[g0 + 1]])
                fm_group([1440 + 128 * i for i in range(4)], ev)
            fm_group([416 + 128 * i for i in range(4)] + [2464 + 128 * i for i in range(4)],
                     lambda i, src, rb: fw.op("act", lambda a: a.activation(out=gT[:, i, :], in_=src, func=AF.Silu),
                                              reads=[rb], writes=[r_gT[i]]))
            for i in range(0, 4, 2):
                b = gbank.next()
                for j in range(2):
                    for cc in range(2):
                        fw.op("pe", lambda t, cc=cc, b=b, i=i, j=j: t.matmul(ps[:, b, j * BLK:(j + 1) * BLK], lhsT=wuq[:, cc, (i + j) * 128:(i + j + 1) * 128],
                                                                             rhs=cqT[:, cc, :], start=(cc == 0), stop=(cc == 1)),
                              reads=[r_cqT, r_wuq], writes=[r_ps[b]])
                fw.op("dve", lambda v, b=b, i=i: v.tensor_copy(out=qB[:, i:i + 2, :], in_=ps[:, b, :].rearrange("p (a t) -> p a t", a=2)),
                      reads=[r_ps[b]], writes=[r_qB])
            for h0 in range(0, 8, 2):
                b = gbank.next()
                for j in range(2):
                    h = h0 + j
                    pb = 64 * (h % 2)
                    mm(ps[:, b, j * BLK:(j + 1) * BLK], wukT[pb:pb + 64, h * 128:(h + 1) * 128], qB[pb:pb + 64, h // 2, :], True, True,
                       [r_qB, r_wuk], r_ps[b])
                for j in range(2):
                    fw.op("act", lambda a, b=b, h0=h0, j=j: a.copy(out=qlat[:, h0 + j, :], in_=ps[:, b, j * BLK:(j + 1) * BLK]),
                          reads=[r_ps[b]], writes=[r_qlat])

        def fin_softmax(slots, ob, db, has_den=True):
            if has_den:
                fw.op("dve", lambda v: v.reciprocal(out=rden[:, :], in_=ps[:, db, :]), reads=[r_ps[db]], writes=[r_rden])
            if len(slots) == 4 and slots[0][2] == 128:
                for par in (0, 1):
                    h, c0, n, gc, g0 = slots[par]
                    pb = 64 * par
                    ov = ps[pb:pb + 64, ob, :].rearrange("p (s q) -> p s q", s=4)[:, par:4:2, :]
                    gv = gT[pb:pb + 64, gc:gc + 2, g0:g0 + 128]
                    rgs = [r_gT[gc], r_gT[gc + 1]]
                    if has_den:
                        t, r_t = fint.next()
                        tv = t[pb:pb + 64, 0:256].rearrange("p (s q) -> p s q", s=2)
                        rv = rden[pb:pb + 64, :].rearrange("p (s q) -> p s q", s=4)[:, par:4:2, :]
                        fw.op("dve", lambda v, tv=tv, ov=ov, rv=rv: v.tensor_tensor(out=tv, in0=ov, in1=rv, op=ALU.mult),
                              reads=[r_ps[ob], r_rden], writes=[r_t])
                        fw.op("dve", lambda v, tv=tv, gv=gv: v.tensor_tensor(out=gv, in0=tv, in1=gv, op=ALU.mult), reads=[r_t] + rgs, writes=rgs)
                    else:
                        fw.op("dve", lambda v, ov=ov, gv=gv: v.tensor_tensor(out=gv, in0=ov, in1=gv, op=ALU.mult), reads=[r_ps[ob]] + rgs, writes=rgs)
                return
            for (h, c0, n, gc, g0) in slots:
                pb = 64 * (h % 2)
                if has_den:
                    t, r_t = fint.next()
                    fw.op("dve", lambda v, t=t, pb=pb, c0=c0, n=n: v.tensor_tensor(out=t[pb:pb + 64, 0:n], in0=ps[pb:pb + 64, ob, c0:c0 + n],
                                                                                   in1=rden[pb:pb + 64, c0:c0 + n], op=ALU.mult),
                          reads=[r_ps[ob], r_rden], writes=[r_t])
                    fw.op("dve", lambda v, t=t, pb=pb, n=n, gc=gc, g0=g0: v.tensor_tensor(out=gT[pb:pb + 64, gc, g0:g0 + n], in0=t[pb:pb + 64, 0:n],
                                                                                          in1=gT[pb:pb + 64, gc, g0:g0 + n], op=ALU.mult),
                          reads=[r_t, r_gT[gc]], writes=[r_gT[gc]])
                else:
                    fw.op("dve", lambda v, pb=pb, c0=c0, n=n, gc=gc, g0=g0: v.tensor_tensor(out=gT[pb:pb + 64, gc, g0:g0 + n], in0=ps[pb:pb + 64, ob, c0:c0 + n],
                                                                                            in1=gT[pb:pb + 64, gc, g0:g0 + n], op=ALU.mult),
                          reads=[r_ps[ob], r_gT[gc]], writes=[r_gT[gc]])

        odpair = RR([(4, 5), (2, 3)])
        sbank = RR([0, 1])
        abank = RR([2, 3])
        obank = RR([4, 5])

        class U:
            hbias = None
            cast = None
            dma = None
            prep = None
            mask = None
            pre = None
            scale = 1.0

        def sprep(u):
            if u.prep is not None:
                u.prep(u)

        def sdma(u):
            if u.dma is not None:
                u.dma(u)

        def scast(u):
            if u.cast is not None:
                u.cast(u)

        def sb_chain(units):
            def s0(u):
                u.sb = sbank.next()
                nk = u.nk
                nz = len(u.zmm)
                for i, (l, r, c0, n) in enumerate(u.zmm):
                    mm(ps[0:u.nk, u.sb, c0:c0 + n], l, r, (i == 0), (u.mask is None and i == nz - 1), u.reads, r_ps[u.sb])
                if u.mask is not None:
                    mm(ps[0:u.nk, u.sb, :], ident[0:u.nk, 0:u.nk], u.mask, False, True, [r_ident, u.rmask], r_ps[u.sb])
                e, r_e = ebuf.next()
                fw.op("act", lambda a, u=u, e=e: a.activation(out=e[0:u.nk, :], in_=ps[0:u.nk, u.sb, :], func=AF.Exp), reads=[r_ps[u.sb]], writes=[r_e])
                u.sp, u.r_sp = spb.next()
                fw.op("act", lambda a, u=u, e=e: a.activation(out=u.sp[0:u.nk, :], in_=e[0:u.nk, :], func=AF.Ln, bias=1.0), reads=[r_e], writes=[u.r_sp])

            def s1(u):
                u.ab = abank.next()
                for i, (l, r, c0, n) in enumerate(u.zmm):
                    mm(ps[0:u.nk, u.ab, c0:c0 + n], l, r, (i == 0), False, u.reads, r_ps[u.ab])
                if u.mask is not None:
                    mm(ps[0:u.nk, u.ab, :], ident[0:u.nk, 0:u.nk], u.mask, False, False, [r_ident, u.rmask], r_ps[u.ab])
                mm(ps[0:u.nk, u.ab, :], negTri[0:u.nk, 0:u.nk], u.sp[0:u.nk, :], False, u.first, [r_negTri, u.r_sp], r_ps[u.ab])
                if not u.first:
                    mm(ps[0:u.nk, u.ab, :], negOnes[:, 0:u.nk], Rbuf[:, :], False, True, [r_negOnes, r_R], r_ps[u.ab])
                if not u.last:
                    if u.first:
                        if u.nk < 128:
                            fw.op("pool", lambda g: g.memset(Rbuf[:, :], 0.0), writes=[r_R])
                        fw.op("pool", lambda g, u=u: g.tensor_copy(out=Rbuf[0:u.nk, :], in_=u.sp[0:u.nk, :]), reads=[u.r_sp], writes=[r_R])
                    else:
                        fw.op("pool", lambda g, u=u: g.tensor_tensor(out=Rbuf[0:u.nk, :], in0=Rbuf[0:u.nk, :], in1=u.sp[0:u.nk, :], op=ALU.add),
                              reads=[u.r_sp, r_R], writes=[r_R])
                u.w, u.r_w = pbf.next()
                fw.op("act", lambda a, u=u: a.activation(out=u.w[0:u.nk, :], in_=ps[0:u.nk, u.ab, :], func=AF.Exp), reads=[r_ps[u.ab]], writes=[u.r_w])

            def s2(u):
                if u.first:
                    u.chain["ob"] = obank.next()
                ob = u.chain["ob"]
                for vi, (lv, c0, n) in enumerate(u.vmm):
                    mm(ps[:, ob, c0:c0 + n], lv, u.w[0:u.nk, c0:c0 + n], (u.first and vi == 0), (u.last and vi == len(u.vmm) - 1),
                       u.vreads + [u.r_w], r_ps[ob])
                if u.last:
                    u.fin(ob)
            pipeline(units, [sdma, scast, sprep, s0, s1, s2], [0, 3, 4, 5, 6, 7])

        def sm_chain(units):
            def s0(u):
                u.sb = sbank.next()
                for (l, r, c0, n, st, sp_) in u.zmm:
                    o = ps[0:u.nk, u.sb, c0:c0 + n]
                    if len(r.shape) == 3:
                        o = o.rearrange("p (h q) -> p h q", h=r.shape[1])
                    mm(o, l, r, st, sp_, u.reads, r_ps[u.sb])
                if u.hbias is not None:
                    u.src, u.r_src = None, None
                elif u.pre is not None:
                    u.src, u.r_src = u.pre(u)
                else:
                    u.src, u.r_src = ps[0:u.nk, u.sb, :], r_ps[u.sb]

            def s1(u):
                u.p, u.r_p = pbf.next()
                if u.hbias is not None:
                    w_ = 512 // len(u.hbias)
                    for j, (bap, rb) in enumerate(u.hbias):
                        fw.op("act", lambda a, u=u, j=j, bap=bap, w_=w_: a.activation(out=u.p[0:u.nk, j * w_:(j + 1) * w_], in_=ps[0:u.nk, u.sb, j * w_:(j + 1) * w_],
                                                                                  func=AF.Exp, bias=bap, scale=u.scale),
                              reads=[r_ps[u.sb], rb], writes=[u.r_p])
                else:
                    fw.op("act", lambda a, u=u: a.activation(out=u.p[0:u.nk, :], in_=u.src, func=AF.Exp, scale=u.scale), reads=[u.r_src], writes=[u.r_p])

            def s2(u):
                if u.first:
                    u.chain["ob"], u.chain["db"] = odpair.next()
                ob, db = u.chain["ob"], u.chain["db"]
                for vi, (lv, c0, n) in enumerate(u.vmm):
                    mm(ps[:, ob, c0:c0 + n], lv, u.p[0:u.nk, c0:c0 + n], (u.first and vi == 0), (u.last and vi == len(u.vmm) - 1),
                       u.vreads + [u.r_p], r_ps[ob])
                mm(ps[:, db, :], onesB[0:u.nk, :], u.p[0:u.nk, :], u.first, u.last, [r_onesB, u.r_p], r_ps[db])
                if u.last:
                    u.fin(ob, db)
            pipeline(units, [sdma, scast, sprep, s0, s1, s2], [0, 3, 4, 5, 6, 7])

        def pre_add(u, in1, r_in1, scale=1.0, extra=None):
            t, r_t = tmpf.next()
            nk = u.nk
            H = in1.shape[1]
            n = 512 // H
            fw.op("dve", lambda v: v.scalar_tensor_tensor(out=t[0:nk, :].rearrange("p (h q) -> p h q", h=H),
                                                          in0=ps[0:nk, u.sb, :].rearrange("p (h q) -> p h q", h=H), scalar=scale,
                                                          in1=in1, op0=ALU.mult, op1=ALU.add), reads=[r_ps[u.sb]] + r_in1, writes=[r_t])
            if extra is not None:
                ex, r_ex = extra
                fw.op("dve", lambda v: v.tensor_tensor(out=t[0:nk, :].rearrange("p (h q) -> p h q", h=H),
                                                       in0=t[0:nk, :].rearrange("p (h q) -> p h q", h=H), in1=ex, op=ALU.add),
                      reads=[r_t] + r_ex, writes=[r_t])
            return t[0:nk, :], r_t

        def mla_fin_factory(slots_fn):
            def fin(ob, db):
                fw.op("act", lambda a: a.copy(out=latb[:, :], in_=ps[:, ob, :]), reads=[r_ps[ob]], writes=[r_latb])
                slots = slots_fn()
                gb = 6
                for (h, c0, n, gc, g0) in slots:
                    fw.op("pe", lambda t, h=h, c0=c0, n=n: t.matmul(ps[:, gb, c0:c0 + n], lhsT=wuv[:, (h // 2) * 128:(h // 2 + 1) * 128],
                                                                    rhs=latb[:, c0:c0 + n], start=True, stop=True),
                          reads=[r_latb, r_wuv], writes=[r_ps[gb]])
                fin_softmax(slots, gb, db)
            return fin

        def attn0_prompt(bi):
            all_sb = []
            all_mla = []
            for qt in (2 * bi, 2 * bi + 1):
                qc = (qt % 2) * 128
                for hg in range(2):
                    chain = {}
                    units = []
                    for kt in range(qt, -1, -1):
                        u = U()
                        u.nk = 128; u.chain = chain
                        u.first = (kt == qt); u.last = (kt == 0)
                        u.zmm = []
                        u.vmm = []
                        for pp in range(2):
                            u.zmm.append((kT1[:, 2 * hg + pp, kt * 128:(kt + 1) * 128], qA[:, 2 * hg + pp, qt % 2, :], pp * 256, 256))
                        for pp in range(2):
                            u.vmm.append((v1[:, kt, (2 * hg + pp) * 128:(2 * hg + pp + 1) * 128], pp * 256, 256))
                        u.mask = maskSB[:, :] if kt == qt else None
                        u.rmask = r_maskSB
                        u.reads = [r_k1[kt], r_qA]
                        u.vreads = [r_k1[kt]]
                        slots = [(4 * hg + s, s * 128, 128, 4 + (4 * hg + s) // 2, qc) for s in range(4)]
                        u.fin = (lambda ob, slots=slots: fin_softmax(slots, ob, None, has_den=False))
                        units.append(u)
                    all_sb += units
                    chain = {}
                    units = []
                    for kt in range(qt, -1, -1):
                        u = U()
                        u.nk = 128; u.chain = chain
                        u.first = (kt == qt); u.last = (kt == 0)
                        u.zmm = [(ckvT[:, kt * 128:(kt + 1) * 128], qlat[:, 4 * hg:4 * hg + 4, qc:qc + 128], 0, 512, True, False),
                                 (krT[0:32, kt * 128:(kt + 1) * 128], qrT[:, 4 * hg:4 * hg + 4, qc:qc + 128], 0, 512, False, True)]
                        u.reads = [r_k3[kt], r_qlat, r_qrT]
                        u.scale = A_SCALE
                        if kt == qt:
                            u.pre = lambda u: pre_add(u, maskCH[:, :].unsqueeze(1).broadcast_to([128, 4, 128]), [r_maskCH], scale=A_SCALE)
                            u.scale = 1.0
                        else:
                            u.pre = None
                        u.vmm = [(ckv_tm[:, kt, :], 0, 512)]
                        u.vreads = [r_k3[kt]]
                        slots = [(4 * hg + s, s * 128, 128, (4 * hg + s) // 2, qc) for s in range(4)]
                        u.fin = mla_fin_factory(lambda slots=slots: slots)
                        units.append(u)
                    all_mla += units
            sb_chain(all_sb)
            sm_chain(all_mla)


        HORD = (0, 2, 4, 6, 1, 3, 5, 7)

        def dma_kv_tile(u, kdram, vdram, s, t):
            u.kf, u.r_kf = cst_f.next()
            fw.dma("sp", u.kf[:, :], kdram[s, t * 128:(t + 1) * 128, :], writes=[u.r_kf])
            u.vf, u.r_vf = cst_f.next()
            fw.dma("sp", u.vf[:, :], vdram[s, t * 128:(t + 1) * 128, :], writes=[u.r_vf])

        def cast_kv_tile(u):
            u.kb, u.r_kb = kbf.next()
            cast(u.kb[:, :], u.kf[:, :], [u.r_kf], [u.r_kb])
            u.vb, u.r_vb = vbf.next()
            cast(u.vb[:, :], u.vf[:, :], [u.r_vf], [u.r_vb])

        def load_kv_tile(u):
            kb, r_kb = u.kb, u.r_kb
            for c in range(4):
                fw.op("pe", lambda t_, c=c, kb=kb: t_.transpose(psb[:, c * 128:(c + 1) * 128], kb[:, c * 128:(c + 1) * 128], ident[:, :]),
                      reads=[r_kb, r_ident], writes=[r_psb])
            kt_, r_kt = cKT.next()
            ee = EVAC_ENGS[0][cast_i[0] % len(EVAC_ENGS[0])]
            if ee == "act":
                fw.op("act", lambda a, kt_=kt_: a.copy(out=kt_[:, :, :], in_=psb[:, 0:512].rearrange("p (c t) -> p c t", c=4)),
                      reads=[r_psb], writes=[r_kt])
            else:
                fw.op("dve", lambda v, kt_=kt_: v.tensor_copy(out=kt_[:, :, :], in_=psb[:, 0:512].rearrange("p (c t) -> p c t", c=4)),
                      reads=[r_psb], writes=[r_kt])
            return kt_, r_kt, u.vb, u.r_vb

        def kv_units(s, ncache, kdram, vdram, qsrc, r_q, knew, vnew, r_new, slots):
            qb, r_qb = qbd[s]
            fw.op("pool", lambda g: g.memset(qb[:, :, :], 0.0), writes=[r_qb])
            so = (s % 2) * 64
            fw.op("pool", lambda g: g.tensor_copy(out=qb[0:64, :, 0:64], in_=qsrc[0:64, :, s // 2, so:so + 64]), reads=[r_q], writes=[r_qb])
            fw.op("pool", lambda g: g.tensor_copy(out=qb[64:128, :, 64:128], in_=qsrc[64:128, :, s // 2, 128 + so:128 + so + 64]), reads=[r_q], writes=[r_qb])
            chain = {}
            units = []
            u = U(); u.nk = 64; u.chain = chain; u.first = True; u.last = False; u.tile = ncache
            u.zmm = []; u.vmm = []
            for p in range(4):
                u.zmm.append((knew[:, p, 64 * s:64 * s + 64], qb[:, p, :], p * 128, 128))
                u.vmm.append((vnew[0:64, s, p * 128:(p + 1) * 128], p * 128, 128))
            u.reads = [r_new, r_qb]; u.vreads = [r_new]
            units.append(u)
            for t in range(ncache - 1, -1, -1):
                u = U(); u.nk = 128; u.chain = chain; u.first = False; u.last = (t == 0); u.tile = t
                u.dma = (lambda u, t=t: dma_kv_tile(u, kdram, vdram, s, t))
                u.cast = cast_kv_tile

                def prep(u, t=t):
                    kt_, r_kt, vb, r_vb = load_kv_tile(u)
                    u.zmm = []; u.vmm = []
                    for p in range(4):
                        u.zmm.append((kt_[:, p, :], qb[:, p, :], p * 128, 128))
                        u.vmm.append((vb[:, p * 128:(p + 1) * 128], p * 128, 128))
                    u.reads = [r_kt, r_qb]; u.vreads = [r_vb]
                u.prep = prep
                units.append(u)
            return units

        def zmm4(u):
            z = []
            for i, (l, r, c0, n) in enumerate(u.zmm):
                z.append((l, r, c0, n, i == 0, i == len(u.zmm) - 1))
            u.zmm = z

        def attn0_sample():
            all_sb = []
            all_mla = []
            for s in range(NSTR):
                slots = [(h, h * 64, 64, 4 + h // 2, 64 * s) for h in range(8)]
                units = kv_units(s, PAST // 128, c_sbk, c_sbv, qA, r_qA, skT1, sv1, r_sk[0], slots)
                units[0].mask = maskSB64[:, :]; units[0].rmask = r_maskSB64
                units[0].reads = [r_sk[0], qbd[s][1]]; units[0].vreads = [r_sk[1]]
                for u in units:
                    u.rmask = r_maskSB64
                    u.fin = (lambda ob, slots=slots: fin_softmax(slots, ob, None, has_den=False))
                all_sb += units
            sb_chain(all_sb)
            for s in range(NSTR):
                chain = {}
                units = []
                slots = [(h, h * 64, 64, h // 2, 64 * s) for h in range(8)]
                u = U(); u.nk = 64; u.chain = chain; u.first = True; u.last = False
                u.zmm = [(sckvT[:, 64 * s:64 * s + 64], qlat[:, :, 64 * s:64 * s + 64], 0, 512, True, False),
                         (skrT[0:32, 64 * s:64 * s + 64], qrT[:, :, 64 * s:64 * s + 64], 0, 512, False, True)]
                u.reads = [r_sk[5], r_qlat, r_qrT]; u.scale = A_SCALE
                u.vmm = [(sckv_tm[0:64, s, :], 0, 512)]; u.vreads = [r_sk[4]]
                units.append(u)
                for t in range(PAST // 128 - 1, -1, -1):
                    u = U(); u.nk = 128; u.chain = chain; u.first = False; u.last = (t == 0); u.scale = A_SCALE

                    def dma_(u, t=t, s=s):
                        u.cf, u.r_cf = cst_f.next()
                        fw.dma("sp", u.cf[:, 0:128], c_ckv[s, t * 128:(t + 1) * 128, :], writes=[u.r_cf])
                        fw.dma("sp", u.cf[:, 128:160], c_kr[s, t * 128:(t + 1) * 128, :], writes=[u.r_cf])
                    u.dma = dma_

                    def cast_(u):
                        u.vb, u.r_vb = vbf.next()
                        cast(u.vb[:, 0:160], u.cf[:, 0:160], [u.r_cf], [u.r_vb])
                    u.cast = cast_

                    def prep(u, t=t, s=s):
                        vb, r_vb = u.vb, u.r_vb
                        fw.op("pe", lambda t_, vb=vb: t_.transpose(psb[:, 0:128], vb[:, 0:128], ident[:, :]), reads=[r_vb, r_ident], writes=[r_psb])
                        fw.op("pe", lambda t_, vb=vb: t_.transpose(psb[0:32, 128:256], vb[:, 128:160], ident[:, :]), reads=[r_vb, r_ident], writes=[r_psb])
                        kt_, r_kt = cKT.next()
                        fw.op("dve", lambda v, kt_=kt_: v.tensor_copy(out=kt_[:, 0, :], in_=psb[:, 0:128]), reads=[r_psb], writes=[r_kt])
                        fw.op("dve", lambda v, kt_=kt_: v.tensor_copy(out=kt_[0:32, 1, :], in_=psb[0:32, 128:256]), reads=[r_psb], writes=[r_kt])
                        u.zmm = [(kt_[:, 0, :], qlat[:, :, 64 * s:64 * s + 64], 0, 512, True, False),
                                 (kt_[0:32, 1, :], qrT[:, :, 64 * s:64 * s + 64], 0, 512, False, True)]
                        u.reads = [r_kt, r_qlat, r_qrT]
                        u.vmm = [(vb[:, 0:128], 0, 512)]; u.vreads = [r_vb]
                    u.prep = prep
                    units.append(u)
                for u in units:
                    u.fin = mla_fin_factory(lambda slots=slots: slots)
                all_mla += units
            sm_chain(all_mla)

        r_kc = [Res() for _ in range(8)]

        def setup_l1_tables():
            tb2 = tblB[:].rearrange("p a b -> p (a b)")
            fw.dma("sp", tblB[:], bass.AP(relb.tensor, 1, [[1, 128], [513, 8], [1, 256]]), writes=[r_tblB])
            for j in range(4):
                fw.op("pe", lambda t_, j=j: t_.matmul(ps[:, j, :], lhsT=flipJ[:, :], rhs=tb2[:, j * 512:(j + 1) * 512], start=True, stop=True),
                      reads=[r_flip, r_tblB], writes=[r_ps[j]])
            for j in range(4):
                fw.op("dve", lambda v, j=j: v.tensor_copy(out=tb2[:, j * 512:(j + 1) * 512], in_=ps[:, j, :]), reads=[r_ps[j]], writes=[r_tblB])
            fw.op("dve", lambda v: v.tensor_tensor(out=tblB[:, :, 0:128], in0=tblB[:, :, 0:128],
                                                   in1=maskCH[:, :].unsqueeze(1).broadcast_to([128, 8, 128]), op=ALU.add),
                  reads=[r_tblB, r_maskCH], writes=[r_tblB])
            fw.dma("sp", cstB[:], bass.AP(relc.tensor, 0, [[0, 128], [1, 8]]), writes=[r_cstB])

        def phase_proj1(tiles, is_s, o_fk, o_fv, o_lf):
            for ti, (nt, row0, col0) in enumerate(tiles):
                gt = (row0 // 128) if not is_s else ti
                b = gbank.next()
                tm_proj(nt, col0, 512, 512, b)
                stg, r_stg = st512.next()
                fw.op("act", lambda a, b=b, stg=stg, nt=nt: a.copy(out=stg[0:nt, :], in_=ps[0:nt, b, :]), reads=[r_ps[b]], writes=[r_stg])
                if is_s:
                    fw.dma("sp", o_bk_s[ti, 448:512, :], stg[0:nt, :], reads=[r_stg])
                elif gt >= 12:
                    fw.dma("sp", o_bk_p[(gt - 12) * 128:(gt - 11) * 128, :], stg[0:nt, :], reads=[r_stg])
                b = gbank.next()
                tm_proj(nt, col0, 1024, 512, b)
                stg, r_stg = st512.next()
                fw.op("act", lambda a, b=b, stg=stg, nt=nt: a.copy(out=stg[0:nt, :], in_=ps[0:nt, b, :]), reads=[r_ps[b]], writes=[r_stg])
                if is_s:
                    fw.dma("sp", o_bv_s[ti, 448:512, :], stg[0:nt, :], reads=[r_stg])
                    fw.op("dve", lambda v, stg=stg, ti=ti, nt=nt: v.tensor_copy(out=sv1[0:nt, ti, :], in_=stg[0:nt, :]), reads=[r_stg], writes=[r_sk[1]])
                else:
                    if gt >= 12:
                        fw.dma("sp", o_bv_p[(gt - 12) * 128:(gt - 11) * 128, :], stg[0:nt, :], reads=[r_stg])
                    fw.op("dve", lambda v, stg=stg, gt=gt: v.tensor_copy(out=vc[:, gt % 8, :], in_=stg[:, :]), reads=[r_stg], writes=[r_kc[gt % 8]])
                b = gbank.next()
                tm_proj(nt, col0, 2560, 512, b)
                stg, r_stg = st512.next()
                fw.op("act", lambda a, b=b, stg=stg, nt=nt: a.copy(out=stg[0:nt, :], in_=ps[0:nt, b, :]), reads=[r_ps[b]], writes=[r_stg])
                fw.dma("sp", o_fk[row0:row0 + nt, :], stg[0:nt, :], reads=[r_stg])
                b = gbank.next()
                tm_proj(nt, col0, 3072, 512, b)
                stg, r_stg = st512.next()
                fw.op("act", lambda a, b=b, stg=stg, nt=nt: a.copy(out=stg[0:nt, :], in_=ps[0:nt, b, :]), reads=[r_ps[b]], writes=[r_stg])
                fw.dma("sp", o_fv[row0:row0 + nt, :], stg[0:nt, :], reads=[r_stg])
                if is_s:
                    fw.op("dve", lambda v, stg=stg, ti=ti, nt=nt: v.tensor_copy(out=sv2[0:nt, ti, :], in_=stg[0:nt, :]), reads=[r_stg], writes=[r_sk[3]])
                else:
                    fw.op("dve", lambda v, stg=stg, gt=gt: v.tensor_copy(out=v2[:, gt, :], in_=stg[:, :]), reads=[r_stg], writes=[r_k2[gt]])
                b = gbank.next()
                tm_proj(nt, col0, 3584, 8, b)
                lf, r_lf = lfb.next()
                fw.op("dve", lambda v, b=b, lf=lf, nt=nt: v.tensor_tensor(out=lf[0:nt, 0:8], in0=ps[0:nt, b, 0:8], in1=fb_bc[0:nt, :], op=ALU.add),
                      reads=[r_ps[b], r_fb], writes=[r_lf])
                fw.op("act", lambda a, lf=lf, nt=nt: a.activation(out=lf[0:nt, 8:16], in_=lf[0:nt, 0:8], func=AF.Exp, scale=-1.0), reads=[r_lf], writes=[r_lf])
                fw.op("act", lambda a, lf=lf, nt=nt: a.activation(out=lf[0:nt, 0:8], in_=lf[0:nt, 8:16], func=AF.Ln, bias=1.0), reads=[r_lf], writes=[r_lf])
                fw.op("dve", lambda v, lf=lf, nt=nt: v.tensor_scalar(out=lf[0:nt, 16:24], in0=lf[0:nt, 0:8], scalar1=-1.0, scalar2=None, op0=ALU.mult),
                      reads=[r_lf], writes=[r_lf])
                fw.dma("sp", o_lf[row0:row0 + nt, :], lf[0:nt, 16:24], reads=[r_lf])
                b2 = gbank.next()
                if not is_s:
                    fw.op("pe", lambda t_, b2=b2, lf=lf: t_.matmul(ps[:, b2, 0:8], lhsT=triF[:, :], rhs=lf[:, 16:24], start=True, stop=False),
                          reads=[r_tri, r_lf], writes=[r_ps[b2]])
                    fw.op("pe", lambda t_, b2=b2, lf=lf: t_.matmul(ps[:, b2, 8:16], lhsT=onesF[:, :], rhs=lf[:, 16:24], start=False, stop=True),
                          reads=[r_onesF, r_lf], writes=[r_ps[b2]])
                    fw.op("dve", lambda v, b2=b2, gt=gt: v.tensor_scalar(out=fxb[:, 0, gt, :], in0=ps[:, b2, 0:8], scalar1=-1.0, scalar2=None, op0=ALU.mult),
                          reads=[r_ps[b2]], writes=[r_fxb])
                    fw.op("dve", lambda v, b2=b2, gt=gt: v.tensor_copy(out=fxb[:, 2, gt, :], in_=ps[:, b2, 8:16]), reads=[r_ps[b2]], writes=[r_fxb])
                    fw.op("dve", lambda v, gt=gt: v.tensor_tensor(out=fxb[:, 1, gt, :], in0=fxb[:, 2, gt, :], in1=fxb[:, 0, gt, :], op=ALU.add),
                          reads=[r_fxb], writes=[r_fxb])
                else:
                    fw.op("pe", lambda t_, b2=b2, lf=lf: t_.matmul(ps[0:64, b2, 0:8], lhsT=triF[0:64, 0:64], rhs=lf[0:64, 16:24], start=True, stop=True),
                          reads=[r_tri, r_lf], writes=[r_ps[b2]])
                    fw.op("dve", lambda v, b2=b2, ti=ti: v.tensor_scalar(out=sfxn[0:64, ti, :], in0=ps[0:64, b2, 0:8], scalar1=-1.0, scalar2=None, op0=ALU.mult),
                          reads=[r_ps[b2]], writes=[r_sfxn])
            fm_group([0 + 128 * i for i in range(4)],
                     lambda i, src, rb: evac_bd(qA, r_qA, i, src, rb))
            fm_group([2048 + 128 * i for i in range(4)],
                     lambda i, src, rb: evac_bd(qBd, r_qB, i, src, rb))
            if is_s:
                fm_group([512 + 128 * i for i in range(4)],
                         lambda i, src, rb: fw.op("dve", lambda v: v.tensor_copy(out=skT1[:, i, :], in_=src), reads=[rb], writes=[r_sk[0]]))
                fm_group([2560 + 128 * i for i in range(4)],
                         lambda i, src, rb: fw.op("dve", lambda v: v.tensor_copy(out=skT2[:, i, :], in_=src), reads=[rb], writes=[r_sk[2]]))
            else:
                t0 = tiles[0][1]
                g0 = t0 // 128
                rc = (t0 % 1024)

                def evc(i, src, rb):
                    fw.op("dve", lambda v: v.tensor_copy(out=kTc[:, i, rc:rc + BLK], in_=src), reads=[rb], writes=[r_kc[g0 % 8], r_kc[(g0 + 1) % 8]])

                def evd(i, src, rb):
                    fw.op("dve", lambda v: v.tensor_copy(out=kT2[:, i, t0:t0 + BLK], in_=src), reads=[rb], writes=[r_k2[g0], r_k2[g0 + 1]])
                fm_group([512 + 128 * i for i in range(4)], evc)
                fm_group([2560 + 128 * i for i in range(4)], evd)
            fm_group([1536 + 128 * i for i in range(4)] + [3592 + 128 * i for i in range(4)],
                     lambda i, src, rb: fw.op("act", lambda a: a.activation(out=gT[:, i, :], in_=src, func=AF.Silu),
                                              reads=[rb], writes=[r_gT[i]]))

        def band_pre(u, dd, hs, nq):
            nk = u.nk
            H = hs.stop - hs.start
            if dd == 0:
                return pre_add(u, tblB[0:nk, hs, 0:nq], [r_tblB])
            if dd == 1:
                return pre_add(u, tblB[0:nk, hs, 128:128 + nq], [r_tblB])
            cst = cstB[0:nk, hs].unsqueeze(2).broadcast_to([nk, H, nq])
            if dd == 4:
                return pre_add(u, cst, [r_cstB], extra=(mask512[:, :].unsqueeze(1).broadcast_to([128, H, 128]), [r_mask512]))
            return pre_add(u, cst, [r_cstB])

        def attn1_prompt(bi):
            all_band = []
            all_fox = []
            for qt in (2 * bi, 2 * bi + 1):
                qc = (qt % 2) * 128
                biasq = _T(biasq2[:, qt % 2])
                for kt in range(qt - 1, -1, -1):
                    if kt == qt - 1:
                        fw.op("dve", lambda v, kt=kt, biasq=biasq: v.tensor_copy(out=biasq[:, kt, :], in_=fxb[:, 1, kt, :]), reads=[r_fxb], writes=[r_biasq])
                        fw.op("dve", lambda v, kt=kt, biasq=biasq: v.tensor_copy(out=accb[:, :], in_=fxb[:, 2, kt, :]), reads=[r_fxb], writes=[r_accb])
                    else:
                        fw.op("dve", lambda v, kt=kt, biasq=biasq: v.tensor_tensor(out=biasq[:, kt, :], in0=fxb[:, 1, kt, :], in1=accb[:, :], op=ALU.add),
                              reads=[r_fxb, r_accb], writes=[r_biasq])
                        fw.op("dve", lambda v, kt=kt, biasq=biasq: v.tensor_tensor(out=accb[:, :], in0=accb[:, :], in1=fxb[:, 2, kt, :], op=ALU.add),
                              reads=[r_fxb, r_accb], writes=[r_accb])
                for hg in range(2):
                    hs = slice(4 * hg, 4 * hg + 4)
                    chain = {}
                    units = []
                    kts = list(range(qt, max(-1, qt - 5), -1))
                    for kt in kts:
                        u = U(); u.nk = 128; u.chain = chain
                        u.first = (kt == kts[0]); u.last = (kt == kts[-1])
                        u.zmm = []; u.vmm = []
                        sl = kt % 8
                        for pp in range(2):
                            u.zmm.append((kTc[:, 2 * hg + pp, sl * 128:(sl + 1) * 128], qA[:, 2 * hg + pp, qt % 2, :], pp * 256, 256))
                        for pp in range(2):
                            u.vmm.append((vc[:, sl, (2 * hg + pp) * 128:(2 * hg + pp + 1) * 128], pp * 256, 256))
                        zmm4(u)
                        u.reads = [r_kc[sl], r_qA]; u.vreads = [r_kc[sl]]
                        if (qt - kt) in (2, 3):
                            u.hbias = [(cstB[:, 4 * hg + j:4 * hg + j + 1], r_cstB) for j in range(4)]
                        u.pre = (lambda u, dd=qt - kt, hs=hs: band_pre(u, dd, hs, 128))
                        slots = [(4 * hg + s_, s_ * 128, 128, (4 * hg + s_) // 2, qc) for s_ in range(4)]
                        u.fin = (lambda ob, db, slots=slots: fin_softmax(slots, ob, db))
                        units.append(u)
                    all_band += units
                    chain = {}
                    units = []
                    for kt in range(qt, -1, -1):
                        u = U(); u.nk = 128; u.chain = chain
                        u.first = (kt == qt); u.last = (kt == 0)
                        u.zmm = []; u.vmm = []
                        for pp in range(2):
                            u.zmm.append((kT2[:, 2 * hg + pp, kt * 128:(kt + 1) * 128], qBd[:, 2 * hg + pp, qt % 2, :], pp * 256, 256))
                        for pp in range(2):
                            u.vmm.append((v2[:, kt, (2 * hg + pp) * 128:(2 * hg + pp + 1) * 128], pp * 256, 256))
                        zmm4(u)
                        u.reads = [r_k2[kt], r_qB]; u.vreads = [r_k2[kt]]
                        if kt == qt:
                            u.pre = (lambda u, qt=qt, hs=hs: pre_add(u, fxb[:, 0, qt, hs].unsqueeze(2).broadcast_to([128, 4, 128]), [r_fxb],
                                                                      extra=(maskFX[:, :].unsqueeze(1).broadcast_to([128, 4, 128]), [r_maskFX])))
                        else:
                            if kt % 2 == 1:
                                u.hbias = [(biasq[:, kt, 4 * hg + j:4 * hg + j + 1], r_biasq) for j in range(4)]
                            u.pre = (lambda u, kt=kt, hs=hs, biasq=biasq: pre_add(u, biasq[:, kt, hs].unsqueeze(2).broadcast_to([128, 4, 128]), [r_biasq]))
                        slots = [(4 * hg + s_, s_ * 128, 128, 4 + (4 * hg + s_) // 2, qc) for s_ in range(4)]
                        u.fin = (lambda ob, db, slots=slots: fin_softmax(slots, ob, db))
                        units.append(u)
                    all_fox += units
            sm_chain(all_band)
            sm_chain(all_fox)

        def attn1_sample():
            hs8 = slice(0, 8)
            for s in range(NSTR):
                fw.dma("sp", o_bk_s[s, 0:448, :], c_bk[s, 64:512, :])
                fw.dma("sp", o_bv_s[s, 0:448, :], c_bv[s, 64:512, :])
                slots = [(h, h * 64, 64, h // 2, 64 * s) for h in range(8)]
                units = kv_units(s, 4, c_bk, c_bv, qA, r_qA, skT1, sv1, r_sk[0], slots)
                units[0].reads = [r_sk[0], qbd[s][1]]; units[0].vreads = [r_sk[1]]
                zmm4(units[0])
                units[0].pre = (lambda u: band_pre(u, 0, hs8, 64))
                for u in units[1:]:
                    dd = 4 - u.tile
                    op_ = u.prep

                    def prep2(u, op_=op_):
                        op_(u)
                        zmm4(u)
                    u.prep = prep2
                    u.pre = (lambda u, dd=dd: band_pre(u, 1 if dd == 1 else 2, hs8, 64))
                for u in units:
                    u.fin = (lambda ob, db, slots=slots: fin_softmax(slots, ob, db))
                sm_chain(units)
            for s in range(NSTR):
                fw.dma("sp", lfc[:], c_lf[s].rearrange("(t p) h -> p t h", p=128), writes=[r_lfc])
                lf2 = lfc[:].rearrange("p t h -> p (t h)")
                b2 = gbank.next()
                fw.op("pe", lambda t_, b2=b2: t_.matmul(ps[:, b2, 0:256], lhsT=triF[:, :], rhs=lf2, start=True, stop=False),
                      reads=[r_tri, r_lfc], writes=[r_ps[b2]])
                fw.op("pe", lambda t_, b2=b2: t_.matmul(ps[:, b2, 256:512], lhsT=onesF[:, :], rhs=lf2, start=False, stop=True),
                      reads=[r_onesF, r_lfc], writes=[r_ps[b2]])
                fw.op("dve", lambda v, b2=b2: v.tensor_copy(out=lf2, in_=ps[:, b2, 256:512]), reads=[r_ps[b2]], writes=[r_lfc])
                sf2 = sfx[:, 0:32, :].rearrange("p t h -> p (t h)")
                fw.op("dve", lambda v, b2=b2: v.tensor_tensor(out=sf2, in0=lf2, in1=ps[:, b2, 0:256], op=ALU.subtract),
                      reads=[r_ps[b2], r_lfc], writes=[r_sfx])
                for t in range(30, -1, -1):
                    if t == 30:
                        fw.op("dve", lambda v: v.tensor_copy(out=accb[:, :], in_=lfc[:, 31, :]), reads=[r_lfc], writes=[r_accb])
                    else:
                        fw.op("dve", lambda v, t=t: v.tensor_tensor(out=accb[:, :], in0=accb[:, :], in1=lfc[:, t + 1, :], op=ALU.add),
                              reads=[r_lfc, r_accb], writes=[r_accb])
                    fw.op("dve", lambda v, t=t: v.tensor_tensor(out=sfx[:, t, :], in0=sfx[:, t, :], in1=accb[:, :], op=ALU.add),
                          reads=[r_sfx, r_accb], writes=[r_sfx])
                slots = [(h, h * 64, 64, 4 + h // 2, 64 * s) for h in range(8)]
                units = kv_units(s, PAST // 128, c_fk, c_fv, qBd, r_qB, skT2, sv2, r_sk[2], slots)
                units[0].reads = [r_sk[2], qbd[s][1]]; units[0].vreads = [r_sk[3]]
                zmm4(units[0])
                units[0].pre = (lambda u, s=s: pre_add(u, sfxn[0:64, s, :].unsqueeze(2).broadcast_to([64, 8, 64]), [r_sfxn],
                                                       extra=(maskFX[0:64, 0:64].unsqueeze(1).broadcast_to([64, 8, 64]), [r_maskFX])))
                for u in units[1:]:
                    op_ = u.prep

                    def prep3(u, op_=op_):
                        op_(u)
                        zmm4(u)
                    u.prep = prep3
                    u.pre = (lambda u: pre_add(u, sfx[:, u.tile, :].unsqueeze(2).broadcast_to([128, 8, 64]), [r_sfx]))
                for u in units:
                    u.fin = (lambda ob, db, slots=slots: fin_softmax(slots, ob, db))
                sm_chain(units)

        def phase_out(l, tiles, xsrc, r_xsrc, ydst, r_ydst):
            for (nt, row0, col0) in tiles:
                b0 = gbank.next(); b1 = gbank.next()
                for half, b in ((0, b0), (1, b1)):
                    for c in range(8):
                        fw.op("pe", lambda t, c=c, b=b, half=half, nt=nt, col0=col0: t.matmul(ps[0:nt, b, :], lhsT=gT[:, c, col0:col0 + nt],
                                                                                               rhs=wout[:, c, half * 512:(half + 1) * 512],
                                                                                               start=(c == 0), stop=(c == 7)),
                              reads=[r_gT[c], r_wout], writes=[r_ps[b]])
                s, r_s = sm.next()
                fw.op("act", lambda a, s=s, nt=nt, b0=b0: a.activation(out=junk[0:nt, :], in_=ps[0:nt, b0, :], func=AF.Square, accum_out=s[0:nt, 4:5]),
                      reads=[r_ps[b0]], writes=[r_junk, r_s])
                fw.op("act", lambda a, s=s, nt=nt, b1=b1: a.activation(out=junk[0:nt, :], in_=ps[0:nt, b1, :], func=AF.Square, accum_out=s[0:nt, 5:6]),
                      reads=[r_ps[b1]], writes=[r_junk, r_s])
                fw.op("dve", lambda v, s=s, nt=nt: v.tensor_tensor(out=s[0:nt, 2:3], in0=s[0:nt, 4:5], in1=s[0:nt, 5:6], op=ALU.add), reads=[r_s], writes=[r_s])
                rs, r_rs = rstd_from_ss(s[0:nt, 2:3], D, nt, r_s)
                xt, r_xt = xin.next()
                fw.dma("act", xt[0:nt, :], xsrc[row0:row0 + nt, :], reads=[r_xsrc[row0 // 64]] if r_xsrc else [], writes=[r_xt])
                for half, b in ((0, b0), (1, b1)):
                    y, r_y = tmpf.next()
                    fw.op("dve", lambda v, half=half, b=b, y=y, rs=rs, nt=nt: v.scalar_tensor_tensor(
                        out=y[0:nt, :], in0=ps[0:nt, b, :], scalar=rs, in1=gpost[0:nt, half * 512:(half + 1) * 512],
                        op0=ALU.mult, op1=ALU.mult), reads=[r_ps[b], r_rs, r_gpost], writes=[r_y])
                    fw.op("pool", lambda g, y=y, xt=xt, nt=nt, half=half: g.tensor_tensor(out=xt[0:nt, half * 512:(half + 1) * 512], in0=y[0:nt, :],
                                                                                          in1=xt[0:nt, half * 512:(half + 1) * 512], op=ALU.add),
                          reads=[r_y, r_xt], writes=[r_xt])
                fw.dma("sp", ydst[row0:row0 + nt, :], xt[0:nt, :], reads=[r_xt], writes=[r_ydst[row0 // 64]] if r_ydst else [])

        r_x1p = [Res() for _ in range(SEQ // 64)]
        r_x1s = [Res() for _ in range(NSTR * DSEQ // 64)]
        ptiles = lambda bi: [(128, bi * BLK, 0), (128, bi * BLK + 128, 128)]
        stiles = [(64, 64 * s, 64 * s) for s in range(NSTR)]

        load_layer_weights(0)
        nblk = min(SEQ // BLK, NBLK_DBG)
        for bi in range(nblk):
            phase_norm(0, ptiles(bi), xp, None)
            phase_proj0(ptiles(bi), False, o_ckv_p, o_kr_p, o_sbk_p, o_sbv_p)
            if STAGES >= 2:
                attn0_prompt(bi)
            phase_out(0, ptiles(bi), xp, None, x1p, r_x1p)
        if STAGES >= 1:
            fw.barrier()
            phase_norm(0, stiles, xs, None)
            phase_proj0(stiles, True, o_ckv_s, o_kr_s, o_sbk_s, o_sbv_s)
            if STAGES >= 3:
                CAST_ENGS[0] = ("dve", "act", "dve", "pool")
                attn0_sample()
                CAST_ENGS[0] = ("pool", "dve", "act")
            phase_out(0, stiles, xs, None, x1s, r_x1s)
        if STAGES >= 4:
            load_layer_weights(1)
            setup_l1_tables()
            fw.op("pool", lambda g: g.memset(qBd[:].rearrange("p a t q -> p (a t q)"), 0.0), writes=[r_qB])
            fw.barrier()
            for bi in range(nblk):
                phase_norm(1, ptiles(bi), x1p, r_x1p)
                phase_proj1(ptiles(bi), False, o_fk_p, o_fv_p, o_lf_p)
                if STAGES >= 5:
                    attn1_prompt(bi)
                phase_out(1, ptiles(bi), x1p, r_x1p, y_p, None)
            fw.barrier()
            phase_norm(1, stiles, x1s, r_x1s)
            phase_proj1(stiles, True, o_fk_s, o_fv_s, o_lf_s)
            if STAGES >= 6:
                CAST_ENGS[0] = ("act", "pool", "act")
                EVAC_ENGS[0] = ("dve", "act")
                attn1_sample()
            phase_out(1, stiles, x1s, r_x1s, y_s, None)

        print("fw ops recorded:", getattr(fw, "nops", 0), {k: e.cnt for k, e in fw.E.items()})
        fw.finish()
        fw.emit()
    return nc


def _rope_tables():
    half = 16
    inv = (10000.0 ** (-np.arange(half, dtype=np.float32) / half)).astype(np.float32)

    def tab(pos):
        ang = pos.astype(np.float32)[:, None] * inv[None, :]
        c = np.cos(ang).astype(np.float32)
        s = np.sin(ang).astype(np.float32)
        return np.concatenate([c, c, -s, s], axis=1).astype(np.float32)
    tp = tab(np.arange(SEQ)).reshape(16, 128, 64).transpose(1, 0, 2).reshape(128, 16 * 64)
    tsm = tab(PAST + np.arange(DSEQ))
    return np.ascontiguousarray(tp), np.ascontiguousarray(tsm)


_NC_CACHE = {}


def kernel(**inp):
    f = lambda a: np.ascontiguousarray(np.asarray(a, dtype=np.float32))
    x_prompt = f(inp["x_prompt"]); x_sample = f(inp["x_sample"])
    rope_p, rope_s = _rope_tables()
    w_uq = f(inp["a_w_uq"])[0]
    w_uq_l = np.concatenate([w_uq[:, :, :64].reshape(256, 512), w_uq[:, :, 64:].reshape(256, 256)], axis=1)
    w_uk = f(inp["a_w_uk"])[0]
    w_ukT = np.transpose(w_uk, (2, 1, 0)).reshape(64, 1024)
    w_ukT = np.concatenate([w_ukT, w_ukT], axis=0)
    relb = f(inp["c_rel_bias"])[0]
    relb_pad = np.concatenate([relb, np.repeat(relb[:, -1:], 256, axis=1)], axis=1)
    shared = {
        "norm_pre": np.ascontiguousarray(f(inp["norm_pre"]).reshape(2, 8, 128).transpose(2, 0, 1).reshape(128, 16)), "norm_post": f(inp["norm_post"]),
        "w_in0": f(inp["w_in_even"])[0], "q_norm": f(inp["a_q_norm"]).reshape(1, 256),
        "w_uq": np.ascontiguousarray(w_uq_l), "kv_norm": f(inp["a_kv_norm"]).reshape(1, 128),
        "w_ukT": np.ascontiguousarray(w_ukT), "w_uv": f(inp["a_w_uv"])[0].reshape(128, 512),
        "w_out0": f(inp["w_out_even"])[0], "w_in1": f(inp["w_in_odd"])[0],
        "relb": np.ascontiguousarray(relb_pad), "fbias": f(inp["d_forget_bias"]).reshape(1, 8),
        "relc": np.ascontiguousarray(relb[:, 256].reshape(1, 8)),
        "w_out1": f(inp["w_out_odd"])[0], "rope_p": rope_p, "rope_s": rope_s,
    }
    caches = {k: f(inp[k])[0] for k in ("cache_mla_ckv", "cache_mla_krope", "cache_sb_k", "cache_sb_v", "cache_band_k",
                                        "cache_band_v", "cache_fox_k", "cache_fox_v", "cache_fox_logf")}
    in_maps = []
    for c in range(NCORES):
        sl = slice(NSTR * c, NSTR * (c + 1))
        m = dict(shared)
        m["xp"] = x_prompt[c]
        m["xs"] = x_sample[sl].reshape(NSTR * DSEQ, D)
        m["c_ckv"] = caches["cache_mla_ckv"][sl]
        m["c_kr"] = caches["cache_mla_krope"][sl]
        m["c_sbk"] = caches["cache_sb_k"][sl].reshape(NSTR, PAST, 512)
        m["c_sbv"] = caches["cache_sb_v"][sl].reshape(NSTR, PAST, 512)
        m["c_bk"] = caches["cache_band_k"][sl].reshape(NSTR, 512, 512)
        m["c_bv"] = caches["cache_band_v"][sl].reshape(NSTR, 512, 512)
        m["c_fk"] = caches["cache_fox_k"][sl].reshape(NSTR, PAST, 512)
        m["c_fv"] = caches["cache_fox_v"][sl].reshape(NSTR, PAST, 512)
        m["c_lf"] = caches["cache_fox_logf"][sl]
        in_maps.append({k: np.ascontiguousarray(v) for k, v in m.items()})
    if "nc" not in _NC_CACHE:
        _NC_CACHE["nc"] = build()
    nc = _NC_CACHE["nc"]
    if KCORES < NCORES:
        res = run_bass_kernel_spmd(nc, in_maps[:KCORES], core_ids=list(range(KCORES)))
        R = list(res.results) + [res.results[0]] * (NCORES - KCORES)
    else:
        res = run_bass_kernel_spmd(nc, in_maps, core_ids=list(range(NCORES)))
        R = res.results
    cat = lambda k: np.stack([R[c][k] for c in range(NCORES)], axis=0)
    B = NCORES
    SB = NCORES * NSTR
    outs = (
        cat("y_p").reshape(B, SEQ, D),
        cat("y_s").reshape(SB, DSEQ, D),
        cat("o_ckv_p").reshape(1, B, SEQ, 128), cat("o_kr_p").reshape(1, B, SEQ, 32),
        cat("o_sbk_p").reshape(1, B, SEQ, 8, 64), cat("o_sbv_p").reshape(1, B, SEQ, 8, 64),
        cat("o_bk_p").reshape(1, B, 512, 8, 64), cat("o_bv_p").reshape(1, B, 512, 8, 64),
        cat("o_fk_p").reshape(1, B, SEQ, 8, 64), cat("o_fv_p").reshape(1, B, SEQ, 8, 64), cat("o_lf_p").reshape(1, B, SEQ, 8),
        cat("o_ckv_s").reshape(1, SB, DSEQ, 128), cat("o_kr_s").reshape(1, SB, DSEQ, 32),
        cat("o_sbk_s").reshape(1, SB, DSEQ, 8, 64), cat("o_sbv_s").reshape(1, SB, DSEQ, 8, 64),
        cat("o_bk_s").reshape(1, SB, 512, 8, 64), cat("o_bv_s").reshape(1, SB, 512, 8, 64),
        cat("o_fk_s").reshape(1, SB, DSEQ, 8, 64), cat("o_fv_s").reshape(1, SB, DSEQ, 8, 64), cat("o_lf_s").reshape(1, SB, DSEQ, 8),
    )
    _NC_CACHE["x1"] = (cat("x1p"), cat("x1s"))
    return tuple(np.ascontiguousarray(o.astype(np.float32)) for o in outs)
```

```python
import numpy as np
from contextlib import ExitStack
import concourse.bass as bass
import concourse.mybir as mybir
from concourse.bass_utils import run_bass_kernel_spmd

F32 = mybir.dt.float32
BF16 = mybir.dt.bfloat16
AF = mybir.ActivationFunctionType
ALU = mybir.AluOpType

NCORES = 8
D = 1024
SEQ = 2048
NSTR = 4
DSEQ = 64
PAST = 4096
EPS = 1e-6
NEGM = -30000.0
A_SCALE = float((64 + 32) ** -0.5)
EVEN_IN = 2976
ODD_IN = 4104
BLK = 256
import os
STAGES = int(os.environ.get('KSTAGES', '6'))
NBLK_DBG = int(os.environ.get('KNBLK', '8'))
OPLIMIT = int(os.environ.get('KOPLIMIT', '100000000'))
KCORES = int(os.environ.get('KCORES', '8'))
KSAME = int(os.environ.get('KSAME', '0'))


class Res:
    __slots__ = ("w", "rs", "x", "rg")

    def __init__(self, x=False):
        self.w = None
        self.rs = []
        self.rg = None
        self.x = x


class Eng:
    def __init__(self, name, sem):
        self.name = name
        self.sem = sem
        self.cnt = 0
        self.waited = {}
        self.prog = []
        self.dq = []
        self.dcnt = []
        self.di = 0


class FW:
    def __init__(self, nc, es, ndq=8):
        self.nc = nc
        self.E = {}
        for name in ("pe", "act", "dve", "pool", "sp"):
            self.E[name] = Eng(name, es.enter_context(nc.semaphore("s_" + name)))
        for qn in ("sp", "act", "pool"):
            e = self.E[qn]
            for i in range(ndq):
                e.dq.append(es.enter_context(nc.semaphore(f"d_{qn}{i}")))
                e.dcnt.append(0)

    def _wait(self, eng, tok, force=False):
        if tok is None:
            return
        sem, val, src = tok
        if src == eng.name and src in ("pe", "sp") and not force:
            return
        key = id(sem)
        if eng.waited.get(key, 0) >= val:
            return
        eng.waited[key] = val
        eng.prog.append(("w", sem, val))

    def _deps(self, eng, reads, writes):
        for r in reads:
            self._wait(eng, r.w)
            if r.x:
                for t in r.rs:
                    if t[2] != eng.name:
                        self._wait(eng, t)
        for w in writes:
            if w.w is not None and (w.w[2] != eng.name or KSAME):
                self._wait(eng, w.w)
            for t in w.rs:
                if t[2] != eng.name or KSAME:
                    self._wait(eng, t)

    def _record(self, tok, reads, writes):
        for r in reads:
            r.rs.append(tok)
            if len(r.rs) > 16:
                best = {}
                for t in r.rs:
                    k = id(t[0])
                    if k not in best or best[k][1] < t[1]:
                        best[k] = t
                r.rs = list(best.values())
        for w in writes:
            w.w = tok
            w.rs = []

    def op(self, engname, fn, reads=(), writes=(), rg=None):
        self.nops = getattr(self, "nops", 0) + 1
        if self.nops > OPLIMIT:
            return None
        eng = self.E[engname]
        self._deps(eng, reads, writes)
        if engname == "pe":
            for w in writes:
                if rg is not None and w.rg is not None and w.rg != rg and w.w is not None and w.w[2] == "pe":
                    self._wait(eng, w.w, force=True)
                w.rg = rg
        eng.cnt += 1
        eng.prog.append(("i", fn, eng.sem, 1))
        tok = (eng.sem, eng.cnt, eng.name)
        self._record(tok, reads, writes)
        return tok

    def dma(self, q, out, in_, reads=(), writes=()):
        self.nops = getattr(self, "nops", 0) + 1
        if self.nops > OPLIMIT:
            return None
        eng = self.E[q]
        self._deps(eng, reads, writes)
        i = eng.di % len(eng.dq)
        eng.di += 1
        sem = eng.dq[i]
        if eng.dcnt[i] > 0:
            self._wait(eng, (sem, eng.dcnt[i], "dma"))
        eng.prog.append(("i", (lambda o, out=out, in_=in_: o.dma_start(out=out, in_=in_)), sem, 16))
        eng.dcnt[i] += 16
        tok = (sem, eng.dcnt[i], "dma")
        self._record(tok, reads, writes)
        return tok

    def barrier(self):
        toks = []
        for q in ("sp", "act", "pool"):
            e = self.E[q]
            for i, sem in enumerate(e.dq):
                if e.dcnt[i] > 0:
                    toks.append((sem, e.dcnt[i], "dma"))
        for n in ("pe", "act", "dve", "pool"):
            e = self.E[n]
            if e.cnt > 0:
                toks.append((e.sem, e.cnt, "x"))
        for n in ("pe", "act", "dve", "pool", "sp"):
            for t in toks:
                self._wait(self.E[n], t)

    def finish(self):
        sp = self.E["sp"]
        for q in ("sp", "act", "pool"):
            e = self.E[q]
            for i, sem in enumerate(e.dq):
                if e.dcnt[i] > 0:
                    self._wait(sp, (sem, e.dcnt[i], "dma"))
        for n in ("pe", "act", "dve", "pool"):
            e = self.E[n]
            if e.cnt > 0:
                self._wait(sp, (e.sem, e.cnt, "x"))

    def emit(self):
        nc = self.nc
        objs = {"pe": None}

        def run(eng):
            def body(obj):
                for a in eng.prog:
                    if a[0] == "w":
                        obj.wait_ge(a[1], a[2])
                    else:
                        a[1](obj).then_inc(a[2], a[3])
            return body
        with nc.Block() as block:
            block.tensor(run(self.E["pe"]))
            block.scalar(run(self.E["act"]))
            block.vector(run(self.E["dve"]))
            block.gpsimd(run(self.E["pool"]))
            block.sync(run(self.E["sp"]))


class RR:
    def __init__(self, items):
        self.items = items
        self.i = 0

    def next(self):
        it = self.items[self.i % len(self.items)]
        self.i += 1
        return it


def pipeline(units, stages, offsets=None, bg=None):
    n = len(units)
    ns = len(stages)
    if offsets is None:
        offsets = list(range(ns))
    for i in range(n + max(offsets)):
        for s, st in enumerate(stages):
            j = i - offsets[s]
            if 0 <= j < n:
                st(units[j])
        if bg is not None:
            next(bg, None)
    if bg is not None:
        for _ in bg:
            pass


def build():
    nc = bass.Bass("TRN2", target_bir_lowering=False)
    din = lambda n, s: nc.dram_tensor(n, s, F32, kind="ExternalInput").ap()
    dout = lambda n, s: nc.dram_tensor(n, s, F32, kind="ExternalOutput").ap()
    xp = din("xp", [SEQ, D])
    xs = din("xs", [NSTR * DSEQ, D])
    c_ckv = din("c_ckv", [NSTR, PAST, 128])
    c_kr = din("c_kr", [NSTR, PAST, 32])
    c_sbk = din("c_sbk", [NSTR, PAST, 512])
    c_sbv = din("c_sbv", [NSTR, PAST, 512])
    c_bk = din("c_bk", [NSTR, 512, 512])
    c_bv = din("c_bv", [NSTR, 512, 512])
    c_fk = din("c_fk", [NSTR, PAST, 512])
    c_fv = din("c_fv", [NSTR, PAST, 512])
    c_lf = din("c_lf", [NSTR, PAST, 8])
    norm_pre = din("norm_pre", [128, 16])
    norm_post = din("norm_post", [2, D])
    w_in0 = din("w_in0", [D, EVEN_IN])
    q_norm = din("q_norm", [1, 256])
    w_uq = din("w_uq", [256, 768])
    kv_norm = din("kv_norm", [1, 128])
    w_ukT = din("w_ukT", [128, 1024])
    w_uv = din("w_uv", [128, 512])
    w_out0 = din("w_out0", [D, D])
    w_in1 = din("w_in1", [D, ODD_IN])
    relb = din("relb", [8, 513])
    fbias = din("fbias", [1, 8])
    relc = din("relc", [1, 8])
    w_out1 = din("w_out1", [D, D])
    rope_p = din("rope_p", [128, 16 * 64])
    rope_s = din("rope_s", [64, 64])
    y_p = dout("y_p", [SEQ, D])
    y_s = dout("y_s", [NSTR * DSEQ, D])
    o_ckv_p = dout("o_ckv_p", [SEQ, 128]); o_kr_p = dout("o_kr_p", [SEQ, 32])
    o_sbk_p = dout("o_sbk_p", [SEQ, 512]); o_sbv_p = dout("o_sbv_p", [SEQ, 512])
    o_bk_p = dout("o_bk_p", [512, 512]); o_bv_p = dout("o_bv_p", [512, 512])
    o_fk_p = dout("o_fk_p", [SEQ, 512]); o_fv_p = dout("o_fv_p", [SEQ, 512]); o_lf_p = dout("o_lf_p", [SEQ, 8])
    o_ckv_s = dout("o_ckv_s", [NSTR * DSEQ, 128]); o_kr_s = dout("o_kr_s", [NSTR * DSEQ, 32])
    o_sbk_s = dout("o_sbk_s", [NSTR * DSEQ, 512]); o_sbv_s = dout("o_sbv_s", [NSTR * DSEQ, 512])
    o_bk_s = dout("o_bk_s", [NSTR, 512, 512]); o_bv_s = dout("o_bv_s", [NSTR, 512, 512])
    o_fk_s = dout("o_fk_s", [NSTR * DSEQ, 512]); o_fv_s = dout("o_fv_s", [NSTR * DSEQ, 512])
    o_lf_s = dout("o_lf_s", [NSTR * DSEQ, 8])
    x1p = dout("x1p", [SEQ, D])
    x1s = dout("x1s", [NSTR * DSEQ, D])

    with ExitStack() as es:
        fw = FW(nc, es)
        ARN = 105500
        AR = es.enter_context(nc.sbuf_tensor("AR", [128, ARN], BF16))
        ar = {"top": 0, "peak": 0}

        class _T:
            def __init__(self, ap):
                self.ap = ap
            def __getitem__(self, k):
                return self.ap[k]

        def sbt(n, s, d=F32):
            nel = int(np.prod(s[1:]))
            nb = nel * (4 if d == F32 else 2)
            nb = (nb + 63) // 64 * 64
            off = ar["top"]
            ar["top"] += nb // 2
            ar["peak"] = max(ar["peak"], ar["top"])
            assert ar["top"] <= ARN, (n, ar["top"])
            v = AR[:, off:off + nb // 2]
            if d == F32:
                v = v.bitcast(F32)
            v = v[:, 0:nel]
            if len(s) == 3:
                v = v.rearrange("p (a b) -> p a b", a=s[1])
            elif len(s) == 4:
                v = v.rearrange("p (a b c) -> p a b c", a=s[1], b=s[2])
            if s[0] < 128:
                v = v[0:s[0]]
            return _T(v)
        ps = es.enter_context(nc.psum_tensor("ps", [128, 7, 512], F32))
        psb = es.enter_context(nc.psum_tensor("psb", [128, 1024], BF16))
        r_ps = [Res(True) for _ in range(7)]
        r_psb = Res(True)
        bank = lambda i: ps[:, i, :]

        ident = sbt("ident", [128, 128], BF16); r_ident = Res()
        fw.op("pool", lambda g: g.memset(ident[:], 1.0), writes=[r_ident])
        fw.op("pool", lambda g: g.affine_select(out=ident[:], in_=ident[:], pattern=[[-1, 128]], compare_op=ALU.is_equal,
                                                fill=0.0, base=0, channel_multiplier=1), reads=[r_ident], writes=[r_ident])
        flipJ = sbt("flipJ", [128, 128], F32); r_flip = Res()
        fw.op("pool", lambda g: g.memset(flipJ[:], 1.0), writes=[r_flip])
        fw.op("pool", lambda g: g.affine_select(out=flipJ[:], in_=flipJ[:], pattern=[[1, 128]], compare_op=ALU.is_equal,
                                                fill=0.0, base=-127, channel_multiplier=1), reads=[r_flip], writes=[r_flip])
        triF = sbt("triF", [128, 128], F32); r_tri = Res()
        fw.op("pool", lambda g: g.memset(triF[:], 1.0), writes=[r_tri])
        fw.op("pool", lambda g: g.affine_select(out=triF[:], in_=triF[:], pattern=[[1, 128]], compare_op=ALU.is_ge,
                                                fill=0.0, base=0, channel_multiplier=-1), reads=[r_tri], writes=[r_tri])
        onesF = sbt("onesF", [128, 128], F32); r_onesF = Res()
        fw.op("pool", lambda g: g.memset(onesF[:], 1.0), writes=[r_onesF])
        onesB = sbt("onesB", [128, 128], BF16); r_onesB = Res()
        fw.op("pool", lambda g: g.memset(onesB[:], 1.0), writes=[r_onesB])
        negOnes = sbt("negOnes", [128, 128], BF16); r_negOnes = Res()
        fw.op("pool", lambda g: g.memset(negOnes[:], -1.0), writes=[r_negOnes])
        negTri = sbt("negTri", [128, 128], BF16); r_negTri = Res()
        fw.op("pool", lambda g: g.memset(negTri[:], -1.0), writes=[r_negTri])
        fw.op("pool", lambda g: g.affine_select(out=negTri[:], in_=negTri[:], pattern=[[-1, 128]], compare_op=ALU.is_ge,
                                                fill=0.0, base=0, channel_multiplier=1), reads=[r_negTri], writes=[r_negTri])
        maskSB = sbt("maskSB", [128, 512], BF16); r_maskSB = Res()
        fw.op("pool", lambda g: g.memset(maskSB[:], 0.0), writes=[r_maskSB])
        for a in range(4):
            fw.op("pool", lambda g, a=a: g.affine_select(out=maskSB[:, a * 128:(a + 1) * 128], in_=maskSB[:, a * 128:(a + 1) * 128],
                                                         pattern=[[1, 128]], compare_op=ALU.is_gt, fill=NEGM, base=0,
                                                         channel_multiplier=-1), reads=[r_maskSB], writes=[r_maskSB])
        maskSB64 = sbt("maskSB64", [64, 512], BF16); r_maskSB64 = Res()
        fw.op("pool", lambda g: g.memset(maskSB64[:], 0.0), writes=[r_maskSB64])
        for a in range(8):
            fw.op("pool", lambda g, a=a: g.affine_select(out=maskSB64[:, a * 64:(a + 1) * 64], in_=maskSB64[:, a * 64:(a + 1) * 64],
                                                         pattern=[[1, 64]], compare_op=ALU.is_gt, fill=NEGM, base=0,
                                                         channel_multiplier=-1), reads=[r_maskSB64], writes=[r_maskSB64])
        maskFX = sbt("maskFX", [128, 128], F32); r_maskFX = Res()
        fw.op("pool", lambda g: g.memset(maskFX[:], 0.0), writes=[r_maskFX])
        fw.op("pool", lambda g: g.affine_select(out=maskFX[:], in_=maskFX[:], pattern=[[1, 128]], compare_op=ALU.is_ge,
                                                fill=NEGM, base=0, channel_multiplier=-1), reads=[r_maskFX], writes=[r_maskFX])
        maskCH = sbt("maskCH", [128, 128], F32); r_maskCH = Res()
        fw.op("pool", lambda g: g.memset(maskCH[:], 0.0), writes=[r_maskCH])
        fw.op("pool", lambda g: g.memset(maskCH[64:128, 0:64], NEGM), reads=[r_maskCH], writes=[r_maskCH])
        mask512 = sbt("mask512", [128, 128], F32); r_mask512 = Res()
        fw.op("pool", lambda g: g.memset(mask512[:], 0.0), writes=[r_mask512])
        fw.op("pool", lambda g: g.memset(mask512[0:64, 64:128], NEGM), reads=[r_mask512], writes=[r_mask512])

        ropePt = RR([(sbt(f"ropeP{i}", [128, 64]), Res()) for i in range(2)])
        ropeS = sbt("ropeS", [64, 64]); r_ropeS = Res()
        fw.dma("sp", ropeS[:], rope_s[:, :], writes=[r_ropeS])
        gpre = sbt("gpre", [128, 2, 8]); r_gpre = Res()
        fw.dma("sp", gpre[:].rearrange("p a b -> p (a b)"), norm_pre[:, :], writes=[r_gpre])
        gpost = sbt("gpost", [128, D]); r_gpost = Res()
        qn_bc = sbt("qn_bc", [128, 256]); r_qn = Res()
        fw.dma("sp", qn_bc[:], bass.AP(q_norm.tensor, 0, [[0, 128], [1, 256]]), writes=[r_qn])
        kvn_bc = sbt("kvn_bc", [128, 128]); r_kvn = Res()
        fw.dma("sp", kvn_bc[:], bass.AP(kv_norm.tensor, 0, [[0, 128], [1, 128]]), writes=[r_kvn])
        fb_bc = sbt("fb_bc", [128, 8]); r_fb = Res()
        fw.dma("sp", fb_bc[:], bass.AP(fbias.tensor, 0, [[0, 128], [1, 8]]), writes=[r_fb])

        wbuf = sbt("wbuf", [128, 8, ODD_IN], BF16); r_w = Res()
        wout = sbt("wout", [128, 8, D], BF16); r_wout = Res()
        wuq = _T(wbuf[:, 0:2, 2976:2976 + 768]); r_wuq = r_w
        wukT = _T(wbuf[:, 2, 2976:2976 + 1024]); r_wuk = r_w
        wuv = _T(wbuf[:, 3, 2976:2976 + 512]); r_wuv = r_w
        WST = 1026
        wst_off = ar["top"]
        wst = RR([(sbt(f"wst{i}", [128, WST]), Res()) for i in range(2)])
        wst_end = ar["top"]
        ar["top"] = wst_off
        cast_i = [0]
        wq_i = [0]

        CAST_ENGS = [("pool", "dve", "act")]
        EVAC_ENGS = [("dve",)]

        def cast(out, in_, reads, writes, engs=None):
            engs = engs or CAST_ENGS[0]
            e = engs[cast_i[0] % len(engs)]
            cast_i[0] += 1
            if e == "act":
                fw.op("act", lambda a: a.copy(out=out, in_=in_), reads=reads, writes=writes)
            else:
                fw.op(e, lambda v: v.tensor_copy(out=out, in_=in_), reads=reads, writes=writes)

        def load_w(dst_fn, src, nrows, ncols, r_dst, q="sp"):
            for c in range(nrows // 128):
                for c0 in range(0, ncols, WST):
                    n = min(WST, ncols - c0)
                    st, r_st = wst.next()
                    wq_i[0] += 1
                    fw.dma(("sp", "act")[wq_i[0] % 2], st[:, 0:n], src[c * 128:(c + 1) * 128, c0:c0 + n], writes=[r_st])
                    cast(dst_fn(c)[:, c0:c0 + n], st[:, 0:n], [r_st], [r_dst], engs=("dve", "act"))

        pbf = RR([(sbt(f"pbf{i}", [128, 512], BF16), Res()) for i in range(3)])
        spb = RR([(sbt(f"spb{i}", [128, 512], BF16), Res()) for i in range(3)])
        ebuf = RR([(sbt(f"ebuf{i}", [128, 512]), Res()) for i in range(2)])
        ar["top"] = max(ar["top"], wst_end)
        KS = 24576
        ks_off = ar["top"]
        ks = sbt("ks", [128, KS], BF16)
        hT = sbt("hT", [128, 8, BLK], BF16); r_hT = Res()
        gT = sbt("gT", [128, 8, BLK], BF16); r_gT = [Res() for _ in range(8)]
        qA = sbt("qA", [128, 4, 2, 256], BF16); r_qA = Res()
        qBd = sbt("qB", [128, 4, 2, 256], BF16); r_qB = Res()
        qB = _T(qBd[:].rearrange("p a t q -> p (a t q)")[:, 0:4 * BLK].rearrange("p (a q) -> p a q", a=4))
        fw.op("pool", lambda g: g.memset(qA[:].rearrange("p a t q -> p (a t q)"), 0.0), writes=[r_qA])
        xin = RR([(sbt(f"xin{i}", [128, D]), Res()) for i in range(2)])
        hb = RR([(sbt(f"hb{i}", [128, D], BF16), Res()) for i in range(1)])
        junk = sbt("junk", [128, 512], BF16); r_junk = Res()
        sm = RR([(sbt(f"sm{i}", [128, 16]), Res()) for i in range(6)])
        st512 = RR([(sbt(f"st512_{i}", [128, 512]), Res()) for i in range(2)])
        tmpf = RR([(sbt(f"tmpf{i}", [128, 512]), Res()) for i in range(2)])
        fint = RR([(sbt(f"fint{i}", [128, 256]), Res()) for i in range(2)])
        Rbuf = sbt("Rbuf", [128, 512], BF16); r_R = Res()
        latb = sbt("latb", [128, 512], BF16); r_latb = Res()
        rden = sbt("rden", [128, 512]); r_rden = Res()
        ov0 = ar["top"]
        qlat = sbt("qlat", [128, 8, BLK], BF16); r_qlat = Res()
        qrT = sbt("qrT", [32, 8, BLK], BF16); r_qrT = Res()
        cqT = sbt("cqT", [128, 2, BLK], BF16); r_cqT = Res()
        cq_b = sbt("cq_b", [128, 256], BF16); r_cqb = Res()
        kvb = sbt("kvb", [128, 128 + 32], BF16); r_kvb = Res()
        qr_b = sbt("qr_b", [128, 256], BF16); r_qrb = Res()
        ropet = RR([(sbt(f"ropet{i}", [128, 256]), Res()) for i in range(2)])
        ov1 = ar["top"]
        ar["top"] = ov0
        fxb = sbt("fxb", [128, 4, 16, 8]); r_fxb = Res()
        biasq2 = sbt("biasq", [128, 2, 16, 8]); r_biasq = Res()
        accb = sbt("accb", [128, 8]); r_accb = Res()
        tblB = sbt("tblB", [128, 8, 256]); r_tblB = Res()
        cstB = sbt("cstB", [128, 8]); r_cstB = Res()
        lfb = RR([(sbt(f"lfb{i}", [128, 24]), Res()) for i in range(3)])
        lfc = sbt("lfc", [128, 32, 8]); r_lfc = Res()
        sfx = sbt("sfx", [128, 33, 8]); r_sfx = Res()
        ar["top"] = max(ar["top"], ov1)
        sv_top = ar["top"]
        ar["top"] = ks_off
        cst_f = RR([(sbt(f"cstf{i}", [128, 512]), Res()) for i in range(8)])
        vbf = RR([(sbt(f"vbf{i}", [128, 512], BF16), Res()) for i in range(6)])
        kbf = RR([(sbt(f"kbf{i}", [128, 512], BF16), Res()) for i in range(2)])
        cKT = RR([(sbt(f"cKT{i}", [128, 4, 128], BF16), Res()) for i in range(4)])
        sfxn = sbt("sfxn", [64, 4, 8]); r_sfxn = Res()
        qbd = [(sbt(f"qbd{i}", [128, 4, 128], BF16), Res()) for i in range(4)]
        skT1 = sbt("skT1", [128, 4, BLK], BF16); skT2 = sbt("skT2", [128, 4, BLK], BF16)
        sv1 = sbt("sv1", [64, 4, 512], BF16); sv2 = sbt("sv2", [64, 4, 512], BF16)
        sckvT = sbt("sckvT", [128, BLK], BF16); skrT = sbt("skrT", [32, BLK], BF16)
        sckv_tm = sbt("sckv_tm", [64, 4, 128], BF16)
        assert ar["top"] <= ks_off + KS
        ar["top"] = sv_top

        def ksv(off, shape):
            n = int(np.prod(shape))
            v = ks[:, off:off + n]
            if len(shape) == 2:
                return v.rearrange("p (a b) -> p a b", a=shape[0])
            return v
        kT1 = ksv(0, [4, SEQ]); v1 = ksv(8192, [16, 512])
        kT2 = ksv(8192, [4, SEQ]); v2 = ksv(16384, [16, 512])
        kTc = ksv(0, [4, 1024]); vc = ksv(4096, [8, 512])
        ckvT = ks[:, 16384:16384 + SEQ]
        krT = ks[:, 16384 + 2048:16384 + 4096]
        ckv_tm = ksv(16384 + 4096, [16, 128])
        r_k1 = [Res() for _ in range(20)]
        r_k2 = [Res() for _ in range(20)]
        r_k3 = [Res() for _ in range(20)]
        SOFF = 24576
        r_sk = [Res() for _ in range(8)]

        def load_layer_weights(l):
            fw.barrier()
            if l == 0:
                load_w(lambda c: wbuf[:, c, :], w_in0, D, EVEN_IN, r_w)
                load_w(lambda c: wout[:, c, :], w_out0, D, D, r_wout)
                load_w(lambda c: wuq[:, c, :], w_uq, 256, 768, r_wuq)
                load_w(lambda c: wukT[:, :], w_ukT, 128, 1024, r_wuk)
                load_w(lambda c: wuv[:, :], w_uv, 128, 512, r_wuv)
            else:
                load_w(lambda c: wbuf[:, c, :], w_in1, D, ODD_IN, r_w)
                load_w(lambda c: wout[:, c, :], w_out1, D, D, r_wout)
            fw.dma("sp", gpost[:], bass.AP(norm_post.tensor, l * D, [[0, 128], [1, D]]), writes=[r_gpost])
            fw.barrier()

        gbank = RR([6, 2, 3, 0, 1, 4, 5])

        def mm(o, l, r, st, sp_, reads, wres):
            K = l.shape[0]
            rg = None if K >= 128 else (l.base_partition(), K)
            fw.op("pe", lambda t: t.matmul(o, lhsT=l, rhs=r, start=st, stop=sp_), reads=reads, writes=[wres], rg=rg)

        def rstd_from_ss(ss_ap, n, nt, r_ss):
            s, r_s = sm.next()
            fw.op("dve", lambda v: v.tensor_scalar(out=s[0:nt, 0:1], in0=ss_ap, scalar1=1.0 / n, scalar2=EPS,
                                                   op0=ALU.mult, op1=ALU.add), reads=[r_ss], writes=[r_s])
            fw.op("act", lambda a_: a_.activation(out=s[0:nt, 3:4], in_=s[0:nt, 0:1], func=AF.Sqrt), reads=[r_s], writes=[r_s])
            fw.op("dve", lambda v: v.reciprocal(out=s[0:nt, 1:2], in_=s[0:nt, 3:4]), reads=[r_s], writes=[r_s])
            return s[0:nt, 1:2], r_s

        def sumsq(in_ap, nt, reads):
            s, r_s = sm.next()
            fw.op("act", lambda a: a.activation(out=junk[0:nt, 0:in_ap.shape[-1]], in_=in_ap, func=AF.Square,
                                                accum_out=s[0:nt, 2:3]), reads=reads, writes=[r_junk, r_s])
            return s[0:nt, 2:3], r_s

        def phase_norm_gen(l, tiles, xsrc, r_xsrc):
            for g_ in range(0, len(tiles), 2):
                yield from norm_pair_gen(l, tiles[g_:g_ + 2], xsrc, r_xsrc)

        def norm_pair_gen(l, tiles, xsrc, r_xsrc):
            st_ = []
            for (nt, row0, col0) in tiles:
                xt, r_xt = xin.next()
                fw.dma("act", xt[0:nt, :], xsrc[row0:row0 + nt, :], reads=[r_xsrc[row0 // 64]] if r_xsrc else [], writes=[r_xt])
                st_.append((xt, r_xt))
            yield
            yield
            for ti_, (nt, row0, col0) in enumerate(tiles):
                xt, r_xt = st_[ti_]
                s_, r_ss = sm.next()
                for hf in range(2):
                    fw.op("act", lambda a, hf=hf, s_=s_, xt=xt, nt=nt: a.activation(out=junk[0:nt, :], in_=xt[0:nt, hf * 512:(hf + 1) * 512], func=AF.Square,
                                                                                    accum_out=s_[0:nt, 4 + hf:5 + hf]), reads=[r_xt], writes=[r_junk, r_ss])
                yield
                fw.op("dve", lambda v, s_=s_, nt=nt: v.tensor_tensor(out=s_[0:nt, 2:3], in0=s_[0:nt, 4:5], in1=s_[0:nt, 5:6], op=ALU.add), reads=[r_ss], writes=[r_ss])
                s2_, r_s2 = sm.next()
                fw.op("dve", lambda v, s_=s_, s2_=s2_, nt=nt: v.tensor_scalar(out=s2_[0:nt, 0:1], in0=s_[0:nt, 2:3], scalar1=1.0 / D, scalar2=EPS,
                                                                               op0=ALU.mult, op1=ALU.add), reads=[r_ss], writes=[r_s2])
                yield
                fw.op("act", lambda a_, s2_=s2_, nt=nt: a_.activation(out=s2_[0:nt, 3:4], in_=s2_[0:nt, 0:1], func=AF.Sqrt), reads=[r_s2], writes=[r_s2])
                yield
                fw.op("dve", lambda v, s2_=s2_, nt=nt: v.reciprocal(out=s2_[0:nt, 1:2], in_=s2_[0:nt, 3:4]), reads=[r_s2], writes=[r_s2])
                rs, r_rs = s2_[0:nt, 1:2], r_s2
                h, r_h = hb.next()
                fw.op("dve", lambda v, h=h, xt=xt, rs=rs, nt=nt: v.tensor_scalar(out=h[0:nt, :], in0=xt[0:nt, :], scalar1=rs, scalar2=None,
                                                                                  op0=ALU.mult), reads=[r_xt, r_rs], writes=[r_h])
                yield
                for c in range(8):
                    fw.op("pe", lambda t, h=h, c=c, nt=nt: t.transpose(psb[:, c * 128:c * 128 + nt], h[0:nt, c * 128:(c + 1) * 128],
                                                                       ident[0:nt, 0:nt]), reads=[r_h, r_ident], writes=[r_psb])
                yield
                fw.op("dve", lambda v, nt=nt, col0=col0: v.tensor_tensor(
                    out=hT[:, :, col0:col0 + nt], in0=psb[:, :].rearrange("p (c t) -> p c t", c=8)[:, :, 0:nt],
                    in1=gpre[:, l, :].unsqueeze(2).broadcast_to([128, 8, nt]), op=ALU.mult),
                    reads=[r_psb, r_gpre], writes=[r_hT])
                yield

        def phase_norm(l, tiles, xsrc, r_xsrc):
            for _ in phase_norm_gen(l, tiles, xsrc, r_xsrc):
                pass

        def tm_proj(nt, col0, wcols, ncols, b):
            for c in range(8):
                fw.op("pe", lambda t, c=c: t.matmul(ps[0:nt, b, 0:ncols], lhsT=hT[:, c, col0:col0 + nt],
                                                    rhs=wbuf[:, c, wcols:wcols + ncols], start=(c == 0), stop=(c == 7)),
                      reads=[r_hT, r_w], writes=[r_ps[b]])

        def fm_proj(wcols, b, half):
            for c in range(8):
                fw.op("pe", lambda t, c=c: t.matmul(ps[:, b, half * BLK:(half + 1) * BLK], lhsT=wbuf[:, c, wcols:wcols + 128],
                                                    rhs=hT[:, c, :], start=(c == 0), stop=(c == 7)),
                      reads=[r_hT, r_w], writes=[r_ps[b]])

        def fm_group(wcol_list, evac):
            for i in range(0, len(wcol_list), 2):
                b = gbank.next()
                n = min(2, len(wcol_list) - i)
                for j in range(n):
                    fm_proj(wcol_list[i + j], b, j)
                for j in range(n):
                    evac(i + j, ps[:, b, j * BLK:(j + 1) * BLK], r_ps[b])

        def rope_tm(src, nheads, nt, tab, r_tab, out_ap, reads, writes):
            t1, r_t1 = ropet.next()
            t2, r_t2 = ropet.next()
            n = nheads * 32
            s3 = src.rearrange("p (h r) -> p h r", h=nheads)
            a1 = t1[0:nt, 0:n].rearrange("p (h r) -> p h r", h=nheads)
            a2 = t2[0:nt, 0:n].rearrange("p (h r) -> p h r", h=nheads)
            cosb = tab[:, 0:32].unsqueeze(1).broadcast_to([nt, nheads, 32])
            sin_lo = tab[:, 32:48].unsqueeze(1).broadcast_to([nt, nheads, 16])
            sin_hi = tab[:, 48:64].unsqueeze(1).broadcast_to([nt, nheads, 16])
            fw.op("dve", lambda v: v.tensor_tensor(out=a1, in0=s3, in1=cosb, op=ALU.mult), reads=reads + [r_tab], writes=[r_t1])
            fw.op("dve", lambda v: v.tensor_tensor(out=a2[:, :, 0:16], in0=s3[:, :, 16:32], in1=sin_lo, op=ALU.mult),
                  reads=reads + [r_tab], writes=[r_t2])
            fw.op("dve", lambda v: v.tensor_tensor(out=a2[:, :, 16:32], in0=s3[:, :, 0:16], in1=sin_hi, op=ALU.mult),
                  reads=reads + [r_tab], writes=[r_t2])
            fw.op("dve", lambda v: v.tensor_tensor(out=out_ap, in0=t1[0:nt, 0:n], in1=t2[0:nt, 0:n], op=ALU.add),
                  reads=[r_t1, r_t2], writes=writes)

        def evac_bd(dst, r_dst, i, src, rb):
            fw.op("act", lambda a: a.activation(out=dst[0:64, i, :, 0:128], in_=src[0:64, :].rearrange("p (t q) -> p t q", t=2),
                                                func=AF.Copy, scale=0.125), reads=[rb], writes=[r_dst])
            fw.op("act", lambda a: a.activation(out=dst[64:128, i, :, 128:256], in_=src[64:128, :].rearrange("p (t q) -> p t q", t=2),
                                                func=AF.Copy, scale=0.125), reads=[rb], writes=[r_dst])

        def phase_proj0(tiles, is_s, o_ckv, o_kr, o_sbk, o_sbv):
            for ti, (nt, row0, col0) in enumerate(tiles):
                gt = (row0 // 128) if not is_s else ti
                b = gbank.next()
                tm_proj(nt, col0, 0, 416, b)
                ssq, r_ssq = sumsq(ps[0:nt, b, 0:256], nt, [r_ps[b]])
                rq, r_rq = rstd_from_ss(ssq, 256, nt, r_ssq)
                fw.op("dve", lambda v, b=b, rq=rq, nt=nt: v.scalar_tensor_tensor(out=cq_b[0:nt, :], in0=ps[0:nt, b, 0:256], scalar=rq,
                                                                                  in1=qn_bc[0:nt, :], op0=ALU.mult, op1=ALU.mult),
                      reads=[r_ps[b], r_rq, r_qn], writes=[r_cqb])
                ssk, r_ssk = sumsq(ps[0:nt, b, 256:384], nt, [r_ps[b]])
                rk, r_rk = rstd_from_ss(ssk, 128, nt, r_ssk)
                stg, r_stg = st512.next()
                fw.op("dve", lambda v, b=b, rk=rk, nt=nt, stg=stg: v.scalar_tensor_tensor(out=stg[0:nt, 0:128], in0=ps[0:nt, b, 256:384], scalar=rk,
                                                                                           in1=kvn_bc[0:nt, :], op0=ALU.mult, op1=ALU.mult),
                      reads=[r_ps[b], r_rk, r_kvn], writes=[r_stg])
                if is_s:
                    tab, r_tab = ropeS[0:nt, :], r_ropeS
                else:
                    tb_, r_tab = ropePt.next()
                    fw.dma("sp", tb_[:, :], rope_p[:, gt * 64:(gt + 1) * 64], writes=[r_tab])
                    tab = tb_[0:nt, :]
                rope_tm(ps[0:nt, b, 384:416], 1, nt, tab, r_tab, stg[0:nt, 128:160], [r_ps[b]], [r_stg])
                fw.dma("sp", o_ckv[row0:row0 + nt, :], stg[0:nt, 0:128], reads=[r_stg])
                fw.dma("sp", o_kr[row0:row0 + nt, :], stg[0:nt, 128:160], reads=[r_stg])
                fw.op("act", lambda a, stg=stg, nt=nt: a.copy(out=kvb[0:nt, :], in_=stg[0:nt, 0:160]), reads=[r_stg], writes=[r_kvb])
                if is_s:
                    fw.op("pool", lambda g, ti=ti, nt=nt: g.tensor_copy(out=sckv_tm[0:nt, ti, :], in_=kvb[0:nt, 0:128]),
                          reads=[r_kvb], writes=[r_sk[4]])
                else:
                    fw.op("pool", lambda g, gt=gt: g.tensor_copy(out=ckv_tm[:, gt, :], in_=kvb[:, 0:128]), reads=[r_kvb], writes=[r_k3[gt]])
                for j in range(2):
                    fw.op("pe", lambda t, j=j, nt=nt: t.transpose(psb[:, j * 128:j * 128 + nt], cq_b[0:nt, j * 128:(j + 1) * 128],
                                                                  ident[0:nt, 0:nt]), reads=[r_cqb, r_ident], writes=[r_psb])
                fw.op("pe", lambda t, nt=nt: t.transpose(psb[:, 256:256 + nt], kvb[0:nt, 0:128], ident[0:nt, 0:nt]),
                      reads=[r_kvb, r_ident], writes=[r_psb])
                fw.op("pe", lambda t, nt=nt: t.transpose(psb[0:32, 384:384 + nt], kvb[0:nt, 128:160], ident[0:nt, 0:nt]),
                      reads=[r_kvb, r_ident], writes=[r_psb])
                fw.op("dve", lambda v, nt=nt, col0=col0: v.tensor_copy(out=cqT[:, :, col0:col0 + nt],
                                                                        in_=psb[:, 0:256].rearrange("p (c t) -> p c t", c=2)[:, :, 0:nt]),
                      reads=[r_psb], writes=[r_cqT])
                if is_s:
                    dckv, dkr, rr = sckvT[:, col0:col0 + nt], skrT[:, col0:col0 + nt], r_sk[5]
                else:
                    dckv, dkr, rr = ckvT[:, row0:row0 + nt], krT[0:32, row0:row0 + nt], r_k3[gt]
                fw.op("dve", lambda v, nt=nt, dckv=dckv: v.tensor_copy(out=dckv, in_=psb[:, 256:256 + nt]), reads=[r_psb], writes=[rr])
                fw.op("dve", lambda v, nt=nt, dkr=dkr: v.tensor_copy(out=dkr, in_=psb[0:32, 384:384 + nt]), reads=[r_psb], writes=[rr])
                b = gbank.next()
                tm_proj(nt, col0, 1440, 512, b)
                stg, r_stg = st512.next()
                fw.op("act", lambda a, b=b, stg=stg, nt=nt: a.copy(out=stg[0:nt, :], in_=ps[0:nt, b, :]), reads=[r_ps[b]], writes=[r_stg])
                fw.dma("sp", o_sbk[row0:row0 + nt, :], stg[0:nt, :], reads=[r_stg])
                b = gbank.next()
                tm_proj(nt, col0, 1952, 512, b)
                stg, r_stg = st512.next()
                fw.op("act", lambda a, b=b, stg=stg, nt=nt: a.copy(out=stg[0:nt, :], in_=ps[0:nt, b, :]), reads=[r_ps[b]], writes=[r_stg])
                fw.dma("sp", o_sbv[row0:row0 + nt, :], stg[0:nt, :], reads=[r_stg])
                if is_s:
                    fw.op("dve", lambda v, b=b, ti=ti, nt=nt: v.tensor_copy(out=sv1[0:nt, ti, :], in_=ps[0:nt, b, :]), reads=[r_ps[b]], writes=[r_sk[1]])
                else:
                    if os.environ.get("KSKIPV1") is None:
                        fw.op("dve", lambda v, b=b, gt=gt: v.tensor_copy(out=v1[:, gt, :], in_=ps[:, b, :]), reads=[r_ps[b]], writes=[r_k1[gt]])
                b = gbank.next()
                for cc in range(2):
                    fw.op("pe", lambda t, cc=cc, b=b, nt=nt, col0=col0: t.matmul(ps[0:nt, b, 0:256], lhsT=cqT[:, cc, col0:col0 + nt],
                                                                                 rhs=wuq[:, cc, 512:768], start=(cc == 0), stop=(cc == 1)),
                          reads=[r_cqT, r_wuq], writes=[r_ps[b]])
                rope_tm(ps[0:nt, b, 0:256], 8, nt, tab, r_tab, qr_b[0:nt, :], [r_ps[b]], [r_qrb])
                for h in range(8):
                    fw.op("pe", lambda t, h=h, nt=nt: t.transpose(psb[0:32, h * 128:h * 128 + nt], qr_b[0:nt, h * 32:(h + 1) * 32],
                                                                  ident[0:nt, 0:nt]), reads=[r_qrb, r_ident], writes=[r_psb])
                fw.op("dve", lambda v, nt=nt, col0=col0: v.tensor_copy(out=qrT[:, :, col0:col0 + nt],
                                                                        in_=psb[0:32, :].rearrange("p (h t) -> p h t", h=8)[:, :, 0:nt]),
                      reads=[r_psb], writes=[r_qrT])
            fm_group([928 + 128 * i for i in range(4)],
                     lambda i, src, rb: evac_bd(qA, r_qA, i, src, rb))
            if is_s:
                fm_group([1440 + 128 * i for i in range(4)],
                         lambda i, src, rb: fw.op("dve", lambda v: v.tensor_copy(out=skT1[:, i, :], in_=src), reads=[rb], writes=[r_sk[0]]))
            else:
                t0 = tiles[0][1]
                g0 = t0 // 128

                def ev(i, src, rb):
                    fw.op("dve", lambda v: v.tensor_copy(out=kT1[:, i, t0:t0 + BLK], in_=src), reads=[rb], writes=[r_k1[g0], r_k1[g0 + 1]])
                fm_group([1440 + 128 * i for i in range(4)], ev)
            fm_group([416 + 128 * i for i in range(4)] + [2464 + 128 * i for i in range(4)],
                     lambda i, src, rb: fw.op("act", lambda a: a.activation(out=gT[:, i, :], in_=src, func=AF.Silu),
                                              reads=[rb], writes=[r_gT[i]]))
            for i in range(0, 4, 2):
                b = gbank.next()
                for j in range(2):
                    for cc in range(2):
                        fw.op("pe", lambda t, cc=cc, b=b, i=i, j=j: t.matmul(ps[:, b, j * BLK:(j + 1) * BLK], lhsT=wuq[:, cc, (i + j) * 128:(i + j + 1) * 128],
                                                                             rhs=cqT[:, cc, :], start=(cc == 0), stop=(cc == 1)),
                              reads=[r_cqT, r_wuq], writes=[r_ps[b]])
                fw.op("dve", lambda v, b=b, i=i: v.tensor_copy(out=qB[:, i:i + 2, :], in_=ps[:, b, :].rearrange("p (a t) -> p a t", a=2)),
                      reads=[r_ps[b]], writes=[r_qB])
            for h0 in range(0, 8, 2):
                b = gbank.next()
                for j in range(2):
                    h = h0 + j
                    pb = 64 * (h % 2)
                    mm(ps[:, b, j * BLK:(j + 1) * BLK], wukT[pb:pb + 64, h * 128:(h + 1) * 128], qB[pb:pb + 64, h // 2, :], True, True,
                       [r_qB, r_wuk], r_ps[b])
                for j in range(2):
                    fw.op("act", lambda a, b=b, h0=h0, j=j: a.copy(out=qlat[:, h0 + j, :], in_=ps[:, b, j * BLK:(j + 1) * BLK]),
                          reads=[r_ps[b]], writes=[r_qlat])

        def fin_softmax(slots, ob, db, has_den=True):
            if has_den:
                fw.op("dve", lambda v: v.reciprocal(out=rden[:, :], in_=ps[:, db, :]), reads=[r_ps[db]], writes=[r_rden])
            if len(slots) == 4 and slots[0][2] == 128:
                for par in (0, 1):
                    h, c0, n, gc, g0 = slots[par]
                    pb = 64 * par
                    ov = ps[pb:pb + 64, ob, :].rearrange("p (s q) -> p s q", s=4)[:, par:4:2, :]
                    gv = gT[pb:pb + 64, gc:gc + 2, g0:g0 + 128]
                    rgs = [r_gT[gc], r_gT[gc + 1]]
                    if has_den:
                        t, r_t = fint.next()
                        tv = t[pb:pb + 64, 0:256].rearrange("p (s q) -> p s q", s=2)
                        rv = rden[pb:pb + 64, :].rearrange("p (s q) -> p s q", s=4)[:, par:4:2, :]
                        fw.op("dve", lambda v, tv=tv, ov=ov, rv=rv: v.tensor_tensor(out=tv, in0=ov, in1=rv, op=ALU.mult),
                              reads=[r_ps[ob], r_rden], writes=[r_t])
                        fw.op("dve", lambda v, tv=tv, gv=gv: v.tensor_tensor(out=gv, in0=tv, in1=gv, op=ALU.mult), reads=[r_t] + rgs, writes=rgs)
                    else:
                        fw.op("dve", lambda v, ov=ov, gv=gv: v.tensor_tensor(out=gv, in0=ov, in1=gv, op=ALU.mult), reads=[r_ps[ob]] + rgs, writes=rgs)
                return
            for (h, c0, n, gc, g0) in slots:
                pb = 64 * (h % 2)
                if has_den:
                    t, r_t = fint.next()
                    fw.op("dve", lambda v, t=t, pb=pb, c0=c0, n=n: v.tensor_tensor(out=t[pb:pb + 64, 0:n], in0=ps[pb:pb + 64, ob, c0:c0 + n],
                                                                                   in1=rden[pb:pb + 64, c0:c0 + n], op=ALU.mult),
                          reads=[r_ps[ob], r_rden], writes=[r_t])
                    fw.op("dve", lambda v, t=t, pb=pb, n=n, gc=gc, g0=g0: v.tensor_tensor(out=gT[pb:pb + 64, gc, g0:g0 + n], in0=t[pb:pb + 64, 0:n],
                                                                                          in1=gT[pb:pb + 64, gc, g0:g0 + n], op=ALU.mult),
                          reads=[r_t, r_gT[gc]], writes=[r_gT[gc]])
                else:
                    fw.op("dve", lambda v, pb=pb, c0=c0, n=n, gc=gc, g0=g0: v.tensor_tensor(out=gT[pb:pb + 64, gc, g0:g0 + n], in0=ps[pb:pb + 64, ob, c0:c0 + n],
                                                                                            in1=gT[pb:pb + 64, gc, g0:g0 + n], op=ALU.mult),
                          reads=[r_ps[ob], r_gT[gc]], writes=[r_gT[gc]])

        odpair = RR([(4, 5), (2, 3)])
        sbank = RR([0, 1])
        abank = RR([2, 3])
        obank = RR([4, 5])

        class U:
            hbias = None
            cast = None
            dma = None
            prep = None
            mask = None
            pre = None
            scale = 1.0

        def sprep(u):
            if u.prep is not None:
                u.prep(u)

        def sdma(u):
            if u.dma is not None:
                u.dma(u)

        def scast(u):
            if u.cast is not None:
                u.cast(u)

        def sb_chain(units, bg=None):
            def s0(u):
                u.sb = sbank.next()
                nk = u.nk
                nz = len(u.zmm)
                for i, (l, r, c0, n) in enumerate(u.zmm):
                    mm(ps[0:u.nk, u.sb, c0:c0 + n], l, r, (i == 0), (u.mask is None and i == nz - 1), u.reads, r_ps[u.sb])
                if u.mask is not None:
                    mm(ps[0:u.nk, u.sb, :], ident[0:u.nk, 0:u.nk], u.mask, False, True, [r_ident, u.rmask], r_ps[u.sb])
                e, r_e = ebuf.next()
                fw.op("act", lambda a, u=u, e=e: a.activation(out=e[0:u.nk, :], in_=ps[0:u.nk, u.sb, :], func=AF.Exp), reads=[r_ps[u.sb]], writes=[r_e])
                u.sp, u.r_sp = spb.next()
                fw.op("act", lambda a, u=u, e=e: a.activation(out=u.sp[0:u.nk, :], in_=e[0:u.nk, :], func=AF.Ln, bias=1.0), reads=[r_e], writes=[u.r_sp])

            def s1(u):
                u.ab = abank.next()
                for i, (l, r, c0, n) in enumerate(u.zmm):
                    mm(ps[0:u.nk, u.ab, c0:c0 + n], l, r, (i == 0), False, u.reads, r_ps[u.ab])
                if u.mask is not None:
                    mm(ps[0:u.nk, u.ab, :], ident[0:u.nk, 0:u.nk], u.mask, False, False, [r_ident, u.rmask], r_ps[u.ab])
                mm(ps[0:u.nk, u.ab, :], negTri[0:u.nk, 0:u.nk], u.sp[0:u.nk, :], False, u.first, [r_negTri, u.r_sp], r_ps[u.ab])
                if not u.first:
                    mm(ps[0:u.nk, u.ab, :], negOnes[:, 0:u.nk], Rbuf[:, :], False, True, [r_negOnes, r_R], r_ps[u.ab])
                if not u.last:
                    if u.first:
                        if u.nk < 128:
                            fw.op("pool", lambda g: g.memset(Rbuf[:, :], 0.0), writes=[r_R])
                        fw.op("pool", lambda g, u=u: g.tensor_copy(out=Rbuf[0:u.nk, :], in_=u.sp[0:u.nk, :]), reads=[u.r_sp], writes=[r_R])
                    else:
                        fw.op("pool", lambda g, u=u: g.tensor_tensor(out=Rbuf[0:u.nk, :], in0=Rbuf[0:u.nk, :], in1=u.sp[0:u.nk, :], op=ALU.add),
                              reads=[u.r_sp, r_R], writes=[r_R])
                u.w, u.r_w = pbf.next()
                fw.op("act", lambda a, u=u: a.activation(out=u.w[0:u.nk, :], in_=ps[0:u.nk, u.ab, :], func=AF.Exp), reads=[r_ps[u.ab]], writes=[u.r_w])

            def s2(u):
                if u.first:
                    u.chain["ob"] = obank.next()
                ob = u.chain["ob"]
                for vi, (lv, c0, n) in enumerate(u.vmm):
                    mm(ps[:, ob, c0:c0 + n], lv, u.w[0:u.nk, c0:c0 + n], (u.first and vi == 0), (u.last and vi == len(u.vmm) - 1),
                       u.vreads + [u.r_w], r_ps[ob])
                if u.last:
                    u.fin(ob)
            pipeline(units, [sdma, scast, sprep, s0, s1, s2], [0, 3, 4, 5, 6, 7], bg=bg)

        def sm_chain(units, bg=None):
            def s0(u):
                u.sb = sbank.next()
                for (l, r, c0, n, st, sp_) in u.zmm:
                    o = ps[0:u.nk, u.sb, c0:c0 + n]
                    if len(r.shape) == 3:
                        o = o.rearrange("p (h q) -> p h q", h=r.shape[1])
                    mm(o, l, r, st, sp_, u.reads, r_ps[u.sb])
                if u.hbias is not None:
                    u.src, u.r_src = None, None
                elif u.pre is not None:
                    u.src, u.r_src = u.pre(u)
                else:
                    u.src, u.r_src = ps[0:u.nk, u.sb, :], r_ps[u.sb]

            def s1(u):
                u.p, u.r_p = pbf.next()
                if u.hbias is not None:
                    w_ = 512 // len(u.hbias)
                    for j, (bap, rb) in enumerate(u.hbias):
                        fw.op("act", lambda a, u=u, j=j, bap=bap, w_=w_: a.activation(out=u.p[0:u.nk, j * w_:(j + 1) * w_], in_=ps[0:u.nk, u.sb, j * w_:(j + 1) * w_],
                                                                                  func=AF.Exp, bias=bap, scale=u.scale),
                              reads=[r_ps[u.sb], rb], writes=[u.r_p])
                else:
                    fw.op("act", lambda a, u=u: a.activation(out=u.p[0:u.nk, :], in_=u.src, func=AF.Exp, scale=u.scale), reads=[u.r_src], writes=[u.r_p])

            def s2(u):
                if u.first:
                    u.chain["ob"], u.chain["db"] = odpair.next()
                ob, db = u.chain["ob"], u.chain["db"]
                for vi, (lv, c0, n) in enumerate(u.vmm):
                    mm(ps[:, ob, c0:c0 + n], lv, u.p[0:u.nk, c0:c0 + n], (u.first and vi == 0), (u.last and vi == len(u.vmm) - 1),
                       u.vreads + [u.r_p], r_ps[ob])
                mm(ps[:, db, :], onesB[0:u.nk, :], u.p[0:u.nk, :], u.first, u.last, [r_onesB, u.r_p], r_ps[db])
                if u.last:
                    u.fin(ob, db)
            pipeline(units, [sdma, scast, sprep, s0, s1, s2], [0, 3, 4, 5, 6, 7], bg=bg)

        def pre_add(u, in1, r_in1, scale=1.0, extra=None):
            t, r_t = tmpf.next()
            nk = u.nk
            H = in1.shape[1]
            n = 512 // H
            fw.op("dve", lambda v: v.scalar_tensor_tensor(out=t[0:nk, :].rearrange("p (h q) -> p h q", h=H),
                                                          in0=ps[0:nk, u.sb, :].rearrange("p (h q) -> p h q", h=H), scalar=scale,
                                                          in1=in1, op0=ALU.mult, op1=ALU.add), reads=[r_ps[u.sb]] + r_in1, writes=[r_t])
            if extra is not None:
                ex, r_ex = extra
                fw.op("dve", lambda v: v.tensor_tensor(out=t[0:nk, :].rearrange("p (h q) -> p h q", h=H),
                                                       in0=t[0:nk, :].rearrange("p (h q) -> p h q", h=H), in1=ex, op=ALU.add),
                      reads=[r_t] + r_ex, writes=[r_t])
            return t[0:nk, :], r_t

        def mla_fin_factory(slots_fn):
            def fin(ob, db):
                fw.op("act", lambda a: a.copy(out=latb[:, :], in_=ps[:, ob, :]), reads=[r_ps[ob]], writes=[r_latb])
                slots = slots_fn()
                gb = 6
                for (h, c0, n, gc, g0) in slots:
                    fw.op("pe", lambda t, h=h, c0=c0, n=n: t.matmul(ps[:, gb, c0:c0 + n], lhsT=wuv[:, (h // 2) * 128:(h // 2 + 1) * 128],
                                                                    rhs=latb[:, c0:c0 + n], start=True, stop=True),
                          reads=[r_latb, r_wuv], writes=[r_ps[gb]])
                fin_softmax(slots, gb, db)
            return fin

        def attn0_prompt(bi, bg=None):
            all_sb = []
            all_mla = []
            for qt in (2 * bi, 2 * bi + 1):
                qc = (qt % 2) * 128
                for hg in range(2):
                    chain = {}
                    units = []
                    for kt in range(qt, -1, -1):
                        u = U()
                        u.nk = 128; u.chain = chain
                        u.first = (kt == qt); u.last = (kt == 0)
                        u.zmm = []
                        u.vmm = []
                        for pp in range(2):
                            u.zmm.append((kT1[:, 2 * hg + pp, kt * 128:(kt + 1) * 128], qA[:, 2 * hg + pp, qt % 2, :], pp * 256, 256))
                        for pp in range(2):
                            u.vmm.append((v1[:, kt, (2 * hg + pp) * 128:(2 * hg + pp + 1) * 128], pp * 256, 256))
                        u.mask = maskSB[:, :] if kt == qt else None
                        u.rmask = r_maskSB
                        u.reads = [r_k1[kt], r_qA]
                        u.vreads = [r_k1[kt]]
                        slots = [(4 * hg + s, s * 128, 128, 4 + (4 * hg + s) // 2, qc) for s in range(4)]
                        u.fin = (lambda ob, slots=slots: fin_softmax(slots, ob, None, has_den=False))
                        units.append(u)
                    all_sb += units
                    chain = {}
                    units = []
                    for kt in range(qt, -1, -1):
                        u = U()
                        u.nk = 128; u.chain = chain
                        u.first = (kt == qt); u.last = (kt == 0)
                        u.zmm = [(ckvT[:, kt * 128:(kt + 1) * 128], qlat[:, 4 * hg:4 * hg + 4, qc:qc + 128], 0, 512, True, False),
                                 (krT[0:32, kt * 128:(kt + 1) * 128], qrT[:, 4 * hg:4 * hg + 4, qc:qc + 128], 0, 512, False, True)]
                        u.reads = [r_k3[kt], r_qlat, r_qrT]
                        u.scale = A_SCALE
                        if kt == qt:
                            u.pre = lambda u: pre_add(u, maskCH[:, :].unsqueeze(1).broadcast_to([128, 4, 128]), [r_maskCH], scale=A_SCALE)
                            u.scale = 1.0
                        else:
                            u.pre = None
                        u.vmm = [(ckv_tm[:, kt, :], 0, 512)]
                        u.vreads = [r_k3[kt]]
                        slots = [(4 * hg + s, s * 128, 128, (4 * hg + s) // 2, qc) for s in range(4)]
                        u.fin = mla_fin_factory(lambda slots=slots: slots)
                        units.append(u)
                    all_mla += units
            sb_chain(all_sb, bg=bg)
            sm_chain(all_mla)


        HORD = (0, 2, 4, 6, 1, 3, 5, 7)

        def dma_kv_tile(u, kdram, vdram, s, t):
            u.kf, u.r_kf = cst_f.next()
            fw.dma("sp", u.kf[:, :], kdram[s, t * 128:(t + 1) * 128, :], writes=[u.r_kf])
            u.vf, u.r_vf = cst_f.next()
            fw.dma("sp", u.vf[:, :], vdram[s, t * 128:(t + 1) * 128, :], writes=[u.r_vf])

        def cast_kv_tile(u):
            u.kb, u.r_kb = kbf.next()
            cast(u.kb[:, :], u.kf[:, :], [u.r_kf], [u.r_kb])
            u.vb, u.r_vb = vbf.next()
            cast(u.vb[:, :], u.vf[:, :], [u.r_vf], [u.r_vb])

        def load_kv_tile(u):
            kb, r_kb = u.kb, u.r_kb
            for c in range(4):
                fw.op("pe", lambda t_, c=c, kb=kb: t_.transpose(psb[:, c * 128:(c + 1) * 128], kb[:, c * 128:(c + 1) * 128], ident[:, :]),
                      reads=[r_kb, r_ident], writes=[r_psb])
            kt_, r_kt = cKT.next()
            ee = EVAC_ENGS[0][cast_i[0] % len(EVAC_ENGS[0])]
            if ee == "act":
                fw.op("act", lambda a, kt_=kt_: a.copy(out=kt_[:, :, :], in_=psb[:, 0:512].rearrange("p (c t) -> p c t", c=4)),
                      reads=[r_psb], writes=[r_kt])
            else:
                fw.op("dve", lambda v, kt_=kt_: v.tensor_copy(out=kt_[:, :, :], in_=psb[:, 0:512].rearrange("p (c t) -> p c t", c=4)),
                      reads=[r_psb], writes=[r_kt])
            return kt_, r_kt, u.vb, u.r_vb

        def kv_units(s, ncache, kdram, vdram, qsrc, r_q, knew, vnew, r_new, slots):
            qb, r_qb = qbd[s]
            fw.op("pool", lambda g: g.memset(qb[:, :, :], 0.0), writes=[r_qb])
            so = (s % 2) * 64
            fw.op("pool", lambda g: g.tensor_copy(out=qb[0:64, :, 0:64], in_=qsrc[0:64, :, s // 2, so:so + 64]), reads=[r_q], writes=[r_qb])
            fw.op("pool", lambda g: g.tensor_copy(out=qb[64:128, :, 64:128], in_=qsrc[64:128, :, s // 2, 128 + so:128 + so + 64]), reads=[r_q], writes=[r_qb])
            chain = {}
            units = []
            u = U(); u.nk = 64; u.chain = chain; u.first = True; u.last = False; u.tile = ncache
            u.zmm = []; u.vmm = []
            for p in range(4):
                u.zmm.append((knew[:, p, 64 * s:64 * s + 64], qb[:, p, :], p * 128, 128))
                u.vmm.append((vnew[0:64, s, p * 128:(p + 1) * 128], p * 128, 128))
            u.reads = [r_new, r_qb]; u.vreads = [r_new]
            units.append(u)
            for t in range(ncache - 1, -1, -1):
                u = U(); u.nk = 128; u.chain = chain; u.first = False; u.last = (t == 0); u.tile = t
                u.dma = (lambda u, t=t: dma_kv_tile(u, kdram, vdram, s, t))
                u.cast = cast_kv_tile

                def prep(u, t=t):
                    kt_, r_kt, vb, r_vb = load_kv_tile(u)
                    u.zmm = []; u.vmm = []
                    for p in range(4):
                        u.zmm.append((kt_[:, p, :], qb[:, p, :], p * 128, 128))
                        u.vmm.append((vb[:, p * 128:(p + 1) * 128], p * 128, 128))
                    u.reads = [r_kt, r_qb]; u.vreads = [r_vb]
                u.prep = prep
                units.append(u)
            return units

        def zmm4(u):
            z = []
            for i, (l, r, c0, n) in enumerate(u.zmm):
                z.append((l, r, c0, n, i == 0, i == len(u.zmm) - 1))
            u.zmm = z

        def attn0_sample():
            all_sb = []
            all_mla = []
            for s in range(NSTR):
                slots = [(h, h * 64, 64, 4 + h // 2, 64 * s) for h in range(8)]
                units = kv_units(s, PAST // 128, c_sbk, c_sbv, qA, r_qA, skT1, sv1, r_sk[0], slots)
                units[0].mask = maskSB64[:, :]; units[0].rmask = r_maskSB64
                units[0].reads = [r_sk[0], qbd[s][1]]; units[0].vreads = [r_sk[1]]
                for u in units:
                    u.rmask = r_maskSB64
                    u.fin = (lambda ob, slots=slots: fin_softmax(slots, ob, None, has_den=False))
                all_sb += units
            sb_chain(all_sb)
            for s in range(NSTR):
                chain = {}
                units = []
                slots = [(h, h * 64, 64, h // 2, 64 * s) for h in range(8)]
                u = U(); u.nk = 64; u.chain = chain; u.first = True; u.last = False
                u.zmm = [(sckvT[:, 64 * s:64 * s + 64], qlat[:, :, 64 * s:64 * s + 64], 0, 512, True, False),
                         (skrT[0:32, 64 * s:64 * s + 64], qrT[:, :, 64 * s:64 * s + 64], 0, 512, False, True)]
                u.reads = [r_sk[5], r_qlat, r_qrT]; u.scale = A_SCALE
                u.vmm = [(sckv_tm[0:64, s, :], 0, 512)]; u.vreads = [r_sk[4]]
                units.append(u)
                for t in range(PAST // 128 - 1, -1, -1):
                    u = U(); u.nk = 128; u.chain = chain; u.first = False; u.last = (t == 0); u.scale = A_SCALE

                    def dma_(u, t=t, s=s):
                        u.cf, u.r_cf = cst_f.next()
                        fw.dma("sp", u.cf[:, 0:128], c_ckv[s, t * 128:(t + 1) * 128, :], writes=[u.r_cf])
                        fw.dma("sp", u.cf[:, 128:160], c_kr[s, t * 128:(t + 1) * 128, :], writes=[u.r_cf])
                    u.dma = dma_

                    def cast_(u):
                        u.vb, u.r_vb = vbf.next()
                        cast(u.vb[:, 0:160], u.cf[:, 0:160], [u.r_cf], [u.r_vb])
                    u.cast = cast_

                    def prep(u, t=t, s=s):
                        vb, r_vb = u.vb, u.r_vb
                        fw.op("pe", lambda t_, vb=vb: t_.transpose(psb[:, 0:128], vb[:, 0:128], ident[:, :]), reads=[r_vb, r_ident], writes=[r_psb])
                        fw.op("pe", lambda t_, vb=vb: t_.transpose(psb[0:32, 128:256], vb[:, 128:160], ident[:, :]), reads=[r_vb, r_ident], writes=[r_psb])
                        kt_, r_kt = cKT.next()
                        fw.op("dve", lambda v, kt_=kt_: v.tensor_copy(out=kt_[:, 0, :], in_=psb[:, 0:128]), reads=[r_psb], writes=[r_kt])
                        fw.op("dve", lambda v, kt_=kt_: v.tensor_copy(out=kt_[0:32, 1, :], in_=psb[0:32, 128:256]), reads=[r_psb], writes=[r_kt])
                        u.zmm = [(kt_[:, 0, :], qlat[:, :, 64 * s:64 * s + 64], 0, 512, True, False),
                                 (kt_[0:32, 1, :], qrT[:, :, 64 * s:64 * s + 64], 0, 512, False, True)]
                        u.reads = [r_kt, r_qlat, r_qrT]
                        u.vmm = [(vb[:, 0:128], 0, 512)]; u.vreads = [r_vb]
                    u.prep = prep
                    units.append(u)
                for u in units:
                    u.fin = mla_fin_factory(lambda slots=slots: slots)
                all_mla += units
            sm_chain(all_mla)

        r_kc = [Res() for _ in range(8)]

        def setup_l1_tables():
            tb2 = tblB[:].rearrange("p a b -> p (a b)")
            fw.dma("sp", tblB[:], bass.AP(relb.tensor, 1, [[1, 128], [513, 8], [1, 256]]), writes=[r_tblB])
            for j in range(4):
                fw.op("pe", lambda t_, j=j: t_.matmul(ps[:, j, :], lhsT=flipJ[:, :], rhs=tb2[:, j * 512:(j + 1) * 512], start=True, stop=True),
                      reads=[r_flip, r_tblB], writes=[r_ps[j]])
            for j in range(4):
                fw.op("dve", lambda v, j=j: v.tensor_copy(out=tb2[:, j * 512:(j + 1) * 512], in_=ps[:, j, :]), reads=[r_ps[j]], writes=[r_tblB])
            fw.op("dve", lambda v: v.tensor_tensor(out=tblB[:, :, 0:128], in0=tblB[:, :, 0:128],
                                                   in1=maskCH[:, :].unsqueeze(1).broadcast_to([128, 8, 128]), op=ALU.add),
                  reads=[r_tblB, r_maskCH], writes=[r_tblB])
            fw.dma("sp", cstB[:], bass.AP(relc.tensor, 0, [[0, 128], [1, 8]]), writes=[r_cstB])

        def phase_proj1(tiles, is_s, o_fk, o_fv, o_lf):
            for ti, (nt, row0, col0) in enumerate(tiles):
                gt = (row0 // 128) if not is_s else ti
                b = gbank.next()
                tm_proj(nt, col0, 512, 512, b)
                stg, r_stg = st512.next()
                fw.op("act", lambda a, b=b, stg=stg, nt=nt: a.copy(out=stg[0:nt, :], in_=ps[0:nt, b, :]), reads=[r_ps[b]], writes=[r_stg])
                if is_s:
                    fw.dma("sp", o_bk_s[ti, 448:512, :], stg[0:nt, :], reads=[r_stg])
                elif gt >= 12:
                    fw.dma("sp", o_bk_p[(gt - 12) * 128:(gt - 11) * 128, :], stg[0:nt, :], reads=[r_stg])
                b = gbank.next()
                tm_proj(nt, col0, 1024, 512, b)
                stg, r_stg = st512.next()
                fw.op("act", lambda a, b=b, stg=stg, nt=nt: a.copy(out=stg[0:nt, :], in_=ps[0:nt, b, :]), reads=[r_ps[b]], writes=[r_stg])
                if is_s:
                    fw.dma("sp", o_bv_s[ti, 448:512, :], stg[0:nt, :], reads=[r_stg])
                    fw.op("dve", lambda v, stg=stg, ti=ti, nt=nt: v.tensor_copy(out=sv1[0:nt, ti, :], in_=stg[0:nt, :]), reads=[r_stg], writes=[r_sk[1]])
                else:
                    if gt >= 12:
                        fw.dma("sp", o_bv_p[(gt - 12) * 128:(gt - 11) * 128, :], stg[0:nt, :], reads=[r_stg])
                    fw.op("dve", lambda v, stg=stg, gt=gt: v.tensor_copy(out=vc[:, gt % 8, :], in_=stg[:, :]), reads=[r_stg], writes=[r_kc[gt % 8]])
                b = gbank.next()
                tm_proj(nt, col0, 2560, 512, b)
                stg, r_stg = st512.next()
                fw.op("act", lambda a, b=b, stg=stg, nt=nt: a.copy(out=stg[0:nt, :], in_=ps[0:nt, b, :]), reads=[r_ps[b]], writes=[r_stg])
                fw.dma("sp", o_fk[row0:row0 + nt, :], stg[0:nt, :], reads=[r_stg])
                b = gbank.next()
                tm_proj(nt, col0, 3072, 512, b)
                stg, r_stg = st512.next()
                fw.op("act", lambda a, b=b, stg=stg, nt=nt: a.copy(out=stg[0:nt, :], in_=ps[0:nt, b, :]), reads=[r_ps[b]], writes=[r_stg])
                fw.dma("sp", o_fv[row0:row0 + nt, :], stg[0:nt, :], reads=[r_stg])
                if is_s:
                    fw.op("dve", lambda v, stg=stg, ti=ti, nt=nt: v.tensor_copy(out=sv2[0:nt, ti, :], in_=stg[0:nt, :]), reads=[r_stg], writes=[r_sk[3]])
                else:
                    fw.op("dve", lambda v, stg=stg, gt=gt: v.tensor_copy(out=v2[:, gt, :], in_=stg[:, :]), reads=[r_stg], writes=[r_k2[gt]])
                b = gbank.next()
                tm_proj(nt, col0, 3584, 8, b)
                lf, r_lf = lfb.next()
                fw.op("dve", lambda v, b=b, lf=lf, nt=nt: v.tensor_tensor(out=lf[0:nt, 0:8], in0=ps[0:nt, b, 0:8], in1=fb_bc[0:nt, :], op=ALU.add),
                      reads=[r_ps[b], r_fb], writes=[r_lf])
                fw.op("act", lambda a, lf=lf, nt=nt: a.activation(out=lf[0:nt, 8:16], in_=lf[0:nt, 0:8], func=AF.Exp, scale=-1.0), reads=[r_lf], writes=[r_lf])
                fw.op("act", lambda a, lf=lf, nt=nt: a.activation(out=lf[0:nt, 0:8], in_=lf[0:nt, 8:16], func=AF.Ln, bias=1.0), reads=[r_lf], writes=[r_lf])
                fw.op("dve", lambda v, lf=lf, nt=nt: v.tensor_scalar(out=lf[0:nt, 16:24], in0=lf[0:nt, 0:8], scalar1=-1.0, scalar2=None, op0=ALU.mult),
                      reads=[r_lf], writes=[r_lf])
                fw.dma("sp", o_lf[row0:row0 + nt, :], lf[0:nt, 16:24], reads=[r_lf])
                b2 = gbank.next()
                if not is_s:
                    fw.op("pe", lambda t_, b2=b2, lf=lf: t_.matmul(ps[:, b2, 0:8], lhsT=triF[:, :], rhs=lf[:, 16:24], start=True, stop=False),
                          reads=[r_tri, r_lf], writes=[r_ps[b2]])
                    fw.op("pe", lambda t_, b2=b2, lf=lf: t_.matmul(ps[:, b2, 8:16], lhsT=onesF[:, :], rhs=lf[:, 16:24], start=False, stop=True),
                          reads=[r_onesF, r_lf], writes=[r_ps[b2]])
                    fw.op("dve", lambda v, b2=b2, gt=gt: v.tensor_scalar(out=fxb[:, 0, gt, :], in0=ps[:, b2, 0:8], scalar1=-1.0, scalar2=None, op0=ALU.mult),
                          reads=[r_ps[b2]], writes=[r_fxb])
                    fw.op("dve", lambda v, b2=b2, gt=gt: v.tensor_copy(out=fxb[:, 2, gt, :], in_=ps[:, b2, 8:16]), reads=[r_ps[b2]], writes=[r_fxb])
                    fw.op("dve", lambda v, gt=gt: v.tensor_tensor(out=fxb[:, 1, gt, :], in0=fxb[:, 2, gt, :], in1=fxb[:, 0, gt, :], op=ALU.add),
                          reads=[r_fxb], writes=[r_fxb])
                else:
                    fw.op("pe", lambda t_, b2=b2, lf=lf: t_.matmul(ps[0:64, b2, 0:8], lhsT=triF[0:64, 0:64], rhs=lf[0:64, 16:24], start=True, stop=True),
                          reads=[r_tri, r_lf], writes=[r_ps[b2]])
                    fw.op("dve", lambda v, b2=b2, ti=ti: v.tensor_scalar(out=sfxn[0:64, ti, :], in0=ps[0:64, b2, 0:8], scalar1=-1.0, scalar2=None, op0=ALU.mult),
                          reads=[r_ps[b2]], writes=[r_sfxn])
            fm_group([0 + 128 * i for i in range(4)],
                     lambda i, src, rb: evac_bd(qA, r_qA, i, src, rb))
            fm_group([2048 + 128 * i for i in range(4)],
                     lambda i, src, rb: evac_bd(qBd, r_qB, i, src, rb))
            if is_s:
                fm_group([512 + 128 * i for i in range(4)],
                         lambda i, src, rb: fw.op("dve", lambda v: v.tensor_copy(out=skT1[:, i, :], in_=src), reads=[rb], writes=[r_sk[0]]))
                fm_group([2560 + 128 * i for i in range(4)],
                         lambda i, src, rb: fw.op("dve", lambda v: v.tensor_copy(out=skT2[:, i, :], in_=src), reads=[rb], writes=[r_sk[2]]))
            else:
                t0 = tiles[0][1]
                g0 = t0 // 128
                rc = (t0 % 1024)

                def evc(i, src, rb):
                    fw.op("dve", lambda v: v.tensor_copy(out=kTc[:, i, rc:rc + BLK], in_=src), reads=[rb], writes=[r_kc[g0 % 8], r_kc[(g0 + 1) % 8]])

                def evd(i, src, rb):
                    fw.op("dve", lambda v: v.tensor_copy(out=kT2[:, i, t0:t0 + BLK], in_=src), reads=[rb], writes=[r_k2[g0], r_k2[g0 + 1]])
                fm_group([512 + 128 * i for i in range(4)], evc)
                fm_group([2560 + 128 * i for i in range(4)], evd)
            fm_group([1536 + 128 * i for i in range(4)] + [3592 + 128 * i for i in range(4)],
                     lambda i, src, rb: fw.op("act", lambda a: a.activation(out=gT[:, i, :], in_=src, func=AF.Silu),
                                              reads=[rb], writes=[r_gT[i]]))

        def band_pre(u, dd, hs, nq):
            nk = u.nk
            H = hs.stop - hs.start
            if dd == 0:
                return pre_add(u, tblB[0:nk, hs, 0:nq], [r_tblB])
            if dd == 1:
                return pre_add(u, tblB[0:nk, hs, 128:128 + nq], [r_tblB])
            cst = cstB[0:nk, hs].unsqueeze(2).broadcast_to([nk, H, nq])
            if dd == 4:
                return pre_add(u, cst, [r_cstB], extra=(mask512[:, :].unsqueeze(1).broadcast_to([128, H, 128]), [r_mask512]))
            return pre_add(u, cst, [r_cstB])

        def attn1_prompt(bi, bg=None):
            all_band = []
            all_fox = []
            for qt in (2 * bi, 2 * bi + 1):
                qc = (qt % 2) * 128
                biasq = _T(biasq2[:, qt % 2])
                for kt in range(qt - 1, -1, -1):
                    if kt == qt - 1:
                        fw.op("dve", lambda v, kt=kt, biasq=biasq: v.tensor_copy(out=biasq[:, kt, :], in_=fxb[:, 1, kt, :]), reads=[r_fxb], writes=[r_biasq])
                        fw.op("dve", lambda v, kt=kt, biasq=biasq: v.tensor_copy(out=accb[:, :], in_=fxb[:, 2, kt, :]), reads=[r_fxb], writes=[r_accb])
                    else:
                        fw.op("dve", lambda v, kt=kt, biasq=biasq: v.tensor_tensor(out=biasq[:, kt, :], in0=fxb[:, 1, kt, :], in1=accb[:, :], op=ALU.add),
                              reads=[r_fxb, r_accb], writes=[r_biasq])
                        fw.op("dve", lambda v, kt=kt, biasq=biasq: v.tensor_tensor(out=accb[:, :], in0=accb[:, :], in1=fxb[:, 2, kt, :], op=ALU.add),
                              reads=[r_fxb, r_accb], writes=[r_accb])
                for hg in range(2):
                    hs = slice(4 * hg, 4 * hg + 4)
                    chain = {}
                    units = []
                    kts = list(range(qt, max(-1, qt - 5), -1))
                    for kt in kts:
                        u = U(); u.nk = 128; u.chain = chain
                        u.first = (kt == kts[0]); u.last = (kt == kts[-1])
                        u.zmm = []; u.vmm = []
                        sl = kt % 8
                        for pp in range(2):
                            u.zmm.append((kTc[:, 2 * hg + pp, sl * 128:(sl + 1) * 128], qA[:, 2 * hg + pp, qt % 2, :], pp * 256, 256))
                        for pp in range(2):
                            u.vmm.append((vc[:, sl, (2 * hg + pp) * 128:(2 * hg + pp + 1) * 128], pp * 256, 256))
                        zmm4(u)
                        u.reads = [r_kc[sl], r_qA]; u.vreads = [r_kc[sl]]
                        if (qt - kt) in (2, 3):
                            u.hbias = [(cstB[:, 4 * hg + j:4 * hg + j + 1], r_cstB) for j in range(4)]
                        u.pre = (lambda u, dd=qt - kt, hs=hs: band_pre(u, dd, hs, 128))
                        slots = [(4 * hg + s_, s_ * 128, 128, (4 * hg + s_) // 2, qc) for s_ in range(4)]
                        u.fin = (lambda ob, db, slots=slots: fin_softmax(slots, ob, db))
                        units.append(u)
                    all_band += units
                    chain = {}
                    units = []
                    for kt in range(qt, -1, -1):
                        u = U(); u.nk = 128; u.chain = chain
                        u.first = (kt == qt); u.last = (kt == 0)
                        u.zmm = []; u.vmm = []
                        for pp in range(2):
                            u.zmm.append((kT2[:, 2 * hg + pp, kt * 128:(kt + 1) * 128], qBd[:, 2 * hg + pp, qt % 2, :], pp * 256, 256))
                        for pp in range(2):
                            u.vmm.append((v2[:, kt, (2 * hg + pp) * 128:(2 * hg + pp + 1) * 128], pp * 256, 256))
                        zmm4(u)
                        u.reads = [r_k2[kt], r_qB]; u.vreads = [r_k2[kt]]
                        if kt == qt:
                            u.pre = (lambda u, qt=qt, hs=hs: pre_add(u, fxb[:, 0, qt, hs].unsqueeze(2).broadcast_to([128, 4, 128]), [r_fxb],
                                                                      extra=(maskFX[:, :].unsqueeze(1).broadcast_to([128, 4, 128]), [r_maskFX])))
                        else:
                            if kt % 2 == 1:
                                u.hbias = [(biasq[:, kt, 4 * hg + j:4 * hg + j + 1], r_biasq) for j in range(4)]
                            u.pre = (lambda u, kt=kt, hs=hs, biasq=biasq: pre_add(u, biasq[:, kt, hs].unsqueeze(2).broadcast_to([128, 4, 128]), [r_biasq]))
                        slots = [(4 * hg + s_, s_ * 128, 128, 4 + (4 * hg + s_) // 2, qc) for s_ in range(4)]
                        u.fin = (lambda ob, db, slots=slots: fin_softmax(slots, ob, db))
                        units.append(u)
                    all_fox += units
            sm_chain(all_band, bg=bg)
            sm_chain(all_fox)

        def attn1_sample():
            hs8 = slice(0, 8)
            for s in range(NSTR):
                fw.dma("sp", o_bk_s[s, 0:448, :], c_bk[s, 64:512, :])
                fw.dma("sp", o_bv_s[s, 0:448, :], c_bv[s, 64:512, :])
                slots = [(h, h * 64, 64, h // 2, 64 * s) for h in range(8)]
                units = kv_units(s, 4, c_bk, c_bv, qA, r_qA, skT1, sv1, r_sk[0], slots)
                units[0].reads = [r_sk[0], qbd[s][1]]; units[0].vreads = [r_sk[1]]
                zmm4(units[0])
                units[0].pre = (lambda u: band_pre(u, 0, hs8, 64))
                for u in units[1:]:
                    dd = 4 - u.tile
                    op_ = u.prep

                    def prep2(u, op_=op_):
                        op_(u)
                        zmm4(u)
                    u.prep = prep2
                    u.pre = (lambda u, dd=dd: band_pre(u, 1 if dd == 1 else 2, hs8, 64))
                for u in units:
                    u.fin = (lambda ob, db, slots=slots: fin_softmax(slots, ob, db))
                sm_chain(units)
            for s in range(NSTR):
                fw.dma("sp", lfc[:], c_lf[s].rearrange("(t p) h -> p t h", p=128), writes=[r_lfc])
                lf2 = lfc[:].rearrange("p t h -> p (t h)")
                b2 = gbank.next()
                fw.op("pe", lambda t_, b2=b2: t_.matmul(ps[:, b2, 0:256], lhsT=triF[:, :], rhs=lf2, start=True, stop=False),
                      reads=[r_tri, r_lfc], writes=[r_ps[b2]])
                fw.op("pe", lambda t_, b2=b2: t_.matmul(ps[:, b2, 256:512], lhsT=onesF[:, :], rhs=lf2, start=False, stop=True),
                      reads=[r_onesF, r_lfc], writes=[r_ps[b2]])
                fw.op("dve", lambda v, b2=b2: v.tensor_copy(out=lf2, in_=ps[:, b2, 256:512]), reads=[r_ps[b2]], writes=[r_lfc])
                sf2 = sfx[:, 0:32, :].rearrange("p t h -> p (t h)")
                fw.op("dve", lambda v, b2=b2: v.tensor_tensor(out=sf2, in0=lf2, in1=ps[:, b2, 0:256], op=ALU.subtract),
                      reads=[r_ps[b2], r_lfc], writes=[r_sfx])
                for t in range(30, -1, -1):
                    if t == 30:
                        fw.op("dve", lambda v: v.tensor_copy(out=accb[:, :], in_=lfc[:, 31, :]), reads=[r_lfc], writes=[r_accb])
                    else:
                        fw.op("dve", lambda v, t=t: v.tensor_tensor(out=accb[:, :], in0=accb[:, :], in1=lfc[:, t + 1, :], op=ALU.add),
                              reads=[r_lfc, r_accb], writes=[r_accb])
                    fw.op("dve", lambda v, t=t: v.tensor_tensor(out=sfx[:, t, :], in0=sfx[:, t, :], in1=accb[:, :], op=ALU.add),
                          reads=[r_sfx, r_accb], writes=[r_sfx])
                slots = [(h, h * 64, 64, 4 + h // 2, 64 * s) for h in range(8)]
                units = kv_units(s, PAST // 128, c_fk, c_fv, qBd, r_qB, skT2, sv2, r_sk[2], slots)
                units[0].reads = [r_sk[2], qbd[s][1]]; units[0].vreads = [r_sk[3]]
                zmm4(units[0])
                units[0].pre = (lambda u, s=s: pre_add(u, sfxn[0:64, s, :].unsqueeze(2).broadcast_to([64, 8, 64]), [r_sfxn],
                                                       extra=(maskFX[0:64, 0:64].unsqueeze(1).broadcast_to([64, 8, 64]), [r_maskFX])))
                for u in units[1:]:
                    op_ = u.prep

                    def prep3(u, op_=op_):
                        op_(u)
                        zmm4(u)
                    u.prep = prep3
                    u.pre = (lambda u: pre_add(u, sfx[:, u.tile, :].unsqueeze(2).broadcast_to([128, 8, 64]), [r_sfx]))
                for u in units:
                    u.fin = (lambda ob, db, slots=slots: fin_softmax(slots, ob, db))
                sm_chain(units)

        def phase_out(l, tiles, xsrc, r_xsrc, ydst, r_ydst):
            for (nt, row0, col0) in tiles:
                b0 = gbank.next(); b1 = gbank.next()
                for half, b in ((0, b0), (1, b1)):
                    for c in range(8):
                        fw.op("pe", lambda t, c=c, b=b, half=half, nt=nt, col0=col0: t.matmul(ps[0:nt, b, :], lhsT=gT[:, c, col0:col0 + nt],
                                                                                               rhs=wout[:, c, half * 512:(half + 1) * 512],
                                                                                               start=(c == 0), stop=(c == 7)),
                              reads=[r_gT[c], r_wout], writes=[r_ps[b]])
                s, r_s = sm.next()
                fw.op("act", lambda a, s=s, nt=nt, b0=b0: a.activation(out=junk[0:nt, :], in_=ps[0:nt, b0, :], func=AF.Square, accum_out=s[0:nt, 4:5]),
                      reads=[r_ps[b0]], writes=[r_junk, r_s])
                fw.op("act", lambda a, s=s, nt=nt, b1=b1: a.activation(out=junk[0:nt, :], in_=ps[0:nt, b1, :], func=AF.Square, accum_out=s[0:nt, 5:6]),
                      reads=[r_ps[b1]], writes=[r_junk, r_s])
                fw.op("dve", lambda v, s=s, nt=nt: v.tensor_tensor(out=s[0:nt, 2:3], in0=s[0:nt, 4:5], in1=s[0:nt, 5:6], op=ALU.add), reads=[r_s], writes=[r_s])
                rs, r_rs = rstd_from_ss(s[0:nt, 2:3], D, nt, r_s)
                xt, r_xt = xin.next()
                fw.dma("act", xt[0:nt, :], xsrc[row0:row0 + nt, :], reads=[r_xsrc[row0 // 64]] if r_xsrc else [], writes=[r_xt])
                for half, b in ((0, b0), (1, b1)):
                    y, r_y = tmpf.next()
                    fw.op("dve", lambda v, half=half, b=b, y=y, rs=rs, nt=nt: v.scalar_tensor_tensor(
                        out=y[0:nt, :], in0=ps[0:nt, b, :], scalar=rs, in1=gpost[0:nt, half * 512:(half + 1) * 512],
                        op0=ALU.mult, op1=ALU.mult), reads=[r_ps[b], r_rs, r_gpost], writes=[r_y])
                    fw.op("pool", lambda g, y=y, xt=xt, nt=nt, half=half: g.tensor_tensor(out=xt[0:nt, half * 512:(half + 1) * 512], in0=y[0:nt, :],
                                                                                          in1=xt[0:nt, half * 512:(half + 1) * 512], op=ALU.add),
                          reads=[r_y, r_xt], writes=[r_xt])
                fw.dma("sp", ydst[row0:row0 + nt, :], xt[0:nt, :], reads=[r_xt], writes=[r_ydst[row0 // 64]] if r_ydst else [])

        r_x1p = [Res() for _ in range(SEQ // 64)]
        r_x1s = [Res() for _ in range(NSTR * DSEQ // 64)]
        ptiles = lambda bi: [(128, bi * BLK, 0), (128, bi * BLK + 128, 128)]
        stiles = [(64, 64 * s, 64 * s) for s in range(NSTR)]

        load_layer_weights(0)
        nblk = min(SEQ // BLK, NBLK_DBG)
        for bi in range(nblk):
            if bi == 0:
                phase_norm(0, ptiles(bi), xp, None)
            phase_proj0(ptiles(bi), False, o_ckv_p, o_kr_p, o_sbk_p, o_sbv_p)
            bg = phase_norm_gen(0, ptiles(bi + 1), xp, None) if bi + 1 < nblk else None
            attn0_prompt(bi, bg)
            phase_out(0, ptiles(bi), xp, None, x1p, r_x1p)
        if STAGES >= 1:
            fw.barrier()
            phase_norm(0, stiles, xs, None)
            phase_proj0(stiles, True, o_ckv_s, o_kr_s, o_sbk_s, o_sbv_s)
            if STAGES >= 3:
                CAST_ENGS[0] = ("dve", "act", "dve", "pool")
                attn0_sample()
                CAST_ENGS[0] = ("pool", "dve", "act")
            phase_out(0, stiles, xs, None, x1s, r_x1s)
        if STAGES >= 4:
            load_layer_weights(1)
            setup_l1_tables()
            fw.op("pool", lambda g: g.memset(qBd[:].rearrange("p a t q -> p (a t q)"), 0.0), writes=[r_qB])
            fw.barrier()
            for bi in range(nblk):
                if bi == 0:
                    phase_norm(1, ptiles(bi), x1p, r_x1p)
                phase_proj1(ptiles(bi), False, o_fk_p, o_fv_p, o_lf_p)
                bg = phase_norm_gen(1, ptiles(bi + 1), x1p, r_x1p) if bi + 1 < nblk else None
                attn1_prompt(bi, bg)
                phase_out(1, ptiles(bi), x1p, r_x1p, y_p, None)
            fw.barrier()
            phase_norm(1, stiles, x1s, r_x1s)
            phase_proj1(stiles, True, o_fk_s, o_fv_s, o_lf_s)
            if STAGES >= 6:
                CAST_ENGS[0] = ("act", "pool", "act")
                EVAC_ENGS[0] = ("dve", "act")
                attn1_sample()
            phase_out(1, stiles, x1s, r_x1s, y_s, None)

        print("fw ops recorded:", getattr(fw, "nops", 0), {k: e.cnt for k, e in fw.E.items()})
        fw.finish()
        fw.emit()
    return nc


def _rope_tables():
    half = 16
    inv = (10000.0 ** (-np.arange(half, dtype=np.float32) / half)).astype(np.float32)

    def tab(pos):
        ang = pos.astype(np.float32)[:, None] * inv[None, :]
        c = np.cos(ang).astype(np.float32)
        s = np.sin(ang).astype(np.float32)
        return np.concatenate([c, c, -s, s], axis=1).astype(np.float32)
    tp = tab(np.arange(SEQ)).reshape(16, 128, 64).transpose(1, 0, 2).reshape(128, 16 * 64)
    tsm = tab(PAST + np.arange(DSEQ))
    return np.ascontiguousarray(tp), np.ascontiguousarray(tsm)


_NC_CACHE = {}


def kernel(**inp):
    f = lambda a: np.ascontiguousarray(np.asarray(a, dtype=np.float32))
    x_prompt = f(inp["x_prompt"]); x_sample = f(inp["x_sample"])
    rope_p, rope_s = _rope_tables()
    w_uq = f(inp["a_w_uq"])[0]
    w_uq_l = np.concatenate([w_uq[:, :, :64].reshape(256, 512), w_uq[:, :, 64:].reshape(256, 256)], axis=1)
    w_uk = f(inp["a_w_uk"])[0]
    w_ukT = np.transpose(w_uk, (2, 1, 0)).reshape(64, 1024)
    w_ukT = np.concatenate([w_ukT, w_ukT], axis=0)
    relb = f(inp["c_rel_bias"])[0]
    relb_pad = np.concatenate([relb, np.repeat(relb[:, -1:], 256, axis=1)], axis=1)
    shared = {
        "norm_pre": np.ascontiguousarray(f(inp["norm_pre"]).reshape(2, 8, 128).transpose(2, 0, 1).reshape(128, 16)), "norm_post": f(inp["norm_post"]),
        "w_in0": f(inp["w_in_even"])[0], "q_norm": f(inp["a_q_norm"]).reshape(1, 256),
        "w_uq": np.ascontiguousarray(w_uq_l), "kv_norm": f(inp["a_kv_norm"]).reshape(1, 128),
        "w_ukT": np.ascontiguousarray(w_ukT), "w_uv": f(inp["a_w_uv"])[0].reshape(128, 512),
        "w_out0": f(inp["w_out_even"])[0], "w_in1": f(inp["w_in_odd"])[0],
        "relb": np.ascontiguousarray(relb_pad), "fbias": f(inp["d_forget_bias"]).reshape(1, 8),
        "relc": np.ascontiguousarray(relb[:, 256].reshape(1, 8)),
        "w_out1": f(inp["w_out_odd"])[0], "rope_p": rope_p, "rope_s": rope_s,
    }
    caches = {k: f(inp[k])[0] for k in ("cache_mla_ckv", "cache_mla_krope", "cache_sb_k", "cache_sb_v", "cache_band_k",
                                        "cache_band_v", "cache_fox_k", "cache_fox_v", "cache_fox_logf")}
    in_maps = []
    for c in range(NCORES):
        sl = slice(NSTR * c, NSTR * (c + 1))
        m = dict(shared)
        m["xp"] = x_prompt[c]
        m["xs"] = x_sample[sl].reshape(NSTR * DSEQ, D)
        m["c_ckv"] = caches["cache_mla_ckv"][sl]
        m["c_kr"] = caches["cache_mla_krope"][sl]
        m["c_sbk"] = caches["cache_sb_k"][sl].reshape(NSTR, PAST, 512)
        m["c_sbv"] = caches["cache_sb_v"][sl].reshape(NSTR, PAST, 512)
        m["c_bk"] = caches["cache_band_k"][sl].reshape(NSTR, 512, 512)
        m["c_bv"] = caches["cache_band_v"][sl].reshape(NSTR, 512, 512)
        m["c_fk"] = caches["cache_fox_k"][sl].reshape(NSTR, PAST, 512)
        m["c_fv"] = caches["cache_fox_v"][sl].reshape(NSTR, PAST, 512)
        m["c_lf"] = caches["cache_fox_logf"][sl]
        in_maps.append({k: np.ascontiguousarray(v) for k, v in m.items()})
    if "nc" not in _NC_CACHE:
        _NC_CACHE["nc"] = build()
    nc = _NC_CACHE["nc"]
    if KCORES < NCORES:
        res = run_bass_kernel_spmd(nc, in_maps[:KCORES], core_ids=list(range(KCORES)))
        R = list(res.results) + [res.results[0]] * (NCORES - KCORES)
    else:
        res = run_bass_kernel_spmd(nc, in_maps, core_ids=list(range(NCORES)))
        R = res.results
    cat = lambda k: np.stack([R[c][k] for c in range(NCORES)], axis=0)
    B = NCORES
    SB = NCORES * NSTR
    outs = (
        cat("y_p").reshape(B, SEQ, D),
        cat("y_s").reshape(SB, DSEQ, D),
        cat("o_ckv_p").reshape(1, B, SEQ, 128), cat("o_kr_p").reshape(1, B, SEQ, 32),
        cat("o_sbk_p").reshape(1, B, SEQ, 8, 64), cat("o_sbv_p").reshape(1, B, SEQ, 8, 64),
        cat("o_bk_p").reshape(1, B, 512, 8, 64), cat("o_bv_p").reshape(1, B, 512, 8, 64),
        cat("o_fk_p").reshape(1, B, SEQ, 8, 64), cat("o_fv_p").reshape(1, B, SEQ, 8, 64), cat("o_lf_p").reshape(1, B, SEQ, 8),
        cat("o_ckv_s").reshape(1, SB, DSEQ, 128), cat("o_kr_s").reshape(1, SB, DSEQ, 32),
        cat("o_sbk_s").reshape(1, SB, DSEQ, 8, 64), cat("o_sbv_s").reshape(1, SB, DSEQ, 8, 64),
        cat("o_bk_s").reshape(1, SB, 512, 8, 64), cat("o_bv_s").reshape(1, SB, 512, 8, 64),
        cat("o_fk_s").reshape(1, SB, DSEQ, 8, 64), cat("o_fv_s").reshape(1, SB, DSEQ, 8, 64), cat("o_lf_s").reshape(1, SB, DSEQ, 8),
    )
    _NC_CACHE["x1"] = (cat("x1p"), cat("x1s"))
    return tuple(np.ascontiguousarray(o.astype(np.float32)) for o in outs)
```

```python
import numpy as np
from contextlib import ExitStack
import concourse.bass as bass
import concourse.mybir as mybir
from concourse.bass_utils import run_bass_kernel_spmd

F32 = mybir.dt.float32
BF16 = mybir.dt.bfloat16
AF = mybir.ActivationFunctionType
ALU = mybir.AluOpType

NCORES = 8
D = 1024
SEQ = 2048
NSTR = 4
DSEQ = 64
PAST = 4096
EPS = 1e-6
NEGM = -30000.0
A_SCALE = float((64 + 32) ** -0.5)
EVEN_IN = 2976
ODD_IN = 4104
BLK = 256
import os
STAGES = int(os.environ.get('KSTAGES', '6'))
NBLK_DBG = int(os.environ.get('KNBLK', '8'))
OPLIMIT = int(os.environ.get('KOPLIMIT', '100000000'))
KCORES = int(os.environ.get('KCORES', '8'))
KSAME = int(os.environ.get('KSAME', '0'))


class Res:
    __slots__ = ("w", "rs", "x", "rg")

    def __init__(self, x=False):
        self.w = None
        self.rs = []
        self.rg = None
        self.x = x


class Eng:
    def __init__(self, name, sem):
        self.name = name
        self.sem = sem
        self.cnt = 0
        self.waited = {}
        self.prog = []
        self.dq = []
        self.dcnt = []
        self.di = 0


class FW:
    def __init__(self, nc, es, ndq=8):
        self.nc = nc
        self.E = {}
        for name in ("pe", "act", "dve", "pool", "sp"):
            self.E[name] = Eng(name, es.enter_context(nc.semaphore("s_" + name)))
        for qn in ("sp", "act", "pool"):
            e = self.E[qn]
            for i in range(ndq):
                e.dq.append(es.enter_context(nc.semaphore(f"d_{qn}{i}")))
                e.dcnt.append(0)

    def _wait(self, eng, tok, force=False):
        if tok is None:
            return
        sem, val, src = tok
        if src == eng.name and src in ("pe", "sp") and not force:
            return
        key = id(sem)
        if eng.waited.get(key, 0) >= val:
            return
        eng.waited[key] = val
        eng.prog.append(("w", sem, val))

    def _deps(self, eng, reads, writes):
        for r in reads:
            self._wait(eng, r.w)
            if r.x:
                for t in r.rs:
                    if t[2] != eng.name:
                        self._wait(eng, t)
        for w in writes:
            if w.w is not None and (w.w[2] != eng.name or KSAME):
                self._wait(eng, w.w)
            for t in w.rs:
                if t[2] != eng.name or KSAME:
                    self._wait(eng, t)

    def _record(self, tok, reads, writes):
        for r in reads:
            r.rs.append(tok)
            if len(r.rs) > 16:
                best = {}
                for t in r.rs:
                    k = id(t[0])
                    if k not in best or best[k][1] < t[1]:
                        best[k] = t
                r.rs = list(best.values())
        for w in writes:
            w.w = tok
            w.rs = []

    def op(self, engname, fn, reads=(), writes=(), rg=None):
        self.nops = getattr(self, "nops", 0) + 1
        if self.nops > OPLIMIT:
            return None
        eng = self.E[engname]
        self._deps(eng, reads, writes)
        if engname == "pe":
            for w in writes:
                if rg is not None and w.rg is not None and w.rg != rg and w.w is not None and w.w[2] == "pe":
                    self._wait(eng, w.w, force=True)
                w.rg = rg
        eng.cnt += 1
        eng.prog.append(("i", fn, eng.sem, 1))
        tok = (eng.sem, eng.cnt, eng.name)
        self._record(tok, reads, writes)
        return tok

    def dma(self, q, out, in_, reads=(), writes=()):
        self.nops = getattr(self, "nops", 0) + 1
        if self.nops > OPLIMIT:
            return None
        eng = self.E[q]
        self._deps(eng, reads, writes)
        i = eng.di % len(eng.dq)
        eng.di += 1
        sem = eng.dq[i]
        if eng.dcnt[i] > 0:
            self._wait(eng, (sem, eng.dcnt[i], "dma"))
        eng.prog.append(("i", (lambda o, out=out, in_=in_: o.dma_start(out=out, in_=in_)), sem, 16))
        eng.dcnt[i] += 16
        tok = (sem, eng.dcnt[i], "dma")
        self._record(tok, reads, writes)
        return tok

    def barrier(self):
        toks = []
        for q in ("sp", "act", "pool"):
            e = self.E[q]
            for i, sem in enumerate(e.dq):
                if e.dcnt[i] > 0:
                    toks.append((sem, e.dcnt[i], "dma"))
        for n in ("pe", "act", "dve", "pool"):
            e = self.E[n]
            if e.cnt > 0:
                toks.append((e.sem, e.cnt, "x"))
        for n in ("pe", "act", "dve", "pool", "sp"):
            for t in toks:
                self._wait(self.E[n], t)

    def finish(self):
        sp = self.E["sp"]
        for q in ("sp", "act", "pool"):
            e = self.E[q]
            for i, sem in enumerate(e.dq):
                if e.dcnt[i] > 0:
                    self._wait(sp, (sem, e.dcnt[i], "dma"))
        for n in ("pe", "act", "dve", "pool"):
            e = self.E[n]
            if e.cnt > 0:
                self._wait(sp, (e.sem, e.cnt, "x"))

    def emit(self):
        nc = self.nc
        objs = {"pe": None}

        def run(eng):
            def body(obj):
                for a in eng.prog:
                    if a[0] == "w":
                        obj.wait_ge(a[1], a[2])
                    else:
                        a[1](obj).then_inc(a[2], a[3])
            return body
        with nc.Block() as block:
            block.tensor(run(self.E["pe"]))
            block.scalar(run(self.E["act"]))
            block.vector(run(self.E["dve"]))
            block.gpsimd(run(self.E["pool"]))
            block.sync(run(self.E["sp"]))


class RR:
    def __init__(self, items):
        self.items = items
        self.i = 0

    def next(self):
        it = self.items[self.i % len(self.items)]
        self.i += 1
        return it


def pipeline(units, stages, offsets=None, bg=None):
    n = len(units)
    ns = len(stages)
    if offsets is None:
        offsets = list(range(ns))
    for i in range(n + max(offsets)):
        for s, st in enumerate(stages):
            j = i - offsets[s]
            if 0 <= j < n:
                st(units[j])
        if bg is not None:
            next(bg, None)
    if bg is not None:
        for _ in bg:
            pass


def build():
    nc = bass.Bass("TRN2", target_bir_lowering=False)
    din = lambda n, s: nc.dram_tensor(n, s, F32, kind="ExternalInput").ap()
    dout = lambda n, s: nc.dram_tensor(n, s, F32, kind="ExternalOutput").ap()
    xp = din("xp", [SEQ, D])
    xs = din("xs", [NSTR * DSEQ, D])
    c_ckv = din("c_ckv", [NSTR, PAST, 128])
    c_kr = din("c_kr", [NSTR, PAST, 32])
    c_sbk = din("c_sbk", [NSTR, PAST, 512])
    c_sbv = din("c_sbv", [NSTR, PAST, 512])
    c_bk = din("c_bk", [NSTR, 512, 512])
    c_bv = din("c_bv", [NSTR, 512, 512])
    c_fk = din("c_fk", [NSTR, PAST, 512])
    c_fv = din("c_fv", [NSTR, PAST, 512])
    c_lf = din("c_lf", [NSTR, PAST, 8])
    norm_pre = din("norm_pre", [128, 16])
    norm_post = din("norm_post", [2, D])
    w_in0 = din("w_in0", [D, EVEN_IN])
    q_norm = din("q_norm", [1, 256])
    w_uq = din("w_uq", [256, 768])
    kv_norm = din("kv_norm", [1, 128])
    w_ukT = din("w_ukT", [128, 1024])
    w_uv = din("w_uv", [128, 512])
    w_out0 = din("w_out0", [D, D])
    w_in1 = din("w_in1", [D, ODD_IN])
    relb = din("relb", [8, 513])
    fbias = din("fbias", [1, 8])
    relc = din("relc", [1, 8])
    w_out1 = din("w_out1", [D, D])
    rope_p = din("rope_p", [128, 16 * 64])
    rope_s = din("rope_s", [64, 64])
    y_p = dout("y_p", [SEQ, D])
    y_s = dout("y_s", [NSTR * DSEQ, D])
    o_ckv_p = dout("o_ckv_p", [SEQ, 128]); o_kr_p = dout("o_kr_p", [SEQ, 32])
    o_sbk_p = dout("o_sbk_p", [SEQ, 512]); o_sbv_p = dout("o_sbv_p", [SEQ, 512])
    o_bk_p = dout("o_bk_p", [512, 512]); o_bv_p = dout("o_bv_p", [512, 512])
    o_fk_p = dout("o_fk_p", [SEQ, 512]); o_fv_p = dout("o_fv_p", [SEQ, 512]); o_lf_p = dout("o_lf_p", [SEQ, 8])
    o_ckv_s = dout("o_ckv_s", [NSTR * DSEQ, 128]); o_kr_s = dout("o_kr_s", [NSTR * DSEQ, 32])
    o_sbk_s = dout("o_sbk_s", [NSTR * DSEQ, 512]); o_sbv_s = dout("o_sbv_s", [NSTR * DSEQ, 512])
    o_bk_s = dout("o_bk_s", [NSTR, 512, 512]); o_bv_s = dout("o_bv_s", [NSTR, 512, 512])
    o_fk_s = dout("o_fk_s", [NSTR * DSEQ, 512]); o_fv_s = dout("o_fv_s", [NSTR * DSEQ, 512])
    o_lf_s = dout("o_lf_s", [NSTR * DSEQ, 8])
    x1p = dout("x1p", [SEQ, D])
    x1s = dout("x1s", [NSTR * DSEQ, D])

    with ExitStack() as es:
        fw = FW(nc, es)
        ARN = 105500
        AR = es.enter_context(nc.sbuf_tensor("AR", [128, ARN], BF16))
        ar = {"top": 0, "peak": 0}

        class _T:
            def __init__(self, ap):
                self.ap = ap
            def __getitem__(self, k):
                return self.ap[k]

        def sbt(n, s, d=F32):
            nel = int(np.prod(s[1:]))
            nb = nel * (4 if d == F32 else 2)
            nb = (nb + 63) // 64 * 64
            off = ar["top"]
            ar["top"] += nb // 2
            ar["peak"] = max(ar["peak"], ar["top"])
            assert ar["top"] <= ARN, (n, ar["top"])
            v = AR[:, off:off + nb // 2]
            if d == F32:
                v = v.bitcast(F32)
            v = v[:, 0:nel]
            if len(s) == 3:
                v = v.rearrange("p (a b) -> p a b", a=s[1])
            elif len(s) == 4:
                v = v.rearrange("p (a b c) -> p a b c", a=s[1], b=s[2])
            if s[0] < 128:
                v = v[0:s[0]]
            return _T(v)
        ps = es.enter_context(nc.psum_tensor("ps", [128, 7, 512], F32))
        psb = es.enter_context(nc.psum_tensor("psb", [128, 1024], BF16))
        r_ps = [Res(True) for _ in range(7)]
        r_psb = Res(True)
        bank = lambda i: ps[:, i, :]

        ident = sbt("ident", [128, 128], BF16); r_ident = Res()
        fw.op("pool", lambda g: g.memset(ident[:], 1.0), writes=[r_ident])
        fw.op("pool", lambda g: g.affine_select(out=ident[:], in_=ident[:], pattern=[[-1, 128]], compare_op=ALU.is_equal,
                                                fill=0.0, base=0, channel_multiplier=1), reads=[r_ident], writes=[r_ident])
        flipJ = sbt("flipJ", [128, 128], F32); r_flip = Res()
        fw.op("pool", lambda g: g.memset(flipJ[:], 1.0), writes=[r_flip])
        fw.op("pool", lambda g: g.affine_select(out=flipJ[:], in_=flipJ[:], pattern=[[1, 128]], compare_op=ALU.is_equal,
                                                fill=0.0, base=-127, channel_multiplier=1), reads=[r_flip], writes=[r_flip])
        triF = sbt("triF", [128, 128], F32); r_tri = Res()
        fw.op("pool", lambda g: g.memset(triF[:], 1.0), writes=[r_tri])
        fw.op("pool", lambda g: g.affine_select(out=triF[:], in_=triF[:], pattern=[[1, 128]], compare_op=ALU.is_ge,
                                                fill=0.0, base=0, channel_multiplier=-1), reads=[r_tri], writes=[r_tri])
        onesF = sbt("onesF", [128, 128], F32); r_onesF = Res()
        fw.op("pool", lambda g: g.memset(onesF[:], 1.0), writes=[r_onesF])
        onesB = sbt("onesB", [128, 128], BF16); r_onesB = Res()
        fw.op("pool", lambda g: g.memset(onesB[:], 1.0), writes=[r_onesB])
        negOnes = sbt("negOnes", [128, 128], BF16); r_negOnes = Res()
        fw.op("pool", lambda g: g.memset(negOnes[:], -1.0), writes=[r_negOnes])
        negTri = sbt("negTri", [128, 128], BF16); r_negTri = Res()
        fw.op("pool", lambda g: g.memset(negTri[:], -1.0), writes=[r_negTri])
        fw.op("pool", lambda g: g.affine_select(out=negTri[:], in_=negTri[:], pattern=[[-1, 128]], compare_op=ALU.is_ge,
                                                fill=0.0, base=0, channel_multiplier=1), reads=[r_negTri], writes=[r_negTri])
        maskSB = sbt("maskSB", [128, 512], BF16); r_maskSB = Res()
        fw.op("pool", lambda g: g.memset(maskSB[:], 0.0), writes=[r_maskSB])
        for a in range(4):
            fw.op("pool", lambda g, a=a: g.affine_select(out=maskSB[:, a * 128:(a + 1) * 128], in_=maskSB[:, a * 128:(a + 1) * 128],
                                                         pattern=[[1, 128]], compare_op=ALU.is_gt, fill=NEGM, base=0,
                                                         channel_multiplier=-1), reads=[r_maskSB], writes=[r_maskSB])
        maskSB64 = sbt("maskSB64", [64, 512], BF16); r_maskSB64 = Res()
        fw.op("pool", lambda g: g.memset(maskSB64[:], 0.0), writes=[r_maskSB64])
        for a in range(8):
            fw.op("pool", lambda g, a=a: g.affine_select(out=maskSB64[:, a * 64:(a + 1) * 64], in_=maskSB64[:, a * 64:(a + 1) * 64],
                                                         pattern=[[1, 64]], compare_op=ALU.is_gt, fill=NEGM, base=0,
                                                         channel_multiplier=-1), reads=[r_maskSB64], writes=[r_maskSB64])
        maskFX = sbt("maskFX", [128, 128], F32); r_maskFX = Res()
        fw.op("pool", lambda g: g.memset(maskFX[:], 0.0), writes=[r_maskFX])
        fw.op("pool", lambda g: g.affine_select(out=maskFX[:], in_=maskFX[:], pattern=[[1, 128]], compare_op=ALU.is_ge,
                                                fill=NEGM, base=0, channel_multiplier=-1), reads=[r_maskFX], writes=[r_maskFX])
        maskCH = sbt("maskCH", [128, 128], F32); r_maskCH = Res()
        fw.op("pool", lambda g: g.memset(maskCH[:], 0.0), writes=[r_maskCH])
        fw.op("pool", lambda g: g.memset(maskCH[64:128, 0:64], NEGM), reads=[r_maskCH], writes=[r_maskCH])
        mask512 = sbt("mask512", [128, 128], F32); r_mask512 = Res()
        fw.op("pool", lambda g: g.memset(mask512[:], 0.0), writes=[r_mask512])
        fw.op("pool", lambda g: g.memset(mask512[0:64, 64:128], NEGM), reads=[r_mask512], writes=[r_mask512])

        ropePt = RR([(sbt(f"ropeP{i}", [128, 64]), Res()) for i in range(2)])
        ropeS = sbt("ropeS", [64, 64]); r_ropeS = Res()
        fw.dma("sp", ropeS[:], rope_s[:, :], writes=[r_ropeS])
        gpre = sbt("gpre", [128, 2, 8]); r_gpre = Res()
        fw.dma("sp", gpre[:].rearrange("p a b -> p (a b)"), norm_pre[:, :], writes=[r_gpre])
        gpost = sbt("gpost", [128, D]); r_gpost = Res()
        qn_bc = sbt("qn_bc", [128, 256]); r_qn = Res()
        fw.dma("sp", qn_bc[:], bass.AP(q_norm.tensor, 0, [[0, 128], [1, 256]]), writes=[r_qn])
        kvn_bc = sbt("kvn_bc", [128, 128]); r_kvn = Res()
        fw.dma("sp", kvn_bc[:], bass.AP(kv_norm.tensor, 0, [[0, 128], [1, 128]]), writes=[r_kvn])
        fb_bc = sbt("fb_bc", [128, 8]); r_fb = Res()
        fw.dma("sp", fb_bc[:], bass.AP(fbias.tensor, 0, [[0, 128], [1, 8]]), writes=[r_fb])

        wbuf = sbt("wbuf", [128, 8, ODD_IN], BF16); r_w = Res()
        wout = sbt("wout", [128, 8, D], BF16); r_wout = Res()
        wuq = _T(wbuf[:, 0:2, 2976:2976 + 768]); r_wuq = r_w
        wukT = _T(wbuf[:, 2, 2976:2976 + 1024]); r_wuk = r_w
        wuv = _T(wbuf[:, 3, 2976:2976 + 512]); r_wuv = r_w
        WST = 1026
        wst_off = ar["top"]
        wst = RR([(sbt(f"wst{i}", [128, WST]), Res()) for i in range(2)])
        wst_end = ar["top"]
        ar["top"] = wst_off
        cast_i = [0]
        wq_i = [0]

        CAST_ENGS = [("pool", "dve", "act")]
        EVAC_ENGS = [("dve",)]

        def cast(out, in_, reads, writes, engs=None):
            engs = engs or CAST_ENGS[0]
            e = engs[cast_i[0] % len(engs)]
            cast_i[0] += 1
            if e == "act":
                fw.op("act", lambda a: a.copy(out=out, in_=in_), reads=reads, writes=writes)
            else:
                fw.op(e, lambda v: v.tensor_copy(out=out, in_=in_), reads=reads, writes=writes)

        def load_w(dst_fn, src, nrows, ncols, r_dst, q="sp"):
            for c in range(nrows // 128):
                for c0 in range(0, ncols, WST):
                    n = min(WST, ncols - c0)
                    st, r_st = wst.next()
                    wq_i[0] += 1
                    fw.dma(("sp", "act")[wq_i[0] % 2], st[:, 0:n], src[c * 128:(c + 1) * 128, c0:c0 + n], writes=[r_st])
                    cast(dst_fn(c)[:, c0:c0 + n], st[:, 0:n], [r_st], [r_dst], engs=("dve", "act"))

        pbf = RR([(sbt(f"pbf{i}", [128, 512], BF16), Res()) for i in range(3)])
        spb = RR([(sbt(f"spb{i}", [128, 512], BF16), Res()) for i in range(3)])
        ebuf = RR([(sbt(f"ebuf{i}", [128, 512]), Res()) for i in range(2)])
        ar["top"] = max(ar["top"], wst_end)
        KS = 24576
        ks_off = ar["top"]
        ks = sbt("ks", [128, KS], BF16)
        hT = sbt("hT", [128, 8, BLK], BF16); r_hT = Res()
        gT = sbt("gT", [128, 8, BLK], BF16); r_gT = [Res() for _ in range(8)]
        qA = sbt("qA", [128, 4, 2, 256], BF16); r_qA = Res()
        qBd = sbt("qB", [128, 4, 2, 256], BF16); r_qB = Res()
        qB = _T(qBd[:].rearrange("p a t q -> p (a t q)")[:, 0:4 * BLK].rearrange("p (a q) -> p a q", a=4))
        fw.op("pool", lambda g: g.memset(qA[:].rearrange("p a t q -> p (a t q)"), 0.0), writes=[r_qA])
        xin = RR([(sbt(f"xin{i}", [128, D]), Res()) for i in range(2)])
        hb = RR([(sbt(f"hb{i}", [128, D], BF16), Res()) for i in range(1)])
        junk = sbt("junk", [128, 512], BF16); r_junk = Res()
        sm = RR([(sbt(f"sm{i}", [128, 16]), Res()) for i in range(6)])
        st512 = RR([(sbt(f"st512_{i}", [128, 512]), Res()) for i in range(2)])
        tmpf = RR([(sbt(f"tmpf{i}", [128, 512]), Res()) for i in range(2)])
        fint = RR([(sbt(f"fint{i}", [128, 256]), Res()) for i in range(2)])
        Rbuf = sbt("Rbuf", [128, 512], BF16); r_R = Res()
        latb = sbt("latb", [128, 512], BF16); r_latb = Res()
        rden = sbt("rden", [128, 512]); r_rden = Res()
        ov0 = ar["top"]
        qlat = sbt("qlat", [128, 8, BLK], BF16); r_qlat = Res()
        qrT = sbt("qrT", [32, 8, BLK], BF16); r_qrT = Res()
        cqT = sbt("cqT", [128, 2, BLK], BF16); r_cqT = Res()
        cq_b = sbt("cq_b", [128, 256], BF16); r_cqb = Res()
        kvb = sbt("kvb", [128, 128 + 32], BF16); r_kvb = Res()
        qr_b = sbt("qr_b", [128, 256], BF16); r_qrb = Res()
        ropet = RR([(sbt(f"ropet{i}", [128, 256]), Res()) for i in range(2)])
        ov1 = ar["top"]
        ar["top"] = ov0
        fxb = sbt("fxb", [128, 4, 16, 8]); r_fxb = Res()
        biasq2 = sbt("biasq", [128, 2, 16, 8]); r_biasq = Res()
        accb = sbt("accb", [128, 8]); r_accb = Res()
        tblB = sbt("tblB", [128, 8, 256]); r_tblB = Res()
        cstB = sbt("cstB", [128, 8]); r_cstB = Res()
        lfb = RR([(sbt(f"lfb{i}", [128, 24]), Res()) for i in range(3)])
        lfc = sbt("lfc", [128, 32, 8]); r_lfc = Res()
        sfx = sbt("sfx", [128, 33, 8]); r_sfx = Res()
        ar["top"] = max(ar["top"], ov1)
        sv_top = ar["top"]
        ar["top"] = ks_off
        cst_f = RR([(sbt(f"cstf{i}", [128, 512]), Res()) for i in range(8)])
        vbf = RR([(sbt(f"vbf{i}", [128, 512], BF16), Res()) for i in range(6)])
        kbf = RR([(sbt(f"kbf{i}", [128, 512], BF16), Res()) for i in range(2)])
        cKT = RR([(sbt(f"cKT{i}", [128, 4, 128], BF16), Res()) for i in range(4)])
        sfxn = sbt("sfxn", [64, 4, 8]); r_sfxn = Res()
        qbd = [(sbt(f"qbd{i}", [128, 4, 128], BF16), Res()) for i in range(4)]
        skT1 = sbt("skT1", [128, 4, BLK], BF16); skT2 = sbt("skT2", [128, 4, BLK], BF16)
        sv1 = sbt("sv1", [64, 4, 512], BF16); sv2 = sbt("sv2", [64, 4, 512], BF16)
        sckvT = sbt("sckvT", [128, BLK], BF16); skrT = sbt("skrT", [32, BLK], BF16)
        sckv_tm = sbt("sckv_tm", [64, 4, 128], BF16)
        assert ar["top"] <= ks_off + KS
        ar["top"] = sv_top

        def ksv(off, shape):
            n = int(np.prod(shape))
            v = ks[:, off:off + n]
            if len(shape) == 2:
                return v.rearrange("p (a b) -> p a b", a=shape[0])
            return v
        kT1 = ksv(0, [4, SEQ]); v1 = ksv(8192, [16, 512])
        kT2 = ksv(8192, [4, SEQ]); v2 = ksv(16384, [16, 512])
        kTc = ksv(0, [4, 1024]); vc = ksv(4096, [8, 512])
        ckvT = ks[:, 16384:16384 + SEQ]
        krT = ks[:, 16384 + 2048:16384 + 4096]
        ckv_tm = ksv(16384 + 4096, [16, 128])
        r_k1 = [Res() for _ in range(20)]
        r_k2 = [Res() for _ in range(20)]
        r_k3 = [Res() for _ in range(20)]
        SOFF = 24576
        r_sk = [Res() for _ in range(8)]

        def load_layer_weights(l):
            fw.barrier()
            if l == 0:
                load_w(lambda c: wbuf[:, c, :], w_in0, D, EVEN_IN, r_w)
                load_w(lambda c: wout[:, c, :], w_out0, D, D, r_wout)
                load_w(lambda c: wuq[:, c, :], w_uq, 256, 768, r_wuq)
                load_w(lambda c: wukT[:, :], w_ukT, 128, 1024, r_wuk)
                load_w(lambda c: wuv[:, :], w_uv, 128, 512, r_wuv)
            else:
                load_w(lambda c: wbuf[:, c, :], w_in1, D, ODD_IN, r_w)
                load_w(lambda c: wout[:, c, :], w_out1, D, D, r_wout)
            fw.dma("sp", gpost[:], bass.AP(norm_post.tensor, l * D, [[0, 128], [1, D]]), writes=[r_gpost])
            fw.barrier()

        gbank = RR([6, 2, 3, 0, 1, 4, 5])

        def mm(o, l, r, st, sp_, reads, wres):
            K = l.shape[0]
            rg = None if K >= 128 else (l.base_partition(), K)
            fw.op("pe", lambda t: t.matmul(o, lhsT=l, rhs=r, start=st, stop=sp_), reads=reads, writes=[wres], rg=rg)

        def rstd_from_ss(ss_ap, n, nt, r_ss):
            s, r_s = sm.next()
            fw.op("dve", lambda v: v.tensor_scalar(out=s[0:nt, 0:1], in0=ss_ap, scalar1=1.0 / n, scalar2=EPS,
                                                   op0=ALU.mult, op1=ALU.add), reads=[r_ss], writes=[r_s])
            fw.op("act", lambda a_: a_.activation(out=s[0:nt, 3:4], in_=s[0:nt, 0:1], func=AF.Sqrt), reads=[r_s], writes=[r_s])
            fw.op("dve", lambda v: v.reciprocal(out=s[0:nt, 1:2], in_=s[0:nt, 3:4]), reads=[r_s], writes=[r_s])
            return s[0:nt, 1:2], r_s

        def sumsq(in_ap, nt, reads):
            s, r_s = sm.next()
            fw.op("act", lambda a: a.activation(out=junk[0:nt, 0:in_ap.shape[-1]], in_=in_ap, func=AF.Square,
                                                accum_out=s[0:nt, 2:3]), reads=reads, writes=[r_junk, r_s])
            return s[0:nt, 2:3], r_s

        def phase_norm_gen(l, tiles, xsrc, r_xsrc):
            for g_ in range(0, len(tiles), 2):
                yield from norm_pair_gen(l, tiles[g_:g_ + 2], xsrc, r_xsrc)

        def norm_pair_gen(l, tiles, xsrc, r_xsrc):
            st_ = []
            for (nt, row0, col0) in tiles:
                xt, r_xt = xin.next()
                fw.dma("act", xt[0:nt, :], xsrc[row0:row0 + nt, :], reads=[r_xsrc[row0 // 64]] if r_xsrc else [], writes=[r_xt])
                st_.append((xt, r_xt))
            yield
            yield
            for ti_, (nt, row0, col0) in enumerate(tiles):
                xt, r_xt = st_[ti_]
                s_, r_ss = sm.next()
                for hf in range(2):
                    fw.op("act", lambda a, hf=hf, s_=s_, xt=xt, nt=nt: a.activation(out=junk[0:nt, :], in_=xt[0:nt, hf * 512:(hf + 1) * 512], func=AF.Square,
                                                                                    accum_out=s_[0:nt, 4 + hf:5 + hf]), reads=[r_xt], writes=[r_junk, r_ss])
                yield
                fw.op("dve", lambda v, s_=s_, nt=nt: v.tensor_tensor(out=s_[0:nt, 2:3], in0=s_[0:nt, 4:5], in1=s_[0:nt, 5:6], op=ALU.add), reads=[r_ss], writes=[r_ss])
                s2_, r_s2 = sm.next()
                fw.op("dve", lambda v, s_=s_, s2_=s2_, nt=nt: v.tensor_scalar(out=s2_[0:nt, 0:1], in0=s_[0:nt, 2:3], scalar1=1.0 / D, scalar2=EPS,
                                                                               op0=ALU.mult, op1=ALU.add), reads=[r_ss], writes=[r_s2])
                yield
                fw.op("act", lambda a_, s2_=s2_, nt=nt: a_.activation(out=s2_[0:nt, 3:4], in_=s2_[0:nt, 0:1], func=AF.Sqrt), reads=[r_s2], writes=[r_s2])
                yield
                fw.op("dve", lambda v, s2_=s2_, nt=nt: v.reciprocal(out=s2_[0:nt, 1:2], in_=s2_[0:nt, 3:4]), reads=[r_s2], writes=[r_s2])
                rs, r_rs = s2_[0:nt, 1:2], r_s2
                h, r_h = hb.next()
                fw.op("dve", lambda v, h=h, xt=xt, rs=rs, nt=nt: v.tensor_scalar(out=h[0:nt, :], in0=xt[0:nt, :], scalar1=rs, scalar2=None,
                                                                                  op0=ALU.mult), reads=[r_xt, r_rs], writes=[r_h])
                yield
                for c in range(8):
                    fw.op("pe", lambda t, h=h, c=c, nt=nt: t.transpose(psb[:, c * 128:c * 128 + nt], h[0:nt, c * 128:(c + 1) * 128],
                                                                       ident[0:nt, 0:nt]), reads=[r_h, r_ident], writes=[r_psb])
                yield
                fw.op("dve", lambda v, nt=nt, col0=col0: v.tensor_tensor(
                    out=hT[:, :, col0:col0 + nt], in0=psb[:, :].rearrange("p (c t) -> p c t", c=8)[:, :, 0:nt],
                    in1=gpre[:, l, :].unsqueeze(2).broadcast_to([128, 8, nt]), op=ALU.mult),
                    reads=[r_psb, r_gpre], writes=[r_hT])
                yield

        def phase_norm(l, tiles, xsrc, r_xsrc):
            for _ in phase_norm_gen(l, tiles, xsrc, r_xsrc):
                pass

        def tm_proj(nt, col0, wcols, ncols, b):
            for c in range(8):
                fw.op("pe", lambda t, c=c: t.matmul(ps[0:nt, b, 0:ncols], lhsT=hT[:, c, col0:col0 + nt],
                                                    rhs=wbuf[:, c, wcols:wcols + ncols], start=(c == 0), stop=(c == 7)),
                      reads=[r_hT, r_w], writes=[r_ps[b]])

        def fm_proj(wcols, b, half):
            for c in range(8):
                fw.op("pe", lambda t, c=c: t.matmul(ps[:, b, half * BLK:(half + 1) * BLK], lhsT=wbuf[:, c, wcols:wcols + 128],
                                                    rhs=hT[:, c, :], start=(c == 0), stop=(c == 7)),
                      reads=[r_hT, r_w], writes=[r_ps[b]])

        def fm_group(wcol_list, evac):
            for i in range(0, len(wcol_list), 2):
                b = gbank.next()
                n = min(2, len(wcol_list) - i)
                for j in range(n):
                    fm_proj(wcol_list[i + j], b, j)
                for j in range(n):
                    evac(i + j, ps[:, b, j * BLK:(j + 1) * BLK], r_ps[b])

        def rope_tm(src, nheads, nt, tab, r_tab, out_ap, reads, writes):
            t1, r_t1 = ropet.next()
            t2, r_t2 = ropet.next()
            n = nheads * 32
            s3 = src.rearrange("p (h r) -> p h r", h=nheads)
            a1 = t1[0:nt, 0:n].rearrange("p (h r) -> p h r", h=nheads)
            a2 = t2[0:nt, 0:n].rearrange("p (h r) -> p h r", h=nheads)
            cosb = tab[:, 0:32].unsqueeze(1).broadcast_to([nt, nheads, 32])
            sin_lo = tab[:, 32:48].unsqueeze(1).broadcast_to([nt, nheads, 16])
            sin_hi = tab[:, 48:64].unsqueeze(1).broadcast_to([nt, nheads, 16])
            fw.op("dve", lambda v: v.tensor_tensor(out=a1, in0=s3, in1=cosb, op=ALU.mult), reads=reads + [r_tab], writes=[r_t1])
            fw.op("dve", lambda v: v.tensor_tensor(out=a2[:, :, 0:16], in0=s3[:, :, 16:32], in1=sin_lo, op=ALU.mult),
                  reads=reads + [r_tab], writes=[r_t2])
            fw.op("dve", lambda v: v.tensor_tensor(out=a2[:, :, 16:32], in0=s3[:, :, 0:16], in1=sin_hi, op=ALU.mult),
                  reads=reads + [r_tab], writes=[r_t2])
            fw.op("dve", lambda v: v.tensor_tensor(out=out_ap, in0=t1[0:nt, 0:n], in1=t2[0:nt, 0:n], op=ALU.add),
                  reads=[r_t1, r_t2], writes=writes)

        def evac_bd(dst, r_dst, i, src, rb):
            fw.op("act", lambda a: a.activation(out=dst[0:64, i, :, 0:128], in_=src[0:64, :].rearrange("p (t q) -> p t q", t=2),
                                                func=AF.Copy, scale=0.125), reads=[rb], writes=[r_dst])
            fw.op("act", lambda a: a.activation(out=dst[64:128, i, :, 128:256], in_=src[64:128, :].rearrange("p (t q) -> p t q", t=2),
                                                func=AF.Copy, scale=0.125), reads=[rb], writes=[r_dst])

        def phase_proj0(tiles, is_s, o_ckv, o_kr, o_sbk, o_sbv):
            for ti, (nt, row0, col0) in enumerate(tiles):
                gt = (row0 // 128) if not is_s else ti
                b = gbank.next()
                tm_proj(nt, col0, 0, 416, b)
                ssq, r_ssq = sumsq(ps[0:nt, b, 0:256], nt, [r_ps[b]])
                rq, r_rq = rstd_from_ss(ssq, 256, nt, r_ssq)
                fw.op("dve", lambda v, b=b, rq=rq, nt=nt: v.scalar_tensor_tensor(out=cq_b[0:nt, :], in0=ps[0:nt, b, 0:256], scalar=rq,
                                                                                  in1=qn_bc[0:nt, :], op0=ALU.mult, op1=ALU.mult),
                      reads=[r_ps[b], r_rq, r_qn], writes=[r_cqb])
                ssk, r_ssk = sumsq(ps[0:nt, b, 256:384], nt, [r_ps[b]])
                rk, r_rk = rstd_from_ss(ssk, 128, nt, r_ssk)
                stg, r_stg = st512.next()
                fw.op("dve", lambda v, b=b, rk=rk, nt=nt, stg=stg: v.scalar_tensor_tensor(out=stg[0:nt, 0:128], in0=ps[0:nt, b, 256:384], scalar=rk,
                                                                                           in1=kvn_bc[0:nt, :], op0=ALU.mult, op1=ALU.mult),
                      reads=[r_ps[b], r_rk, r_kvn], writes=[r_stg])
                if is_s:
                    tab, r_tab = ropeS[0:nt, :], r_ropeS
                else:
                    tb_, r_tab = ropePt.next()
                    fw.dma("sp", tb_[:, :], rope_p[:, gt * 64:(gt + 1) * 64], writes=[r_tab])
                    tab = tb_[0:nt, :]
                rope_tm(ps[0:nt, b, 384:416], 1, nt, tab, r_tab, stg[0:nt, 128:160], [r_ps[b]], [r_stg])
                fw.dma("sp", o_ckv[row0:row0 + nt, :], stg[0:nt, 0:128], reads=[r_stg])
                fw.dma("sp", o_kr[row0:row0 + nt, :], stg[0:nt, 128:160], reads=[r_stg])
                fw.op("act", lambda a, stg=stg, nt=nt: a.copy(out=kvb[0:nt, :], in_=stg[0:nt, 0:160]), reads=[r_stg], writes=[r_kvb])
                if is_s:
                    fw.op("pool", lambda g, ti=ti, nt=nt: g.tensor_copy(out=sckv_tm[0:nt, ti, :], in_=kvb[0:nt, 0:128]),
                          reads=[r_kvb], writes=[r_sk[4]])
                else:
                    fw.op("pool", lambda g, gt=gt: g.tensor_copy(out=ckv_tm[:, gt, :], in_=kvb[:, 0:128]), reads=[r_kvb], writes=[r_k3[gt]])
                for j in range(2):
                    fw.op("pe", lambda t, j=j, nt=nt: t.transpose(psb[:, j * 128:j * 128 + nt], cq_b[0:nt, j * 128:(j + 1) * 128],
                                                                  ident[0:nt, 0:nt]), reads=[r_cqb, r_ident], writes=[r_psb])
                fw.op("pe", lambda t, nt=nt: t.transpose(psb[:, 256:256 + nt], kvb[0:nt, 0:128], ident[0:nt, 0:nt]),
                      reads=[r_kvb, r_ident], writes=[r_psb])
                fw.op("pe", lambda t, nt=nt: t.transpose(psb[0:32, 384:384 + nt], kvb[0:nt, 128:160], ident[0:nt, 0:nt]),
                      reads=[r_kvb, r_ident], writes=[r_psb])
                fw.op("dve", lambda v, nt=nt, col0=col0: v.tensor_copy(out=cqT[:, :, col0:col0 + nt],
                                                                        in_=psb[:, 0:256].rearrange("p (c t) -> p c t", c=2)[:, :, 0:nt]),
                      reads=[r_psb], writes=[r_cqT])
                if is_s:
                    dckv, dkr, rr = sckvT[:, col0:col0 + nt], skrT[:, col0:col0 + nt], r_sk[5]
                else:
                    dckv, dkr, rr = ckvT[:, row0:row0 + nt], krT[0:32, row0:row0 + nt], r_k3[gt]
                fw.op("dve", lambda v, nt=nt, dckv=dckv: v.tensor_copy(out=dckv, in_=psb[:, 256:256 + nt]), reads=[r_psb], writes=[rr])
                fw.op("dve", lambda v, nt=nt, dkr=dkr: v.tensor_copy(out=dkr, in_=psb[0:32, 384:384 + nt]), reads=[r_psb], writes=[rr])
                b = gbank.next()
                tm_proj(nt, col0, 1440, 512, b)
                stg, r_stg = st512.next()
                fw.op("act", lambda a, b=b, stg=stg, nt=nt: a.copy(out=stg[0:nt, :], in_=ps[0:nt, b, :]), reads=[r_ps[b]], writes=[r_stg])
                fw.dma("sp", o_sbk[row0:row0 + nt, :], stg[0:nt, :], reads=[r_stg])
                b = gbank.next()
                tm_proj(nt, col0, 1952, 512, b)
                stg, r_stg = st512.next()
                fw.op("act", lambda a, b=b, stg=stg, nt=nt: a.copy(out=stg[0:nt, :], in_=ps[0:nt, b, :]), reads=[r_ps[b]], writes=[r_stg])
                fw.dma("sp", o_sbv[row0:row0 + nt, :], stg[0:nt, :], reads=[r_stg])
                if is_s:
                    fw.op("dve", lambda v, b=b, ti=ti, nt=nt: v.tensor_copy(out=sv1[0:nt, ti, :], in_=ps[0:nt, b, :]), reads=[r_ps[b]], writes=[r_sk[1]])
                else:
                    if os.environ.get("KSKIPV1") is None:
                        fw.op("dve", lambda v, b=b, gt=gt: v.tensor_copy(out=v1[:, gt, :], in_=ps[:, b, :]), reads=[r_ps[b]], writes=[r_k1[gt]])
                b = gbank.next()
                for cc in range(2):
                    fw.op("pe", lambda t, cc=cc, b=b, nt=nt, col0=col0: t.matmul(ps[0:nt, b, 0:256], lhsT=cqT[:, cc, col0:col0 + nt],
                                                                                 rhs=wuq[:, cc, 512:768], start=(cc == 0), stop=(cc == 1)),
                          reads=[r_cqT, r_wuq], writes=[r_ps[b]])
                rope_tm(ps[0:nt, b, 0:256], 8, nt, tab, r_tab, qr_b[0:nt, :], [r_ps[b]], [r_qrb])
                for h in range(8):
                    fw.op("pe", lambda t, h=h, nt=nt: t.transpose(psb[0:32, h * 128:h * 128 + nt], qr_b[0:nt, h * 32:(h + 1) * 32],
                                                                  ident[0:nt, 0:nt]), reads=[r_qrb, r_ident], writes=[r_psb])
                fw.op("dve", lambda v, nt=nt, col0=col0: v.tensor_copy(out=qrT[:, :, col0:col0 + nt],
                                                                        in_=psb[0:32, :].rearrange("p (h t) -> p h t", h=8)[:, :, 0:nt]),
                      reads=[r_psb], writes=[r_qrT])
            fm_group([928 + 128 * i for i in range(4)],
                     lambda i, src, rb: evac_bd(qA, r_qA, i, src, rb))
            if is_s:
                fm_group([1440 + 128 * i for i in range(4)],
                         lambda i, src, rb: fw.op("dve", lambda v: v.tensor_copy(out=skT1[:, i, :], in_=src), reads=[rb], writes=[r_sk[0]]))
            else:
                t0 = tiles[0][1]
                g0 = t0 // 128

                def ev(i, src, rb):
                    fw.op("dve", lambda v: v.tensor_copy(out=kT1[:, i, t0:t0 + BLK], in_=src), reads=[rb], writes=[r_k1[g0], r_k1[g0 + 1]])
                fm_group([1440 + 128 * i for i in range(4)], ev)
            fm_group([416 + 128 * i for i in range(4)] + [2464 + 128 * i for i in range(4)],
                     lambda i, src, rb: fw.op("act", lambda a: a.activation(out=gT[:, i, :], in_=src, func=AF.Silu),
                                              reads=[rb], writes=[r_gT[i]]))
            for i in range(0, 4, 2):
                b = gbank.next()
                for j in range(2):
                    for cc in range(2):
                        fw.op("pe", lambda t, cc=cc, b=b, i=i, j=j: t.matmul(ps[:, b, j * BLK:(j + 1) * BLK], lhsT=wuq[:, cc, (i + j) * 128:(i + j + 1) * 128],
                                                                             rhs=cqT[:, cc, :], start=(cc == 0), stop=(cc == 1)),
                              reads=[r_cqT, r_wuq], writes=[r_ps[b]])
                fw.op("dve", lambda v, b=b, i=i: v.tensor_copy(out=qB[:, i:i + 2, :], in_=ps[:, b, :].rearrange("p (a t) -> p a t", a=2)),
                      reads=[r_ps[b]], writes=[r_qB])
            for h0 in range(0, 8, 2):
                b = gbank.next()
                for j in range(2):
                    h = h0 + j
                    pb = 64 * (h % 2)
                    mm(ps[:, b, j * BLK:(j + 1) * BLK], wukT[pb:pb + 64, h * 128:(h + 1) * 128], qB[pb:pb + 64, h // 2, :], True, True,
                       [r_qB, r_wuk], r_ps[b])
                for j in range(2):
                    fw.op("act", lambda a, b=b, h0=h0, j=j: a.copy(out=qlat[:, h0 + j, :], in_=ps[:, b, j * BLK:(j + 1) * BLK]),
                          reads=[r_ps[b]], writes=[r_qlat])

        def fin_softmax(slots, ob, db, has_den=True):
            if has_den:
                fw.op("dve", lambda v: v.reciprocal(out=rden[:, :], in_=ps[:, db, :]), reads=[r_ps[db]], writes=[r_rden])
            if len(slots) == 4 and slots[0][2] == 128:
                for par in (0, 1):
                    h, c0, n, gc, g0 = slots[par]
                    pb = 64 * par
                    ov = ps[pb:pb + 64, ob, :].rearrange("p (s q) -> p s q", s=4)[:, par:4:2, :]
                    gv = gT[pb:pb + 64, gc:gc + 2, g0:g0 + 128]
                    rgs = [r_gT[gc], r_gT[gc + 1]]
                    if has_den:
                        t, r_t = fint.next()
                        tv = t[pb:pb + 64, 0:256].rearrange("p (s q) -> p s q", s=2)
                        rv = rden[pb:pb + 64, :].rearrange("p (s q) -> p s q", s=4)[:, par:4:2, :]
                        fw.op("dve", lambda v, tv=tv, ov=ov, rv=rv: v.tensor_tensor(out=tv, in0=ov, in1=rv, op=ALU.mult),
                              reads=[r_ps[ob], r_rden], writes=[r_t])
                        fw.op("dve", lambda v, tv=tv, gv=gv: v.tensor_tensor(out=gv, in0=tv, in1=gv, op=ALU.mult), reads=[r_t] + rgs, writes=rgs)
                    else:
                        fw.op("dve", lambda v, ov=ov, gv=gv: v.tensor_tensor(out=gv, in0=ov, in1=gv, op=ALU.mult), reads=[r_ps[ob]] + rgs, writes=rgs)
                return
            for (h, c0, n, gc, g0) in slots:
                pb = 64 * (h % 2)
                if has_den:
                    t, r_t = fint.next()
                    fw.op("dve", lambda v, t=t, pb=pb, c0=c0, n=n: v.tensor_tensor(out=t[pb:pb + 64, 0:n], in0=ps[pb:pb + 64, ob, c0:c0 + n],
                                                                                   in1=rden[pb:pb + 64, c0:c0 + n], op=ALU.mult),
                          reads=[r_ps[ob], r_rden], writes=[r_t])
                    fw.op("dve", lambda v, t=t, pb=pb, n=n, gc=gc, g0=g0: v.tensor_tensor(out=gT[pb:pb + 64, gc, g0:g0 + n], in0=t[pb:pb + 64, 0:n],
                                                                                          in1=gT[pb:pb + 64, gc, g0:g0 + n], op=ALU.mult),
                          reads=[r_t, r_gT[gc]], writes=[r_gT[gc]])
                else:
                    fw.op("dve", lambda v, pb=pb, c0=c0, n=n, gc=gc, g0=g0: v.tensor_tensor(out=gT[pb:pb + 64, gc, g0:g0 + n], in0=ps[pb:pb + 64, ob, c0:c0 + n],
                                                                                            in1=gT[pb:pb + 64, gc, g0:g0 + n], op=ALU.mult),
                          reads=[r_ps[ob], r_gT[gc]], writes=[r_gT[gc]])

        odpair = RR([(4, 5), (2, 3)])
        sbank = RR([0, 1])
        abank = RR([2, 3])
        obank = RR([4, 5])

        class U:
            hbias = None
            cast = None
            dma = None
            prep = None
            mask = None
            pre = None
            scale = 1.0

        def sprep(u):
            if u.prep is not None:
                u.prep(u)

        def sdma(u):
            if u.dma is not None:
                u.dma(u)

        def scast(u):
            if u.cast is not None:
                u.cast(u)

        def sb_chain(units, bg=None):
            def s0(u):
                u.sb = sbank.next()
                nk = u.nk
                nz = len(u.zmm)
                for i, (l, r, c0, n) in enumerate(u.zmm):
                    mm(ps[0:u.nk, u.sb, c0:c0 + n], l, r, (i == 0), (u.mask is None and i == nz - 1), u.reads, r_ps[u.sb])
                if u.mask is not None:
                    mm(ps[0:u.nk, u.sb, :], ident[0:u.nk, 0:u.nk], u.mask, False, True, [r_ident, u.rmask], r_ps[u.sb])
                e, r_e = ebuf.next()
                fw.op("act", lambda a, u=u, e=e: a.activation(out=e[0:u.nk, :], in_=ps[0:u.nk, u.sb, :], func=AF.Exp), reads=[r_ps[u.sb]], writes=[r_e])
                u.sp, u.r_sp = spb.next()
                fw.op("act", lambda a, u=u, e=e: a.activation(out=u.sp[0:u.nk, :], in_=e[0:u.nk, :], func=AF.Ln, bias=1.0), reads=[r_e], writes=[u.r_sp])

            def s1(u):
                u.ab = abank.next()
                for i, (l, r, c0, n) in enumerate(u.zmm):
                    mm(ps[0:u.nk, u.ab, c0:c0 + n], l, r, (i == 0), False, u.reads, r_ps[u.ab])
                if u.mask is not None:
                    mm(ps[0:u.nk, u.ab, :], ident[0:u.nk, 0:u.nk], u.mask, False, False, [r_ident, u.rmask], r_ps[u.ab])
                mm(ps[0:u.nk, u.ab, :], negTri[0:u.nk, 0:u.nk], u.sp[0:u.nk, :], False, u.first, [r_negTri, u.r_sp], r_ps[u.ab])
                if not u.first:
                    mm(ps[0:u.nk, u.ab, :], negOnes[:, 0:u.nk], Rbuf[:, :], False, True, [r_negOnes, r_R], r_ps[u.ab])
                if not u.last:
                    if u.first:
                        if u.nk < 128:
                            fw.op("pool", lambda g: g.memset(Rbuf[:, :], 0.0), writes=[r_R])
                        fw.op("pool", lambda g, u=u: g.tensor_copy(out=Rbuf[0:u.nk, :], in_=u.sp[0:u.nk, :]), reads=[u.r_sp], writes=[r_R])
                    else:
                        fw.op("pool", lambda g, u=u: g.tensor_tensor(out=Rbuf[0:u.nk, :], in0=Rbuf[0:u.nk, :], in1=u.sp[0:u.nk, :], op=ALU.add),
                              reads=[u.r_sp, r_R], writes=[r_R])
                u.w, u.r_w = pbf.next()
                fw.op("act", lambda a, u=u: a.activation(out=u.w[0:u.nk, :], in_=ps[0:u.nk, u.ab, :], func=AF.Exp), reads=[r_ps[u.ab]], writes=[u.r_w])

            def s2(u):
                if u.first:
                    u.chain["ob"] = obank.next()
                ob = u.chain["ob"]
                for vi, (lv, c0, n) in enumerate(u.vmm):
                    mm(ps[:, ob, c0:c0 + n], lv, u.w[0:u.nk, c0:c0 + n], (u.first and vi == 0), (u.last and vi == len(u.vmm) - 1),
                       u.vreads + [u.r_w], r_ps[ob])
                if u.last:
                    u.fin(ob)
            pipeline(units, [sdma, scast, sprep, s0, s1, s2], [0, 3, 4, 5, 6, 7], bg=bg)

        def sm_chain(units, bg=None):
            def s0(u):
                u.sb = sbank.next()
                for (l, r, c0, n, st, sp_) in u.zmm:
                    o = ps[0:u.nk, u.sb, c0:c0 + n]
                    if len(r.shape) == 3:
                        o = o.rearrange("p (h q) -> p h q", h=r.shape[1])
                    mm(o, l, r, st, sp_, u.reads, r_ps[u.sb])
                if u.hbias is not None:
                    u.src, u.r_src = None, None
                elif u.pre is not None:
                    u.src, u.r_src = u.pre(u)
                else:
                    u.src, u.r_src = ps[0:u.nk, u.sb, :], r_ps[u.sb]

            def s1(u):
                u.p, u.r_p = pbf.next()
                if u.hbias is not None:
                    w_ = 512 // len(u.hbias)
                    for j, (bap, rb) in enumerate(u.hbias):
                        fw.op("act", lambda a, u=u, j=j, bap=bap, w_=w_: a.activation(out=u.p[0:u.nk, j * w_:(j + 1) * w_], in_=ps[0:u.nk, u.sb, j * w_:(j + 1) * w_],
                                                                                  func=AF.Exp, bias=bap, scale=u.scale),
                              reads=[r_ps[u.sb], rb], writes=[u.r_p])
                else:
                    fw.op("act", lambda a, u=u: a.activation(out=u.p[0:u.nk, :], in_=u.src, func=AF.Exp, scale=u.scale), reads=[u.r_src], writes=[u.r_p])

            def s2(u):
                if u.first:
                    u.chain["ob"], u.chain["db"] = odpair.next()
                ob, db = u.chain["ob"], u.chain["db"]
                for vi, (lv, c0, n) in enumerate(u.vmm):
                    mm(ps[:, ob, c0:c0 + n], lv, u.p[0:u.nk, c0:c0 + n], (u.first and vi == 0), (u.last and vi == len(u.vmm) - 1),
                       u.vreads + [u.r_p], r_ps[ob])
                mm(ps[:, db, :], onesB[0:u.nk, :], u.p[0:u.nk, :], u.first, u.last, [r_onesB, u.r_p], r_ps[db])
                if u.last:
                    u.fin(ob, db)
            pipeline(units, [sdma, scast, sprep, s0, s1, s2], [0, 3, 4, 5, 6, 7], bg=bg)

        def pre_add(u, in1, r_in1, scale=1.0, extra=None):
            t, r_t = tmpf.next()
            nk = u.nk
            H = in1.shape[1]
            n = 512 // H
            fw.op("dve", lambda v: v.scalar_tensor_tensor(out=t[0:nk, :].rearrange("p (h q) -> p h q", h=H),
                                                          in0=ps[0:nk, u.sb, :].rearrange("p (h q) -> p h q", h=H), scalar=scale,
                                                          in1=in1, op0=ALU.mult, op1=ALU.add), reads=[r_ps[u.sb]] + r_in1, writes=[r_t])
            if extra is not None:
                ex, r_ex = extra
                fw.op("dve", lambda v: v.tensor_tensor(out=t[0:nk, :].rearrange("p (h q) -> p h q", h=H),
                                                       in0=t[0:nk, :].rearrange("p (h q) -> p h q", h=H), in1=ex, op=ALU.add),
                      reads=[r_t] + r_ex, writes=[r_t])
            return t[0:nk, :], r_t

        def mla_fin_factory(slots_fn):
            def fin(ob, db):
                fw.op("dve", lambda v: v.tensor_copy(out=latb[:, :], in_=ps[:, ob, :]), reads=[r_ps[ob]], writes=[r_latb])
                slots = slots_fn()
                gb = 6
                for (h, c0, n, gc, g0) in slots:
                    fw.op("pe", lambda t, h=h, c0=c0, n=n: t.matmul(ps[:, gb, c0:c0 + n], lhsT=wuv[:, (h // 2) * 128:(h // 2 + 1) * 128],
                                                                    rhs=latb[:, c0:c0 + n], start=True, stop=True),
                          reads=[r_latb, r_wuv], writes=[r_ps[gb]])
                fin_softmax(slots, gb, db)
            return fin

        def attn0_prompt(bi, bg=None):
            all_sb = []
            all_mla = []
            for qt in (2 * bi, 2 * bi + 1):
                qc = (qt % 2) * 128
                for hg in range(2):
                    chain = {}
                    units = []
                    for kt in range(qt, -1, -1):
                        u = U()
                        u.nk = 128; u.chain = chain
                        u.first = (kt == qt); u.last = (kt == 0)
                        u.zmm = []
                        u.vmm = []
                        for pp in range(2):
                            u.zmm.append((kT1[:, 2 * hg + pp, kt * 128:(kt + 1) * 128], qA[:, 2 * hg + pp, qt % 2, :], pp * 256, 256))
                        for pp in range(2):
                            u.vmm.append((v1[:, kt, (2 * hg + pp) * 128:(2 * hg + pp + 1) * 128], pp * 256, 256))
                        u.mask = maskSB[:, :] if kt == qt else None
                        u.rmask = r_maskSB
                        u.reads = [r_k1[kt], r_qA]
                        u.vreads = [r_k1[kt]]
                        slots = [(4 * hg + s, s * 128, 128, 4 + (4 * hg + s) // 2, qc) for s in range(4)]
                        u.fin = (lambda ob, slots=slots: fin_softmax(slots, ob, None, has_den=False))
                        units.append(u)
                    all_sb += units
                    chain = {}
                    units = []
                    for kt in range(qt, -1, -1):
                        u = U()
                        u.nk = 128; u.chain = chain
                        u.first = (kt == qt); u.last = (kt == 0)
                        u.zmm = [(ckvT[:, kt * 128:(kt + 1) * 128], qlat[:, 4 * hg:4 * hg + 4, qc:qc + 128], 0, 512, True, False),
                                 (krT[0:32, kt * 128:(kt + 1) * 128], qrT[:, 4 * hg:4 * hg + 4, qc:qc + 128], 0, 512, False, True)]
                        u.reads = [r_k3[kt], r_qlat, r_qrT]
                        u.scale = A_SCALE
                        if kt == qt:
                            u.pre = lambda u: pre_add(u, maskCH[:, :].unsqueeze(1).broadcast_to([128, 4, 128]), [r_maskCH], scale=A_SCALE)
                            u.scale = 1.0
                        else:
                            u.pre = None
                        u.vmm = [(ckv_tm[:, kt, :], 0, 512)]
                        u.vreads = [r_k3[kt]]
                        slots = [(4 * hg + s, s * 128, 128, (4 * hg + s) // 2, qc) for s in range(4)]
                        u.fin = mla_fin_factory(lambda slots=slots: slots)
                        units.append(u)
                    all_mla += units
            sb_chain(all_sb, bg=bg)
            sm_chain(all_mla)


        HORD = (0, 2, 4, 6, 1, 3, 5, 7)

        def dma_kv_tile(u, kdram, vdram, s, t):
            u.kf, u.r_kf = cst_f.next()
            fw.dma("sp", u.kf[:, :], kdram[s, t * 128:(t + 1) * 128, :], writes=[u.r_kf])
            u.vf, u.r_vf = cst_f.next()
            fw.dma("sp", u.vf[:, :], vdram[s, t * 128:(t + 1) * 128, :], writes=[u.r_vf])

        def cast_kv_tile(u):
            u.kb, u.r_kb = kbf.next()
            cast(u.kb[:, :], u.kf[:, :], [u.r_kf], [u.r_kb])
            u.vb, u.r_vb = vbf.next()
            cast(u.vb[:, :], u.vf[:, :], [u.r_vf], [u.r_vb])

        def load_kv_tile(u):
            kb, r_kb = u.kb, u.r_kb
            for c in range(4):
                fw.op("pe", lambda t_, c=c, kb=kb: t_.transpose(psb[:, c * 128:(c + 1) * 128], kb[:, c * 128:(c + 1) * 128], ident[:, :]),
                      reads=[r_kb, r_ident], writes=[r_psb])
            kt_, r_kt = cKT.next()
            ee = EVAC_ENGS[0][cast_i[0] % len(EVAC_ENGS[0])]
            if ee == "act":
                fw.op("act", lambda a, kt_=kt_: a.copy(out=kt_[:, :, :], in_=psb[:, 0:512].rearrange("p (c t) -> p c t", c=4)),
                      reads=[r_psb], writes=[r_kt])
            else:
                fw.op("dve", lambda v, kt_=kt_: v.tensor_copy(out=kt_[:, :, :], in_=psb[:, 0:512].rearrange("p (c t) -> p c t", c=4)),
                      reads=[r_psb], writes=[r_kt])
            return kt_, r_kt, u.vb, u.r_vb

        def kv_units(s, ncache, kdram, vdram, qsrc, r_q, knew, vnew, r_new, slots):
            qb, r_qb = qbd[s]
            fw.op("pool", lambda g: g.memset(qb[:, :, :], 0.0), writes=[r_qb])
            so = (s % 2) * 64
            fw.op("pool", lambda g: g.tensor_copy(out=qb[0:64, :, 0:64], in_=qsrc[0:64, :, s // 2, so:so + 64]), reads=[r_q], writes=[r_qb])
            fw.op("pool", lambda g: g.tensor_copy(out=qb[64:128, :, 64:128], in_=qsrc[64:128, :, s // 2, 128 + so:128 + so + 64]), reads=[r_q], writes=[r_qb])
            chain = {}
            units = []
            u = U(); u.nk = 64; u.chain = chain; u.first = True; u.last = False; u.tile = ncache
            u.zmm = []; u.vmm = []
            for p in range(4):
                u.zmm.append((knew[:, p, 64 * s:64 * s + 64], qb[:, p, :], p * 128, 128))
                u.vmm.append((vnew[0:64, s, p * 128:(p + 1) * 128], p * 128, 128))
            u.reads = [r_new, r_qb]; u.vreads = [r_new]
            units.append(u)
            for t in range(ncache - 1, -1, -1):
                u = U(); u.nk = 128; u.chain = chain; u.first = False; u.last = (t == 0); u.tile = t
                u.dma = (lambda u, t=t: dma_kv_tile(u, kdram, vdram, s, t))
                u.cast = cast_kv_tile

                def prep(u, t=t):
                    kt_, r_kt, vb, r_vb = load_kv_tile(u)
                    u.zmm = []; u.vmm = []
                    for p in range(4):
                        u.zmm.append((kt_[:, p, :], qb[:, p, :], p * 128, 128))
                        u.vmm.append((vb[:, p * 128:(p + 1) * 128], p * 128, 128))
                    u.reads = [r_kt, r_qb]; u.vreads = [r_vb]
                u.prep = prep
                units.append(u)
            return units

        def zmm4(u):
            z = []
            for i, (l, r, c0, n) in enumerate(u.zmm):
                z.append((l, r, c0, n, i == 0, i == len(u.zmm) - 1))
            u.zmm = z

        def attn0_sample():
            all_sb = []
            all_mla = []
            for s in range(NSTR):
                slots = [(h, h * 64, 64, 4 + h // 2, 64 * s) for h in range(8)]
                units = kv_units(s, PAST // 128, c_sbk, c_sbv, qA, r_qA, skT1, sv1, r_sk[0], slots)
                units[0].mask = maskSB64[:, :]; units[0].rmask = r_maskSB64
                units[0].reads = [r_sk[0], qbd[s][1]]; units[0].vreads = [r_sk[1]]
                for u in units:
                    u.rmask = r_maskSB64
                    u.fin = (lambda ob, slots=slots: fin_softmax(slots, ob, None, has_den=False))
                all_sb += units
            sb_chain(all_sb)
            for s in range(NSTR):
                chain = {}
                units = []
                slots = [(h, h * 64, 64, h // 2, 64 * s) for h in range(8)]
                u = U(); u.nk = 64; u.chain = chain; u.first = True; u.last = False
                u.zmm = [(sckvT[:, 64 * s:64 * s + 64], qlat[:, :, 64 * s:64 * s + 64], 0, 512, True, False),
                         (skrT[0:32, 64 * s:64 * s + 64], qrT[:, :, 64 * s:64 * s + 64], 0, 512, False, True)]
                u.reads = [r_sk[5], r_qlat, r_qrT]; u.scale = A_SCALE
                u.vmm = [(sckv_tm[0:64, s, :], 0, 512)]; u.vreads = [r_sk[4]]
                units.append(u)
                for t in range(PAST // 128 - 1, -1, -1):
                    u = U(); u.nk = 128; u.chain = chain; u.first = False; u.last = (t == 0); u.scale = A_SCALE

                    def dma_(u, t=t, s=s):
                        u.cf, u.r_cf = cst_f.next()
                        fw.dma("sp", u.cf[:, 0:128], c_ckv[s, t * 128:(t + 1) * 128, :], writes=[u.r_cf])
                        fw.dma("sp", u.cf[:, 128:160], c_kr[s, t * 128:(t + 1) * 128, :], writes=[u.r_cf])
                    u.dma = dma_

                    def cast_(u):
                        u.vb, u.r_vb = vbf.next()
                        cast(u.vb[:, 0:160], u.cf[:, 0:160], [u.r_cf], [u.r_vb])
                    u.cast = cast_

                    def prep(u, t=t, s=s):
                        vb, r_vb = u.vb, u.r_vb
                        fw.op("pe", lambda t_, vb=vb: t_.transpose(psb[:, 0:128], vb[:, 0:128], ident[:, :]), reads=[r_vb, r_ident], writes=[r_psb])
                        fw.op("pe", lambda t_, vb=vb: t_.transpose(psb[0:32, 128:256], vb[:, 128:160], ident[:, :]), reads=[r_vb, r_ident], writes=[r_psb])
                        kt_, r_kt = cKT.next()
                        fw.op("dve", lambda v, kt_=kt_: v.tensor_copy(out=kt_[:, 0, :], in_=psb[:, 0:128]), reads=[r_psb], writes=[r_kt])
                        fw.op("dve", lambda v, kt_=kt_: v.tensor_copy(out=kt_[0:32, 1, :], in_=psb[0:32, 128:256]), reads=[r_psb], writes=[r_kt])
                        u.zmm = [(kt_[:, 0, :], qlat[:, :, 64 * s:64 * s + 64], 0, 512, True, False),
                                 (kt_[0:32, 1, :], qrT[:, :, 64 * s:64 * s + 64], 0, 512, False, True)]
                        u.reads = [r_kt, r_qlat, r_qrT]
                        u.vmm = [(vb[:, 0:128], 0, 512)]; u.vreads = [r_vb]
                    u.prep = prep
                    units.append(u)
                for u in units:
                    u.fin = mla_fin_factory(lambda slots=slots: slots)
                all_mla += units
            sm_chain(all_mla)

        r_kc = [Res() for _ in range(8)]

        def setup_l1_tables():
            tb2 = tblB[:].rearrange("p a b -> p (a b)")
            fw.dma("sp", tblB[:], bass.AP(relb.tensor, 1, [[1, 128], [513, 8], [1, 256]]), writes=[r_tblB])
            for j in range(4):
                fw.op("pe", lambda t_, j=j: t_.matmul(ps[:, j, :], lhsT=flipJ[:, :], rhs=tb2[:, j * 512:(j + 1) * 512], start=True, stop=True),
                      reads=[r_flip, r_tblB], writes=[r_ps[j]])
            for j in range(4):
                fw.op("dve", lambda v, j=j: v.tensor_copy(out=tb2[:, j * 512:(j + 1) * 512], in_=ps[:, j, :]), reads=[r_ps[j]], writes=[r_tblB])
            fw.op("dve", lambda v: v.tensor_tensor(out=tblB[:, :, 0:128], in0=tblB[:, :, 0:128],
                                                   in1=maskCH[:, :].unsqueeze(1).broadcast_to([128, 8, 128]), op=ALU.add),
                  reads=[r_tblB, r_maskCH], writes=[r_tblB])
            fw.dma("sp", cstB[:], bass.AP(relc.tensor, 0, [[0, 128], [1, 8]]), writes=[r_cstB])

        def phase_proj1(tiles, is_s, o_fk, o_fv, o_lf):
            for ti, (nt, row0, col0) in enumerate(tiles):
                gt = (row0 // 128) if not is_s else ti
                b = gbank.next()
                tm_proj(nt, col0, 512, 512, b)
                stg, r_stg = st512.next()
                fw.op("act", lambda a, b=b, stg=stg, nt=nt: a.copy(out=stg[0:nt, :], in_=ps[0:nt, b, :]), reads=[r_ps[b]], writes=[r_stg])
                if is_s:
                    fw.dma("sp", o_bk_s[ti, 448:512, :], stg[0:nt, :], reads=[r_stg])
                elif gt >= 12:
                    fw.dma("sp", o_bk_p[(gt - 12) * 128:(gt - 11) * 128, :], stg[0:nt, :], reads=[r_stg])
                b = gbank.next()
                tm_proj(nt, col0, 1024, 512, b)
                stg, r_stg = st512.next()
                fw.op("act", lambda a, b=b, stg=stg, nt=nt: a.copy(out=stg[0:nt, :], in_=ps[0:nt, b, :]), reads=[r_ps[b]], writes=[r_stg])
                if is_s:
                    fw.dma("sp", o_bv_s[ti, 448:512, :], stg[0:nt, :], reads=[r_stg])
                    fw.op("dve", lambda v, stg=stg, ti=ti, nt=nt: v.tensor_copy(out=sv1[0:nt, ti, :], in_=stg[0:nt, :]), reads=[r_stg], writes=[r_sk[1]])
                else:
                    if gt >= 12:
                        fw.dma("sp", o_bv_p[(gt - 12) * 128:(gt - 11) * 128, :], stg[0:nt, :], reads=[r_stg])
                    fw.op("dve", lambda v, stg=stg, gt=gt: v.tensor_copy(out=vc[:, gt % 8, :], in_=stg[:, :]), reads=[r_stg], writes=[r_kc[gt % 8]])
                b = gbank.next()
                tm_proj(nt, col0, 2560, 512, b)
                stg, r_stg = st512.next()
                fw.op("act", lambda a, b=b, stg=stg, nt=nt: a.copy(out=stg[0:nt, :], in_=ps[0:nt, b, :]), reads=[r_ps[b]], writes=[r_stg])
                fw.dma("sp", o_fk[row0:row0 + nt, :], stg[0:nt, :], reads=[r_stg])
                b = gbank.next()
                tm_proj(nt, col0, 3072, 512, b)
                stg, r_stg = st512.next()
                fw.op("act", lambda a, b=b, stg=stg, nt=nt: a.copy(out=stg[0:nt, :], in_=ps[0:nt, b, :]), reads=[r_ps[b]], writes=[r_stg])
                fw.dma("sp", o_fv[row0:row0 + nt, :], stg[0:nt, :], reads=[r_stg])
                if is_s:
                    fw.op("dve", lambda v, stg=stg, ti=ti, nt=nt: v.tensor_copy(out=sv2[0:nt, ti, :], in_=stg[0:nt, :]), reads=[r_stg], writes=[r_sk[3]])
                else:
                    fw.op("dve", lambda v, stg=stg, gt=gt: v.tensor_copy(out=v2[:, gt, :], in_=stg[:, :]), reads=[r_stg], writes=[r_k2[gt]])
                b = gbank.next()
                tm_proj(nt, col0, 3584, 8, b)
                lf, r_lf = lfb.next()
                fw.op("dve", lambda v, b=b, lf=lf, nt=nt: v.tensor_tensor(out=lf[0:nt, 0:8], in0=ps[0:nt, b, 0:8], in1=fb_bc[0:nt, :], op=ALU.add),
                      reads=[r_ps[b], r_fb], writes=[r_lf])
                fw.op("act", lambda a, lf=lf, nt=nt: a.activation(out=lf[0:nt, 8:16], in_=lf[0:nt, 0:8], func=AF.Exp, scale=-1.0), reads=[r_lf], writes=[r_lf])
                fw.op("act", lambda a, lf=lf, nt=nt: a.activation(out=lf[0:nt, 0:8], in_=lf[0:nt, 8:16], func=AF.Ln, bias=1.0), reads=[r_lf], writes=[r_lf])
                fw.op("dve", lambda v, lf=lf, nt=nt: v.tensor_scalar(out=lf[0:nt, 16:24], in0=lf[0:nt, 0:8], scalar1=-1.0, scalar2=None, op0=ALU.mult),
                      reads=[r_lf], writes=[r_lf])
                fw.dma("sp", o_lf[row0:row0 + nt, :], lf[0:nt, 16:24], reads=[r_lf])
                b2 = gbank.next()
                if not is_s:
                    fw.op("pe", lambda t_, b2=b2, lf=lf: t_.matmul(ps[:, b2, 0:8], lhsT=triF[:, :], rhs=lf[:, 16:24], start=True, stop=False),
                          reads=[r_tri, r_lf], writes=[r_ps[b2]])
                    fw.op("pe", lambda t_, b2=b2, lf=lf: t_.matmul(ps[:, b2, 8:16], lhsT=onesF[:, :], rhs=lf[:, 16:24], start=False, stop=True),
                          reads=[r_onesF, r_lf], writes=[r_ps[b2]])
                    fw.op("dve", lambda v, b2=b2, gt=gt: v.tensor_scalar(out=fxb[:, 0, gt, :], in0=ps[:, b2, 0:8], scalar1=-1.0, scalar2=None, op0=ALU.mult),
                          reads=[r_ps[b2]], writes=[r_fxb])
                    fw.op("dve", lambda v, b2=b2, gt=gt: v.tensor_copy(out=fxb[:, 2, gt, :], in_=ps[:, b2, 8:16]), reads=[r_ps[b2]], writes=[r_fxb])
                    fw.op("dve", lambda v, gt=gt: v.tensor_tensor(out=fxb[:, 1, gt, :], in0=fxb[:, 2, gt, :], in1=fxb[:, 0, gt, :], op=ALU.add),
                          reads=[r_fxb], writes=[r_fxb])
                else:
                    fw.op("pe", lambda t_, b2=b2, lf=lf: t_.matmul(ps[0:64, b2, 0:8], lhsT=triF[0:64, 0:64], rhs=lf[0:64, 16:24], start=True, stop=True),
                          reads=[r_tri, r_lf], writes=[r_ps[b2]])
                    fw.op("dve", lambda v, b2=b2, ti=ti: v.tensor_scalar(out=sfxn[0:64, ti, :], in0=ps[0:64, b2, 0:8], scalar1=-1.0, scalar2=None, op0=ALU.mult),
                          reads=[r_ps[b2]], writes=[r_sfxn])
            fm_group([0 + 128 * i for i in range(4)],
                     lambda i, src, rb: evac_bd(qA, r_qA, i, src, rb))
            fm_group([2048 + 128 * i for i in range(4)],
                     lambda i, src, rb: evac_bd(qBd, r_qB, i, src, rb))
            if is_s:
                fm_group([512 + 128 * i for i in range(4)],
                         lambda i, src, rb: fw.op("dve", lambda v: v.tensor_copy(out=skT1[:, i, :], in_=src), reads=[rb], writes=[r_sk[0]]))
                fm_group([2560 + 128 * i for i in range(4)],
                         lambda i, src, rb: fw.op("dve", lambda v: v.tensor_copy(out=skT2[:, i, :], in_=src), reads=[rb], writes=[r_sk[2]]))
            else:
                t0 = tiles[0][1]
                g0 = t0 // 128
                rc = (t0 % 1024)

                def evc(i, src, rb):
                    fw.op("dve", lambda v: v.tensor_copy(out=kTc[:, i, rc:rc + BLK], in_=src), reads=[rb], writes=[r_kc[g0 % 8], r_kc[(g0 + 1) % 8]])

                def evd(i, src, rb):
                    fw.op("dve", lambda v: v.tensor_copy(out=kT2[:, i, t0:t0 + BLK], in_=src), reads=[rb], writes=[r_k2[g0], r_k2[g0 + 1]])
                fm_group([512 + 128 * i for i in range(4)], evc)
                fm_group([2560 + 128 * i for i in range(4)], evd)
            fm_group([1536 + 128 * i for i in range(4)] + [3592 + 128 * i for i in range(4)],
                     lambda i, src, rb: fw.op("act", lambda a: a.activation(out=gT[:, i, :], in_=src, func=AF.Silu),
                                              reads=[rb], writes=[r_gT[i]]))

        def band_pre(u, dd, hs, nq):
            nk = u.nk
            H = hs.stop - hs.start
            if dd == 0:
                return pre_add(u, tblB[0:nk, hs, 0:nq], [r_tblB])
            if dd == 1:
                return pre_add(u, tblB[0:nk, hs, 128:128 + nq], [r_tblB])
            cst = cstB[0:nk, hs].unsqueeze(2).broadcast_to([nk, H, nq])
            if dd == 4:
                return pre_add(u, cst, [r_cstB], extra=(mask512[:, :].unsqueeze(1).broadcast_to([128, H, 128]), [r_mask512]))
            return pre_add(u, cst, [r_cstB])

        def attn1_prompt(bi, bg=None):
            all_band = []
            all_fox = []
            for qt in (2 * bi, 2 * bi + 1):
                qc = (qt % 2) * 128
                biasq = _T(biasq2[:, qt % 2])
                for kt in range(qt - 1, -1, -1):
                    if kt == qt - 1:
                        fw.op("dve", lambda v, kt=kt, biasq=biasq: v.tensor_copy(out=biasq[:, kt, :], in_=fxb[:, 1, kt, :]), reads=[r_fxb], writes=[r_biasq])
                        fw.op("dve", lambda v, kt=kt, biasq=biasq: v.tensor_copy(out=accb[:, :], in_=fxb[:, 2, kt, :]), reads=[r_fxb], writes=[r_accb])
                    else:
                        fw.op("dve", lambda v, kt=kt, biasq=biasq: v.tensor_tensor(out=biasq[:, kt, :], in0=fxb[:, 1, kt, :], in1=accb[:, :], op=ALU.add),
                              reads=[r_fxb, r_accb], writes=[r_biasq])
                        fw.op("dve", lambda v, kt=kt, biasq=biasq: v.tensor_tensor(out=accb[:, :], in0=accb[:, :], in1=fxb[:, 2, kt, :], op=ALU.add),
                              reads=[r_fxb, r_accb], writes=[r_accb])
                for hg in range(2):
                    hs = slice(4 * hg, 4 * hg + 4)
                    chain = {}
                    units = []
                    kts = list(range(qt, max(-1, qt - 5), -1))
                    for kt in kts:
                        u = U(); u.nk = 128; u.chain = chain
                        u.first = (kt == kts[0]); u.last = (kt == kts[-1])
                        u.zmm = []; u.vmm = []
                        sl = kt % 8
                        for pp in range(2):
                            u.zmm.append((kTc[:, 2 * hg + pp, sl * 128:(sl + 1) * 128], qA[:, 2 * hg + pp, qt % 2, :], pp * 256, 256))
                        for pp in range(2):
                            u.vmm.append((vc[:, sl, (2 * hg + pp) * 128:(2 * hg + pp + 1) * 128], pp * 256, 256))
                        zmm4(u)
                        u.reads = [r_kc[sl], r_qA]; u.vreads = [r_kc[sl]]
                        if (qt - kt) in (2, 3):
                            u.hbias = [(cstB[:, 4 * hg + j:4 * hg + j + 1], r_cstB) for j in range(4)]
                        u.pre = (lambda u, dd=qt - kt, hs=hs: band_pre(u, dd, hs, 128))
                        slots = [(4 * hg + s_, s_ * 128, 128, (4 * hg + s_) // 2, qc) for s_ in range(4)]
                        u.fin = (lambda ob, db, slots=slots: fin_softmax(slots, ob, db))
                        units.append(u)
                    all_band += units
                    chain = {}
                    units = []
                    for kt in range(qt, -1, -1):
                        u = U(); u.nk = 128; u.chain = chain
                        u.first = (kt == qt); u.last = (kt == 0)
                        u.zmm = []; u.vmm = []
                        for pp in range(2):
                            u.zmm.append((kT2[:, 2 * hg + pp, kt * 128:(kt + 1) * 128], qBd[:, 2 * hg + pp, qt % 2, :], pp * 256, 256))
                        for pp in range(2):
                            u.vmm.append((v2[:, kt, (2 * hg + pp) * 128:(2 * hg + pp + 1) * 128], pp * 256, 256))
                        zmm4(u)
                        u.reads = [r_k2[kt], r_qB]; u.vreads = [r_k2[kt]]
                        if kt == qt:
                            u.pre = (lambda u, qt=qt, hs=hs: pre_add(u, fxb[:, 0, qt, hs].unsqueeze(2).broadcast_to([128, 4, 128]), [r_fxb],
                                                                      extra=(maskFX[:, :].unsqueeze(1).broadcast_to([128, 4, 128]), [r_maskFX])))
                        else:
                            if kt % 2 == 1:
                                u.hbias = [(biasq[:, kt, 4 * hg + j:4 * hg + j + 1], r_biasq) for j in range(4)]
                            u.pre = (lambda u, kt=kt, hs=hs, biasq=biasq: pre_add(u, biasq[:, kt, hs].unsqueeze(2).broadcast_to([128, 4, 128]), [r_biasq]))
                        slots = [(4 * hg + s_, s_ * 128, 128, 4 + (4 * hg + s_) // 2, qc) for s_ in range(4)]
                        u.fin = (lambda ob, db, slots=slots: fin_softmax(slots, ob, db))
                        units.append(u)
                    all_fox += units
            sm_chain(all_band, bg=bg)
            sm_chain(all_fox)

        def attn1_sample():
            hs8 = slice(0, 8)
            for s in range(NSTR):
                fw.dma("sp", o_bk_s[s, 0:448, :], c_bk[s, 64:512, :])
                fw.dma("sp", o_bv_s[s, 0:448, :], c_bv[s, 64:512, :])
                slots = [(h, h * 64, 64, h // 2, 64 * s) for h in range(8)]
                units = kv_units(s, 4, c_bk, c_bv, qA, r_qA, skT1, sv1, r_sk[0], slots)
                units[0].reads = [r_sk[0], qbd[s][1]]; units[0].vreads = [r_sk[1]]
                zmm4(units[0])
                units[0].pre = (lambda u: band_pre(u, 0, hs8, 64))
                for u in units[1:]:
                    dd = 4 - u.tile
                    op_ = u.prep

                    def prep2(u, op_=op_):
                        op_(u)
                        zmm4(u)
                    u.prep = prep2
                    u.pre = (lambda u, dd=dd: band_pre(u, 1 if dd == 1 else 2, hs8, 64))
                for u in units:
                    u.fin = (lambda ob, db, slots=slots: fin_softmax(slots, ob, db))
                sm_chain(units)
            for s in range(NSTR):
                fw.dma("sp", lfc[:], c_lf[s].rearrange("(t p) h -> p t h", p=128), writes=[r_lfc])
                lf2 = lfc[:].rearrange("p t h -> p (t h)")
                b2 = gbank.next()
                fw.op("pe", lambda t_, b2=b2: t_.matmul(ps[:, b2, 0:256], lhsT=triF[:, :], rhs=lf2, start=True, stop=False),
                      reads=[r_tri, r_lfc], writes=[r_ps[b2]])
                fw.op("pe", lambda t_, b2=b2: t_.matmul(ps[:, b2, 256:512], lhsT=onesF[:, :], rhs=lf2, start=False, stop=True),
                      reads=[r_onesF, r_lfc], writes=[r_ps[b2]])
                fw.op("dve", lambda v, b2=b2: v.tensor_copy(out=lf2, in_=ps[:, b2, 256:512]), reads=[r_ps[b2]], writes=[r_lfc])
                sf2 = sfx[:, 0:32, :].rearrange("p t h -> p (t h)")
                fw.op("dve", lambda v, b2=b2: v.tensor_tensor(out=sf2, in0=lf2, in1=ps[:, b2, 0:256], op=ALU.subtract),
                      reads=[r_ps[b2], r_lfc], writes=[r_sfx])
                for t in range(30, -1, -1):
                    if t == 30:
                        fw.op("dve", lambda v: v.tensor_copy(out=accb[:, :], in_=lfc[:, 31, :]), reads=[r_lfc], writes=[r_accb])
                    else:
                        fw.op("dve", lambda v, t=t: v.tensor_tensor(out=accb[:, :], in0=accb[:, :], in1=lfc[:, t + 1, :], op=ALU.add),
                              reads=[r_lfc, r_accb], writes=[r_accb])
                    fw.op("dve", lambda v, t=t: v.tensor_tensor(out=sfx[:, t, :], in0=sfx[:, t, :], in1=accb[:, :], op=ALU.add),
                          reads=[r_sfx, r_accb], writes=[r_sfx])
                slots = [(h, h * 64, 64, 4 + h // 2, 64 * s) for h in range(8)]
                units = kv_units(s, PAST // 128, c_fk, c_fv, qBd, r_qB, skT2, sv2, r_sk[2], slots)
                units[0].reads = [r_sk[2], qbd[s][1]]; units[0].vreads = [r_sk[3]]
                zmm4(units[0])
                units[0].pre = (lambda u, s=s: pre_add(u, sfxn[0:64, s, :].unsqueeze(2).broadcast_to([64, 8, 64]), [r_sfxn],
                                                       extra=(maskFX[0:64, 0:64].unsqueeze(1).broadcast_to([64, 8, 64]), [r_maskFX])))
                for u in units[1:]:
                    op_ = u.prep

                    def prep3(u, op_=op_):
                        op_(u)
                        zmm4(u)
                    u.prep = prep3
                    u.pre = (lambda u: pre_add(u, sfx[:, u.tile, :].unsqueeze(2).broadcast_to([128, 8, 64]), [r_sfx]))
                for u in units:
                    u.fin = (lambda ob, db, slots=slots: fin_softmax(slots, ob, db))
                sm_chain(units)

        def phase_out(l, tiles, xsrc, r_xsrc, ydst, r_ydst):
            for (nt, row0, col0) in tiles:
                b0 = gbank.next(); b1 = gbank.next()
                for half, b in ((0, b0), (1, b1)):
                    for c in range(8):
                        fw.op("pe", lambda t, c=c, b=b, half=half, nt=nt, col0=col0: t.matmul(ps[0:nt, b, :], lhsT=gT[:, c, col0:col0 + nt],
                                                                                               rhs=wout[:, c, half * 512:(half + 1) * 512],
                                                                                               start=(c == 0), stop=(c == 7)),
                              reads=[r_gT[c], r_wout], writes=[r_ps[b]])
                s, r_s = sm.next()
                fw.op("act", lambda a, s=s, nt=nt, b0=b0: a.activation(out=junk[0:nt, :], in_=ps[0:nt, b0, :], func=AF.Square, accum_out=s[0:nt, 4:5]),
                      reads=[r_ps[b0]], writes=[r_junk, r_s])
                fw.op("act", lambda a, s=s, nt=nt, b1=b1: a.activation(out=junk[0:nt, :], in_=ps[0:nt, b1, :], func=AF.Square, accum_out=s[0:nt, 5:6]),
                      reads=[r_ps[b1]], writes=[r_junk, r_s])
                fw.op("dve", lambda v, s=s, nt=nt: v.tensor_tensor(out=s[0:nt, 2:3], in0=s[0:nt, 4:5], in1=s[0:nt, 5:6], op=ALU.add), reads=[r_s], writes=[r_s])
                rs, r_rs = rstd_from_ss(s[0:nt, 2:3], D, nt, r_s)
                xt, r_xt = xin.next()
                fw.dma("act", xt[0:nt, :], xsrc[row0:row0 + nt, :], reads=[r_xsrc[row0 // 64]] if r_xsrc else [], writes=[r_xt])
                for half, b in ((0, b0), (1, b1)):
                    y, r_y = tmpf.next()
                    fw.op("dve", lambda v, half=half, b=b, y=y, rs=rs, nt=nt: v.scalar_tensor_tensor(
                        out=y[0:nt, :], in0=ps[0:nt, b, :], scalar=rs, in1=gpost[0:nt, half * 512:(half + 1) * 512],
                        op0=ALU.mult, op1=ALU.mult), reads=[r_ps[b], r_rs, r_gpost], writes=[r_y])
                    fw.op("pool", lambda g, y=y, xt=xt, nt=nt, half=half: g.tensor_tensor(out=xt[0:nt, half * 512:(half + 1) * 512], in0=y[0:nt, :],
                                                                                          in1=xt[0:nt, half * 512:(half + 1) * 512], op=ALU.add),
                          reads=[r_y, r_xt], writes=[r_xt])
                fw.dma("sp", ydst[row0:row0 + nt, :], xt[0:nt, :], reads=[r_xt], writes=[r_ydst[row0 // 64]] if r_ydst else [])

        r_x1p = [Res() for _ in range(SEQ // 64)]
        r_x1s = [Res() for _ in range(NSTR * DSEQ // 64)]
        ptiles = lambda bi: [(128, bi * BLK, 0), (128, bi * BLK + 128, 128)]
        stiles = [(64, 64 * s, 64 * s) for s in range(NSTR)]

        load_layer_weights(0)
        nblk = min(SEQ // BLK, NBLK_DBG)
        for bi in range(nblk):
            if bi == 0:
                phase_norm(0, ptiles(bi), xp, None)
            phase_proj0(ptiles(bi), False, o_ckv_p, o_kr_p, o_sbk_p, o_sbv_p)
            bg = phase_norm_gen(0, ptiles(bi + 1), xp, None) if bi + 1 < nblk else phase_norm_gen(0, stiles, xs, None)
            attn0_prompt(bi, bg)
            phase_out(0, ptiles(bi), xp, None, x1p, r_x1p)
        if STAGES >= 1:
            fw.barrier()
            phase_proj0(stiles, True, o_ckv_s, o_kr_s, o_sbk_s, o_sbv_s)
            if STAGES >= 3:
                CAST_ENGS[0] = ("dve", "act", "dve", "pool")
                attn0_sample()
                CAST_ENGS[0] = ("pool", "dve", "act")
            phase_out(0, stiles, xs, None, x1s, r_x1s)
        if STAGES >= 4:
            load_layer_weights(1)
            setup_l1_tables()
            fw.op("pool", lambda g: g.memset(qBd[:].rearrange("p a t q -> p (a t q)"), 0.0), writes=[r_qB])
            fw.barrier()
            for bi in range(nblk):
                if bi == 0:
                    phase_norm(1, ptiles(bi), x1p, r_x1p)
                phase_proj1(ptiles(bi), False, o_fk_p, o_fv_p, o_lf_p)
                bg = phase_norm_gen(1, ptiles(bi + 1), x1p, r_x1p) if bi + 1 < nblk else phase_norm_gen(1, stiles, x1s, r_x1s)
                attn1_prompt(bi, bg)
                phase_out(1, ptiles(bi), x1p, r_x1p, y_p, None)
            fw.barrier()
            phase_proj1(stiles, True, o_fk_s, o_fv_s, o_lf_s)
            if STAGES >= 6:
                CAST_ENGS[0] = ("act", "pool", "act")
                EVAC_ENGS[0] = ("dve", "act")
                attn1_sample()
            phase_out(1, stiles, x1s, r_x1s, y_s, None)

        print("fw ops recorded:", getattr(fw, "nops", 0), {k: e.cnt for k, e in fw.E.items()})
        fw.finish()
        fw.emit()
    return nc


def _rope_tables():
    half = 16
    inv = (10000.0 ** (-np.arange(half, dtype=np.float32) / half)).astype(np.float32)

    def tab(pos):
        ang = pos.astype(np.float32)[:, None] * inv[None, :]
        c = np.cos(ang).astype(np.float32)
        s = np.sin(ang).astype(np.float32)
        return np.concatenate([c, c, -s, s], axis=1).astype(np.float32)
    tp = tab(np.arange(SEQ)).reshape(16, 128, 64).transpose(1, 0, 2).reshape(128, 16 * 64)
    tsm = tab(PAST + np.arange(DSEQ))
    return np.ascontiguousarray(tp), np.ascontiguousarray(tsm)


_NC_CACHE = {}


def kernel(**inp):
    f = lambda a: np.ascontiguousarray(np.asarray(a, dtype=np.float32))
    x_prompt = f(inp["x_prompt"]); x_sample = f(inp["x_sample"])
    rope_p, rope_s = _rope_tables()
    w_uq = f(inp["a_w_uq"])[0]
    w_uq_l = np.concatenate([w_uq[:, :, :64].reshape(256, 512), w_uq[:, :, 64:].reshape(256, 256)], axis=1)
    w_uk = f(inp["a_w_uk"])[0]
    w_ukT = np.transpose(w_uk, (2, 1, 0)).reshape(64, 1024)
    w_ukT = np.concatenate([w_ukT, w_ukT], axis=0)
    relb = f(inp["c_rel_bias"])[0]
    relb_pad = np.concatenate([relb, np.repeat(relb[:, -1:], 256, axis=1)], axis=1)
    shared = {
        "norm_pre": np.ascontiguousarray(f(inp["norm_pre"]).reshape(2, 8, 128).transpose(2, 0, 1).reshape(128, 16)), "norm_post": f(inp["norm_post"]),
        "w_in0": f(inp["w_in_even"])[0], "q_norm": f(inp["a_q_norm"]).reshape(1, 256),
        "w_uq": np.ascontiguousarray(w_uq_l), "kv_norm": f(inp["a_kv_norm"]).reshape(1, 128),
        "w_ukT": np.ascontiguousarray(w_ukT), "w_uv": f(inp["a_w_uv"])[0].reshape(128, 512),
        "w_out0": f(inp["w_out_even"])[0], "w_in1": f(inp["w_in_odd"])[0],
        "relb": np.ascontiguousarray(relb_pad), "fbias": f(inp["d_forget_bias"]).reshape(1, 8),
        "relc": np.ascontiguousarray(relb[:, 256].reshape(1, 8)),
        "w_out1": f(inp["w_out_odd"])[0], "rope_p": rope_p, "rope_s": rope_s,
    }
    caches = {k: f(inp[k])[0] for k in ("cache_mla_ckv", "cache_mla_krope", "cache_sb_k", "cache_sb_v", "cache_band_k",
                                        "cache_band_v", "cache_fox_k", "cache_fox_v", "cache_fox_logf")}
    in_maps = []
    for c in range(NCORES):
        sl = slice(NSTR * c, NSTR * (c + 1))
        m = dict(shared)
        m["xp"] = x_prompt[c]
        m["xs"] = x_sample[sl].reshape(NSTR * DSEQ, D)
        m["c_ckv"] = caches["cache_mla_ckv"][sl]
        m["c_kr"] = caches["cache_mla_krope"][sl]
        m["c_sbk"] = caches["cache_sb_k"][sl].reshape(NSTR, PAST, 512)
        m["c_sbv"] = caches["cache_sb_v"][sl].reshape(NSTR, PAST, 512)
        m["c_bk"] = caches["cache_band_k"][sl].reshape(NSTR, 512, 512)
        m["c_bv"] = caches["cache_band_v"][sl].reshape(NSTR, 512, 512)
        m["c_fk"] = caches["cache_fox_k"][sl].reshape(NSTR, PAST, 512)
        m["c_fv"] = caches["cache_fox_v"][sl].reshape(NSTR, PAST, 512)
        m["c_lf"] = caches["cache_fox_logf"][sl]
        in_maps.append({k: np.ascontiguousarray(v) for k, v in m.items()})
    if "nc" not in _NC_CACHE:
        _NC_CACHE["nc"] = build()
    nc = _NC_CACHE["nc"]
    if KCORES < NCORES:
        res = run_bass_kernel_spmd(nc, in_maps[:KCORES], core_ids=list(range(KCORES)))
        R = list(res.results) + [res.results[0]] * (NCORES - KCORES)
    else:
        res = run_bass_kernel_spmd(nc, in_maps, core_ids=list(range(NCORES)))
        R = res.results
    cat = lambda k: np.stack([R[c][k] for c in range(NCORES)], axis=0)
    B = NCORES
    SB = NCORES * NSTR
    outs = (
        cat("y_p").reshape(B, SEQ, D),
        cat("y_s").reshape(SB, DSEQ, D),
        cat("o_ckv_p").reshape(1, B, SEQ, 128), cat("o_kr_p").reshape(1, B, SEQ, 32),
        cat("o_sbk_p").reshape(1, B, SEQ, 8, 64), cat("o_sbv_p").reshape(1, B, SEQ, 8, 64),
        cat("o_bk_p").reshape(1, B, 512, 8, 64), cat("o_bv_p").reshape(1, B, 512, 8, 64),
        cat("o_fk_p").reshape(1, B, SEQ, 8, 64), cat("o_fv_p").reshape(1, B, SEQ, 8, 64), cat("o_lf_p").reshape(1, B, SEQ, 8),
        cat("o_ckv_s").reshape(1, SB, DSEQ, 128), cat("o_kr_s").reshape(1, SB, DSEQ, 32),
        cat("o_sbk_s").reshape(1, SB, DSEQ, 8, 64), cat("o_sbv_s").reshape(1, SB, DSEQ, 8, 64),
        cat("o_bk_s").reshape(1, SB, 512, 8, 64), cat("o_bv_s").reshape(1, SB, 512, 8, 64),
        cat("o_fk_s").reshape(1, SB, DSEQ, 8, 64), cat("o_fv_s").reshape(1, SB, DSEQ, 8, 64), cat("o_lf_s").reshape(1, SB, DSEQ, 8),
    )
    _NC_CACHE["x1"] = (cat("x1p"), cat("x1s"))
    return tuple(np.ascontiguousarray(o.astype(np.float32)) for o in outs)
```
